# Optimizing a Trainium2 kernel written in Bass

```python
import functools
import math
import jax
import jax.numpy as jnp
from jax import lax
import numpy as np

D_MODEL = 1024
BATCH = 4
SEQ = 4096
DEPTH = 4

GRID_W = 64
CTX_LEN = 256
CHUNK = 64
CONV_K = 3
HEAD_DIM = D_MODEL // 8
GDN_HEADS = 4
RET_HEADS = 4
MLSTM_HEADS = 4
GDN_W = GDN_HEADS * HEAD_DIM
RET_W = RET_HEADS * HEAD_DIM
MLSTM_W = MLSTM_HEADS * HEAD_DIM
S5_CH = D_MODEL // 2
S5_GROUP = 16
S5_GROUPS = S5_CH // S5_GROUP
S5_STATE = 64
N_EXPERTS = 16
EXPERT_FF = 2 * D_MODEL
CAPACITY_FACTOR = 2
ALPHA = (2 * DEPTH) ** 0.25
BETA = (8 * DEPTH) ** -0.25
N_EVEN = (DEPTH + 1) // 2
N_ODD = DEPTH // 2
EPS = 1e-5
EVEN_COLS = (GDN_W, GDN_W, GDN_W, GDN_W, GDN_HEADS, GDN_HEADS, GDN_HEADS, GDN_HEADS,
             RET_W, RET_W, RET_W, RET_W)
ODD_COLS = (MLSTM_W, MLSTM_W, MLSTM_W, MLSTM_W, MLSTM_HEADS, MLSTM_HEADS, MLSTM_HEADS, MLSTM_HEADS, S5_CH)
EVEN_IN = sum(EVEN_COLS)
ODD_IN = sum(ODD_COLS)
EVEN_MIX = GDN_W + RET_W
ODD_MIX = MLSTM_W + S5_CH
F32 = jnp.float32

kernel_name = 'hybrid_gdn_retention_mlstm_s5_ecmoe_dit'


def split_cols(t, sizes):
    return jnp.split(t, np.cumsum(sizes)[:-1].tolist(), axis=-1)


def heads(t, n):
    return t.reshape(t.shape[:-1] + (n, t.shape[-1] // n))


def flip_t(t):
    return None if t is None else jnp.flip(t, axis=1)


def l2norm(t):
    return t * lax.rsqrt(jnp.sum(t * t, axis=-1, keepdims=True) + 1e-6)


def layer_norm(t, g, b):
    tf = t.astype(F32)
    mu = tf.mean(-1, keepdims=True)
    var = jnp.square(tf - mu).mean(-1, keepdims=True)
    return ((tf - mu) * lax.rsqrt(var + EPS)).astype(t.dtype) * g + b


def rms_norm_heads(o, g):
    y = o * lax.rsqrt(jnp.mean(o * o, axis=-1, keepdims=True) + EPS) * g
    return y.reshape(o.shape[:2] + (-1,))


def group_norm_heads(o, g):
    mu = o.mean(-1, keepdims=True)
    var = jnp.square(o - mu).mean(-1, keepdims=True)
    return ((o - mu) * lax.rsqrt(var + EPS)).reshape(o.shape[:2] + (-1,)) * g


def short_conv(t, w, on_grid):
    ch = t.shape[-1]
    w = w.astype(t.dtype)
    if on_grid:
        b, n = t.shape[:2]
        rows = n // GRID_W
        tg = t.reshape(b, rows, GRID_W, ch)
        y = lax.conv_general_dilated(tg, w[:, :, None, :], (1, 1), 'SAME',
                                     dimension_numbers=('NHWC', 'HWIO', 'NHWC'), feature_group_count=ch)
        return y.reshape(b, n, ch)
    return lax.conv_general_dilated(t, w[CONV_K // 2][:, None, :], (1,), 'SAME',
                                    dimension_numbers=('NWC', 'WIO', 'NWC'), feature_group_count=ch)


def to_chunks(t):
    b, n, h = t.shape[:3]
    t = t.reshape((b, n // CHUNK, CHUNK, h) + t.shape[3:])
    return jnp.moveaxis(jnp.moveaxis(t, 3, 1), 2, 0)


def from_chunks(t):
    t = jnp.moveaxis(jnp.moveaxis(t, 0, 2), 1, 3)
    return t.reshape((t.shape[0], -1) + t.shape[3:])


def linear_scan(q, k, v, g, beta, s0, want_out):
    b, _, h, dk = k.shape
    dv = v.shape[-1]
    kc, vc = to_chunks(k), to_chunks(v)
    gcum = jnp.cumsum(to_chunks(g), axis=-1)
    diff = gcum[..., :, None] - gcum[..., None, :]
    incl = jnp.tril(jnp.ones((CHUNK, CHUNK), bool))
    if beta is None:
        u, w = vc, None
    else:
        bc = to_chunks(beta)[..., None]
        kb = kc * bc
        strict = jnp.tril(jnp.ones((CHUNK, CHUNK), bool), -1)
        a = jnp.where(strict, jnp.einsum('...ik,...jk->...ij', kb, kc) * jnp.exp(jnp.where(strict, diff, 0.0)), 0.0)
        rhs = jnp.concatenate([vc * bc, kb * jnp.exp(gcum)[..., None]], axis=-1)
        sol = lax.linalg.triangular_solve(a + jnp.eye(CHUNK, dtype=a.dtype), rhs, left_side=True,
                                          lower=True, unit_diagonal=True)
        u, w = sol[..., :dv], sol[..., dv:]
    k_end = kc * jnp.exp(gcum[..., -1:] - gcum)[..., None]
    g_end = jnp.exp(gcum[..., -1])[..., None, None]
    if want_out:
        qc = to_chunks(q)
        q_dec = qc * jnp.exp(gcum)[..., None]
        a_qk = jnp.where(incl, jnp.einsum('...ik,...jk->...ij', qc, kc) * jnp.exp(jnp.where(incl, diff, 0.0)), 0.0)
    else:
        q_dec, a_qk = None, None
    if s0 is None:
        s0 = jnp.zeros((b, h, dk, dv), F32)

    def step(s, inp):
        qd, ke, uc, wc, aqk, ge = inp
        vn = uc if wc is None else uc - jnp.einsum('bhck,bhkv->bhcv', wc, s)
        s_new = ge * s + jnp.einsum('bhck,bhcv->bhkv', ke, vn)
        if qd is None:
            return s_new, None
        return s_new, jnp.einsum('bhck,bhkv->bhcv', qd, s) + jnp.einsum('bhij,bhjv->bhiv', aqk, vn)

    s_fin, o = lax.scan(step, s0, (q_dec, k_end, u, w, a_qk, g_end))
    return (from_chunks(o) if want_out else None), s_fin


def mlstm_scan(q, k, v, log_i, log_f, s0, want_out):
    b, _, h, dk = k.shape
    dv = v.shape[-1]
    if s0 is None:
        s0 = (jnp.zeros((b, h, dk, dv), F32), jnp.zeros((b, h, dk), F32), jnp.zeros((b, h), F32))
    incl = jnp.tril(jnp.ones((CHUNK, CHUNK), bool))

    def step(carry, inp):
        c_prev, n_prev, m_prev = carry
        qc, kc, vc, ic, fc = inp
        bcum = jnp.cumsum(fc, axis=-1)
        b_end = bcum[..., -1]
        a = b_end[..., None] - bcum + ic
        m_new = jnp.maximum(b_end + m_prev, a.max(-1))
        w_state = jnp.exp(a - m_new[..., None])
        decay = jnp.exp(b_end + m_prev - m_new)
        c_new = decay[..., None, None] * c_prev + jnp.einsum('bhck,bhcv->bhkv', kc * w_state[..., None], vc)
        n_new = decay[..., None] * n_prev + jnp.einsum('bhck,bhc->bhk', kc, w_state)
        carry_new = (c_new, n_new, m_new)
        if qc is None:
            return carry_new, None
        dlog = jnp.where(incl, bcum[..., :, None] - bcum[..., None, :] + ic[..., None, :], -jnp.inf)
        inter = bcum + m_prev[..., None]
        m_t = jnp.maximum(inter, dlog.max(-1))
        s = jnp.einsum('bhik,bhjk->bhij', qc, kc) * jnp.exp(dlog - m_t[..., None])
        w_inter = jnp.exp(inter - m_t)[..., None]
        num = jnp.einsum('bhij,bhjv->bhiv', s, vc) + w_inter * jnp.einsum('bhik,bhkv->bhiv', qc, c_prev)
        den = s.sum(-1, keepdims=True) + w_inter * jnp.einsum('bhik,bhk->bhi', qc, n_prev)[..., None]
        return carry_new, num / jnp.maximum(jnp.abs(den), jnp.exp(-m_t)[..., None])

    xs = (to_chunks(q) if want_out else None, to_chunks(k), to_chunks(v), to_chunks(log_i), to_chunks(log_f))
    s_fin, hs = lax.scan(step, s0, xs)
    return (from_chunks(hs) if want_out else None), s_fin


def s5_discretize(lam_re, lam_im, log_dt, b_re, b_im):
    lr = jnp.minimum(lam_re.astype(F32), -1e-4)
    li = lam_im.astype(F32)
    dt = jnp.exp(log_dt.astype(F32))[:, None]
    mag = jnp.exp(lr * dt)
    ab_re, ab_im = mag * jnp.cos(li * dt), mag * jnp.sin(li * dt)
    xr, xi, den = ab_re - 1.0, ab_im, lr * lr + li * li
    f_re = (xr * lr + xi * li) / den
    f_im = (xi * lr - xr * li) / den
    br, bi = b_re.astype(F32), b_im.astype(F32)
    bb_re = f_re[..., None] * br - f_im[..., None] * bi
    bb_im = f_re[..., None] * bi + f_im[..., None] * br
    return ab_re, ab_im, bb_re, bb_im


def complex_affine_combine(e1, e2):
    a1r, a1i, b1r, b1i = e1
    a2r, a2i, b2r, b2i = e2
    return (a1r * a2r - a1i * a2i, a1r * a2i + a1i * a2r,
            a2r * b1r - a2i * b1i + b2r, a2r * b1i + a2i * b1r + b2i)


def s5_scan(ab_re, ab_im, bb_re, bb_im, c_re, c_im, u, s0, want_out):
    bu_re = jnp.einsum('btgh,gph->btgp', u, bb_re)
    bu_im = jnp.einsum('btgh,gph->btgp', u, bb_im)
    if s0 is not None:
        h0r, h0i = s0
        bu_re = bu_re.at[:, 0].add(ab_re * h0r - ab_im * h0i)
        bu_im = bu_im.at[:, 0].add(ab_re * h0i + ab_im * h0r)
    n = u.shape[1]
    a_re = jnp.broadcast_to(ab_re, (1, n) + ab_re.shape)
    a_im = jnp.broadcast_to(ab_im, (1, n) + ab_im.shape)
    _, _, hr, hi = lax.associative_scan(complex_affine_combine, (a_re, a_im, bu_re, bu_im), axis=1)
    s_fin = (hr[:, -1], hi[:, -1])
    if not want_out:
        return None, s_fin
    y = jnp.einsum('btgp,ghp->btgh', hr, c_re) - jnp.einsum('btgp,ghp->btgh', hi, c_im)
    return y, s_fin


def two_pass(scan_fn, ctx_args, lat_args, ctx_out, reverse):
    if reverse:
        ctx_args = [flip_t(t) for t in ctx_args]
        lat_args = [flip_t(t) for t in lat_args]
    o_ctx, s_ctx = scan_fn(*ctx_args, None, ctx_out)
    o_lat, _ = scan_fn(*lat_args, s_ctx, True)
    if reverse:
        o_ctx, o_lat = flip_t(o_ctx), flip_t(o_lat)
    return o_ctx, o_lat


def retention_log_decay(direction):
    expo = 5.0 + 2.0 * jnp.arange(RET_HEADS, dtype=F32) + direction
    return jnp.log1p(-jnp.exp2(-expo))


def gdn_retention_mixer(h_ctx, h_lat, w_in, w_out, conv_w, a_log, dt_bias, gdn_gain, ret_gain, ctx_out):
    def prep(h, on_grid):
        qa, ka, va, za, af, ab, bf, bb, qr, kr, vr, zr = split_cols(h @ w_in, EVEN_COLS)
        qkv = jax.nn.silu(short_conv(jnp.concatenate([qa, ka, va], -1), conv_w, on_grid)).astype(F32)
        qa, ka, va = split_cols(qkv, EVEN_COLS[:3])
        q_g = l2norm(heads(qa, GDN_HEADS)) * HEAD_DIM ** -0.5
        k_g = l2norm(heads(ka, GDN_HEADS))
        v_g = heads(va, GDN_HEADS)
        gdn_dirs = []
        for d, (a_pre, b_pre) in enumerate(((af, bf), (ab, bb))):
            g = -jnp.exp(a_log[d].astype(F32)) * jax.nn.softplus(a_pre.astype(F32) + dt_bias[d].astype(F32))
            gdn_dirs.append((q_g, k_g, v_g, g, jax.nn.sigmoid(b_pre.astype(F32))))
        q_r = heads(qr, RET_HEADS).astype(F32)
        k_r = heads(kr, RET_HEADS).astype(F32) * HEAD_DIM ** -0.5
        v_r = heads(vr, RET_HEADS).astype(F32)
        ret_dirs = [(q_r, k_r, v_r, jnp.broadcast_to(retention_log_decay(d), v_r.shape[:3]), None) for d in range(2)]
        return gdn_dirs, ret_dirs, za, zr

    gc, rc, za_c, zr_c = prep(h_ctx, False)
    gl, rl, za_l, zr_l = prep(h_lat, True)
    gdn = [two_pass(linear_scan, gc[d], gl[d], ctx_out, d == 1) for d in range(2)]
    ret = [two_pass(linear_scan, rc[d], rl[d], ctx_out, d == 1) for d in range(2)]

    def merge(o_g, o_r, za, zr, dtype):
        y_g = rms_norm_heads(o_g, gdn_gain) * jax.nn.silu(za.astype(F32))
        y_r = group_norm_heads(o_r, ret_gain) * jax.nn.silu(zr.astype(F32))
        return jnp.concatenate([y_g, y_r], -1).astype(dtype) @ w_out

    y_lat = merge(gdn[0][1] + gdn[1][1], ret[0][1] + ret[1][1], za_l, zr_l, h_lat.dtype)
    y_ctx = merge(gdn[0][0] + gdn[1][0], ret[0][0] + ret[1][0], za_c, zr_c, h_ctx.dtype) if ctx_out else None
    return y_ctx, y_lat


def mlstm_s5_mixer(h_ctx, h_lat, w_in, w_out, conv_w, gate_bias, mlstm_gain, lam_re, lam_im, log_dt,
                   b_re, b_im, c_re, c_im, d_skip, w_glu, b_glu, ctx_out):
    gb = gate_bias.astype(F32)

    def prep(h, on_grid):
        qm, km, vm, om, i_f, i_b, f_f, f_b, u = split_cols(h @ w_in, ODD_COLS)
        qk = jax.nn.silu(short_conv(jnp.concatenate([qm, km], -1), conv_w, on_grid)).astype(F32)
        qm, km = split_cols(qk, ODD_COLS[:2])
        q = heads(qm, MLSTM_HEADS)
        k = heads(km, MLSTM_HEADS) * HEAD_DIM ** -0.5
        v = heads(vm, MLSTM_HEADS).astype(F32)
        dirs = [(q, k, v, i_pre.astype(F32) + gb[d, 0], jax.nn.log_sigmoid(f_pre.astype(F32) + gb[d, 1]))
                for d, (i_pre, f_pre) in enumerate(((i_f, f_f), (i_b, f_b)))]
        u = u.astype(F32)
        return dirs, u, om

    mc, u_c, o_c = prep(h_ctx, False)
    ml, u_l, o_l = prep(h_lat, True)
    grp = lambda t: t.reshape(t.shape[:2] + (S5_GROUPS, S5_GROUP))
    cr, ci = c_re.astype(F32), c_im.astype(F32)
    mls = [two_pass(mlstm_scan, mc[d], ml[d], ctx_out, d == 1) for d in range(2)]
    ssm = [two_pass(functools.partial(s5_scan, *s5_discretize(lam_re[d], lam_im[d], log_dt[d], b_re, b_im), cr, ci),
                    [grp(u_c)], [grp(u_l)], ctx_out, d == 1) for d in range(2)]

    def merge(h_m, y_s, u, og, dtype):
        y_m = group_norm_heads(h_m, mlstm_gain) * jax.nn.sigmoid(og.astype(F32))
        y = jax.nn.gelu(y_s.reshape(u.shape) + d_skip * u)
        y = y * jax.nn.sigmoid(y @ w_glu + b_glu)
        return jnp.concatenate([y_m, y], -1).astype(dtype) @ w_out

    y_lat = merge(mls[0][1] + mls[1][1], ssm[0][1] + ssm[1][1], u_l, o_l, h_lat.dtype)
    y_ctx = merge(mls[0][0] + mls[1][0], ssm[0][0] + ssm[1][0], u_c, o_c, h_ctx.dtype) if ctx_out else None
    return y_ctx, y_lat


def expert_choice_ffn(h, w_router, w_gate, w_up, w_down):
    n, dm = h.shape[1], h.shape[2]
    cap = CAPACITY_FACTOR * n // N_EXPERTS
    aff = jax.nn.softmax((h @ w_router).astype(F32), axis=-1)
    gate, idx = lax.top_k(jnp.swapaxes(aff, 1, 2), cap)
    xs = jax.vmap(lambda hb, ib: hb[ib])(h, idx)
    hid = jax.nn.silu(jnp.einsum('becd,edf->becf', xs, w_gate)) * jnp.einsum('becd,edf->becf', xs, w_up)
    ys = jnp.einsum('becf,efd->becd', hid, w_down) * gate[..., None].astype(h.dtype)
    return jax.vmap(lambda ib, yb: jnp.zeros((n, dm), yb.dtype).at[ib.reshape(-1)].add(yb.reshape(-1, dm)))(idx, ys)


def setup_inputs(seed: int = 0) -> dict:
    key = jax.random.key(seed)
    ks = iter(jax.random.split(key, 48))

    def nrm(shape, scale):
        return jax.random.normal(next(ks), shape, jnp.float32) * scale

    def unif(shape, lo, hi):
        return jax.random.uniform(next(ks), shape, jnp.float32, lo, hi)

    D = D_MODEL
    L = DEPTH
    dt_gdn = jnp.exp(unif((N_EVEN, 2, GDN_HEADS), math.log(1e-3), math.log(1e-1)))
    gate_bias = jnp.stack([nrm((N_ODD, 2, MLSTM_HEADS), 0.1),
                           jnp.linspace(3.0, 6.0, MLSTM_HEADS) + nrm((N_ODD, 2, MLSTM_HEADS), 0.1)], axis=2)
    lam_im = jnp.pi * jnp.arange(S5_STATE, dtype=jnp.float32) + nrm((N_ODD, 2, S5_GROUPS, S5_STATE), 0.01)
    return {
        'x': nrm((BATCH, SEQ, D), 1.0),
        'c': nrm((BATCH, D), 1.0),
        'ctx': nrm((BATCH, CTX_LEN, D), 1.0),
        'c_ctx': nrm((D,), 1.0),
        'w_mod': nrm((L, D, 6 * D), 0.5 * D ** -0.5),
        'b_mod': nrm((L, 6 * D), 0.02),
        'ln1_g': 1.0 + nrm((L, D), 0.02),
        'ln1_b': nrm((L, D), 0.02),
        'ln2_g': 1.0 + nrm((L, D), 0.02),
        'ln2_b': nrm((L, D), 0.02),
        'w_router': nrm((L, D, N_EXPERTS), D ** -0.5),
        'w_gate': nrm((L, N_EXPERTS, D, EXPERT_FF), D ** -0.5),
        'w_up': nrm((L, N_EXPERTS, D, EXPERT_FF), D ** -0.5),
        'w_down': nrm((L, N_EXPERTS, EXPERT_FF, D), BETA * EXPERT_FF ** -0.5),
        'ev_w_in': nrm((N_EVEN, D, EVEN_IN), D ** -0.5),
        'ev_w_out': nrm((N_EVEN, EVEN_MIX, D), BETA * EVEN_MIX ** -0.5),
        'ev_conv': nrm((N_EVEN, CONV_K, CONV_K, 3 * GDN_W), 1.0 / CONV_K),
        'ev_a_log': jnp.log(unif((N_EVEN, 2, GDN_HEADS), 1.0, 16.0)),
        'ev_dt_bias': dt_gdn + jnp.log(-jnp.expm1(-dt_gdn)),
        'ev_gdn_norm': 1.0 + nrm((N_EVEN, HEAD_DIM), 0.02),
        'ev_ret_norm': 1.0 + nrm((N_EVEN, RET_W), 0.02),
        'od_w_in': nrm((N_ODD, D, ODD_IN), D ** -0.5),
        'od_w_out': nrm((N_ODD, ODD_MIX, D), BETA * ODD_MIX ** -0.5),
        'od_conv': nrm((N_ODD, CONV_K, CONV_K, 2 * MLSTM_W), 1.0 / CONV_K),
        'od_gate_bias': gate_bias,
        'od_mlstm_norm': 1.0 + nrm((N_ODD, MLSTM_W), 0.02),
        'od_lam_re': -0.5 + nrm((N_ODD, 2, S5_GROUPS, S5_STATE), 0.01),
        'od_lam_im': lam_im,
        'od_log_dt': unif((N_ODD, 2, S5_GROUPS), math.log(1e-3), math.log(1e-1)),
        'od_b_re': nrm((N_ODD, S5_GROUPS, S5_STATE, S5_GROUP), (2 * S5_GROUP) ** -0.5),
        'od_b_im': nrm((N_ODD, S5_GROUPS, S5_STATE, S5_GROUP), (2 * S5_GROUP) ** -0.5),
        'od_c_re': nrm((N_ODD, S5_GROUPS, S5_GROUP, S5_STATE), S5_STATE ** -0.5),
        'od_c_im': nrm((N_ODD, S5_GROUPS, S5_GROUP, S5_STATE), S5_STATE ** -0.5),
        'od_d_skip': nrm((N_ODD, S5_CH), 1.0),
        'od_w_glu': nrm((N_ODD, S5_CH, S5_CH), S5_CH ** -0.5),
        'od_b_glu': nrm((N_ODD, S5_CH), 0.02),
    }


def reference(x, c, ctx, c_ctx, w_mod, b_mod, ln1_g, ln1_b, ln2_g, ln2_b, w_router, w_gate, w_up, w_down,
              ev_w_in, ev_w_out, ev_conv, ev_a_log, ev_dt_bias, ev_gdn_norm, ev_ret_norm,
              od_w_in, od_w_out, od_conv, od_gate_bias, od_mlstm_norm, od_lam_re, od_lam_im, od_log_dt,
              od_b_re, od_b_im, od_c_re, od_c_im, od_d_skip, od_w_glu, od_b_glu):
    h_lat, h_ctx = x, ctx
    s_lat = jax.nn.silu(c)
    s_ctx = jax.nn.silu(c_ctx)
    for l in range(DEPTH):
        last = l == DEPTH - 1
        sh1, sc1, g1, sh2, sc2, g2 = jnp.split((s_lat @ w_mod[l] + b_mod[l])[:, None, :], 6, axis=-1)
        csh1, csc1, cg1, csh2, csc2, cg2 = jnp.split(s_ctx @ w_mod[l] + b_mod[l], 6, axis=-1)
        in_lat = h_lat * (1.0 + sc1) + sh1
        in_ctx = h_ctx * (1.0 + csc1) + csh1
        if l % 2 == 0:
            e = l // 2
            y_ctx, y_lat = gdn_retention_mixer(in_ctx, in_lat, ev_w_in[e], ev_w_out[e], ev_conv[e], ev_a_log[e],
                                               ev_dt_bias[e], ev_gdn_norm[e], ev_ret_norm[e], not last)
        else:
            o = l // 2
            y_ctx, y_lat = mlstm_s5_mixer(in_ctx, in_lat, od_w_in[o], od_w_out[o], od_conv[o], od_gate_bias[o],
                                          od_mlstm_norm[o], od_lam_re[o], od_lam_im[o], od_log_dt[o],
                                          od_b_re[o], od_b_im[o], od_c_re[o], od_c_im[o], od_d_skip[o],
                                          od_w_glu[o], od_b_glu[o], not last)
        h_lat = layer_norm(ALPHA * h_lat + g1 * y_lat, ln1_g[l], ln1_b[l])
        f_lat = expert_choice_ffn(h_lat * (1.0 + sc2) + sh2, w_router[l], w_gate[l], w_up[l], w_down[l])
        h_lat = layer_norm(ALPHA * h_lat + g2 * f_lat, ln2_g[l], ln2_b[l])
        if not last:
            h_ctx = layer_norm(ALPHA * h_ctx + cg1 * y_ctx, ln1_g[l], ln1_b[l])
            f_ctx = expert_choice_ffn(h_ctx * (1.0 + csc2) + csh2, w_router[l], w_gate[l], w_up[l], w_down[l])
            h_ctx = layer_norm(ALPHA * h_ctx + cg2 * f_ctx, ln2_g[l], ln2_b[l])
    return h_lat
```

```python
import bisect
import numpy as np
import concourse.bass as bass
import concourse.mybir as mybir
from concourse.bass_utils import run_bass_kernel_spmd

F32 = mybir.dt.float32
BF16 = mybir.dt.bfloat16
ALU = mybir.AluOpType
AF = mybir.ActivationFunctionType
AX = mybir.AxisListType

SEM_GEN = 30000
N_DMA_SEMS = 12


class IntervalMap:
    def __init__(self):
        self.bounds = [0]
        self.state = [(None, {})]

    def _split(self, x):
        i = bisect.bisect_right(self.bounds, x) - 1
        if self.bounds[i] == x:
            return i
        w, r = self.state[i]
        self.bounds.insert(i + 1, x)
        self.state.insert(i + 1, (w, dict(r)))
        return i + 1

    def segs(self, lo, hi):
        i0 = self._split(lo)
        i1 = self._split(hi)
        return range(i0, i1)


class T:
    def __init__(self, ap, key, lo, hi):
        self.ap = ap
        self.key = key
        self.lo = lo
        self.hi = hi

    def __getitem__(self, idx):
        return T(self.ap[idx], self.key, self.lo, self.hi)

    def v(self, ap):
        return T(ap, self.key, self.lo, self.hi)


class Prog:
    ENGS = ["pe", "act", "dve", "pool", "sp"]

    def __init__(self, nc, sb_bytes=206 * 1024):
        self.nc = nc
        self.ops = {e: [] for e in self.ENGS}
        self.cnt = {e: 0 for e in self.ENGS}
        self.maps = {}
        self.observed = {e: {} for e in self.ENGS}
        self.dma_cnt = {"sp": 0, "pool": 0, "act": 0}
        self.dma_sem_uses = {}
        self.sem_names = set()
        self.sb_bytes = sb_bytes
        self.sb_top = 0
        self.sb_stack = []
        self.ps_top = 0
        self.dram = {}
        self.arena = None
        self.psarena = None
        self.final_waits = []

    def setup_mem(self, es):
        nc = self.nc
        self.arena = es.enter_context(nc.sbuf_tensor("arena", [128, self.sb_bytes // 4], F32))
        self.psarena = es.enter_context(nc.psum_tensor("psarena", [128, 8 * 512], F32))

    def push(self):
        self.sb_stack.append(self.sb_top)

    def pop(self):
        self.sb_top = self.sb_stack.pop()

    def sb(self, shape, dtype=F32, name=None):
        esz = 4 if dtype == F32 else 2
        n = int(np.prod(shape))
        nbytes = (n * esz + 31) // 32 * 32
        off = self.sb_top
        self.sb_top += nbytes
        assert self.sb_top <= self.sb_bytes, f"SBUF overflow {self.sb_top}"
        ap = self.arena[:, off // 4:(off + nbytes) // 4]
        if dtype != F32:
            ap = ap.bitcast(dtype)
        ap = ap[:, 0:n]
        if len(shape) == 2:
            ap = ap.rearrange("p (a b) -> p a b", a=shape[0])
        elif len(shape) == 3:
            ap = ap.rearrange("p (a b c) -> p a b c", a=shape[0], b=shape[1])
        return T(ap, "sb", off, off + nbytes)

    def ps(self, bank, ncols=512, dtype=F32, col0=0):
        off = bank * 512 + col0
        ap = self.psarena[:, off:off + ncols]
        return T(ap, "ps", off * 4, (off + ncols) * 4)

    def dram_t(self, name, shape, dtype=F32, kind="Internal"):
        h = self.nc.dram_tensor(name, list(shape), dtype, kind=kind)
        self.dram[name] = h
        return h

    def dview(self, name, ap, lo=0, hi=1 << 40):
        return T(ap, "d:" + name, lo, hi)

    def _deps(self, eng, reads, writes, token):
        deps = set()
        for t in reads:
            m = self.maps.setdefault(t.key, IntervalMap())
            for i in m.segs(t.lo, t.hi):
                w, r = m.state[i]
                if w is not None:
                    deps.add(w)
        for t in writes:
            m = self.maps.setdefault(t.key, IntervalMap())
            for i in m.segs(t.lo, t.hi):
                w, r = m.state[i]
                if w is not None:
                    deps.add(w)
                for tok in r.values():
                    deps.add(tok)
        for t in reads:
            m = self.maps[t.key]
            for i in m.segs(t.lo, t.hi):
                m.state[i][1][token[0] if eng.startswith("dma") else eng] = token
        for t in writes:
            m = self.maps[t.key]
            for i in m.segs(t.lo, t.hi):
                m.state[i] = (token, {})
        return deps

    def _waits(self, eng, deps):
        obs = self.observed[eng]
        best = {}
        for (sem, val, src) in deps:
            if src == "pe" and eng == "pe":
                continue
            if obs.get(sem, 0) >= val:
                continue
            if best.get(sem, 0) < val:
                best[sem] = val
        for sem, val in best.items():
            obs[sem] = val
        return list(best.items())

    limit = None
    count = 0

    def _lim(self):
        if self.limit is not None:
            if self.count >= self.limit:
                return True
            self.count += 1
        return False

    def op(self, eng, fn, reads=(), writes=()):
        if self._lim():
            return
        n = self.cnt[eng]
        gen, idx = divmod(n, SEM_GEN)
        sem = f"{eng}{gen}"
        self.sem_names.add(sem)
        token = (sem, idx + 1, eng)
        self.cnt[eng] = n + 1
        rd2, wr2 = [], []
        for t in reads:
            if t.key == "ps":
                wr2.append(T(t.ap, "ps", t.lo // 2048 * 2048, (t.hi + 2047) // 2048 * 2048))
            else:
                rd2.append(t)
        for t in writes:
            if t.key == "ps":
                wr2.append(T(t.ap, "ps", t.lo // 2048 * 2048, (t.hi + 2047) // 2048 * 2048))
            else:
                wr2.append(t)
        reads, writes = rd2, wr2
        deps = self._deps(eng, reads, writes, token)
        waits = self._waits(eng, deps)
        self.ops[eng].append((waits, fn, (sem, 1)))

    def dma(self, out, in_, q="sp", **kw):
        if self._lim():
            return
        k = self.dma_cnt[q]
        self.dma_cnt[q] = k + 1
        sem = f"dma_{q}{k % N_DMA_SEMS}"
        self.sem_names.add(sem)
        uses = self.dma_sem_uses.get(sem, 0)
        self.dma_sem_uses[sem] = uses + 1
        token = (sem, 16 * (uses + 1), "dma")
        deps = self._deps("dma_" + q, [in_], [out], token)
        if uses > 0:
            deps.add((sem, 16 * uses, "dma"))
        waits = self._waits(q, deps)
        oap, iap = out.ap, in_.ap

        def fn(e, oap=oap, iap=iap, kw=kw):
            return e.dma_start(out=oap, in_=iap, allow_slow_non_contiguous=True, **kw)
        self.ops[q].append((waits, fn, (sem, 16)))
        return token

    def wait_all_dma(self, eng="sp"):
        deps = set()
        for sem, uses in self.dma_sem_uses.items():
            deps.add((sem, 16 * uses, "dma"))
        waits = self._waits(eng, deps)
        self.ops[eng].append((waits, None, None))

    def emit(self, es):
        nc = self.nc
        sems = {}
        for name in sorted(self.sem_names):
            sems[name] = es.enter_context(nc.semaphore(name))
        block = es.enter_context(nc.Block())
        ops = self.ops

        def run(e, lst):
            for waits, fn, inc in lst:
                for sem, val in waits:
                    e.wait_ge(sems[sem], val)
                if fn is not None:
                    ins = fn(e)
                    ins.then_inc(sems[inc[0]], inc[1])

        @block.sync
        def _(e):
            run(e, ops["sp"])

        @block.tensor
        def _(e):
            run(e, ops["pe"])

        @block.vector
        def _(e):
            run(e, ops["dve"])

        @block.scalar
        def _(e):
            run(e, ops["act"])

        @block.gpsimd
        def _(e):
            run(e, ops["pool"])


def _ap(x):
    return x.ap if isinstance(x, T) else x


def _ts(*xs):
    return [x for x in xs if isinstance(x, T)]


class Ops:
    def __init__(self, p):
        self.p = p

    def mm(self, out, lhsT, rhs, start=True, stop=True):
        self.p.op("pe", lambda e: e.matmul(out.ap, lhsT.ap, rhs.ap, start=start, stop=stop), [lhsT, rhs], [out])

    def tr(self, out, in_, ident):
        self.p.op("pe", lambda e: e.transpose(out.ap, in_.ap, ident.ap), [in_, ident], [out])

    def act(self, out, in_, func, bias=None, scale=None, accum=None, eng="act"):
        kw = {}
        if bias is not None:
            kw["bias"] = _ap(bias)
        if scale is not None:
            kw["scale"] = _ap(scale)
        if accum is not None:
            kw["accum_out"] = accum.ap
        self.p.op("act", lambda e: e.activation(out.ap, in_.ap, func, **kw),
                  _ts(in_, bias, scale), _ts(out, accum))

    def tt(self, out, a, b, op, eng="dve"):
        self.p.op(eng, lambda e: e.tensor_tensor(out.ap, a.ap, b.ap, op), [a, b], [out])

    def ts(self, out, a, s1, s2, op0, op1=None, accum=None, eng="dve"):
        kw = {}
        if accum is not None:
            kw["accum_out"] = accum.ap
        if op1 is None:
            fn = lambda e: e.tensor_scalar(out.ap, a.ap, _ap(s1), None, op0, **kw)
        else:
            fn = lambda e: e.tensor_scalar(out.ap, a.ap, _ap(s1), _ap(s2), op0, op1, **kw)
        self.p.op(eng, fn, _ts(a, s1, s2), _ts(out, accum))

    def stt(self, out, a, s, b, op0, op1, eng="dve"):
        self.p.op(eng, lambda e: e.scalar_tensor_tensor(out.ap, a.ap, _ap(s), b.ap, op0, op1), _ts(a, s, b), [out])

    def cp(self, out, in_, eng="dve"):
        if eng == "act":
            self.p.op("act", lambda e: e.copy(out.ap, in_.ap), [in_], [out])
        else:
            self.p.op(eng, lambda e: e.tensor_copy(out.ap, in_.ap), [in_], [out])

    def memset(self, out, val, eng="pool"):
        self.p.op(eng, lambda e: e.memset(out.ap, val), [], [out])

    def recip(self, out, in_):
        self.p.op("dve", lambda e: e.reciprocal(out.ap, in_.ap), [in_], [out])

from contextlib import ExitStack
import math

D = 1024
TT = 4352
NT = 34
NCTX = 2
ALPHA = 8 ** 0.25
EPS = 1e-5
NEG = -30000.0
EVEN_IN = 4112
ODD_IN = 2576

CONST_NAMES = ["IDENT", "ONES", "CSf", "CSb", "MTf", "MTb", "MSf", "MSb", "CSs", "MK"]


def make_consts():
    p = np.arange(128)[:, None]
    f = np.arange(128)[None, :]
    c = {}
    c["IDENT"] = (p == f)
    c["ONES"] = np.ones((128, 128))
    c["CSf"] = (p <= f)
    c["CSb"] = (p >= f)
    c["MTf"] = np.where(p <= f, 0.0, NEG)
    c["MTb"] = np.where(p >= f, 0.0, NEG)
    c["MSf"] = np.where(f < p, 0.0, NEG)
    c["MSb"] = np.where(f > p, 0.0, NEG)
    c["CSs"] = (p < f)
    mk = np.zeros((128, 128))
    mk[:64, 0] = 1; mk[64:, 1] = 1; mk[:16, 2] = 1; mk[16:32, 3] = 1
    for s4 in range(4):
        mk[s4 * 32:(s4 + 1) * 32, 4 + s4] = 1
    c["MK"] = mk
    arr = np.concatenate([c[k].astype(np.float32) for k in CONST_NAMES], axis=1)
    return np.ascontiguousarray(arr)


def ret_log_decay(d):
    expo = 5.0 + 2.0 * np.arange(4, dtype=np.float32) + d
    return np.log1p(-np.exp2(-expo)).astype(np.float32)


class K:
    pass


def build(n_layers=4, dbg=(), stage=99, layers=None, dbg_layer=0):
    nc = bass.Bass("TRN2", target_bir_lowering=False)
    k = K()
    k.nc = nc
    k.dbg_on = set(dbg)
    k.stage = stage
    k.dbg_layer = dbg_layer
    k.cur = -1
    SHAPES = dict(h0=[TT, D], cvec=[2, D], cst=[128, 128 * len(CONST_NAMES)], retg=[1, 8],
                  w_mod=[4, D, 6 * D], b_mod=[4, 6 * D], ln1_g=[4, D], ln1_b=[4, D], ln2_g=[4, D], ln2_b=[4, D],
                  w_router=[4, D, 16], w_gate=[4, 16, D, 2048], w_up=[4, 16, D, 2048], w_down=[4, 16, 2048, D],
                  ev_w_in=[2, D, EVEN_IN], ev_w_out=[2, D, D], ev_conv=[2, 3, 3, 1536], ev_a_log=[2, 2, 4],
                  ev_dt_bias=[2, 2, 4], ev_gdn_norm=[2, 128], ev_ret_norm=[2, 512],
                  od_w_in=[2, D, ODD_IN], od_w_out=[2, D, D], od_conv=[2, 3, 3, 1024], od_gate_bias=[2, 2, 2, 4],
                  od_mlstm_norm=[2, 512], od_lam_re=[2, 2, 32, 64], od_lam_im=[2, 2, 32, 64], od_log_dt=[2, 2, 32],
                  od_b_re=[2, 32, 64, 16], od_b_im=[2, 32, 64, 16], od_c_re=[2, 32, 16, 64], od_c_im=[2, 32, 16, 64],
                  od_d_skip=[2, 512], od_w_glu=[2, 512, 512], od_b_glu=[2, 512])

    class LazyIn(dict):
        def __missing__(self, name):
            self[name] = nc.dram_tensor(name, list(SHAPES[name]), F32, kind="ExternalInput")
            return self[name]
    IN = LazyIn()
    k.IN = IN
    with ExitStack() as es:
        p = Prog(nc)
        p.setup_mem(es)
        o = Ops(p)
        k.p, k.o = p, o
        k.H = p.dram_t("H", [TT, D])
        k.MOD = p.dram_t("MOD", [4, 2, 6 * D])
        k.FM = p.dram_t("FM", [24, 128, TT])
        k.ZS = p.dram_t("ZS", [TT, D])
        k.GATES = p.dram_t("GATES", [TT, 16])
        k.OUTM = p.dram_t("OUTM", [8, 2, TT, 128])
        k.OUTO = p.dram_t("OUTO", [4, 2, TT, 132])
        k.YS5 = p.dram_t("YS5", [2, 4, 128, TT])
        k.YOUT = nc.dram_tensor("y_out", [TT, D], F32, kind="ExternalOutput")
        cst = p.sb([128 * len(CONST_NAMES)])
        p.dma(cst, p.dview("cst", IN["cst"].ap()))
        k.C = {n: cst[:, i * 128:(i + 1) * 128] for i, n in enumerate(CONST_NAMES)}
        eps_c = p.sb([4])
        o.memset(eps_c[:, 0:1], EPS); o.memset(eps_c[:, 1:2], 1e-6); o.memset(eps_c[:, 2:3], 1.0); o.memset(eps_c[:, 3:4], 0.0)
        k.eps = eps_c

        phase_mod(k)
        p.dma(p.dview("H", k.H.ap()), p.dview("h0", IN["h0"].ap()))
        for l in (layers if layers is not None else range(n_layers)):
            if k.stage <= 0:
                break
            k.cur = l
            mixer_layer(k, l)
        p.limit = None
        if not (k.dbg_on - {'none'}) and k.stage >= 5:
            p.dma(p.dview('y_out', k.YOUT.ap()), p.dview('H', k.H.ap()))
        if k.stage < 5:
            p.dma(p.dview('y_out', k.YOUT.ap()[0:8, :]), p.dview('MOD', k.MOD.ap().rearrange('l r (a f) -> (l r a) f', f=1024)[0:8, :]))
        p.wait_all_dma("sp")
        p.wait_all_dma("pool")
        p.emit(es)
    nc.used_inputs = list(IN.keys())
    return nc


def rowbc(k, dst, src_ap, name):
    k.p.dma(dst, k.p.dview(name, src_ap.partition_broadcast(128)))


def phase_mod(k):
    p, o, IN = k.p, k.o, k.IN
    p.push()
    cT = p.sb([2, 8]); sT = p.sb([2, 8])
    for r in range(2):
        p.dma(cT[:, r, :], p.dview("cvec", IN["cvec"].ap()[r].rearrange("(kt q) -> q kt", q=128)))
    o.act(sT, cT, AF.Silu)
    wt = [p.sb([8, 512]) for _ in range(2)]
    bm = p.sb([512]); res = [p.sb([512]) for _ in range(2)]
    i = 0
    for l in range(4):
        for cc in range(12):
            w = wt[i % 2]; r = res[i % 2]
            p.dma(w, p.dview("w_mod", IN["w_mod"].ap()[l, :, cc * 512:(cc + 1) * 512].rearrange("(kt q) f -> q kt f", q=128)))
            p.dma(bm[0:2, :], p.dview("b_mod", IN["b_mod"].ap()[l:l + 1, cc * 512:(cc + 1) * 512].partition_broadcast(2)))
            ps = p.ps(i % 2)
            for kt in range(8):
                o.mm(ps[0:2, :], sT[:, :, kt], w[:, kt, :], start=(kt == 0), stop=(kt == 7))
            o.tt(r[0:2, :], ps[0:2, :], bm[0:2, :], ALU.add)
            p.dma(p.dview("MOD", k.MOD.ap()[l, :, cc * 512:(cc + 1) * 512]), r[0:2, :], q="pool")
            i += 1
    p.pop()


def load_modT(k, l, off, plus1):
    p, o = k.p, k.o
    t = p.sb([2, 8])
    for r in range(2):
        p.dma(t[:, r, :], p.dview("MOD", k.MOD.ap()[l, r, off:off + D].rearrange("(kt q) -> q kt", q=128)))
    if plus1:
        o.ts(t, t, 1.0, None, ALU.add)
    return t


def build_inT(k, l, scT, shT, inT, extra=None):
    p, o = k.p, k.o
    p.push()
    ht = [p.sb([D]) for _ in range(2)]
    for t in range(NT):
        h = ht[t % 2]
        p.dma(h, p.dview("H", k.H.ap()[t * 128:(t + 1) * 128, :], t * 128 * D, (t + 1) * 128 * D))
        r = 1 if t < NCTX else 0
        for half in range(2):
            ps = p.ps(half)
            for q in range(4):
                kt = half * 4 + q
                o.tr(ps[:, q * 128:(q + 1) * 128], h[:, kt * 128:(kt + 1) * 128], k.C["IDENT"])
            for q in range(4):
                kt = half * 4 + q
                o.act(inT[:, kt, t * 128:(t + 1) * 128], ps[:, q * 128:(q + 1) * 128], AF.Identity,
                      bias=shT[:, r, kt:kt + 1], scale=scT[:, r, kt:kt + 1])
                if extra is not None:
                    extra(t, kt, ps[:, q * 128:(q + 1) * 128], r)
    p.pop()


TOKCH = [(i * 512, 512) for i in range(8)] + [(4096, 256)]


def dbg_here(k, name):
    return name in k.dbg_on and k.cur == k.dbg_layer


def mixer_layer(k, l):
    p, o, IN = k.p, k.o, k.IN
    odd = (l % 2 == 1)
    e = l // 2
    C = k.C
    wname = "od_w_in" if odd else "ev_w_in"
    cname = "od_conv" if odd else "ev_conv"
    p.push()
    scT = load_modT(k, l, 1024, True)
    shT = load_modT(k, l, 0, False)
    inT = p.sb([8, TT], BF16)
    build_inT(k, l, scT, shT, inT)
    if k.stage <= 1:
        p.pop(); return
    if not odd:
        specs = [(0 + 128 * h, h, "l2q") for h in range(4)] + [(512 + 128 * h, 4 + h, "l2k") for h in range(4)] + \
                [(1024 + 128 * h, 8 + h, None) for h in range(4)] + [(2064 + 128 * h, None, None) for h in range(4)] + \
                [(2576 + 128 * h, None, "scale") for h in range(4)] + [(3088 + 128 * h, None, None) for h in range(4)]
        nconv = 12
    else:
        specs = [(0 + 128 * h, h, None) for h in range(4)] + [(512 + 128 * h, 4 + h, "scale") for h in range(4)] + \
                [(1024 + 128 * h, None, None) for h in range(4)] + [(2064 + 128 * h, None, None) for h in range(4)]
        nconv = 8
    p.push()
    wconv = p.sb([nconv, 9])
    for t12 in range(nconv):
        p.dma(wconv[:, t12, :], p.dview(cname, IN[cname].ap()[e].rearrange("a b c -> c (a b)")[t12 * 128:(t12 + 1) * 128, :]))
    wf = [p.sb([8, 128]) for _ in range(2)]
    wb = [p.sb([8, 128], BF16) for _ in range(2)]
    ft = [p.sb([TT]) for _ in range(2)]
    yt = p.sb([TT])
    sq = p.sb([512]); rs = p.sb([512])
    for s, (col, cv, mode) in enumerate(specs):
        w32, w16, X = wf[s % 2], wb[s % 2], ft[s % 2]
        p.dma(w32, p.dview(wname, IN[wname].ap()[e, :, col:col + 128].rearrange("(kt q) f -> q kt f", q=128)))
        o.cp(w16, w32, eng="pool")
        for ci, (t0, n) in enumerate(TOKCH):
            ps = p.ps(2 + ci % 2, n)
            for kt in range(8):
                o.mm(ps, w16[:, kt, :], inT[:, kt, t0:t0 + n], start=(kt == 0), stop=(kt == 7))
            o.cp(X[:, t0:t0 + n], ps, eng=("act" if ci % 2 else "dve"))
        if cv is not None:
            wc = wconv[:, cv, :]
            Y = yt
            o.ts(Y[:, 0:256], X[:, 0:256], wc[:, 4:5], None, ALU.mult)
            o.stt(Y[:, 1:256], X[:, 0:255], wc[:, 3:4], Y[:, 1:256], ALU.mult, ALU.add)
            o.stt(Y[:, 0:255], X[:, 1:256], wc[:, 5:6], Y[:, 0:255], ALU.mult, ALU.add)
            Xg = X.v(X.ap[:, 256:TT].rearrange("q (r c) -> q r c", c=64))
            Yg = Y.v(Y.ap[:, 256:TT].rearrange("q (r c) -> q r c", c=64))
            o.ts(Y[:, 256:TT], X[:, 256:TT], wc[:, 4:5], None, ALU.mult)
            for dy in range(3):
                for dx in range(3):
                    if dy == 1 and dx == 1:
                        continue
                    oy, ox = dy - 1, dx - 1
                    r0, r1 = max(0, -oy), 64 - max(0, oy)
                    c0, c1 = max(0, -ox), 64 - max(0, ox)
                    o.stt(Yg[:, r0:r1, c0:c1], Xg[:, r0 + oy:r1 + oy, c0 + ox:c1 + ox], wc[:, dy * 3 + dx:dy * 3 + dx + 1],
                          Yg[:, r0:r1, c0:c1], ALU.mult, ALU.add)
            o.act(X, Y, AF.Silu)
        if mode in ("l2q", "l2k"):
            for ci, (t0, n) in enumerate(TOKCH):
                o.tt(sq[:, 0:n], X[:, t0:t0 + n], X[:, t0:t0 + n], ALU.mult)
                ps = p.ps(4 + ci % 2, n)
                o.mm(ps, C["ONES"], sq[:, 0:n])
                o.act(rs[:, 0:n], ps, AF.Sqrt, bias=k.eps[:, 1:2])
                o.recip(sq[:, 0:n], rs[:, 0:n])
                if mode == "l2q":
                    o.stt(X[:, t0:t0 + n], X[:, t0:t0 + n], 128 ** -0.5, sq[:, 0:n], ALU.mult, ALU.mult)
                else:
                    o.tt(X[:, t0:t0 + n], X[:, t0:t0 + n], sq[:, 0:n], ALU.mult)
        elif mode == "scale":
            o.ts(X, X, 128 ** -0.5, None, ALU.mult)
        p.dma(p.dview("FM", k.FM.ap()[s], s * 128 * TT, (s + 1) * 128 * TT), X, q="pool")
    p.pop()
    if k.stage <= 2:
        p.pop(); return
    p.push()
    wz = p.sb([8, 1040], BF16)
    wst = [p.sb([8, 260]) for _ in range(2)]
    zcols = [(1536, 512), (2048, 16), (3600, 512)] if not odd else [(1536, 512), (2048, 16), (2064, 512)]
    dst = 0
    i = 0
    for (c0, n) in zcols:
        for j in range(0, n, 260):
            m = min(260, n - j)
            w32 = wst[i % 2]
            p.dma(w32[:, :, 0:m], p.dview(wname, IN[wname].ap()[e, :, c0 + j:c0 + j + m].rearrange("(kt q) f -> q kt f", q=128)))
            o.cp(wz[:, :, dst:dst + m], w32[:, :, 0:m], eng="pool")
            dst += m
            i += 1
    if not odd:
        alog = p.sb([8]); dtb = p.sb([8]); nea = p.sb([8])
        rowbc(k, alog, IN["ev_a_log"].ap()[e:e + 1].rearrange("a d h -> a (d h)"), "ev_a_log")
        rowbc(k, dtb, IN["ev_dt_bias"].ap()[e:e + 1].rearrange("a d h -> a (d h)"), "ev_dt_bias")
        o.act(nea, alog, AF.Exp)
    else:
        gbi = p.sb([8]); gbf = p.sb([8])
        for d in range(2):
            rowbc(k, gbi[:, d * 4:(d + 1) * 4], IN["od_gate_bias"].ap()[e, d, 0:1, :], "od_gate_bias")
            rowbc(k, gbf[:, d * 4:(d + 1) * 4], IN["od_gate_bias"].ap()[e, d, 1:2, :], "od_gate_bias")
    zt = [p.sb([D]) for _ in range(2)]
    gt = [p.sb([16]) for _ in range(2)]
    tmp8 = p.sb([8]); tmp8b = p.sb([8])
    for t in range(NT):
        z, g = zt[t % 2], gt[t % 2]
        lt = lambda kt: inT[:, kt, t * 128:(t + 1) * 128]
        psa, psb, psg = p.ps(0), p.ps(1), p.ps(2, 16)
        for kt in range(8):
            o.mm(psa, lt(kt), wz[:, kt, 0:512], start=(kt == 0), stop=(kt == 7))
        for kt in range(8):
            o.mm(psb, lt(kt), wz[:, kt, 528:1040], start=(kt == 0), stop=(kt == 7))
        for kt in range(8):
            o.mm(psg, lt(kt), wz[:, kt, 512:528], start=(kt == 0), stop=(kt == 7))
        if not odd:
            o.act(z[:, 0:512], psa, AF.Silu)
            o.act(z[:, 512:1024], psb, AF.Silu)
            o.tt(tmp8, psg[:, 0:8], dtb, ALU.add)
            o.act(g[:, 0:8], tmp8, AF.Exp)
            o.act(tmp8, g[:, 0:8], AF.Ln, bias=k.eps[:, 2:3])
            o.stt(g[:, 0:8], tmp8, -1.0, nea, ALU.mult, ALU.mult)
            o.act(g[:, 8:16], psg[:, 8:16], AF.Sigmoid)
        else:
            o.act(z[:, 0:512], psa, AF.Sigmoid)
            o.cp(z[:, 512:1024], psb, eng="dve")
            o.tt(tmp8, psg[:, 8:16], gbf, ALU.add)
            o.act(tmp8b, tmp8, AF.Exp, scale=-1.0)
            o.act(tmp8, tmp8b, AF.Ln, bias=k.eps[:, 2:3])
            o.ts(g[:, 0:8], tmp8, -1.0, None, ALU.mult)
            o.tt(tmp8b, psg[:, 0:8], gbi, ALU.add)
            o.act(g[:, 8:16], tmp8b, AF.Exp)
        p.dma(p.dview("ZS", k.ZS.ap()[t * 128:(t + 1) * 128, :], t * 128 * D, (t + 1) * 128 * D), z, q="pool")
        p.dma(p.dview("GATES", k.GATES.ap()[t * 128:(t + 1) * 128, :], t * 128 * 16, (t + 1) * 128 * 16), g, q="pool")
    p.pop()
    p.pop()
    if k.stage <= 3:
        return
    if not odd:
        scan_layer(k, l, [0, 1])
    else:
        scan_layer(k, l, [2])
        if k.stage > 4:
            s5_scan(k, l)
    if k.stage <= 4:
        return
    if not odd:
        merge_even(k, l)
    else:
        merge_odd(k, l)
    if k.stage <= 5:
        return
    moe_layer(k, l)


def chain_order(d):
    return list(range(NT)) if d == 0 else [1, 0] + list(range(NT - 1, 1, -1))


def scan_layer(k, l, types):
    p, o, IN, C = k.p, k.o, k.IN, k.C
    import os
    if 'OPLIMIT' in os.environ:
        p.limit = int(os.environ['OPLIMIT']); p.count = 0
    p.push()
    gates = p.sb([NT, 16])
    p.dma(gates, p.dview("GATES", k.GATES.ap().rearrange("(c q) g -> q c g", q=128)))
    retg = p.sb([8])
    negb = p.sb([NT, 8])
    if 1 in types:
        rowbc(k, retg, IN["retg"].ap(), "retg")
    if 0 in types:
        o.ts(negb, gates[:, :, 8:16], -1.0, None, ALU.mult)
    RING = 4
    W = lambda n=128: [p.sb([n]) for _ in range(RING)]
    names = ["qT", "kT", "vT", "G1", "cc", "tmp", "ET", "EXPR", "kgT", "qgT", "kend", "bv", "E", "N", "M", "N2", "M2",
             "TTa", "TTb", "AQ", "br", "vn", "o", "o2", "tmp2", "sc"]
    ring = {n: (W(132) if n in ("vn", "o") else W()) for n in names}
    chains = []
    for typ in types:
        for h in range(4):
            for d in range(2):
                chains.append(dict(typ=typ, h=h, d=d, S=[p.sb([132]), p.sb([132])], order=chain_order(d), dec=None))
    for ch in chains:
        o.memset(ch["S"][0], 0.0, eng="dve")
    step_i = [0]

    def decay(ch, gcol, slot):
        d = ch["d"]
        R = {n: ring[n][slot] for n in names}
        CS = C["CSf"] if d == 0 else C["CSb"]
        MT = C["MTf"] if d == 0 else C["MTb"]
        o.ts(R["G1"], C["ONES"], gcol, None, ALU.mult)
        psr = p.ps(0, 128)
        psc = p.ps(1, 256)
        o.mm(psr, R["G1"], CS)
        o.mm(psc[:, 0:128], CS, R["G1"])
        o.mm(psc[:, 128:256], C["ONES"], R["G1"])
        cc = R["cc"]
        o.cp(cc[:, 0:1], psc[:, 0:1], eng="act")
        o.cp(cc[:, 1:2], psc[:, 128:129], eng="act")
        o.stt(R["tmp"], psr, cc[:, 0:1], MT, ALU.subtract, ALU.add)
        o.act(R["ET"], R["tmp"], AF.Exp)
        o.act(R["EXPR"], psr, AF.Exp)
        o.tt(cc[:, 4:5], cc[:, 1:2], cc[:, 0:1], ALU.subtract)
        o.act(cc[:, 2:3], cc[:, 4:5], AF.Exp)
        o.act(cc[:, 3:4], cc[:, 1:2], AF.Exp)
        return dict(ET=R["ET"], EXPR=R["EXPR"], cc=cc, psr=psr)

    def step(ch, c, si):
        typ, h, d = ch["typ"], ch["h"], ch["d"]
        slot = si % RING
        R = {n: ring[n][slot] for n in names}
        sq, sk, sv = (h, 4 + h, 8 + h) if typ != 1 else (12 + h, 16 + h, 20 + h)
        dv = 129 if typ == 2 else 128
        tok = slice(c * 128, (c + 1) * 128)
        for nm, s in (("qT", sq), ("kT", sk), ("vT", sv)):
            p.dma(R[nm], p.dview("FM", k.FM.ap()[s][:, tok], s * 128 * TT, (s + 1) * 128 * TT))
        qT, kT, vT = R["qT"], R["kT"], R["vT"]
        if typ != 1:
            gcol = gates[:, c, d * 4 + h:d * 4 + h + 1]
            dec = decay(ch, gcol, slot)
        else:
            if ch["dec"] is None:
                gcol = retg[:, d * 4 + h:d * 4 + h + 1]
                dslot = 0
                own = {n: p.sb([128]) for n in ["G1", "cc", "tmp", "ET", "EXPR"]}
                save = {n: ring[n][dslot] for n in own}
                for n in own:
                    ring[n][dslot] = own[n]
                ch["dec"] = decay(ch, gcol, dslot)
                for n in own:
                    ring[n][dslot] = save[n]
            dec = ch["dec"]
        cc = dec["cc"]
        pst = p.ps(2, 256)
        o.tr(pst[:, 0:128], kT, C["IDENT"])
        o.tr(pst[:, 128:256], vT, C["IDENT"])
        if typ == 2:
            ei = gates[:, c, 8 + d * 4 + h:8 + d * 4 + h + 1]
            o.ts(R["kend"], pst[:, 0:128], cc[:, 2:3], ei, ALU.mult, ALU.mult)
        else:
            o.ts(R["kend"], pst[:, 0:128], cc[:, 2:3], None, ALU.mult)
        o.tt(R["qgT"], qT, dec["EXPR"], ALU.mult)
        psq = p.ps(3, 128)
        o.mm(psq, kT, qT)
        if typ == 2:
            o.stt(R["AQ"], psq, ei, dec["ET"], ALU.mult, ALU.mult)
        else:
            o.tt(R["AQ"], psq, dec["ET"], ALU.mult)
        S = ch["S"][si % 2][:, 0:dv]
        Sn = ch["S"][(si + 1) % 2][:, 0:dv]
        if typ == 0:
            MS = C["MSf"] if d == 0 else C["MSb"]
            nb = negb[:, c, d * 4 + h:d * 4 + h + 1]
            o.ts(R["bv"], pst[:, 128:256], gates[:, c, 8 + d * 4 + h:8 + d * 4 + h + 1], None, ALU.mult)
            o.tt(R["kgT"], kT, dec["EXPR"], ALU.mult)
            o.stt(R["tmp2"], dec["psr"], cc[:, 0:1], MS, ALU.subtract, ALU.subtract)
            o.act(R["E"], R["tmp2"], AF.Exp, scale=-1.0)
            psk = p.ps(4, 128)
            o.mm(psk, kT, kT)
            o.stt(R["N"], psk, nb, R["E"], ALU.mult, ALU.mult)
            psm = p.ps(5, 128)
            o.tr(psm, R["N"], C["IDENT"])
            o.cp(R["M"], psm, eng="act")
            o.tt(R["TTa"], C["IDENT"], R["M"], ALU.add)
            Nc, Mc, Nn, Mn = R["N"], R["M"], R["N2"], R["M2"]
            Tc, Tn = R["TTa"], R["TTb"]
            for lev in range(1, 7):
                ps1 = p.ps(4, 128)
                o.mm(ps1, Mc, Nc)
                o.cp(Nn, ps1, eng="act")
                if lev < 6:
                    ps2 = p.ps(5, 128)
                    o.mm(ps2, Nc, Mc)
                    o.cp(Mn, ps2, eng="dve")
                ps3 = p.ps(6, 128)
                o.mm(ps3, Nn, Tc)
                o.tt(Tn, ps3, Tc, ALU.add)
                Nc, Nn = Nn, Nc
                Mc, Mn = Mn, Mc
                Tc, Tn = Tn, Tc
            psr2 = p.ps(7, 128)
            o.mm(psr2, R["kgT"], S)
            o.stt(R["br"], psr2, nb, R["bv"], ALU.mult, ALU.add)
            psv = p.ps(7, 128)
            o.mm(psv, Tc, R["br"])
            o.cp(R["vn"][:, 0:128], psv, eng="act")
        else:
            o.cp(R["vn"][:, 0:128], pst[:, 128:256], eng="act")
            if typ == 2:
                o.memset(R["vn"][:, 128:129], 1.0, eng="dve")
        vn = R["vn"][:, 0:dv]
        pso = p.ps(6, dv) if typ != 0 else p.ps(5, dv)
        o.mm(pso, R["qgT"], S, start=True, stop=False)
        o.mm(pso, R["AQ"], vn, start=False, stop=True)
        o.cp(R["o"][:, 0:dv], pso, eng="act")
        if typ == 2:
            p.dma(p.dview("OUTO%d_%d" % (h, d), k.OUTO.ap()[h, d, tok, 0:dv], c * 128 * 132, (c + 1) * 128 * 132), R["o"][:, 0:dv], q="pool")
        else:
            slot8 = typ * 4 + h
            p.dma(p.dview("OUTM%d_%d" % (slot8, d), k.OUTM.ap()[slot8, d, tok, :], c * 128 * 128, (c + 1) * 128 * 128), R["o"][:, 0:128], q="pool")
        pss = p.ps(7, dv)
        o.mm(pss, R["kend"], vn)
        o.stt(Sn, S, cc[:, 3:4], pss, ALU.mult, ALU.add)

    for si in range(NT):
        for ci, ch in enumerate(chains):
            step(ch, ch["order"][si], si)
    p.pop()


def merge_even(k, l):
    p, o, IN, C = k.p, k.o, k.IN, k.C
    e = l // 2
    p.push()
    wout = p.sb([8, D], BF16)
    wst = [p.sb([8, 256]) for _ in range(2)]
    for j in range(4):
        p.dma(wst[j % 2], p.dview("ev_w_out", IN["ev_w_out"].ap()[e, :, j * 256:(j + 1) * 256].rearrange("(kt q) f -> q kt f", q=128)))
        o.cp(wout[:, :, j * 256:(j + 1) * 256], wst[j % 2], eng="pool")
    gg = p.sb([128]); rg = p.sb([512])
    rowbc(k, gg, IN["ev_gdn_norm"].ap()[e:e + 1, :], "ev_gdn_norm")
    rowbc(k, rg, IN["ev_ret_norm"].ap()[e:e + 1, :], "ev_ret_norm")
    g1 = [p.sb([D]), p.sb([D])]
    for r in range(2):
        rowbc(k, g1[r], k.MOD.ap()[l, r:r + 1, 2048:3072], "MOD")
    lng = p.sb([D]); lnb = p.sb([D])
    rowbc(k, lng, IN["ln1_g"].ap()[l:l + 1, :], "ln1_g")
    rowbc(k, lnb, IN["ln1_b"].ap()[l:l + 1, :], "ln1_b")
    of = [p.sb([8, 128]) for _ in range(2)]
    ob = [p.sb([8, 128]) for _ in range(2)]
    zt = [p.sb([D]) for _ in range(2)]
    ht = [p.sb([D]) for _ in range(2)]
    Y = [p.sb([D]) for _ in range(2)]
    YT = [p.sb([8, 128], BF16) for _ in range(2)]
    st = [p.sb([32]) for _ in range(2)]
    junk = p.sb([D])
    U = [p.sb([D]) for _ in range(2)]
    for t in range(NT):
        r = 1 if t < NCTX else 0
        a, b, z, h, y, yT, s, u = of[t % 2], ob[t % 2], zt[t % 2], ht[t % 2], Y[t % 2], YT[t % 2], st[t % 2], U[t % 2]
        tok = slice(t * 128, (t + 1) * 128)
        for hs in range(8):
            p.dma(a[:, hs, :], p.dview("OUTM%d_0" % hs, k.OUTM.ap()[hs, 0, tok, :], t * 128 * 128, (t + 1) * 128 * 128))
            p.dma(b[:, hs, :], p.dview("OUTM%d_1" % hs, k.OUTM.ap()[hs, 1, tok, :], t * 128 * 128, (t + 1) * 128 * 128))
        p.dma(z, p.dview("ZS", k.ZS.ap()[tok, :], t * 128 * D, (t + 1) * 128 * D))
        p.dma(h, p.dview("H", k.H.ap()[tok, :], t * 128 * D, (t + 1) * 128 * D))
        o.tt(a, a, b, ALU.add)
        for hs in range(8):
            o.act(junk[:, 0:128], a[:, hs, :], AF.Square, accum=s[:, hs:hs + 1])
        for hs in range(4, 8):
            o.act(junk[:, 0:128], a[:, hs, :], AF.Identity, accum=s[:, 8 + hs:9 + hs])
        o.act(s[:, 16:20], s[:, 0:4], AF.Sqrt, bias=k.eps[:, 0:1], scale=1.0 / 128)
        o.recip(s[:, 20:24], s[:, 16:20])
        o.ts(s[:, 24:28], s[:, 12:16], 1.0 / 128, None, ALU.mult)
        o.tt(s[:, 28:32], s[:, 24:28], s[:, 24:28], ALU.mult)
        o.stt(s[:, 16:20], s[:, 4:8], 1.0 / 128, s[:, 28:32], ALU.mult, ALU.subtract)
        o.act(s[:, 28:32], s[:, 16:20], AF.Sqrt, bias=k.eps[:, 0:1])
        o.recip(s[:, 16:20], s[:, 28:32])
        for hs in range(4):
            o.stt(y[:, hs * 128:(hs + 1) * 128], a[:, hs, :], s[:, 20 + hs:21 + hs], gg, ALU.mult, ALU.mult)
        for hs in range(4):
            o.ts(junk[:, 0:128], a[:, 4 + hs, :], s[:, 24 + hs:25 + hs], s[:, 16 + hs:17 + hs], ALU.subtract, ALU.mult)
            o.tt(y[:, 512 + hs * 128:512 + (hs + 1) * 128], junk[:, 0:128], rg[:, hs * 128:(hs + 1) * 128], ALU.mult)
        o.tt(y, y, z, ALU.mult)
        for half in range(2):
            ps = p.ps(half)
            for q in range(4):
                kt = half * 4 + q
                o.tr(ps[:, q * 128:(q + 1) * 128], y[:, kt * 128:(kt + 1) * 128], C["IDENT"])
            o.cp(yT.v(yT.ap[:, half * 4:(half + 1) * 4, :].rearrange("q a b -> q (a b)")), ps, eng="act")
        for half in range(2):
            ps = p.ps(2 + half)
            for kt in range(8):
                o.mm(ps, yT[:, kt, :], wout[:, kt, half * 512:(half + 1) * 512], start=(kt == 0), stop=(kt == 7))
            hs_ = slice(half * 512, (half + 1) * 512)
            o.tt(u[:, hs_], ps, g1[r][:, hs_], ALU.mult)
        if dbg_here(k, "y"):
            p.dma(p.dview("y_out", k.YOUT.ap()[tok, :], t * 128 * D, (t + 1) * 128 * D), u, q="pool")
        o.stt(u, h, ALPHA, u, ALU.mult, ALU.add)
        layer_norm(k, u, h, lng, lnb, s, junk)
        p.dma(p.dview("H", k.H.ap()[tok, :], t * 128 * D, (t + 1) * 128 * D), h, q="pool")
        if dbg_here(k, "h1"):
            p.dma(p.dview("y_out", k.YOUT.ap()[tok, :], t * 128 * D, (t + 1) * 128 * D), h, q="pool")
    p.pop()


def layer_norm(k, u, out, g, b, s, junk):
    o = k.o
    o.act(junk, u, AF.Identity, accum=s[:, 0:1])
    o.act(junk, u, AF.Square, accum=s[:, 1:2])
    o.ts(s[:, 2:3], s[:, 0:1], 1.0 / D, None, ALU.mult)
    o.tt(s[:, 3:4], s[:, 2:3], s[:, 2:3], ALU.mult)
    o.stt(s[:, 4:5], s[:, 1:2], 1.0 / D, s[:, 3:4], ALU.mult, ALU.subtract)
    o.act(s[:, 5:6], s[:, 4:5], AF.Sqrt, bias=k.eps[:, 0:1])
    o.recip(s[:, 6:7], s[:, 5:6])
    o.ts(junk, u, s[:, 2:3], s[:, 6:7], ALU.subtract, ALU.mult)
    o.tt(junk, junk, g, ALU.mult)
    o.tt(out, junk, b, ALU.add)


def moe_layer(k, l, update_ctx=True):
    p, o, IN, C = k.p, k.o, k.IN, k.C
    p.push()
    g2 = [p.sb([D]), p.sb([D])]
    for r in range(2):
        rowbc(k, g2[r], k.MOD.ap()[l, r:r + 1, 5120:6144], "MOD")
    lng = p.sb([D]); lnb = p.sb([D])
    rowbc(k, lng, IN["ln2_g"].ap()[l:l + 1, :], "ln2_g")
    rowbc(k, lnb, IN["ln2_b"].ap()[l:l + 1, :], "ln2_b")
    wr = p.sb([8, 16])
    p.dma(wr, p.dview("w_router", IN["w_router"].ap()[l].rearrange("(kt q) e -> q kt e", q=128)))
    in2T = p.sb([8, TT], BF16)
    aff = p.sb([NT, 16]); gw = p.sb([NT, 16])
    p.push()
    affT = p.sb([TT])
    sc2 = [p.sb([D]), p.sb([D])]; sh2 = [p.sb([D]), p.sb([D])]
    for r in range(2):
        rowbc(k, sc2[r], k.MOD.ap()[l, r:r + 1, 4096:5120], "MOD")
        o.ts(sc2[r], sc2[r], 1.0, None, ALU.add)
        rowbc(k, sh2[r], k.MOD.ap()[l, r:r + 1, 3072:4096], "MOD")
    p.push()
    ht = [p.sb([D]) for _ in range(2)]
    x2 = [p.sb([D]) for _ in range(2)]
    xTf = [p.sb([8, 128]) for _ in range(2)]
    sm = [p.sb([40]) for _ in range(2)]
    for t in range(NT):
        r = 1 if t < NCTX else 0
        h, x, xf, s = ht[t % 2], x2[t % 2], xTf[t % 2], sm[t % 2]
        tok = slice(t * 128, (t + 1) * 128)
        p.dma(h, p.dview("H", k.H.ap()[tok, :], t * 128 * D, (t + 1) * 128 * D))
        o.tt(x, h, sc2[r], ALU.mult)
        o.tt(x, x, sh2[r], ALU.add)
        for half in range(2):
            ps = p.ps(half)
            for q in range(4):
                kt = half * 4 + q
                o.tr(ps[:, q * 128:(q + 1) * 128], x[:, kt * 128:(kt + 1) * 128], C["IDENT"])
            ps3 = ps.v(ps.ap.rearrange("q (a b) -> q a b", a=4))
            o.cp(xf[:, half * 4:(half + 1) * 4, :], ps3, eng="act")
            o.cp(in2T[:, half * 4:(half + 1) * 4, tok], ps3, eng="dve")
        psl = p.ps(2, 16)
        for kt in range(8):
            o.mm(psl, xf[:, kt, :], wr[:, kt, :], start=(kt == 0), stop=(kt == 7))
        p.op("dve", lambda e, s=s, psl=psl: e.reduce_max(s.ap[:, 0:1], psl.ap, AX.X), [psl], [s])
        o.ts(s[:, 1:2], s[:, 0:1], -1.0, None, ALU.mult)
        o.act(s[:, 8:24], psl, AF.Exp, bias=s[:, 1:2], accum=s[:, 2:3])
        o.recip(s[:, 3:4], s[:, 2:3])
        o.ts(aff[:, t, :], s[:, 8:24], s[:, 3:4], None, ALU.mult)
        pst = p.ps(3, 128)
        o.tr(pst[0:16, :], aff[:, t, :], C["IDENT"])
        o.cp(affT[0:16, tok], pst[0:16, :], eng="act")
    p.pop()
    p.push()
    st = p.sb([16]); junk = p.sb([4096]); thr = [p.sb([16]), p.sb([16])]; dt = p.sb([16])
    sets = [(0, 256, 32.0, 1), (256, TT, 512.0, 0)]
    for (c0, c1, cap, r) in sets:
        lo, hi, mid, cnt, ge, d1 = [st[0:16, i:i + 1] for i in range(6)]
        o.memset(lo, 0.0, eng="dve"); o.memset(hi, 1.0, eng="dve")
        for it in range(32):
            o.tt(mid, lo, hi, ALU.add)
            o.ts(mid, mid, 0.5, None, ALU.mult)
            o.ts(junk[0:16, 0:c1 - c0], affT[0:16, c0:c1], mid, None, ALU.is_ge, ALU.add, accum=cnt)
            o.ts(ge, cnt, cap - 0.5, None, ALU.is_ge)
            o.tt(d1, mid, lo, ALU.subtract)
            o.stt(lo, d1, ge, lo, ALU.mult, ALU.add)
            o.tt(d1, hi, mid, ALU.subtract)
            o.stt(hi, d1, ge, mid, ALU.mult, ALU.add)
        o.ts(dt[0:16, 0:16], C["IDENT"][0:16, 0:16], lo, None, ALU.mult)
        pth = p.ps(2, 16)
        o.mm(pth, C["ONES"][0:16, :], dt[0:16, 0:16])
        o.cp(thr[r], pth, eng="act")
    for t in range(NT):
        r = 1 if t < NCTX else 0
        o.tt(gw[:, t, :], aff[:, t, :], thr[r], ALU.is_ge)
        o.tt(gw[:, t, :], gw[:, t, :], aff[:, t, :], ALU.mult)
    p.pop()
    p.pop()
    p.push()
    wgf = [p.sb([8, 256]) for _ in range(2)]; wuf = [p.sb([8, 256]) for _ in range(2)]
    wgb = [p.sb([8, 256], BF16) for _ in range(2)]; wub = [p.sb([8, 256], BF16) for _ in range(2)]
    wdf = [p.sb([2, 512]) for _ in range(2)]; wdb = [p.sb([2, 512], BF16) for _ in range(2)]
    hT = p.sb([16, 512], BF16)
    sg = [p.sb([512]) for _ in range(2)]
    acc = [p.sb([D]) for _ in range(4)]
    hh = [p.sb([D])] * 2
    s8 = p.sb([8]); junk2 = p.sb([D])
    wi = 0
    di = 0
    import os
    nexp = int(os.environ.get("MOE_NEXP", 16))
    for (t0, n) in TOKCH:
        nst = n // 128
        for e in range(nexp):
            for fg in range(8):
                a32, b32, a16, b16 = wgf[wi % 2], wuf[wi % 2], wgb[wi % 2], wub[wi % 2]
                wi += 1
                p.dma(a32, p.dview("w_gate", IN["w_gate"].ap()[l, e, :, fg * 256:(fg + 1) * 256].rearrange("(kt q) f -> q kt f", q=128)))
                p.dma(b32, p.dview("w_up", IN["w_up"].ap()[l, e, :, fg * 256:(fg + 1) * 256].rearrange("(kt q) f -> q kt f", q=128)))
                o.cp(a16, a32, eng="pool")
                o.cp(b16, b32, eng="pool")
                for f2 in range(2):
                    ft = fg * 2 + f2
                    psg = p.ps(4 + (ft % 2) * 2, n)
                    psu = p.ps(5 + (ft % 2) * 2, n)
                    for kt in range(8):
                        o.mm(psg, a16[:, kt, f2 * 128:(f2 + 1) * 128], in2T[:, kt, t0:t0 + n], start=(kt == 0), stop=(kt == 7))
                    for kt in range(8):
                        o.mm(psu, b16[:, kt, f2 * 128:(f2 + 1) * 128], in2T[:, kt, t0:t0 + n], start=(kt == 0), stop=(kt == 7))
                    s_ = sg[ft % 2]
                    o.act(s_[:, 0:n], psg, AF.Silu)
                    o.tt(hT[:, ft, 0:n], s_[:, 0:n], psu, ALU.mult)
            for half in range(2):
                psd = [p.ps(st_, 512) for st_ in range(nst)]
                for fp_ in range(8):
                    d32, d16 = wdf[di % 2], wdb[di % 2]
                    di += 1
                    p.dma(d32, p.dview("w_down", IN["w_down"].ap()[l, e, fp_ * 256:(fp_ + 1) * 256, half * 512:(half + 1) * 512].rearrange("(a q) d -> q a d", q=128)))
                    o.cp(d16, d32, eng="pool")
                    for f2 in range(2):
                        ft = fp_ * 2 + f2
                        for st_ in range(nst):
                            o.mm(psd[st_], hT[:, ft, st_ * 128:(st_ + 1) * 128], d16[:, f2, :], start=(ft == 0), stop=(ft == 15))
                for st_ in range(nst):
                    t = t0 // 128 + st_
                    a_ = acc[st_][:, half * 512:(half + 1) * 512]
                    if e == 0:
                        o.ts(a_, psd[st_], gw[:, t, e:e + 1], None, ALU.mult)
                    else:
                        o.stt(a_, psd[st_], gw[:, t, e:e + 1], a_, ALU.mult, ALU.add)
        for st_ in range(nst):
            t = t0 // 128 + st_
            r = 1 if t < NCTX else 0
            tok = slice(t * 128, (t + 1) * 128)
            if dbg_here(k, "f"):
                p.dma(p.dview("y_out", k.YOUT.ap()[tok, :], t * 128 * D, (t + 1) * 128 * D), acc[st_], q="pool")
            h = hh[st_ % 2]
            p.dma(h, p.dview("H", k.H.ap()[tok, :], t * 128 * D, (t + 1) * 128 * D))
            u = acc[st_]
            o.tt(u, u, g2[r], ALU.mult)
            o.stt(u, h, ALPHA, u, ALU.mult, ALU.add)
            layer_norm(k, u, h, lng, lnb, s8, junk2)
            if update_ctx or r == 0:
                p.dma(p.dview("H", k.H.ap()[tok, :], t * 128 * D, (t + 1) * 128 * D), h, q="pool")
    p.pop()
    p.pop()


def rev(t, n):
    a = t.ap
    st = a.ap[-1][0]
    return t.v(bass.AP(a.tensor, a.offset + (n - 1) * st, [list(a.ap[0]), [-st, n]]))


def bcast_cols(t, n):
    a = t.ap
    return t.v(bass.AP(a.tensor, a.offset, [list(a.ap[0]), [0, n]]))


SIN_C = [-1.0 / 6, 1.0 / 120, -1.0 / 5040, 1.0 / 362880, -1.0 / 39916800]
COS_C = [-0.5, 1.0 / 24, -1.0 / 720, 1.0 / 40320, -1.0 / 3628800, 1.0 / 479001600]


def s5_scan(k, l):
    p, o, IN, C = k.p, k.o, k.IN, k.C
    e = l // 2
    MK = C["MK"]
    p.push()
    WCf = [[p.sb([128]) for _ in range(16)] for _ in range(2)]
    WB = [[[p.sb([128]) for _ in range(16)] for _ in range(2)] for _ in range(2)]
    mag = [p.sb([16]) for _ in range(2)]
    CLv = [p.sb([16, 10]) for _ in range(2)]; SLv = [p.sb([16, 10]) for _ in range(2)]; NSLv = [p.sb([16, 10]) for _ in range(2)]
    p.push()
    BR = p.sb([16, 16]); BI = p.sb([16, 16])
    CLr = p.sb([16, 64]); CLi = p.sb([16, 64])
    for gl in range(2):
        pr = slice(gl * 64, (gl + 1) * 64)
        p.dma(BR[pr], p.dview("od_b_re", IN["od_b_re"].ap()[e].rearrange("(st gl) q h -> gl q st h", gl=2)[gl]))
        p.dma(BI[pr], p.dview("od_b_im", IN["od_b_im"].ap()[e].rearrange("(st gl) q h -> gl q st h", gl=2)[gl]))
        pc = slice(gl * 16, (gl + 1) * 16)
        p.dma(CLr[pc], p.dview("od_c_re", IN["od_c_re"].ap()[e].rearrange("(st gl) h q -> gl h st q", gl=2)[gl]))
        p.dma(CLi[pc], p.dview("od_c_im", IN["od_c_im"].ap()[e].rearrange("(st gl) h q -> gl h st q", gl=2)[gl]))
    X = [p.sb([128]) for _ in range(2)]
    for ri, CLx in enumerate((CLr, CLi)):
        for st in range(16):
            s4 = st % 4
            x = X[st % 2]
            for gl2 in range(2):
                o.ts(x[0:32, gl2 * 64:(gl2 + 1) * 64], CLx[0:32, st, :], MK[0:32, 2 + gl2:3 + gl2], None, ALU.mult)
            ps = p.ps(st % 2, 32)
            o.tr(ps, x[0:32, :], C["IDENT"][0:32, 0:32])
            w = WCf[ri][st]
            o.memset(w, 0.0, eng="pool")
            if ri == 0:
                o.cp(w[:, s4 * 32:(s4 + 1) * 32], ps, eng="act")
            else:
                o.ts(w[:, s4 * 32:(s4 + 1) * 32], ps, -1.0, None, ALU.mult)
    LR = p.sb([16]); LI = p.sb([16]); DT = p.sb([16])
    tl = {n: p.sb([16]) for n in ["lr", "dt", "a", "th", "x", "z", "q", "s", "c", "cc", "ss", "cs", "abr", "abi", "xr", "den",
                                  "t1", "t2", "fre", "fim", "nfim"]}
    fm = {n: p.sb([16]) for n in ["fre0", "fre1", "fim0", "fim1", "nfim0", "nfim1"]}
    INre = [p.sb([128]) for _ in range(4)]; INim = [p.sb([128]) for _ in range(4)]
    tb = p.sb([16])
    for d in range(2):
        for gl in range(2):
            pr = slice(gl * 64, (gl + 1) * 64)
            p.dma(LR[pr], p.dview("od_lam_re", IN["od_lam_re"].ap()[e, d].rearrange("(st gl) q -> gl q st", gl=2)[gl]))
            p.dma(LI[pr], p.dview("od_lam_im", IN["od_lam_im"].ap()[e, d].rearrange("(st gl) q -> gl q st", gl=2)[gl]))
            p.dma(DT[pr], p.dview("od_log_dt", IN["od_log_dt"].ap()[e, d:d + 1, :].rearrange("a (st gl) -> a gl st", gl=2)[:, gl, :].partition_broadcast(64)))
        T_ = tl
        o.ts(T_["lr"], LR, -1e-4, None, ALU.min)
        o.act(T_["dt"], DT, AF.Exp)
        o.tt(T_["a"], T_["lr"], T_["dt"], ALU.mult)
        o.act(mag[d], T_["a"], AF.Exp)
        o.tt(T_["th"], LI, T_["dt"], ALU.mult)
        o.ts(T_["x"], T_["th"], 1.0 / 16, None, ALU.mult)
        o.tt(T_["z"], T_["x"], T_["x"], ALU.mult)
        o.ts(T_["q"], T_["z"], SIN_C[4], None, ALU.mult)
        for a_ in (SIN_C[3], SIN_C[2], SIN_C[1], SIN_C[0]):
            o.stt(T_["q"], T_["q"], a_, T_["z"], ALU.add, ALU.mult)
        o.stt(T_["s"], T_["q"], 1.0, T_["x"], ALU.add, ALU.mult)
        o.ts(T_["q"], T_["z"], COS_C[5], None, ALU.mult)
        for a_ in (COS_C[4], COS_C[3], COS_C[2], COS_C[1], COS_C[0]):
            o.stt(T_["q"], T_["q"], a_, T_["z"], ALU.add, ALU.mult)
        o.ts(T_["c"], T_["q"], 1.0, None, ALU.add)
        for _ in range(4):
            o.tt(T_["cc"], T_["c"], T_["c"], ALU.mult)
            o.tt(T_["ss"], T_["s"], T_["s"], ALU.mult)
            o.tt(T_["cs"], T_["c"], T_["s"], ALU.mult)
            o.tt(T_["c"], T_["cc"], T_["ss"], ALU.subtract)
            o.ts(T_["s"], T_["cs"], 2.0, None, ALU.mult)
        o.cp(CLv[d][:, :, 0], T_["c"]); o.cp(SLv[d][:, :, 0], T_["s"])
        for kk in range(1, 10):
            o.tt(T_["cc"], CLv[d][:, :, kk - 1], CLv[d][:, :, kk - 1], ALU.mult)
            o.tt(T_["ss"], SLv[d][:, :, kk - 1], SLv[d][:, :, kk - 1], ALU.mult)
            o.tt(T_["cs"], CLv[d][:, :, kk - 1], SLv[d][:, :, kk - 1], ALU.mult)
            o.tt(CLv[d][:, :, kk], T_["cc"], T_["ss"], ALU.subtract)
            o.ts(SLv[d][:, :, kk], T_["cs"], 2.0, None, ALU.mult)
        o.ts(NSLv[d], SLv[d], -1.0, None, ALU.mult)
        o.tt(T_["abr"], mag[d], T_["c"], ALU.mult)
        o.tt(T_["abi"], mag[d], T_["s"], ALU.mult)
        o.ts(T_["xr"], T_["abr"], -1.0, None, ALU.add)
        o.tt(T_["t1"], T_["lr"], T_["lr"], ALU.mult)
        o.tt(T_["t2"], LI, LI, ALU.mult)
        o.tt(T_["den"], T_["t1"], T_["t2"], ALU.add)
        o.recip(T_["den"], T_["den"])
        o.tt(T_["t1"], T_["xr"], T_["lr"], ALU.mult)
        o.tt(T_["t2"], T_["abi"], LI, ALU.mult)
        o.tt(T_["t1"], T_["t1"], T_["t2"], ALU.add)
        o.tt(T_["fre"], T_["t1"], T_["den"], ALU.mult)
        o.tt(T_["t1"], T_["abi"], T_["lr"], ALU.mult)
        o.tt(T_["t2"], T_["xr"], LI, ALU.mult)
        o.tt(T_["t1"], T_["t1"], T_["t2"], ALU.subtract)
        o.tt(T_["fim"], T_["t1"], T_["den"], ALU.mult)
        o.ts(T_["nfim"], T_["fim"], -1.0, None, ALU.mult)
        for gl in range(2):
            o.ts(fm["fre%d" % gl], T_["fre"], MK[:, gl:gl + 1], None, ALU.mult)
            o.ts(fm["fim%d" % gl], T_["fim"], MK[:, gl:gl + 1], None, ALU.mult)
            o.ts(fm["nfim%d" % gl], T_["nfim"], MK[:, gl:gl + 1], None, ALU.mult)
        for st in range(16):
            ft_, s4 = divmod(st, 4)
            for gl in range(2):
                cs_ = slice(s4 * 32 + gl * 16, s4 * 32 + gl * 16 + 16)
                fre, fim, nfim = fm["fre%d" % gl][:, st:st + 1], fm["fim%d" % gl][:, st:st + 1], fm["nfim%d" % gl][:, st:st + 1]
                o.ts(tb, BR[:, st, :], fre, None, ALU.mult)
                o.stt(INre[ft_][:, cs_], BI[:, st, :], nfim, tb, ALU.mult, ALU.add)
                o.ts(tb, BR[:, st, :], fim, None, ALU.mult)
                o.stt(INim[ft_][:, cs_], BI[:, st, :], fre, tb, ALU.mult, ALU.add)
        for ft_ in range(4):
            for ri, INx in enumerate((INre, INim)):
                ps = p.ps(2 + ri, 128)
                o.tr(ps, INx[ft_], C["IDENT"])
                for s4 in range(4):
                    o.ts(WB[d][ri][ft_ * 4 + s4], ps, MK[:, 4 + s4:5 + s4], None, ALU.mult)
    p.pop()
    NSEG = [(0, 256)] + [(256 + 512 * i, 512) for i in range(8)]
    cosT = [p.sb([516]) for _ in range(2)]; sinT = [p.sb([516]) for _ in range(2)]
    tA = p.sb([256]); tB = p.sb([256])
    uF = p.sb([TT]); yacc = p.sb([TT])
    wk = {n: [p.sb([512]) for _ in range(2)] for n in ["t1", "t2", "t3", "t4", "wre", "wim", "gre", "gim", "hre", "him"]}
    ini = [p.sb([8]) for _ in range(2)]
    cr = p.sb([8])
    ti = 0
    wi = 0
    for d in range(2):
        segs = NSEG if d == 0 else [NSEG[0]] + NSEG[:0:-1]
        for ft_ in range(4):
            p.dma(uF, p.dview("FM", k.FM.ap()[12 + ft_], (12 + ft_) * 128 * TT, (13 + ft_) * 128 * TT))
            for s4 in range(4):
                st = ft_ * 4 + s4
                cosv, sinv = cosT[ti % 2], sinT[ti % 2]
                ti += 1
                o.memset(cosv[:, 0:1], 1.0, eng="dve"); o.memset(sinv[:, 0:1], 0.0, eng="dve")
                for kk in range(9):
                    w = 1 << kk
                    c_, s_, ns_ = CLv[d][:, st, kk:kk + 1], SLv[d][:, st, kk:kk + 1], NSLv[d][:, st, kk:kk + 1]
                    o.ts(tA[:, 0:w], cosv[:, 0:w], c_, None, ALU.mult)
                    o.ts(tB[:, 0:w], sinv[:, 0:w], c_, None, ALU.mult)
                    o.stt(tA[:, 0:w], sinv[:, 0:w], ns_, tA[:, 0:w], ALU.mult, ALU.add)
                    o.stt(tB[:, 0:w], cosv[:, 0:w], s_, tB[:, 0:w], ALU.mult, ALU.add)
                    o.cp(cosv[:, w:2 * w], tA[:, 0:w]); o.cp(sinv[:, w:2 * w], tB[:, 0:w])
                o.cp(cosv[:, 512:513], CLv[d][:, st, 9:10]); o.cp(sinv[:, 512:513], SLv[d][:, st, 9:10])
                magb = lambda n: bcast_cols(mag[d][:, st:st + 1], n)
                cur = ini[0]
                o.memset(cur[:, 0:2], 0.0, eng="dve")
                for si, (t0, n) in enumerate(segs):
                    W_ = {nm: wk[nm][wi % 2] for nm in wk}
                    wi += 1
                    psr = p.ps(4 + 2 * (wi % 2), n); psi = p.ps(5 + 2 * (wi % 2), n)
                    o.mm(psr, WB[d][0][st], uF[:, t0:t0 + n])
                    o.mm(psi, WB[d][1][st], uF[:, t0:t0 + n])
                    V = (lambda t_: rev(t_, n)) if d == 1 else (lambda t_: t_[:, 0:n])
                    cs_n, sn_n = cosv[:, 0:n], sinv[:, 0:n]
                    o.tt(W_["t1"][:, 0:n], V(psr), cs_n, ALU.mult)
                    o.tt(W_["t2"][:, 0:n], V(psi), sn_n, ALU.mult)
                    o.tt(W_["wre"][:, 0:n], W_["t1"][:, 0:n], W_["t2"][:, 0:n], ALU.add)
                    o.tt(W_["t3"][:, 0:n], V(psi), cs_n, ALU.mult)
                    o.tt(W_["t4"][:, 0:n], V(psr), sn_n, ALU.mult)
                    o.tt(W_["wim"][:, 0:n], W_["t3"][:, 0:n], W_["t4"][:, 0:n], ALU.subtract)
                    gre, gim = W_["gre"], W_["gim"]
                    mb = magb(n)
                    p.op("dve", lambda e_, gre=gre, mb=mb, w=W_["wre"], cur=cur, n=n: e_.tensor_tensor_scan(
                        gre.ap[:, 0:n], mb.ap, w.ap[:, 0:n], cur.ap[:, 0:1], ALU.mult, ALU.add), [mb, W_["wre"], cur], [gre])
                    p.op("dve", lambda e_, gim=gim, mb=mb, w=W_["wim"], cur=cur, n=n: e_.tensor_tensor_scan(
                        gim.ap[:, 0:n], mb.ap, w.ap[:, 0:n], cur.ap[:, 1:2], ALU.mult, ALU.add), [mb, W_["wim"], cur], [gim])
                    o.tt(W_["t1"][:, 0:n], gre[:, 0:n], cs_n, ALU.mult)
                    o.tt(W_["t2"][:, 0:n], gim[:, 0:n], sn_n, ALU.mult)
                    o.tt(V(W_["hre"]), W_["t1"][:, 0:n], W_["t2"][:, 0:n], ALU.subtract)
                    o.tt(W_["t3"][:, 0:n], gim[:, 0:n], cs_n, ALU.mult)
                    o.tt(W_["t4"][:, 0:n], gre[:, 0:n], sn_n, ALU.mult)
                    o.tt(V(W_["him"]), W_["t3"][:, 0:n], W_["t4"][:, 0:n], ALU.add)
                    nxt = ini[(si + 1) % 2]
                    o.ts(cr[:, 0:1], gre[:, n - 1:n], cosv[:, n:n + 1], None, ALU.mult)
                    o.ts(cr[:, 1:2], gim[:, n - 1:n], sinv[:, n:n + 1], None, ALU.mult)
                    o.tt(nxt[:, 0:1], cr[:, 0:1], cr[:, 1:2], ALU.subtract)
                    o.ts(cr[:, 2:3], gim[:, n - 1:n], cosv[:, n:n + 1], None, ALU.mult)
                    o.stt(nxt[:, 1:2], gre[:, n - 1:n], sinv[:, n:n + 1], cr[:, 2:3], ALU.mult, ALU.add)
                    cur = nxt
                    psy = p.ps(wi % 2, n)
                    o.mm(psy, WCf[0][st], W_["hre"][:, 0:n], start=True, stop=False)
                    o.mm(psy, WCf[1][st], W_["him"][:, 0:n], start=False, stop=True)
                    if s4 == 0:
                        o.cp(yacc[:, t0:t0 + n], psy, eng="act")
                    else:
                        o.tt(yacc[:, t0:t0 + n], psy, yacc[:, t0:t0 + n], ALU.add)
            p.dma(p.dview("YS5", k.YS5.ap()[d, ft_], (d * 4 + ft_) * 128 * TT, (d * 4 + ft_ + 1) * 128 * TT), yacc, q="pool")
    p.pop()


def merge_odd(k, l):
    p, o, IN, C = k.p, k.o, k.IN, k.C
    e = l // 2
    p.push()
    wout = p.sb([8, D], BF16)
    wst = [p.sb([8, 256]) for _ in range(2)]
    for j in range(4):
        p.dma(wst[j % 2], p.dview("od_w_out", IN["od_w_out"].ap()[e, :, j * 256:(j + 1) * 256].rearrange("(kt q) f -> q kt f", q=128)))
        o.cp(wout[:, :, j * 256:(j + 1) * 256], wst[j % 2], eng="pool")
    wglu = p.sb([4, 512], BF16)
    for j in range(2):
        w32 = wst[j % 2]
        w3 = w32.v(w32.ap.rearrange("q a b -> q (a b)")[:, 0:1024].rearrange("q (a b) -> q a b", a=4))
        p.dma(w3, p.dview("od_w_glu", IN["od_w_glu"].ap()[e, :, j * 256:(j + 1) * 256].rearrange("(kt q) f -> q kt f", q=128)))
        o.cp(wglu[:, :, j * 256:(j + 1) * 256], w3, eng="pool")
    mg = p.sb([512]); dsk = p.sb([512]); bgl = p.sb([512])
    rowbc(k, mg, IN["od_mlstm_norm"].ap()[e:e + 1, :], "od_mlstm_norm")
    rowbc(k, dsk, IN["od_d_skip"].ap()[e:e + 1, :], "od_d_skip")
    rowbc(k, bgl, IN["od_b_glu"].ap()[e:e + 1, :], "od_b_glu")
    g1 = [p.sb([D]), p.sb([D])]
    for r in range(2):
        rowbc(k, g1[r], k.MOD.ap()[l, r:r + 1, 2048:3072], "MOD")
    lng = p.sb([D]); lnb = p.sb([D])
    rowbc(k, lng, IN["ln1_g"].ap()[l:l + 1, :], "ln1_g")
    rowbc(k, lnb, IN["ln1_b"].ap()[l:l + 1, :], "ln1_b")
    A2 = [p.sb([2, 4, 132]) for _ in range(2)]
    YS = [p.sb([2, 4, 128]) for _ in range(2)]
    zt = [p.sb([D]) for _ in range(2)]
    ht = [p.sb([D]) for _ in range(2)]
    Y = [p.sb([D]) for _ in range(2)]
    YT = [p.sb([8, 128], BF16) for _ in range(2)]
    st_ = [p.sb([48]) for _ in range(2)]
    junk = p.sb([D])
    U = [p.sb([D]) for _ in range(2)]
    HM = [p.sb([4, 128]) for _ in range(2)]
    YG = [p.sb([512]) for _ in range(2)]
    YGT = [p.sb([4, 128], BF16) for _ in range(2)]
    for t in range(NT):
        r = 1 if t < NCTX else 0
        a2, ys, z, h, y, yT, s, u, hm, yg, ygT = (A2[t % 2], YS[t % 2], zt[t % 2], ht[t % 2], Y[t % 2], YT[t % 2], st_[t % 2],
                                                  U[t % 2], HM[t % 2], YG[t % 2], YGT[t % 2])
        tok = slice(t * 128, (t + 1) * 128)
        for d in range(2):
            for hh in range(4):
                p.dma(a2[:, d, hh, 0:129], p.dview("OUTO%d_%d" % (hh, d), k.OUTO.ap()[hh, d, tok, 0:129], t * 128 * 132, (t + 1) * 128 * 132))
                fi = d * 4 + hh
                p.dma(ys[:, d, hh, :], p.dview("YS5", k.YS5.ap()[d, hh][:, tok], fi * 128 * TT, (fi + 1) * 128 * TT))
        p.dma(z, p.dview("ZS", k.ZS.ap()[tok, :], t * 128 * D, (t + 1) * 128 * D))
        p.dma(h, p.dview("H", k.H.ap()[tok, :], t * 128 * D, (t + 1) * 128 * D))
        den = a2[:, :, :, 128]
        s3 = lambda c0: s.v(s.ap[:, c0:c0 + 8].rearrange("q (a b) -> q a b", a=2))
        o.ts(s3(0), den, -1.0, None, ALU.mult)
        o.tt(s3(0), s3(0), den, ALU.max)
        o.ts(s3(0), s3(0), 1.0, None, ALU.max)
        o.recip(s[:, 8:16], s[:, 0:8])
        for hh in range(4):
            o.ts(junk[:, 0:128], a2[:, 0, hh, 0:128], s[:, 8 + hh:9 + hh], None, ALU.mult)
            o.stt(hm[:, hh, :], a2[:, 1, hh, 0:128], s[:, 12 + hh:13 + hh], junk[:, 0:128], ALU.mult, ALU.add)
        for hh in range(4):
            o.act(junk[:, 0:128], hm[:, hh, :], AF.Square, accum=s[:, 16 + hh:17 + hh])
            o.act(junk[:, 128:256], hm[:, hh, :], AF.Identity, accum=s[:, 20 + hh:21 + hh])
        o.ts(s[:, 24:28], s[:, 20:24], 1.0 / 128, None, ALU.mult)
        o.tt(s[:, 28:32], s[:, 24:28], s[:, 24:28], ALU.mult)
        o.stt(s[:, 32:36], s[:, 16:20], 1.0 / 128, s[:, 28:32], ALU.mult, ALU.subtract)
        o.act(s[:, 36:40], s[:, 32:36], AF.Sqrt, bias=k.eps[:, 0:1])
        o.recip(s[:, 40:44], s[:, 36:40])
        for hh in range(4):
            o.ts(junk[:, 0:128], hm[:, hh, :], s[:, 24 + hh:25 + hh], s[:, 40 + hh:41 + hh], ALU.subtract, ALU.mult)
            o.tt(y[:, hh * 128:(hh + 1) * 128], junk[:, 0:128], mg[:, hh * 128:(hh + 1) * 128], ALU.mult)
        o.tt(y[:, 0:512], y[:, 0:512], z[:, 0:512], ALU.mult)
        o.tt(ys[:, 0], ys[:, 0], ys[:, 1], ALU.add)
        ps = p.ps(4)
        for ft_ in range(4):
            o.tr(ps[:, ft_ * 128:(ft_ + 1) * 128], ys[:, 0, ft_, :], C["IDENT"])
        o.tt(junk[:, 0:512], z[:, 512:1024], dsk, ALU.mult)
        o.tt(junk[:, 0:512], junk[:, 0:512], ps, ALU.add)
        xg = junk[:, 0:512]
        o.tt(junk[:, 512:1024], xg, xg, ALU.mult)
        o.ts(junk[:, 512:1024], junk[:, 512:1024], 0.044715, 1.0, ALU.mult, ALU.add)
        o.tt(junk[:, 512:1024], junk[:, 512:1024], xg, ALU.mult)
        o.act(yg, junk[:, 512:1024], AF.Tanh, scale=math.sqrt(2.0 / math.pi))
        o.stt(yg, yg, 1.0, xg, ALU.add, ALU.mult)
        o.ts(yg, yg, 0.5, None, ALU.mult)
        ps2 = p.ps(5)
        for ft_ in range(4):
            o.tr(ps2[:, ft_ * 128:(ft_ + 1) * 128], yg[:, ft_ * 128:(ft_ + 1) * 128], C["IDENT"])
        o.cp(ygT.v(ygT.ap.rearrange("q a b -> q (a b)")), ps2, eng="act")
        ps3 = p.ps(6)
        for kt in range(4):
            o.mm(ps3, ygT[:, kt, :], wglu[:, kt, :], start=(kt == 0), stop=(kt == 3))
        o.tt(junk[:, 0:512], ps3, bgl, ALU.add)
        o.act(junk[:, 512:1024], junk[:, 0:512], AF.Sigmoid)
        o.tt(y[:, 512:1024], yg, junk[:, 512:1024], ALU.mult)
        for half in range(2):
            ps = p.ps(half)
            for q in range(4):
                kt = half * 4 + q
                o.tr(ps[:, q * 128:(q + 1) * 128], y[:, kt * 128:(kt + 1) * 128], C["IDENT"])
            o.cp(yT.v(yT.ap[:, half * 4:(half + 1) * 4, :].rearrange("q a b -> q (a b)")), ps, eng="act")
        for half in range(2):
            ps = p.ps(2 + half)
            for kt in range(8):
                o.mm(ps, yT[:, kt, :], wout[:, kt, half * 512:(half + 1) * 512], start=(kt == 0), stop=(kt == 7))
            hs_ = slice(half * 512, (half + 1) * 512)
            o.tt(u[:, hs_], ps, g1[r][:, hs_], ALU.mult)
        if dbg_here(k, "y"):
            p.dma(p.dview("y_out", k.YOUT.ap()[tok, :], t * 128 * D, (t + 1) * 128 * D), u, q="pool")
        o.stt(u, h, ALPHA, u, ALU.mult, ALU.add)
        layer_norm(k, u, h, lng, lnb, s, junk)
        p.dma(p.dview("H", k.H.ap()[tok, :], t * 128 * D, (t + 1) * 128 * D), h, q="pool")
        if dbg_here(k, "h1"):
            p.dma(p.dview("y_out", k.YOUT.ap()[tok, :], t * 128 * D, (t + 1) * 128 * D), h, q="pool")
    p.pop()


_W_NAMES = ["w_mod", "b_mod", "ln1_g", "ln1_b", "ln2_g", "ln2_b", "w_router", "w_gate", "w_up", "w_down",
            "ev_w_in", "ev_w_out", "ev_conv", "ev_a_log", "ev_dt_bias", "ev_gdn_norm", "ev_ret_norm",
            "od_w_in", "od_w_out", "od_conv", "od_gate_bias", "od_mlstm_norm", "od_lam_re", "od_lam_im", "od_log_dt",
            "od_b_re", "od_b_im", "od_c_re", "od_c_im", "od_d_skip", "od_w_glu", "od_b_glu"]


def kernel(**inputs):
    nc = build(n_layers=4)
    cst = make_consts()
    retg = np.concatenate([ret_log_decay(0), ret_log_decay(1)])[None, :].astype(np.float32)
    in_maps = []
    B = inputs["x"].shape[0]
    for b in range(B):
        m = {}
        m["h0"] = np.ascontiguousarray(np.concatenate([inputs["ctx"][b], inputs["x"][b]], 0).astype(np.float32))
        m["cvec"] = np.ascontiguousarray(np.stack([inputs["c"][b], inputs["c_ctx"]], 0).astype(np.float32))
        m["cst"] = cst
        m["retg"] = retg
        for n in _W_NAMES:
            if n in nc.used_inputs:
                m[n] = np.ascontiguousarray(inputs[n], dtype=np.float32)
        in_maps.append({n: m[n] for n in nc.used_inputs})
    res = run_bass_kernel_spmd(nc, in_maps, core_ids=list(range(B)))
    out = np.stack([np.asarray(res.results[b]["y_out"])[256:] for b in range(B)], 0)
    return out.astype(np.float32)
```

```python
import bisect
import numpy as np
import concourse.bass as bass
import concourse.mybir as mybir
from concourse.bass_utils import run_bass_kernel_spmd

F32 = mybir.dt.float32
BF16 = mybir.dt.bfloat16
ALU = mybir.AluOpType
AF = mybir.ActivationFunctionType
AX = mybir.AxisListType

SEM_GEN = 30000
N_DMA_SEMS = 12


class IntervalMap:
    def __init__(self):
        self.bounds = [0]
        self.state = [(None, {})]

    def _split(self, x):
        i = bisect.bisect_right(self.bounds, x) - 1
        if self.bounds[i] == x:
            return i
        w, r = self.state[i]
        self.bounds.insert(i + 1, x)
        self.state.insert(i + 1, (w, dict(r)))
        return i + 1

    def segs(self, lo, hi):
        i0 = self._split(lo)
        i1 = self._split(hi)
        return range(i0, i1)


class T:
    def __init__(self, ap, key, lo, hi):
        self.ap = ap
        self.key = key
        self.lo = lo
        self.hi = hi

    def __getitem__(self, idx):
        return T(self.ap[idx], self.key, self.lo, self.hi)

    def v(self, ap):
        return T(ap, self.key, self.lo, self.hi)


class Prog:
    ENGS = ["pe", "act", "dve", "pool", "sp"]

    def __init__(self, nc, sb_bytes=206 * 1024):
        self.nc = nc
        self.ops = {e: [] for e in self.ENGS}
        self.cnt = {e: 0 for e in self.ENGS}
        self.maps = {}
        self.observed = {e: {} for e in self.ENGS}
        self.dma_cnt = {"sp": 0, "pool": 0, "act": 0}
        self.dma_sem_uses = {}
        self.sem_names = set()
        self.sb_bytes = sb_bytes
        self.sb_top = 0
        self.sb_stack = []
        self.ps_top = 0
        self.dram = {}
        self.arena = None
        self.psarena = None
        self.final_waits = []

    def setup_mem(self, es):
        nc = self.nc
        self.arena = es.enter_context(nc.sbuf_tensor("arena", [128, self.sb_bytes // 4], F32))
        self.psarena = es.enter_context(nc.psum_tensor("psarena", [128, 8 * 512], F32))

    def push(self):
        self.sb_stack.append(self.sb_top)

    def pop(self):
        self.sb_top = self.sb_stack.pop()

    def sb(self, shape, dtype=F32, name=None):
        esz = 4 if dtype == F32 else 2
        n = int(np.prod(shape))
        nbytes = (n * esz + 31) // 32 * 32
        off = self.sb_top
        self.sb_top += nbytes
        assert self.sb_top <= self.sb_bytes, f"SBUF overflow {self.sb_top}"
        ap = self.arena[:, off // 4:(off + nbytes) // 4]
        if dtype != F32:
            ap = ap.bitcast(dtype)
        ap = ap[:, 0:n]
        if len(shape) == 2:
            ap = ap.rearrange("p (a b) -> p a b", a=shape[0])
        elif len(shape) == 3:
            ap = ap.rearrange("p (a b c) -> p a b c", a=shape[0], b=shape[1])
        return T(ap, "sb", off, off + nbytes)

    def ps(self, bank, ncols=512, dtype=F32, col0=0):
        off = bank * 512 + col0
        ap = self.psarena[:, off:off + ncols]
        return T(ap, "ps", off * 4, (off + ncols) * 4)

    def dram_t(self, name, shape, dtype=F32, kind="Internal"):
        h = self.nc.dram_tensor(name, list(shape), dtype, kind=kind)
        self.dram[name] = h
        return h

    def dview(self, name, ap, lo=0, hi=1 << 40):
        return T(ap, "d:" + name, lo, hi)

    def _deps(self, eng, reads, writes, token):
        deps = set()
        for t in reads:
            m = self.maps.setdefault(t.key, IntervalMap())
            for i in m.segs(t.lo, t.hi):
                w, r = m.state[i]
                if w is not None:
                    deps.add(w)
        for t in writes:
            m = self.maps.setdefault(t.key, IntervalMap())
            for i in m.segs(t.lo, t.hi):
                w, r = m.state[i]
                if w is not None:
                    deps.add(w)
                for tok in r.values():
                    deps.add(tok)
        for t in reads:
            m = self.maps[t.key]
            for i in m.segs(t.lo, t.hi):
                m.state[i][1][token[0] if eng.startswith("dma") else eng] = token
        for t in writes:
            m = self.maps[t.key]
            for i in m.segs(t.lo, t.hi):
                m.state[i] = (token, {})
        return deps

    def _waits(self, eng, deps):
        obs = self.observed[eng]
        best = {}
        for (sem, val, src) in deps:
            if src == "pe" and eng == "pe":
                continue
            if obs.get(sem, 0) >= val:
                continue
            if best.get(sem, 0) < val:
                best[sem] = val
        for sem, val in best.items():
            obs[sem] = val
        return list(best.items())

    limit = None
    count = 0

    def _lim(self):
        if self.limit is not None:
            if self.count >= self.limit:
                return True
            self.count += 1
        return False

    def op(self, eng, fn, reads=(), writes=()):
        if self._lim():
            return
        n = self.cnt[eng]
        gen, idx = divmod(n, SEM_GEN)
        sem = f"{eng}{gen}"
        self.sem_names.add(sem)
        token = (sem, idx + 1, eng)
        self.cnt[eng] = n + 1
        rd2, wr2 = [], []
        for t in reads:
            if t.key == "ps":
                wr2.append(T(t.ap, "ps", t.lo // 2048 * 2048, (t.hi + 2047) // 2048 * 2048))
            else:
                rd2.append(t)
        for t in writes:
            if t.key == "ps":
                wr2.append(T(t.ap, "ps", t.lo // 2048 * 2048, (t.hi + 2047) // 2048 * 2048))
            else:
                wr2.append(t)
        reads, writes = rd2, wr2
        deps = self._deps(eng, reads, writes, token)
        waits = self._waits(eng, deps)
        self.ops[eng].append((waits, fn, (sem, 1)))

    def dma(self, out, in_, q="sp", **kw):
        if self._lim():
            return
        k = self.dma_cnt[q]
        self.dma_cnt[q] = k + 1
        sem = f"dma_{q}{k % N_DMA_SEMS}"
        self.sem_names.add(sem)
        uses = self.dma_sem_uses.get(sem, 0)
        self.dma_sem_uses[sem] = uses + 1
        token = (sem, 16 * (uses + 1), "dma")
        deps = self._deps("dma_" + q, [in_], [out], token)
        if uses > 0:
            deps.add((sem, 16 * uses, "dma"))
        waits = self._waits(q, deps)
        oap, iap = out.ap, in_.ap

        def fn(e, oap=oap, iap=iap, kw=kw):
            return e.dma_start(out=oap, in_=iap, allow_slow_non_contiguous=True, **kw)
        self.ops[q].append((waits, fn, (sem, 16)))
        return token

    def wait_all_dma(self, eng="sp"):
        deps = set()
        for sem, uses in self.dma_sem_uses.items():
            deps.add((sem, 16 * uses, "dma"))
        waits = self._waits(eng, deps)
        self.ops[eng].append((waits, None, None))

    def emit(self, es):
        nc = self.nc
        sems = {}
        for name in sorted(self.sem_names):
            sems[name] = es.enter_context(nc.semaphore(name))
        block = es.enter_context(nc.Block())
        ops = self.ops

        def run(e, lst):
            for waits, fn, inc in lst:
                for sem, val in waits:
                    e.wait_ge(sems[sem], val)
                if fn is not None:
                    ins = fn(e)
                    ins.then_inc(sems[inc[0]], inc[1])

        @block.sync
        def _(e):
            run(e, ops["sp"])

        @block.tensor
        def _(e):
            run(e, ops["pe"])

        @block.vector
        def _(e):
            run(e, ops["dve"])

        @block.scalar
        def _(e):
            run(e, ops["act"])

        @block.gpsimd
        def _(e):
            run(e, ops["pool"])


def _ap(x):
    return x.ap if isinstance(x, T) else x


def _ts(*xs):
    return [x for x in xs if isinstance(x, T)]


class Ops:
    def __init__(self, p):
        self.p = p

    def mm(self, out, lhsT, rhs, start=True, stop=True):
        self.p.op("pe", lambda e: e.matmul(out.ap, lhsT.ap, rhs.ap, start=start, stop=stop), [lhsT, rhs], [out])

    def tr(self, out, in_, ident):
        self.p.op("pe", lambda e: e.transpose(out.ap, in_.ap, ident.ap), [in_, ident], [out])

    def act(self, out, in_, func, bias=None, scale=None, accum=None, eng="act"):
        kw = {}
        if bias is not None:
            kw["bias"] = _ap(bias)
        if scale is not None:
            kw["scale"] = _ap(scale)
        if accum is not None:
            kw["accum_out"] = accum.ap
        self.p.op("act", lambda e: e.activation(out.ap, in_.ap, func, **kw),
                  _ts(in_, bias, scale), _ts(out, accum))

    def tt(self, out, a, b, op, eng="dve"):
        self.p.op(eng, lambda e: e.tensor_tensor(out.ap, a.ap, b.ap, op), [a, b], [out])

    def ts(self, out, a, s1, s2, op0, op1=None, accum=None, eng="dve"):
        kw = {}
        if accum is not None:
            kw["accum_out"] = accum.ap
        if op1 is None:
            fn = lambda e: e.tensor_scalar(out.ap, a.ap, _ap(s1), None, op0, **kw)
        else:
            fn = lambda e: e.tensor_scalar(out.ap, a.ap, _ap(s1), _ap(s2), op0, op1, **kw)
        self.p.op(eng, fn, _ts(a, s1, s2), _ts(out, accum))

    def stt(self, out, a, s, b, op0, op1, eng="dve"):
        self.p.op(eng, lambda e: e.scalar_tensor_tensor(out.ap, a.ap, _ap(s), b.ap, op0, op1), _ts(a, s, b), [out])

    def cp(self, out, in_, eng="dve"):
        if eng == "act":
            self.p.op("act", lambda e: e.copy(out.ap, in_.ap), [in_], [out])
        else:
            self.p.op(eng, lambda e: e.tensor_copy(out.ap, in_.ap), [in_], [out])

    def memset(self, out, val, eng="pool"):
        self.p.op(eng, lambda e: e.memset(out.ap, val), [], [out])

    def recip(self, out, in_):
        self.p.op("dve", lambda e: e.reciprocal(out.ap, in_.ap), [in_], [out])

from contextlib import ExitStack
import math

D = 1024
TT = 4352
NT = 34
NCTX = 2
ALPHA = 8 ** 0.25
EPS = 1e-5
NEG = -30000.0
EVEN_IN = 4112
ODD_IN = 2576

CONST_NAMES = ["IDENT", "ONES", "CSf", "CSb", "MTf", "MTb", "MSf", "MSb", "CSs", "MK", "IO0", "IO1", "IO2", "IO3", "IC"] + ["OH%d" % i for i in range(16)]


def make_consts():
    p = np.arange(128)[:, None]
    f = np.arange(128)[None, :]
    c = {}
    c["IDENT"] = (p == f)
    c["ONES"] = np.ones((128, 128))
    c["CSf"] = (p <= f)
    c["CSb"] = (p >= f)
    c["MTf"] = np.where(p <= f, 0.0, NEG)
    c["MTb"] = np.where(p >= f, 0.0, NEG)
    c["MSf"] = np.where(f < p, 0.0, NEG)
    c["MSb"] = np.where(f > p, 0.0, NEG)
    c["CSs"] = (p < f)
    mk = np.zeros((128, 128))
    mk[:64, 0] = 1; mk[64:, 1] = 1; mk[:16, 2] = 1; mk[16:32, 3] = 1
    for s4 in range(4):
        mk[s4 * 32:(s4 + 1) * 32, 4 + s4] = 1
    c["MK"] = mk
    for i in range(4):
        c["IO%d" % i] = np.broadcast_to(f + 128 * i, (128, 128))
    ic = np.zeros((128, 128))
    for i in range(4):
        ic[:, i] = np.arange(128) + 128 * i
    c["IC"] = ic
    for i in range(16):
        oh = np.zeros((128, 128)); oh[i, :] = 1
        c["OH%d" % i] = oh
    arr = np.concatenate([c[k].astype(np.float32) for k in CONST_NAMES], axis=1)
    return np.ascontiguousarray(arr)


def ret_log_decay(d):
    expo = 5.0 + 2.0 * np.arange(4, dtype=np.float32) + d
    return np.log1p(-np.exp2(-expo)).astype(np.float32)


class K:
    pass


def build(n_layers=4, dbg=(), stage=99, layers=None, dbg_layer=0):
    nc = bass.Bass("TRN2", target_bir_lowering=False)
    k = K()
    k.nc = nc
    k.dbg_on = set(dbg)
    k.stage = stage
    k.dbg_layer = dbg_layer
    k.cur = -1
    SHAPES = dict(h0=[TT, D], cvec=[2, D], cst=[128, 128 * len(CONST_NAMES)], retg=[1, 8],
                  w_mod=[4, D, 6 * D], b_mod=[4, 6 * D], ln1_g=[4, D], ln1_b=[4, D], ln2_g=[4, D], ln2_b=[4, D],
                  w_router=[4, D, 16], w_gate=[4, 16, D, 2048], w_up=[4, 16, D, 2048], w_down=[4, 16, 2048, D],
                  ev_w_in=[2, D, EVEN_IN], ev_w_out=[2, D, D], ev_conv=[2, 3, 3, 1536], ev_a_log=[2, 2, 4],
                  ev_dt_bias=[2, 2, 4], ev_gdn_norm=[2, 128], ev_ret_norm=[2, 512],
                  od_w_in=[2, D, ODD_IN], od_w_out=[2, D, D], od_conv=[2, 3, 3, 1024], od_gate_bias=[2, 2, 2, 4],
                  od_mlstm_norm=[2, 512], od_lam_re=[2, 2, 32, 64], od_lam_im=[2, 2, 32, 64], od_log_dt=[2, 2, 32],
                  od_b_re=[2, 32, 64, 16], od_b_im=[2, 32, 64, 16], od_c_re=[2, 32, 16, 64], od_c_im=[2, 32, 16, 64],
                  od_d_skip=[2, 512], od_w_glu=[2, 512, 512], od_b_glu=[2, 512])

    class LazyIn(dict):
        def __missing__(self, name):
            self[name] = nc.dram_tensor(name, list(SHAPES[name]), F32, kind="ExternalInput")
            return self[name]
    IN = LazyIn()
    k.IN = IN
    with ExitStack() as es:
        p = Prog(nc)
        p.setup_mem(es)
        o = Ops(p)
        k.p, k.o = p, o
        k.H = p.dram_t("H", [TT, D])
        k.MOD = p.dram_t("MOD", [4, 2, 6 * D])
        k.FM = p.dram_t("FM", [24, 128, TT])
        k.ZS = p.dram_t("ZS", [TT, D])
        k.GATES = p.dram_t("GATES", [TT, 16])
        k.OUTM = p.dram_t("OUTM", [8, 2, TT, 128])
        k.OUTO = p.dram_t("OUTO", [4, 2, TT, 132])
        k.YS5 = p.dram_t("YS5", [2, 4, 128, TT])
        k.YSD = [p.dram_t("YSDl", [16, 512, D], BF16), p.dram_t("YSDc", [16, 128, D], BF16)]
        k.YOUT = nc.dram_tensor("y_out", [TT, D], F32, kind="ExternalOutput")
        cst = p.sb([128 * len(CONST_NAMES)])
        p.dma(cst, p.dview("cst", IN["cst"].ap()))
        k.C = {n: cst[:, i * 128:(i + 1) * 128] for i, n in enumerate(CONST_NAMES)}
        i_io = CONST_NAMES.index('IO0')
        k.IOTA = cst[:, i_io * 128:(i_io + 4) * 128]
        eps_c = p.sb([4])
        o.memset(eps_c[:, 0:1], EPS); o.memset(eps_c[:, 1:2], 1e-6); o.memset(eps_c[:, 2:3], 1.0); o.memset(eps_c[:, 3:4], 0.0)
        k.eps = eps_c

        phase_mod(k)
        p.dma(p.dview("H", k.H.ap()), p.dview("h0", IN["h0"].ap()))
        for l in (layers if layers is not None else range(n_layers)):
            if k.stage <= 0:
                break
            k.cur = l
            mixer_layer(k, l)
        p.limit = None
        if not (k.dbg_on - {'none'}) and k.stage >= 5:
            p.dma(p.dview('y_out', k.YOUT.ap()), p.dview('H', k.H.ap()))
        if k.stage < 5:
            p.dma(p.dview('y_out', k.YOUT.ap()[0:8, :]), p.dview('MOD', k.MOD.ap().rearrange('l r (a f) -> (l r a) f', f=1024)[0:8, :]))
        p.wait_all_dma("sp")
        p.wait_all_dma("pool")
        p.emit(es)
    nc.used_inputs = list(IN.keys())
    return nc


def rowbc(k, dst, src_ap, name):
    k.p.dma(dst, k.p.dview(name, src_ap.partition_broadcast(128)))


def phase_mod(k):
    p, o, IN = k.p, k.o, k.IN
    p.push()
    cT = p.sb([2, 8]); sT = p.sb([2, 8])
    for r in range(2):
        p.dma(cT[:, r, :], p.dview("cvec", IN["cvec"].ap()[r].rearrange("(kt q) -> q kt", q=128)))
    o.act(sT, cT, AF.Silu)
    wt = [p.sb([8, 512]) for _ in range(2)]
    bm = p.sb([512]); res = [p.sb([512]) for _ in range(2)]
    i = 0
    for l in range(4):
        for cc in range(12):
            w = wt[i % 2]; r = res[i % 2]
            p.dma(w, p.dview("w_mod", IN["w_mod"].ap()[l, :, cc * 512:(cc + 1) * 512].rearrange("(kt q) f -> q kt f", q=128)))
            p.dma(bm[0:2, :], p.dview("b_mod", IN["b_mod"].ap()[l:l + 1, cc * 512:(cc + 1) * 512].partition_broadcast(2)))
            ps = p.ps(i % 2)
            for kt in range(8):
                o.mm(ps[0:2, :], sT[:, :, kt], w[:, kt, :], start=(kt == 0), stop=(kt == 7))
            o.tt(r[0:2, :], ps[0:2, :], bm[0:2, :], ALU.add)
            p.dma(p.dview("MOD", k.MOD.ap()[l, :, cc * 512:(cc + 1) * 512]), r[0:2, :], q="pool")
            i += 1
    p.pop()


def load_modT(k, l, off, plus1):
    p, o = k.p, k.o
    t = p.sb([2, 8])
    for r in range(2):
        p.dma(t[:, r, :], p.dview("MOD", k.MOD.ap()[l, r, off:off + D].rearrange("(kt q) -> q kt", q=128)))
    if plus1:
        o.ts(t, t, 1.0, None, ALU.add)
    return t


def build_inT(k, l, scT, shT, inT, extra=None):
    p, o = k.p, k.o
    p.push()
    ht = [p.sb([D]) for _ in range(2)]
    for t in range(NT):
        h = ht[t % 2]
        p.dma(h, p.dview("H", k.H.ap()[t * 128:(t + 1) * 128, :], t * 128 * D, (t + 1) * 128 * D))
        r = 1 if t < NCTX else 0
        for half in range(2):
            ps = p.ps(half)
            for q in range(4):
                kt = half * 4 + q
                o.tr(ps[:, q * 128:(q + 1) * 128], h[:, kt * 128:(kt + 1) * 128], k.C["IDENT"])
            for q in range(4):
                kt = half * 4 + q
                o.act(inT[:, kt, t * 128:(t + 1) * 128], ps[:, q * 128:(q + 1) * 128], AF.Identity,
                      bias=shT[:, r, kt:kt + 1], scale=scT[:, r, kt:kt + 1])
                if extra is not None:
                    extra(t, kt, ps[:, q * 128:(q + 1) * 128], r)
    p.pop()


TOKCH = [(i * 512, 512) for i in range(8)] + [(4096, 256)]


def dbg_here(k, name):
    return name in k.dbg_on and k.cur == k.dbg_layer


def mixer_layer(k, l):
    p, o, IN = k.p, k.o, k.IN
    odd = (l % 2 == 1)
    e = l // 2
    C = k.C
    wname = "od_w_in" if odd else "ev_w_in"
    cname = "od_conv" if odd else "ev_conv"
    p.push()
    scT = load_modT(k, l, 1024, True)
    shT = load_modT(k, l, 0, False)
    inT = p.sb([8, TT], BF16)
    build_inT(k, l, scT, shT, inT)
    if k.stage <= 1:
        p.pop(); return
    if not odd:
        specs = [(0 + 128 * h, h, "l2q") for h in range(4)] + [(512 + 128 * h, 4 + h, "l2k") for h in range(4)] + \
                [(1024 + 128 * h, 8 + h, None) for h in range(4)] + [(2064 + 128 * h, None, None) for h in range(4)] + \
                [(2576 + 128 * h, None, "scale") for h in range(4)] + [(3088 + 128 * h, None, None) for h in range(4)]
        nconv = 12
    else:
        specs = [(0 + 128 * h, h, None) for h in range(4)] + [(512 + 128 * h, 4 + h, "scale") for h in range(4)] + \
                [(1024 + 128 * h, None, None) for h in range(4)] + [(2064 + 128 * h, None, None) for h in range(4)]
        nconv = 8
    p.push()
    wconv = p.sb([nconv, 9])
    for t12 in range(nconv):
        p.dma(wconv[:, t12, :], p.dview(cname, IN[cname].ap()[e].rearrange("a b c -> c (a b)")[t12 * 128:(t12 + 1) * 128, :]))
    wf = [p.sb([8, 128]) for _ in range(2)]
    wb = [p.sb([8, 128], BF16) for _ in range(2)]
    ft = [p.sb([TT]) for _ in range(2)]
    yt = p.sb([TT])
    sq = p.sb([512]); rs = p.sb([512])
    for s, (col, cv, mode) in enumerate(specs):
        w32, w16, X = wf[s % 2], wb[s % 2], ft[s % 2]
        p.dma(w32, p.dview(wname, IN[wname].ap()[e, :, col:col + 128].rearrange("(kt q) f -> q kt f", q=128)))
        o.cp(w16, w32, eng="pool")
        for ci, (t0, n) in enumerate(TOKCH):
            ps = p.ps(2 + ci % 2, n)
            for kt in range(8):
                o.mm(ps, w16[:, kt, :], inT[:, kt, t0:t0 + n], start=(kt == 0), stop=(kt == 7))
            o.cp(X[:, t0:t0 + n], ps, eng=("act" if ci % 2 else "dve"))
        if cv is not None:
            wc = wconv[:, cv, :]
            Y = yt
            o.ts(Y[:, 0:256], X[:, 0:256], wc[:, 4:5], None, ALU.mult)
            o.stt(Y[:, 1:256], X[:, 0:255], wc[:, 3:4], Y[:, 1:256], ALU.mult, ALU.add)
            o.stt(Y[:, 0:255], X[:, 1:256], wc[:, 5:6], Y[:, 0:255], ALU.mult, ALU.add)
            Xg = X.v(X.ap[:, 256:TT].rearrange("q (r c) -> q r c", c=64))
            Yg = Y.v(Y.ap[:, 256:TT].rearrange("q (r c) -> q r c", c=64))
            o.ts(Y[:, 256:TT], X[:, 256:TT], wc[:, 4:5], None, ALU.mult)
            for dy in range(3):
                for dx in range(3):
                    if dy == 1 and dx == 1:
                        continue
                    oy, ox = dy - 1, dx - 1
                    r0, r1 = max(0, -oy), 64 - max(0, oy)
                    c0, c1 = max(0, -ox), 64 - max(0, ox)
                    o.stt(Yg[:, r0:r1, c0:c1], Xg[:, r0 + oy:r1 + oy, c0 + ox:c1 + ox], wc[:, dy * 3 + dx:dy * 3 + dx + 1],
                          Yg[:, r0:r1, c0:c1], ALU.mult, ALU.add)
            o.act(X, Y, AF.Silu)
        if mode in ("l2q", "l2k"):
            for ci, (t0, n) in enumerate(TOKCH):
                o.tt(sq[:, 0:n], X[:, t0:t0 + n], X[:, t0:t0 + n], ALU.mult)
                ps = p.ps(4 + ci % 2, n)
                o.mm(ps, C["ONES"], sq[:, 0:n])
                o.act(rs[:, 0:n], ps, AF.Sqrt, bias=k.eps[:, 1:2])
                o.recip(sq[:, 0:n], rs[:, 0:n])
                if mode == "l2q":
                    o.stt(X[:, t0:t0 + n], X[:, t0:t0 + n], 128 ** -0.5, sq[:, 0:n], ALU.mult, ALU.mult)
                else:
                    o.tt(X[:, t0:t0 + n], X[:, t0:t0 + n], sq[:, 0:n], ALU.mult)
        elif mode == "scale":
            o.ts(X, X, 128 ** -0.5, None, ALU.mult)
        p.dma(p.dview("FM", k.FM.ap()[s], s * 128 * TT, (s + 1) * 128 * TT), X, q="pool")
    p.pop()
    if k.stage <= 2:
        p.pop(); return
    p.push()
    wz = p.sb([8, 1040], BF16)
    wst = [p.sb([8, 260]) for _ in range(2)]
    zcols = [(1536, 512), (2048, 16), (3600, 512)] if not odd else [(1536, 512), (2048, 16), (2064, 512)]
    dst = 0
    i = 0
    for (c0, n) in zcols:
        for j in range(0, n, 260):
            m = min(260, n - j)
            w32 = wst[i % 2]
            p.dma(w32[:, :, 0:m], p.dview(wname, IN[wname].ap()[e, :, c0 + j:c0 + j + m].rearrange("(kt q) f -> q kt f", q=128)))
            o.cp(wz[:, :, dst:dst + m], w32[:, :, 0:m], eng="pool")
            dst += m
            i += 1
    if not odd:
        alog = p.sb([8]); dtb = p.sb([8]); nea = p.sb([8])
        rowbc(k, alog, IN["ev_a_log"].ap()[e:e + 1].rearrange("a d h -> a (d h)"), "ev_a_log")
        rowbc(k, dtb, IN["ev_dt_bias"].ap()[e:e + 1].rearrange("a d h -> a (d h)"), "ev_dt_bias")
        o.act(nea, alog, AF.Exp)
    else:
        gbi = p.sb([8]); gbf = p.sb([8])
        for d in range(2):
            rowbc(k, gbi[:, d * 4:(d + 1) * 4], IN["od_gate_bias"].ap()[e, d, 0:1, :], "od_gate_bias")
            rowbc(k, gbf[:, d * 4:(d + 1) * 4], IN["od_gate_bias"].ap()[e, d, 1:2, :], "od_gate_bias")
    zt = [p.sb([D]) for _ in range(2)]
    gt = [p.sb([16]) for _ in range(2)]
    tmp8 = p.sb([8]); tmp8b = p.sb([8])
    for t in range(NT):
        z, g = zt[t % 2], gt[t % 2]
        lt = lambda kt: inT[:, kt, t * 128:(t + 1) * 128]
        psa, psb, psg = p.ps(0), p.ps(1), p.ps(2, 16)
        for kt in range(8):
            o.mm(psa, lt(kt), wz[:, kt, 0:512], start=(kt == 0), stop=(kt == 7))
        for kt in range(8):
            o.mm(psb, lt(kt), wz[:, kt, 528:1040], start=(kt == 0), stop=(kt == 7))
        for kt in range(8):
            o.mm(psg, lt(kt), wz[:, kt, 512:528], start=(kt == 0), stop=(kt == 7))
        if not odd:
            o.act(z[:, 0:512], psa, AF.Silu)
            o.act(z[:, 512:1024], psb, AF.Silu)
            o.tt(tmp8, psg[:, 0:8], dtb, ALU.add)
            o.act(g[:, 0:8], tmp8, AF.Exp)
            o.act(tmp8, g[:, 0:8], AF.Ln, bias=k.eps[:, 2:3])
            o.stt(g[:, 0:8], tmp8, -1.0, nea, ALU.mult, ALU.mult)
            o.act(g[:, 8:16], psg[:, 8:16], AF.Sigmoid)
        else:
            o.act(z[:, 0:512], psa, AF.Sigmoid)
            o.cp(z[:, 512:1024], psb, eng="dve")
            o.tt(tmp8, psg[:, 8:16], gbf, ALU.add)
            o.act(tmp8b, tmp8, AF.Exp, scale=-1.0)
            o.act(tmp8, tmp8b, AF.Ln, bias=k.eps[:, 2:3])
            o.ts(g[:, 0:8], tmp8, -1.0, None, ALU.mult)
            o.tt(tmp8b, psg[:, 0:8], gbi, ALU.add)
            o.act(g[:, 8:16], tmp8b, AF.Exp)
        p.dma(p.dview("ZS", k.ZS.ap()[t * 128:(t + 1) * 128, :], t * 128 * D, (t + 1) * 128 * D), z, q="pool")
        p.dma(p.dview("GATES", k.GATES.ap()[t * 128:(t + 1) * 128, :], t * 128 * 16, (t + 1) * 128 * 16), g, q="pool")
    p.pop()
    p.pop()
    if k.stage <= 3:
        return
    if not odd:
        scan_layer(k, l, [0, 1])
    else:
        scan_layer(k, l, [2])
        if k.stage > 4:
            s5_scan(k, l)
    if k.stage <= 4:
        return
    if not odd:
        merge_even(k, l)
    else:
        merge_odd(k, l)
    if k.stage <= 5:
        return
    moe_layer2(k, l)


def chain_order(d):
    return list(range(NT)) if d == 0 else [1, 0] + list(range(NT - 1, 1, -1))


def scan_layer(k, l, types):
    p, o, IN, C = k.p, k.o, k.IN, k.C
    import os
    if 'OPLIMIT' in os.environ:
        p.limit = int(os.environ['OPLIMIT']); p.count = 0
    p.push()
    gates = p.sb([NT, 16])
    p.dma(gates, p.dview("GATES", k.GATES.ap().rearrange("(c q) g -> q c g", q=128)))
    retg = p.sb([8])
    negb = p.sb([NT, 8])
    if 1 in types:
        rowbc(k, retg, IN["retg"].ap(), "retg")
    if 0 in types:
        o.ts(negb, gates[:, :, 8:16], -1.0, None, ALU.mult)
    RING = 4
    W = lambda n=128: [p.sb([n]) for _ in range(RING)]
    names = ["qT", "kT", "vT", "G1", "cc", "tmp", "ET", "EXPR", "kgT", "qgT", "kend", "bv", "E", "N", "M", "N2", "M2",
             "TTa", "TTb", "AQ", "br", "vn", "o", "o2", "tmp2", "sc"]
    ring = {n: (W(132) if n in ("vn", "o") else W()) for n in names}
    chains = []
    for typ in types:
        for h in range(4):
            for d in range(2):
                chains.append(dict(typ=typ, h=h, d=d, S=[p.sb([132]), p.sb([132])], order=chain_order(d), dec=None))
    for ch in chains:
        o.memset(ch["S"][0], 0.0, eng="dve")
    step_i = [0]

    def decay(ch, gcol, slot):
        d = ch["d"]
        R = {n: ring[n][slot] for n in names}
        CS = C["CSf"] if d == 0 else C["CSb"]
        MT = C["MTf"] if d == 0 else C["MTb"]
        o.ts(R["G1"], C["ONES"], gcol, None, ALU.mult)
        psr = p.ps(0, 128)
        psc = p.ps(1, 256)
        o.mm(psr, R["G1"], CS)
        o.mm(psc[:, 0:128], CS, R["G1"])
        o.mm(psc[:, 128:256], C["ONES"], R["G1"])
        cc = R["cc"]
        o.cp(cc[:, 0:1], psc[:, 0:1], eng="act")
        o.cp(cc[:, 1:2], psc[:, 128:129], eng="act")
        o.stt(R["tmp"], psr, cc[:, 0:1], MT, ALU.subtract, ALU.add)
        o.act(R["ET"], R["tmp"], AF.Exp)
        o.act(R["EXPR"], psr, AF.Exp)
        o.tt(cc[:, 4:5], cc[:, 1:2], cc[:, 0:1], ALU.subtract)
        o.act(cc[:, 2:3], cc[:, 4:5], AF.Exp)
        o.act(cc[:, 3:4], cc[:, 1:2], AF.Exp)
        return dict(ET=R["ET"], EXPR=R["EXPR"], cc=cc, psr=psr)

    def step(ch, c, si):
        typ, h, d = ch["typ"], ch["h"], ch["d"]
        slot = si % RING
        R = {n: ring[n][slot] for n in names}
        sq, sk, sv = (h, 4 + h, 8 + h) if typ != 1 else (12 + h, 16 + h, 20 + h)
        dv = 129 if typ == 2 else 128
        tok = slice(c * 128, (c + 1) * 128)
        for nm, s in (("qT", sq), ("kT", sk), ("vT", sv)):
            p.dma(R[nm], p.dview("FM", k.FM.ap()[s][:, tok], s * 128 * TT, (s + 1) * 128 * TT))
        qT, kT, vT = R["qT"], R["kT"], R["vT"]
        if typ != 1:
            gcol = gates[:, c, d * 4 + h:d * 4 + h + 1]
            dec = decay(ch, gcol, slot)
        else:
            if ch["dec"] is None:
                gcol = retg[:, d * 4 + h:d * 4 + h + 1]
                dslot = 0
                own = {n: p.sb([128]) for n in ["G1", "cc", "tmp", "ET", "EXPR"]}
                save = {n: ring[n][dslot] for n in own}
                for n in own:
                    ring[n][dslot] = own[n]
                ch["dec"] = decay(ch, gcol, dslot)
                for n in own:
                    ring[n][dslot] = save[n]
            dec = ch["dec"]
        cc = dec["cc"]
        pst = p.ps(2, 256)
        o.tr(pst[:, 0:128], kT, C["IDENT"])
        o.tr(pst[:, 128:256], vT, C["IDENT"])
        if typ == 2:
            ei = gates[:, c, 8 + d * 4 + h:8 + d * 4 + h + 1]
            o.ts(R["kend"], pst[:, 0:128], cc[:, 2:3], ei, ALU.mult, ALU.mult)
        else:
            o.ts(R["kend"], pst[:, 0:128], cc[:, 2:3], None, ALU.mult)
        o.tt(R["qgT"], qT, dec["EXPR"], ALU.mult)
        psq = p.ps(3, 128)
        o.mm(psq, kT, qT)
        if typ == 2:
            o.stt(R["AQ"], psq, ei, dec["ET"], ALU.mult, ALU.mult)
        else:
            o.tt(R["AQ"], psq, dec["ET"], ALU.mult)
        S = ch["S"][si % 2][:, 0:dv]
        Sn = ch["S"][(si + 1) % 2][:, 0:dv]
        if typ == 0:
            MS = C["MSf"] if d == 0 else C["MSb"]
            nb = negb[:, c, d * 4 + h:d * 4 + h + 1]
            o.ts(R["bv"], pst[:, 128:256], gates[:, c, 8 + d * 4 + h:8 + d * 4 + h + 1], None, ALU.mult)
            o.tt(R["kgT"], kT, dec["EXPR"], ALU.mult)
            o.stt(R["tmp2"], dec["psr"], cc[:, 0:1], MS, ALU.subtract, ALU.subtract)
            o.act(R["E"], R["tmp2"], AF.Exp, scale=-1.0)
            psk = p.ps(4, 128)
            o.mm(psk, kT, kT)
            o.stt(R["N"], psk, nb, R["E"], ALU.mult, ALU.mult)
            psm = p.ps(5, 128)
            o.tr(psm, R["N"], C["IDENT"])
            o.cp(R["M"], psm, eng="act")
            o.tt(R["TTa"], C["IDENT"], R["M"], ALU.add)
            Nc, Mc, Nn, Mn = R["N"], R["M"], R["N2"], R["M2"]
            Tc, Tn = R["TTa"], R["TTb"]
            for lev in range(1, 7):
                ps1 = p.ps(4, 128)
                o.mm(ps1, Mc, Nc)
                o.cp(Nn, ps1, eng="act")
                if lev < 6:
                    ps2 = p.ps(5, 128)
                    o.mm(ps2, Nc, Mc)
                    o.cp(Mn, ps2, eng="dve")
                ps3 = p.ps(6, 128)
                o.mm(ps3, Nn, Tc)
                o.tt(Tn, ps3, Tc, ALU.add)
                Nc, Nn = Nn, Nc
                Mc, Mn = Mn, Mc
                Tc, Tn = Tn, Tc
            psr2 = p.ps(7, 128)
            o.mm(psr2, R["kgT"], S)
            o.stt(R["br"], psr2, nb, R["bv"], ALU.mult, ALU.add)
            psv = p.ps(7, 128)
            o.mm(psv, Tc, R["br"])
            o.cp(R["vn"][:, 0:128], psv, eng="act")
        else:
            o.cp(R["vn"][:, 0:128], pst[:, 128:256], eng="act")
            if typ == 2:
                o.memset(R["vn"][:, 128:129], 1.0, eng="dve")
        vn = R["vn"][:, 0:dv]
        pso = p.ps(6, dv) if typ != 0 else p.ps(5, dv)
        o.mm(pso, R["qgT"], S, start=True, stop=False)
        o.mm(pso, R["AQ"], vn, start=False, stop=True)
        o.cp(R["o"][:, 0:dv], pso, eng="act")
        if typ == 2:
            p.dma(p.dview("OUTO%d_%d" % (h, d), k.OUTO.ap()[h, d, tok, 0:dv], c * 128 * 132, (c + 1) * 128 * 132), R["o"][:, 0:dv], q="pool")
        else:
            slot8 = typ * 4 + h
            p.dma(p.dview("OUTM%d_%d" % (slot8, d), k.OUTM.ap()[slot8, d, tok, :], c * 128 * 128, (c + 1) * 128 * 128), R["o"][:, 0:128], q="pool")
        pss = p.ps(7, dv)
        o.mm(pss, R["kend"], vn)
        o.stt(Sn, S, cc[:, 3:4], pss, ALU.mult, ALU.add)

    for si in range(NT):
        for ci, ch in enumerate(chains):
            step(ch, ch["order"][si], si)
    p.pop()


def merge_even(k, l):
    p, o, IN, C = k.p, k.o, k.IN, k.C
    e = l // 2
    p.push()
    wout = p.sb([8, D], BF16)
    wst = [p.sb([8, 256]) for _ in range(2)]
    for j in range(4):
        p.dma(wst[j % 2], p.dview("ev_w_out", IN["ev_w_out"].ap()[e, :, j * 256:(j + 1) * 256].rearrange("(kt q) f -> q kt f", q=128)))
        o.cp(wout[:, :, j * 256:(j + 1) * 256], wst[j % 2], eng="pool")
    gg = p.sb([128]); rg = p.sb([512])
    rowbc(k, gg, IN["ev_gdn_norm"].ap()[e:e + 1, :], "ev_gdn_norm")
    rowbc(k, rg, IN["ev_ret_norm"].ap()[e:e + 1, :], "ev_ret_norm")
    g1 = [p.sb([D]), p.sb([D])]
    for r in range(2):
        rowbc(k, g1[r], k.MOD.ap()[l, r:r + 1, 2048:3072], "MOD")
    lng = p.sb([D]); lnb = p.sb([D])
    rowbc(k, lng, IN["ln1_g"].ap()[l:l + 1, :], "ln1_g")
    rowbc(k, lnb, IN["ln1_b"].ap()[l:l + 1, :], "ln1_b")
    of = [p.sb([8, 128]) for _ in range(2)]
    ob = [p.sb([8, 128]) for _ in range(2)]
    zt = [p.sb([D]) for _ in range(2)]
    ht = [p.sb([D]) for _ in range(2)]
    Y = [p.sb([D]) for _ in range(2)]
    YT = [p.sb([8, 128], BF16) for _ in range(2)]
    st = [p.sb([32]) for _ in range(2)]
    junk = p.sb([D])
    U = [p.sb([D]) for _ in range(2)]
    for t in range(NT):
        r = 1 if t < NCTX else 0
        a, b, z, h, y, yT, s, u = of[t % 2], ob[t % 2], zt[t % 2], ht[t % 2], Y[t % 2], YT[t % 2], st[t % 2], U[t % 2]
        tok = slice(t * 128, (t + 1) * 128)
        for hs in range(8):
            p.dma(a[:, hs, :], p.dview("OUTM%d_0" % hs, k.OUTM.ap()[hs, 0, tok, :], t * 128 * 128, (t + 1) * 128 * 128))
            p.dma(b[:, hs, :], p.dview("OUTM%d_1" % hs, k.OUTM.ap()[hs, 1, tok, :], t * 128 * 128, (t + 1) * 128 * 128))
        p.dma(z, p.dview("ZS", k.ZS.ap()[tok, :], t * 128 * D, (t + 1) * 128 * D))
        p.dma(h, p.dview("H", k.H.ap()[tok, :], t * 128 * D, (t + 1) * 128 * D))
        o.tt(a, a, b, ALU.add)
        for hs in range(8):
            o.act(junk[:, 0:128], a[:, hs, :], AF.Square, accum=s[:, hs:hs + 1])
        for hs in range(4, 8):
            o.act(junk[:, 0:128], a[:, hs, :], AF.Identity, accum=s[:, 8 + hs:9 + hs])
        o.act(s[:, 16:20], s[:, 0:4], AF.Sqrt, bias=k.eps[:, 0:1], scale=1.0 / 128)
        o.recip(s[:, 20:24], s[:, 16:20])
        o.ts(s[:, 24:28], s[:, 12:16], 1.0 / 128, None, ALU.mult)
        o.tt(s[:, 28:32], s[:, 24:28], s[:, 24:28], ALU.mult)
        o.stt(s[:, 16:20], s[:, 4:8], 1.0 / 128, s[:, 28:32], ALU.mult, ALU.subtract)
        o.act(s[:, 28:32], s[:, 16:20], AF.Sqrt, bias=k.eps[:, 0:1])
        o.recip(s[:, 16:20], s[:, 28:32])
        for hs in range(4):
            o.stt(y[:, hs * 128:(hs + 1) * 128], a[:, hs, :], s[:, 20 + hs:21 + hs], gg, ALU.mult, ALU.mult)
        for hs in range(4):
            o.ts(junk[:, 0:128], a[:, 4 + hs, :], s[:, 24 + hs:25 + hs], s[:, 16 + hs:17 + hs], ALU.subtract, ALU.mult)
            o.tt(y[:, 512 + hs * 128:512 + (hs + 1) * 128], junk[:, 0:128], rg[:, hs * 128:(hs + 1) * 128], ALU.mult)
        o.tt(y, y, z, ALU.mult)
        for half in range(2):
            ps = p.ps(half)
            for q in range(4):
                kt = half * 4 + q
                o.tr(ps[:, q * 128:(q + 1) * 128], y[:, kt * 128:(kt + 1) * 128], C["IDENT"])
            o.cp(yT.v(yT.ap[:, half * 4:(half + 1) * 4, :].rearrange("q a b -> q (a b)")), ps, eng="act")
        for half in range(2):
            ps = p.ps(2 + half)
            for kt in range(8):
                o.mm(ps, yT[:, kt, :], wout[:, kt, half * 512:(half + 1) * 512], start=(kt == 0), stop=(kt == 7))
            hs_ = slice(half * 512, (half + 1) * 512)
            o.tt(u[:, hs_], ps, g1[r][:, hs_], ALU.mult)
        if dbg_here(k, "y"):
            p.dma(p.dview("y_out", k.YOUT.ap()[tok, :], t * 128 * D, (t + 1) * 128 * D), u, q="pool")
        o.stt(u, h, ALPHA, u, ALU.mult, ALU.add)
        layer_norm(k, u, h, lng, lnb, s, junk)
        p.dma(p.dview("H", k.H.ap()[tok, :], t * 128 * D, (t + 1) * 128 * D), h, q="pool")
        if dbg_here(k, "h1"):
            p.dma(p.dview("y_out", k.YOUT.ap()[tok, :], t * 128 * D, (t + 1) * 128 * D), h, q="pool")
    p.pop()


def layer_norm(k, u, out, g, b, s, junk):
    o = k.o
    o.act(junk, u, AF.Identity, accum=s[:, 0:1])
    o.act(junk, u, AF.Square, accum=s[:, 1:2])
    o.ts(s[:, 2:3], s[:, 0:1], 1.0 / D, None, ALU.mult)
    o.tt(s[:, 3:4], s[:, 2:3], s[:, 2:3], ALU.mult)
    o.stt(s[:, 4:5], s[:, 1:2], 1.0 / D, s[:, 3:4], ALU.mult, ALU.subtract)
    o.act(s[:, 5:6], s[:, 4:5], AF.Sqrt, bias=k.eps[:, 0:1])
    o.recip(s[:, 6:7], s[:, 5:6])
    o.ts(junk, u, s[:, 2:3], s[:, 6:7], ALU.subtract, ALU.mult)
    o.tt(junk, junk, g, ALU.mult)
    o.tt(out, junk, b, ALU.add)


def moe_layer(k, l, update_ctx=True):
    p, o, IN, C = k.p, k.o, k.IN, k.C
    p.push()
    g2 = [p.sb([D]), p.sb([D])]
    for r in range(2):
        rowbc(k, g2[r], k.MOD.ap()[l, r:r + 1, 5120:6144], "MOD")
    lng = p.sb([D]); lnb = p.sb([D])
    rowbc(k, lng, IN["ln2_g"].ap()[l:l + 1, :], "ln2_g")
    rowbc(k, lnb, IN["ln2_b"].ap()[l:l + 1, :], "ln2_b")
    wr = p.sb([8, 16])
    p.dma(wr, p.dview("w_router", IN["w_router"].ap()[l].rearrange("(kt q) e -> q kt e", q=128)))
    in2T = p.sb([8, TT], BF16)
    aff = p.sb([NT, 16]); gw = p.sb([NT, 16])
    p.push()
    affT = p.sb([TT])
    sc2 = [p.sb([D]), p.sb([D])]; sh2 = [p.sb([D]), p.sb([D])]
    for r in range(2):
        rowbc(k, sc2[r], k.MOD.ap()[l, r:r + 1, 4096:5120], "MOD")
        o.ts(sc2[r], sc2[r], 1.0, None, ALU.add)
        rowbc(k, sh2[r], k.MOD.ap()[l, r:r + 1, 3072:4096], "MOD")
    p.push()
    ht = [p.sb([D]) for _ in range(2)]
    x2 = [p.sb([D]) for _ in range(2)]
    xTf = [p.sb([8, 128]) for _ in range(2)]
    sm = [p.sb([40]) for _ in range(2)]
    for t in range(NT):
        r = 1 if t < NCTX else 0
        h, x, xf, s = ht[t % 2], x2[t % 2], xTf[t % 2], sm[t % 2]
        tok = slice(t * 128, (t + 1) * 128)
        p.dma(h, p.dview("H", k.H.ap()[tok, :], t * 128 * D, (t + 1) * 128 * D))
        o.tt(x, h, sc2[r], ALU.mult)
        o.tt(x, x, sh2[r], ALU.add)
        for half in range(2):
            ps = p.ps(half)
            for q in range(4):
                kt = half * 4 + q
                o.tr(ps[:, q * 128:(q + 1) * 128], x[:, kt * 128:(kt + 1) * 128], C["IDENT"])
            ps3 = ps.v(ps.ap.rearrange("q (a b) -> q a b", a=4))
            o.cp(xf[:, half * 4:(half + 1) * 4, :], ps3, eng="act")
            o.cp(in2T[:, half * 4:(half + 1) * 4, tok], ps3, eng="dve")
        psl = p.ps(2, 16)
        for kt in range(8):
            o.mm(psl, xf[:, kt, :], wr[:, kt, :], start=(kt == 0), stop=(kt == 7))
        p.op("dve", lambda e, s=s, psl=psl: e.reduce_max(s.ap[:, 0:1], psl.ap, AX.X), [psl], [s])
        o.ts(s[:, 1:2], s[:, 0:1], -1.0, None, ALU.mult)
        o.act(s[:, 8:24], psl, AF.Exp, bias=s[:, 1:2], accum=s[:, 2:3])
        o.recip(s[:, 3:4], s[:, 2:3])
        o.ts(aff[:, t, :], s[:, 8:24], s[:, 3:4], None, ALU.mult)
        pst = p.ps(3, 128)
        o.tr(pst[0:16, :], aff[:, t, :], C["IDENT"])
        o.cp(affT[0:16, tok], pst[0:16, :], eng="act")
    p.pop()
    p.push()
    st = p.sb([16]); junk = p.sb([4096]); thr = [p.sb([16]), p.sb([16])]; dt = p.sb([16])
    sets = [(0, 256, 32.0, 1), (256, TT, 512.0, 0)]
    for (c0, c1, cap, r) in sets:
        lo, hi, mid, cnt, ge, d1 = [st[0:16, i:i + 1] for i in range(6)]
        o.memset(lo, 0.0, eng="dve"); o.memset(hi, 1.0, eng="dve")
        for it in range(32):
            o.tt(mid, lo, hi, ALU.add)
            o.ts(mid, mid, 0.5, None, ALU.mult)
            o.ts(junk[0:16, 0:c1 - c0], affT[0:16, c0:c1], mid, None, ALU.is_ge, ALU.add, accum=cnt)
            o.ts(ge, cnt, cap - 0.5, None, ALU.is_ge)
            o.tt(d1, mid, lo, ALU.subtract)
            o.stt(lo, d1, ge, lo, ALU.mult, ALU.add)
            o.tt(d1, hi, mid, ALU.subtract)
            o.stt(hi, d1, ge, mid, ALU.mult, ALU.add)
        o.ts(dt[0:16, 0:16], C["IDENT"][0:16, 0:16], lo, None, ALU.mult)
        pth = p.ps(2, 16)
        o.mm(pth, C["ONES"][0:16, :], dt[0:16, 0:16])
        o.cp(thr[r], pth, eng="act")
    for t in range(NT):
        r = 1 if t < NCTX else 0
        o.tt(gw[:, t, :], aff[:, t, :], thr[r], ALU.is_ge)
        o.tt(gw[:, t, :], gw[:, t, :], aff[:, t, :], ALU.mult)
    p.pop()
    p.pop()
    p.push()
    wgf = [p.sb([8, 256]) for _ in range(2)]; wuf = [p.sb([8, 256]) for _ in range(2)]
    wgb = [p.sb([8, 256], BF16) for _ in range(2)]; wub = [p.sb([8, 256], BF16) for _ in range(2)]
    wdf = [p.sb([2, 512]) for _ in range(2)]; wdb = [p.sb([2, 512], BF16) for _ in range(2)]
    hT = p.sb([16, 512], BF16)
    sg = [p.sb([512]) for _ in range(2)]
    acc = [p.sb([D]) for _ in range(4)]
    hh = [p.sb([D])] * 2
    s8 = p.sb([8]); junk2 = p.sb([D])
    wi = 0
    di = 0
    import os
    nexp = int(os.environ.get("MOE_NEXP", 16))
    for (t0, n) in TOKCH:
        nst = n // 128
        for e in range(nexp):
            for fg in range(8):
                a32, b32, a16, b16 = wgf[wi % 2], wuf[wi % 2], wgb[wi % 2], wub[wi % 2]
                wi += 1
                p.dma(a32, p.dview("w_gate", IN["w_gate"].ap()[l, e, :, fg * 256:(fg + 1) * 256].rearrange("(kt q) f -> q kt f", q=128)))
                p.dma(b32, p.dview("w_up", IN["w_up"].ap()[l, e, :, fg * 256:(fg + 1) * 256].rearrange("(kt q) f -> q kt f", q=128)))
                o.cp(a16, a32, eng="pool")
                o.cp(b16, b32, eng="pool")
                for f2 in range(2):
                    ft = fg * 2 + f2
                    psg = p.ps(4 + (ft % 2) * 2, n)
                    psu = p.ps(5 + (ft % 2) * 2, n)
                    for kt in range(8):
                        o.mm(psg, a16[:, kt, f2 * 128:(f2 + 1) * 128], in2T[:, kt, t0:t0 + n], start=(kt == 0), stop=(kt == 7))
                    for kt in range(8):
                        o.mm(psu, b16[:, kt, f2 * 128:(f2 + 1) * 128], in2T[:, kt, t0:t0 + n], start=(kt == 0), stop=(kt == 7))
                    s_ = sg[ft % 2]
                    o.act(s_[:, 0:n], psg, AF.Silu)
                    o.tt(hT[:, ft, 0:n], s_[:, 0:n], psu, ALU.mult)
            for half in range(2):
                psd = [p.ps(st_, 512) for st_ in range(nst)]
                for fp_ in range(8):
                    d32, d16 = wdf[di % 2], wdb[di % 2]
                    di += 1
                    p.dma(d32, p.dview("w_down", IN["w_down"].ap()[l, e, fp_ * 256:(fp_ + 1) * 256, half * 512:(half + 1) * 512].rearrange("(a q) d -> q a d", q=128)))
                    o.cp(d16, d32, eng="pool")
                    for f2 in range(2):
                        ft = fp_ * 2 + f2
                        for st_ in range(nst):
                            o.mm(psd[st_], hT[:, ft, st_ * 128:(st_ + 1) * 128], d16[:, f2, :], start=(ft == 0), stop=(ft == 15))
                for st_ in range(nst):
                    t = t0 // 128 + st_
                    a_ = acc[st_][:, half * 512:(half + 1) * 512]
                    if e == 0:
                        o.ts(a_, psd[st_], gw[:, t, e:e + 1], None, ALU.mult)
                    else:
                        o.stt(a_, psd[st_], gw[:, t, e:e + 1], a_, ALU.mult, ALU.add)
        for st_ in range(nst):
            t = t0 // 128 + st_
            r = 1 if t < NCTX else 0
            tok = slice(t * 128, (t + 1) * 128)
            if dbg_here(k, "f"):
                p.dma(p.dview("y_out", k.YOUT.ap()[tok, :], t * 128 * D, (t + 1) * 128 * D), acc[st_], q="pool")
            h = hh[st_ % 2]
            p.dma(h, p.dview("H", k.H.ap()[tok, :], t * 128 * D, (t + 1) * 128 * D))
            u = acc[st_]
            o.tt(u, u, g2[r], ALU.mult)
            o.stt(u, h, ALPHA, u, ALU.mult, ALU.add)
            layer_norm(k, u, h, lng, lnb, s8, junk2)
            if update_ctx or r == 0:
                p.dma(p.dview("H", k.H.ap()[tok, :], t * 128 * D, (t + 1) * 128 * D), h, q="pool")
    p.pop()
    p.pop()


def rev(t, n):
    a = t.ap
    st = a.ap[-1][0]
    return t.v(bass.AP(a.tensor, a.offset + (n - 1) * st, [list(a.ap[0]), [-st, n]]))


def bcast_cols(t, n):
    a = t.ap
    return t.v(bass.AP(a.tensor, a.offset, [list(a.ap[0]), [0, n]]))


SIN_C = [-1.0 / 6, 1.0 / 120, -1.0 / 5040, 1.0 / 362880, -1.0 / 39916800]
COS_C = [-0.5, 1.0 / 24, -1.0 / 720, 1.0 / 40320, -1.0 / 3628800, 1.0 / 479001600]


def s5_scan(k, l):
    p, o, IN, C = k.p, k.o, k.IN, k.C
    e = l // 2
    MK = C["MK"]
    p.push()
    WCf = [[p.sb([128]) for _ in range(16)] for _ in range(2)]
    WB = [[[p.sb([128]) for _ in range(16)] for _ in range(2)] for _ in range(2)]
    mag = [p.sb([16]) for _ in range(2)]
    CLv = [p.sb([16, 10]) for _ in range(2)]; SLv = [p.sb([16, 10]) for _ in range(2)]; NSLv = [p.sb([16, 10]) for _ in range(2)]
    p.push()
    BR = p.sb([16, 16]); BI = p.sb([16, 16])
    CLr = p.sb([16, 64]); CLi = p.sb([16, 64])
    for gl in range(2):
        pr = slice(gl * 64, (gl + 1) * 64)
        p.dma(BR[pr], p.dview("od_b_re", IN["od_b_re"].ap()[e].rearrange("(st gl) q h -> gl q st h", gl=2)[gl]))
        p.dma(BI[pr], p.dview("od_b_im", IN["od_b_im"].ap()[e].rearrange("(st gl) q h -> gl q st h", gl=2)[gl]))
        pc = slice(gl * 16, (gl + 1) * 16)
        p.dma(CLr[pc], p.dview("od_c_re", IN["od_c_re"].ap()[e].rearrange("(st gl) h q -> gl h st q", gl=2)[gl]))
        p.dma(CLi[pc], p.dview("od_c_im", IN["od_c_im"].ap()[e].rearrange("(st gl) h q -> gl h st q", gl=2)[gl]))
    X = [p.sb([128]) for _ in range(2)]
    for ri, CLx in enumerate((CLr, CLi)):
        for st in range(16):
            s4 = st % 4
            x = X[st % 2]
            for gl2 in range(2):
                o.ts(x[0:32, gl2 * 64:(gl2 + 1) * 64], CLx[0:32, st, :], MK[0:32, 2 + gl2:3 + gl2], None, ALU.mult)
            ps = p.ps(st % 2, 32)
            o.tr(ps, x[0:32, :], C["IDENT"][0:32, 0:32])
            w = WCf[ri][st]
            o.memset(w, 0.0, eng="pool")
            if ri == 0:
                o.cp(w[:, s4 * 32:(s4 + 1) * 32], ps, eng="act")
            else:
                o.ts(w[:, s4 * 32:(s4 + 1) * 32], ps, -1.0, None, ALU.mult)
    LR = p.sb([16]); LI = p.sb([16]); DT = p.sb([16])
    tl = {n: p.sb([16]) for n in ["lr", "dt", "a", "th", "x", "z", "q", "s", "c", "cc", "ss", "cs", "abr", "abi", "xr", "den",
                                  "t1", "t2", "fre", "fim", "nfim"]}
    fm = {n: p.sb([16]) for n in ["fre0", "fre1", "fim0", "fim1", "nfim0", "nfim1"]}
    INre = [p.sb([128]) for _ in range(4)]; INim = [p.sb([128]) for _ in range(4)]
    tb = p.sb([16])
    for d in range(2):
        for gl in range(2):
            pr = slice(gl * 64, (gl + 1) * 64)
            p.dma(LR[pr], p.dview("od_lam_re", IN["od_lam_re"].ap()[e, d].rearrange("(st gl) q -> gl q st", gl=2)[gl]))
            p.dma(LI[pr], p.dview("od_lam_im", IN["od_lam_im"].ap()[e, d].rearrange("(st gl) q -> gl q st", gl=2)[gl]))
            p.dma(DT[pr], p.dview("od_log_dt", IN["od_log_dt"].ap()[e, d:d + 1, :].rearrange("a (st gl) -> a gl st", gl=2)[:, gl, :].partition_broadcast(64)))
        T_ = tl
        o.ts(T_["lr"], LR, -1e-4, None, ALU.min)
        o.act(T_["dt"], DT, AF.Exp)
        o.tt(T_["a"], T_["lr"], T_["dt"], ALU.mult)
        o.act(mag[d], T_["a"], AF.Exp)
        o.tt(T_["th"], LI, T_["dt"], ALU.mult)
        o.ts(T_["x"], T_["th"], 1.0 / 16, None, ALU.mult)
        o.tt(T_["z"], T_["x"], T_["x"], ALU.mult)
        o.ts(T_["q"], T_["z"], SIN_C[4], None, ALU.mult)
        for a_ in (SIN_C[3], SIN_C[2], SIN_C[1], SIN_C[0]):
            o.stt(T_["q"], T_["q"], a_, T_["z"], ALU.add, ALU.mult)
        o.stt(T_["s"], T_["q"], 1.0, T_["x"], ALU.add, ALU.mult)
        o.ts(T_["q"], T_["z"], COS_C[5], None, ALU.mult)
        for a_ in (COS_C[4], COS_C[3], COS_C[2], COS_C[1], COS_C[0]):
            o.stt(T_["q"], T_["q"], a_, T_["z"], ALU.add, ALU.mult)
        o.ts(T_["c"], T_["q"], 1.0, None, ALU.add)
        for _ in range(4):
            o.tt(T_["cc"], T_["c"], T_["c"], ALU.mult)
            o.tt(T_["ss"], T_["s"], T_["s"], ALU.mult)
            o.tt(T_["cs"], T_["c"], T_["s"], ALU.mult)
            o.tt(T_["c"], T_["cc"], T_["ss"], ALU.subtract)
            o.ts(T_["s"], T_["cs"], 2.0, None, ALU.mult)
        o.cp(CLv[d][:, :, 0], T_["c"]); o.cp(SLv[d][:, :, 0], T_["s"])
        for kk in range(1, 10):
            o.tt(T_["cc"], CLv[d][:, :, kk - 1], CLv[d][:, :, kk - 1], ALU.mult)
            o.tt(T_["ss"], SLv[d][:, :, kk - 1], SLv[d][:, :, kk - 1], ALU.mult)
            o.tt(T_["cs"], CLv[d][:, :, kk - 1], SLv[d][:, :, kk - 1], ALU.mult)
            o.tt(CLv[d][:, :, kk], T_["cc"], T_["ss"], ALU.subtract)
            o.ts(SLv[d][:, :, kk], T_["cs"], 2.0, None, ALU.mult)
        o.ts(NSLv[d], SLv[d], -1.0, None, ALU.mult)
        o.tt(T_["abr"], mag[d], T_["c"], ALU.mult)
        o.tt(T_["abi"], mag[d], T_["s"], ALU.mult)
        o.ts(T_["xr"], T_["abr"], -1.0, None, ALU.add)
        o.tt(T_["t1"], T_["lr"], T_["lr"], ALU.mult)
        o.tt(T_["t2"], LI, LI, ALU.mult)
        o.tt(T_["den"], T_["t1"], T_["t2"], ALU.add)
        o.recip(T_["den"], T_["den"])
        o.tt(T_["t1"], T_["xr"], T_["lr"], ALU.mult)
        o.tt(T_["t2"], T_["abi"], LI, ALU.mult)
        o.tt(T_["t1"], T_["t1"], T_["t2"], ALU.add)
        o.tt(T_["fre"], T_["t1"], T_["den"], ALU.mult)
        o.tt(T_["t1"], T_["abi"], T_["lr"], ALU.mult)
        o.tt(T_["t2"], T_["xr"], LI, ALU.mult)
        o.tt(T_["t1"], T_["t1"], T_["t2"], ALU.subtract)
        o.tt(T_["fim"], T_["t1"], T_["den"], ALU.mult)
        o.ts(T_["nfim"], T_["fim"], -1.0, None, ALU.mult)
        for gl in range(2):
            o.ts(fm["fre%d" % gl], T_["fre"], MK[:, gl:gl + 1], None, ALU.mult)
            o.ts(fm["fim%d" % gl], T_["fim"], MK[:, gl:gl + 1], None, ALU.mult)
            o.ts(fm["nfim%d" % gl], T_["nfim"], MK[:, gl:gl + 1], None, ALU.mult)
        for st in range(16):
            ft_, s4 = divmod(st, 4)
            for gl in range(2):
                cs_ = slice(s4 * 32 + gl * 16, s4 * 32 + gl * 16 + 16)
                fre, fim, nfim = fm["fre%d" % gl][:, st:st + 1], fm["fim%d" % gl][:, st:st + 1], fm["nfim%d" % gl][:, st:st + 1]
                o.ts(tb, BR[:, st, :], fre, None, ALU.mult)
                o.stt(INre[ft_][:, cs_], BI[:, st, :], nfim, tb, ALU.mult, ALU.add)
                o.ts(tb, BR[:, st, :], fim, None, ALU.mult)
                o.stt(INim[ft_][:, cs_], BI[:, st, :], fre, tb, ALU.mult, ALU.add)
        for ft_ in range(4):
            for ri, INx in enumerate((INre, INim)):
                ps = p.ps(2 + ri, 128)
                o.tr(ps, INx[ft_], C["IDENT"])
                for s4 in range(4):
                    o.ts(WB[d][ri][ft_ * 4 + s4], ps, MK[:, 4 + s4:5 + s4], None, ALU.mult)
    p.pop()
    NSEG = [(0, 256)] + [(256 + 512 * i, 512) for i in range(8)]
    cosT = [p.sb([516]) for _ in range(2)]; sinT = [p.sb([516]) for _ in range(2)]
    tA = p.sb([256]); tB = p.sb([256])
    uF = p.sb([TT]); yacc = p.sb([TT])
    wk = {n: [p.sb([512]) for _ in range(2)] for n in ["t1", "t2", "t3", "t4", "wre", "wim", "gre", "gim", "hre", "him"]}
    ini = [p.sb([8]) for _ in range(2)]
    cr = p.sb([8])
    ti = 0
    wi = 0
    for d in range(2):
        segs = NSEG if d == 0 else [NSEG[0]] + NSEG[:0:-1]
        for ft_ in range(4):
            p.dma(uF, p.dview("FM", k.FM.ap()[12 + ft_], (12 + ft_) * 128 * TT, (13 + ft_) * 128 * TT))
            for s4 in range(4):
                st = ft_ * 4 + s4
                cosv, sinv = cosT[ti % 2], sinT[ti % 2]
                ti += 1
                o.memset(cosv[:, 0:1], 1.0, eng="dve"); o.memset(sinv[:, 0:1], 0.0, eng="dve")
                for kk in range(9):
                    w = 1 << kk
                    c_, s_, ns_ = CLv[d][:, st, kk:kk + 1], SLv[d][:, st, kk:kk + 1], NSLv[d][:, st, kk:kk + 1]
                    o.ts(tA[:, 0:w], cosv[:, 0:w], c_, None, ALU.mult)
                    o.ts(tB[:, 0:w], sinv[:, 0:w], c_, None, ALU.mult)
                    o.stt(tA[:, 0:w], sinv[:, 0:w], ns_, tA[:, 0:w], ALU.mult, ALU.add)
                    o.stt(tB[:, 0:w], cosv[:, 0:w], s_, tB[:, 0:w], ALU.mult, ALU.add)
                    o.cp(cosv[:, w:2 * w], tA[:, 0:w]); o.cp(sinv[:, w:2 * w], tB[:, 0:w])
                o.cp(cosv[:, 512:513], CLv[d][:, st, 9:10]); o.cp(sinv[:, 512:513], SLv[d][:, st, 9:10])
                magb = lambda n: bcast_cols(mag[d][:, st:st + 1], n)
                cur = ini[0]
                o.memset(cur[:, 0:2], 0.0, eng="dve")
                for si, (t0, n) in enumerate(segs):
                    W_ = {nm: wk[nm][wi % 2] for nm in wk}
                    wi += 1
                    psr = p.ps(4 + 2 * (wi % 2), n); psi = p.ps(5 + 2 * (wi % 2), n)
                    o.mm(psr, WB[d][0][st], uF[:, t0:t0 + n])
                    o.mm(psi, WB[d][1][st], uF[:, t0:t0 + n])
                    V = (lambda t_: rev(t_, n)) if d == 1 else (lambda t_: t_[:, 0:n])
                    cs_n, sn_n = cosv[:, 0:n], sinv[:, 0:n]
                    o.tt(W_["t1"][:, 0:n], V(psr), cs_n, ALU.mult)
                    o.tt(W_["t2"][:, 0:n], V(psi), sn_n, ALU.mult)
                    o.tt(W_["wre"][:, 0:n], W_["t1"][:, 0:n], W_["t2"][:, 0:n], ALU.add)
                    o.tt(W_["t3"][:, 0:n], V(psi), cs_n, ALU.mult)
                    o.tt(W_["t4"][:, 0:n], V(psr), sn_n, ALU.mult)
                    o.tt(W_["wim"][:, 0:n], W_["t3"][:, 0:n], W_["t4"][:, 0:n], ALU.subtract)
                    gre, gim = W_["gre"], W_["gim"]
                    mb = magb(n)
                    p.op("dve", lambda e_, gre=gre, mb=mb, w=W_["wre"], cur=cur, n=n: e_.tensor_tensor_scan(
                        gre.ap[:, 0:n], mb.ap, w.ap[:, 0:n], cur.ap[:, 0:1], ALU.mult, ALU.add), [mb, W_["wre"], cur], [gre])
                    p.op("dve", lambda e_, gim=gim, mb=mb, w=W_["wim"], cur=cur, n=n: e_.tensor_tensor_scan(
                        gim.ap[:, 0:n], mb.ap, w.ap[:, 0:n], cur.ap[:, 1:2], ALU.mult, ALU.add), [mb, W_["wim"], cur], [gim])
                    o.tt(W_["t1"][:, 0:n], gre[:, 0:n], cs_n, ALU.mult)
                    o.tt(W_["t2"][:, 0:n], gim[:, 0:n], sn_n, ALU.mult)
                    o.tt(V(W_["hre"]), W_["t1"][:, 0:n], W_["t2"][:, 0:n], ALU.subtract)
                    o.tt(W_["t3"][:, 0:n], gim[:, 0:n], cs_n, ALU.mult)
                    o.tt(W_["t4"][:, 0:n], gre[:, 0:n], sn_n, ALU.mult)
                    o.tt(V(W_["him"]), W_["t3"][:, 0:n], W_["t4"][:, 0:n], ALU.add)
                    nxt = ini[(si + 1) % 2]
                    o.ts(cr[:, 0:1], gre[:, n - 1:n], cosv[:, n:n + 1], None, ALU.mult)
                    o.ts(cr[:, 1:2], gim[:, n - 1:n], sinv[:, n:n + 1], None, ALU.mult)
                    o.tt(nxt[:, 0:1], cr[:, 0:1], cr[:, 1:2], ALU.subtract)
                    o.ts(cr[:, 2:3], gim[:, n - 1:n], cosv[:, n:n + 1], None, ALU.mult)
                    o.stt(nxt[:, 1:2], gre[:, n - 1:n], sinv[:, n:n + 1], cr[:, 2:3], ALU.mult, ALU.add)
                    cur = nxt
                    psy = p.ps(wi % 2, n)
                    o.mm(psy, WCf[0][st], W_["hre"][:, 0:n], start=True, stop=False)
                    o.mm(psy, WCf[1][st], W_["him"][:, 0:n], start=False, stop=True)
                    if s4 == 0:
                        o.cp(yacc[:, t0:t0 + n], psy, eng="act")
                    else:
                        o.tt(yacc[:, t0:t0 + n], psy, yacc[:, t0:t0 + n], ALU.add)
            p.dma(p.dview("YS5", k.YS5.ap()[d, ft_], (d * 4 + ft_) * 128 * TT, (d * 4 + ft_ + 1) * 128 * TT), yacc, q="pool")
    p.pop()


def merge_odd(k, l):
    p, o, IN, C = k.p, k.o, k.IN, k.C
    e = l // 2
    p.push()
    wout = p.sb([8, D], BF16)
    wst = [p.sb([8, 256]) for _ in range(2)]
    for j in range(4):
        p.dma(wst[j % 2], p.dview("od_w_out", IN["od_w_out"].ap()[e, :, j * 256:(j + 1) * 256].rearrange("(kt q) f -> q kt f", q=128)))
        o.cp(wout[:, :, j * 256:(j + 1) * 256], wst[j % 2], eng="pool")
    wglu = p.sb([4, 512], BF16)
    for j in range(2):
        w32 = wst[j % 2]
        w3 = w32.v(w32.ap.rearrange("q a b -> q (a b)")[:, 0:1024].rearrange("q (a b) -> q a b", a=4))
        p.dma(w3, p.dview("od_w_glu", IN["od_w_glu"].ap()[e, :, j * 256:(j + 1) * 256].rearrange("(kt q) f -> q kt f", q=128)))
        o.cp(wglu[:, :, j * 256:(j + 1) * 256], w3, eng="pool")
    mg = p.sb([512]); dsk = p.sb([512]); bgl = p.sb([512])
    rowbc(k, mg, IN["od_mlstm_norm"].ap()[e:e + 1, :], "od_mlstm_norm")
    rowbc(k, dsk, IN["od_d_skip"].ap()[e:e + 1, :], "od_d_skip")
    rowbc(k, bgl, IN["od_b_glu"].ap()[e:e + 1, :], "od_b_glu")
    g1 = [p.sb([D]), p.sb([D])]
    for r in range(2):
        rowbc(k, g1[r], k.MOD.ap()[l, r:r + 1, 2048:3072], "MOD")
    lng = p.sb([D]); lnb = p.sb([D])
    rowbc(k, lng, IN["ln1_g"].ap()[l:l + 1, :], "ln1_g")
    rowbc(k, lnb, IN["ln1_b"].ap()[l:l + 1, :], "ln1_b")
    A2 = [p.sb([2, 4, 132]) for _ in range(2)]
    YS = [p.sb([2, 4, 128]) for _ in range(2)]
    zt = [p.sb([D]) for _ in range(2)]
    ht = [p.sb([D]) for _ in range(2)]
    Y = [p.sb([D]) for _ in range(2)]
    YT = [p.sb([8, 128], BF16) for _ in range(2)]
    st_ = [p.sb([48]) for _ in range(2)]
    junk = p.sb([D])
    U = [p.sb([D]) for _ in range(2)]
    HM = [p.sb([4, 128]) for _ in range(2)]
    YG = [p.sb([512]) for _ in range(2)]
    YGT = [p.sb([4, 128], BF16) for _ in range(2)]
    for t in range(NT):
        r = 1 if t < NCTX else 0
        a2, ys, z, h, y, yT, s, u, hm, yg, ygT = (A2[t % 2], YS[t % 2], zt[t % 2], ht[t % 2], Y[t % 2], YT[t % 2], st_[t % 2],
                                                  U[t % 2], HM[t % 2], YG[t % 2], YGT[t % 2])
        tok = slice(t * 128, (t + 1) * 128)
        for d in range(2):
            for hh in range(4):
                p.dma(a2[:, d, hh, 0:129], p.dview("OUTO%d_%d" % (hh, d), k.OUTO.ap()[hh, d, tok, 0:129], t * 128 * 132, (t + 1) * 128 * 132))
                fi = d * 4 + hh
                p.dma(ys[:, d, hh, :], p.dview("YS5", k.YS5.ap()[d, hh][:, tok], fi * 128 * TT, (fi + 1) * 128 * TT))
        p.dma(z, p.dview("ZS", k.ZS.ap()[tok, :], t * 128 * D, (t + 1) * 128 * D))
        p.dma(h, p.dview("H", k.H.ap()[tok, :], t * 128 * D, (t + 1) * 128 * D))
        den = a2[:, :, :, 128]
        s3 = lambda c0: s.v(s.ap[:, c0:c0 + 8].rearrange("q (a b) -> q a b", a=2))
        o.ts(s3(0), den, -1.0, None, ALU.mult)
        o.tt(s3(0), s3(0), den, ALU.max)
        o.ts(s3(0), s3(0), 1.0, None, ALU.max)
        o.recip(s[:, 8:16], s[:, 0:8])
        for hh in range(4):
            o.ts(junk[:, 0:128], a2[:, 0, hh, 0:128], s[:, 8 + hh:9 + hh], None, ALU.mult)
            o.stt(hm[:, hh, :], a2[:, 1, hh, 0:128], s[:, 12 + hh:13 + hh], junk[:, 0:128], ALU.mult, ALU.add)
        for hh in range(4):
            o.act(junk[:, 0:128], hm[:, hh, :], AF.Square, accum=s[:, 16 + hh:17 + hh])
            o.act(junk[:, 128:256], hm[:, hh, :], AF.Identity, accum=s[:, 20 + hh:21 + hh])
        o.ts(s[:, 24:28], s[:, 20:24], 1.0 / 128, None, ALU.mult)
        o.tt(s[:, 28:32], s[:, 24:28], s[:, 24:28], ALU.mult)
        o.stt(s[:, 32:36], s[:, 16:20], 1.0 / 128, s[:, 28:32], ALU.mult, ALU.subtract)
        o.act(s[:, 36:40], s[:, 32:36], AF.Sqrt, bias=k.eps[:, 0:1])
        o.recip(s[:, 40:44], s[:, 36:40])
        for hh in range(4):
            o.ts(junk[:, 0:128], hm[:, hh, :], s[:, 24 + hh:25 + hh], s[:, 40 + hh:41 + hh], ALU.subtract, ALU.mult)
            o.tt(y[:, hh * 128:(hh + 1) * 128], junk[:, 0:128], mg[:, hh * 128:(hh + 1) * 128], ALU.mult)
        o.tt(y[:, 0:512], y[:, 0:512], z[:, 0:512], ALU.mult)
        o.tt(ys[:, 0], ys[:, 0], ys[:, 1], ALU.add)
        ps = p.ps(4)
        for ft_ in range(4):
            o.tr(ps[:, ft_ * 128:(ft_ + 1) * 128], ys[:, 0, ft_, :], C["IDENT"])
        o.tt(junk[:, 0:512], z[:, 512:1024], dsk, ALU.mult)
        o.tt(junk[:, 0:512], junk[:, 0:512], ps, ALU.add)
        xg = junk[:, 0:512]
        o.tt(junk[:, 512:1024], xg, xg, ALU.mult)
        o.ts(junk[:, 512:1024], junk[:, 512:1024], 0.044715, 1.0, ALU.mult, ALU.add)
        o.tt(junk[:, 512:1024], junk[:, 512:1024], xg, ALU.mult)
        o.act(yg, junk[:, 512:1024], AF.Tanh, scale=math.sqrt(2.0 / math.pi))
        o.stt(yg, yg, 1.0, xg, ALU.add, ALU.mult)
        o.ts(yg, yg, 0.5, None, ALU.mult)
        ps2 = p.ps(5)
        for ft_ in range(4):
            o.tr(ps2[:, ft_ * 128:(ft_ + 1) * 128], yg[:, ft_ * 128:(ft_ + 1) * 128], C["IDENT"])
        o.cp(ygT.v(ygT.ap.rearrange("q a b -> q (a b)")), ps2, eng="act")
        ps3 = p.ps(6)
        for kt in range(4):
            o.mm(ps3, ygT[:, kt, :], wglu[:, kt, :], start=(kt == 0), stop=(kt == 3))
        o.tt(junk[:, 0:512], ps3, bgl, ALU.add)
        o.act(junk[:, 512:1024], junk[:, 0:512], AF.Sigmoid)
        o.tt(y[:, 512:1024], yg, junk[:, 512:1024], ALU.mult)
        for half in range(2):
            ps = p.ps(half)
            for q in range(4):
                kt = half * 4 + q
                o.tr(ps[:, q * 128:(q + 1) * 128], y[:, kt * 128:(kt + 1) * 128], C["IDENT"])
            o.cp(yT.v(yT.ap[:, half * 4:(half + 1) * 4, :].rearrange("q a b -> q (a b)")), ps, eng="act")
        for half in range(2):
            ps = p.ps(2 + half)
            for kt in range(8):
                o.mm(ps, yT[:, kt, :], wout[:, kt, half * 512:(half + 1) * 512], start=(kt == 0), stop=(kt == 7))
            hs_ = slice(half * 512, (half + 1) * 512)
            o.tt(u[:, hs_], ps, g1[r][:, hs_], ALU.mult)
        if dbg_here(k, "y"):
            p.dma(p.dview("y_out", k.YOUT.ap()[tok, :], t * 128 * D, (t + 1) * 128 * D), u, q="pool")
        o.stt(u, h, ALPHA, u, ALU.mult, ALU.add)
        layer_norm(k, u, h, lng, lnb, s, junk)
        p.dma(p.dview("H", k.H.ap()[tok, :], t * 128 * D, (t + 1) * 128 * D), h, q="pool")
        if dbg_here(k, "h1"):
            p.dma(p.dview("y_out", k.YOUT.ap()[tok, :], t * 128 * D, (t + 1) * 128 * D), h, q="pool")
    p.pop()


def moe_layer2(k, l):
    p, o, IN, C = k.p, k.o, k.IN, k.C
    IOTA = k.IOTA
    IC = C["IC"]
    SETS = [(list(range(NCTX, NT)), 512, [(0, 128), (128, 128), (256, 128), (384, 128)], 0),
            (list(range(0, NCTX)), 32, [(0, 32)], 1)]
    p.push()
    wr = p.sb([8, 16])
    p.dma(wr, p.dview("w_router", IN["w_router"].ap()[l].rearrange("(kt q) e -> q kt e", q=128)))
    aff = p.sb([NT, 16]); gw = p.sb([NT, 16]); msk = p.sb([NT, 16]); posm = p.sb([NT, 16])
    posmT = p.sb([TT])
    ghl = p.sb([NT, 16, 2], BF16)
    p.push()
    in2tm = p.sb([NT, D], BF16)
    p.push()
    affT = p.sb([TT])
    sc2 = [p.sb([D]), p.sb([D])]; sh2 = [p.sb([D]), p.sb([D])]
    for r in range(2):
        rowbc(k, sc2[r], k.MOD.ap()[l, r:r + 1, 4096:5120], "MOD")
        o.ts(sc2[r], sc2[r], 1.0, None, ALU.add)
        rowbc(k, sh2[r], k.MOD.ap()[l, r:r + 1, 3072:4096], "MOD")
    ht = [p.sb([D]) for _ in range(2)]
    x2 = [p.sb([D]) for _ in range(2)]
    xTf = [p.sb([8, 128]) for _ in range(2)]
    sm = [p.sb([40]) for _ in range(2)]
    for t in range(NT):
        r = 1 if t < NCTX else 0
        h, x, xf, s = ht[t % 2], x2[t % 2], xTf[t % 2], sm[t % 2]
        tok = slice(t * 128, (t + 1) * 128)
        p.dma(h, p.dview("H", k.H.ap()[tok, :], t * 128 * D, (t + 1) * 128 * D))
        o.tt(x, h, sc2[r], ALU.mult)
        o.tt(x, x, sh2[r], ALU.add)
        o.cp(in2tm[:, t, :], x, eng="pool")
        for half in range(2):
            ps = p.ps(half)
            for q in range(4):
                kt = half * 4 + q
                o.tr(ps[:, q * 128:(q + 1) * 128], x[:, kt * 128:(kt + 1) * 128], C["IDENT"])
            ps3 = ps.v(ps.ap.rearrange("q (a b) -> q a b", a=4))
            o.cp(xf[:, half * 4:(half + 1) * 4, :], ps3, eng="act")
        psl = p.ps(2, 16)
        for kt in range(8):
            o.mm(psl, xf[:, kt, :], wr[:, kt, :], start=(kt == 0), stop=(kt == 7))
        p.op("dve", lambda e, s=s, psl=psl: e.reduce_max(s.ap[:, 0:1], psl.ap, AX.X), [psl], [s])
        o.ts(s[:, 1:2], s[:, 0:1], -1.0, None, ALU.mult)
        o.act(s[:, 8:24], psl, AF.Exp, bias=s[:, 1:2], accum=s[:, 2:3])
        o.recip(s[:, 3:4], s[:, 2:3])
        o.ts(aff[:, t, :], s[:, 8:24], s[:, 3:4], None, ALU.mult)
        pst = p.ps(3, 128)
        o.tr(pst[0:16, :], aff[:, t, :], C["IDENT"])
        o.cp(affT[0:16, tok], pst[0:16, :], eng="act")
    st = p.sb([16]); junk = p.sb([4096]); thr = [p.sb([16]), p.sb([16])]; dt = p.sb([16])
    sets = [(0, 256, 32.0, 1), (256, TT, 512.0, 0)]
    for (c0, c1, cap, r) in sets:
        lo, hi, mid, cnt, ge, d1 = [st[0:16, i:i + 1] for i in range(6)]
        o.memset(lo, 0.0, eng="dve"); o.memset(hi, 1.0, eng="dve")
        for it in range(32):
            o.tt(mid, lo, hi, ALU.add)
            o.ts(mid, mid, 0.5, None, ALU.mult)
            o.ts(junk[0:16, 0:c1 - c0], affT[0:16, c0:c1], mid, None, ALU.is_ge, ALU.add, accum=cnt)
            o.ts(ge, cnt, cap - 0.5, None, ALU.is_ge)
            o.tt(d1, mid, lo, ALU.subtract)
            o.stt(lo, d1, ge, lo, ALU.mult, ALU.add)
            o.tt(d1, hi, mid, ALU.subtract)
            o.stt(hi, d1, ge, mid, ALU.mult, ALU.add)
        o.ts(dt[0:16, 0:16], C["IDENT"][0:16, 0:16], lo, None, ALU.mult)
        pth = p.ps(2, 16)
        o.mm(pth, C["ONES"][0:16, :], dt[0:16, 0:16])
        o.cp(thr[r], pth, eng="act")
    tot = p.sb([16]); tmp16 = p.sb([16])
    for (tiles, cap, jts, yi) in SETS:
        r = 1 if tiles[0] < NCTX else 0
        o.memset(tot, 0.0, eng="dve")
        for t in tiles:
            tok = slice(t * 128, (t + 1) * 128)
            o.tt(msk[:, t, :], aff[:, t, :], thr[r], ALU.is_ge)
            o.tt(gw[:, t, :], msk[:, t, :], aff[:, t, :], ALU.mult)
            ps = p.ps(0, 16); ps2 = p.ps(1, 16)
            o.mm(ps, C["CSs"], msk[:, t, :])
            o.mm(ps2, C["ONES"], msk[:, t, :])
            o.tt(tmp16, ps, tot, ALU.add)
            o.tt(tot, tot, ps2, ALU.add)
            o.stt(tmp16, tmp16, 1.0, msk[:, t, :], ALU.add, ALU.mult)
            o.ts(posm[:, t, :], tmp16, -1.0, None, ALU.add)
            pst = p.ps(3, 128)
            o.tr(pst[0:16, :], posm[:, t, :], C["IDENT"])
            o.cp(posmT[0:16, tok], pst[0:16, :], eng="act")
    hi16 = p.sb([NT, 16], BF16); hi32 = p.sb([NT, 16]); lo32 = p.sb([NT, 16])
    o.cp(hi16, gw); o.cp(hi32, hi16); o.tt(lo32, gw, hi32, ALU.subtract)
    o.cp(ghl[:, :, :, 0], hi32); o.cp(ghl[:, :, :, 1], lo32)
    p.pop()
    p.push()
    wgf = [p.sb([8, 128]) for _ in range(2)]; wuf = [p.sb([8, 128]) for _ in range(2)]
    wgb = [p.sb([8, 128], BF16) for _ in range(2)]; wub = [p.sb([8, 128], BF16) for _ in range(2)]
    wdf = [p.sb([2, 512]) for _ in range(2)]; wdb = [p.sb([2, 512], BF16) for _ in range(2)]
    xsT = [p.sb([8, 512], BF16), p.sb([8, 32], BF16)]
    hT = [p.sb([16, 512], BF16), p.sb([16, 32], BF16)]
    sg = [p.sb([512]) for _ in range(2)]
    sel = [p.sb([512], BF16) for _ in range(3)]
    gsel = [p.sb([4]), p.sb([4])]
    ysb = [[p.sb([D], BF16) for _ in range(5)] for _ in range(2)]
    wi = 0; di = 0; si = 0
    for e in range(16):
        for (tiles, cap, jts, yi) in SETS:
            for ps_ in range(2):
                psx = [p.ps(q, cap) for q in range(4)]
                psg = [p.ps(4 + ji, 2) for ji in range(len(jts))]
                for t in tiles:
                    sl = sel[si % 3]; si += 1
                    o.ts(sl[:, 0:cap], IOTA[:, 0:cap], posm[:, t, e:e + 1], None, ALU.is_equal)
                    for q in range(4):
                        kt = ps_ * 4 + q
                        o.mm(psx[q], in2tm[:, t, kt * 128:(kt + 1) * 128], sl[:, 0:cap], start=(t == tiles[0]), stop=(t == tiles[-1]))
                    if ps_ == 0:
                        for ji, (j0, nj) in enumerate(jts):
                            o.mm(psg[ji][0:nj, :], sl[:, j0:j0 + nj], ghl[:, t, e, :], start=(t == tiles[0]), stop=(t == tiles[-1]))
                for q in range(4):
                    kt = ps_ * 4 + q
                    o.cp(xsT[yi][:, kt, 0:cap], psx[q], eng=("act" if q % 2 else "dve"))
                if ps_ == 0:
                    for ji, (j0, nj) in enumerate(jts):
                        gs_, pg_ = gsel[yi][0:nj, ji:ji + 1], psg[ji][0:nj, 0:2]
                        p.op("dve", lambda e_, gs_=gs_, pg_=pg_: e_.reduce_sum(gs_.ap, pg_.ap, AX.X), [pg_], [gs_])
        for ft_ in range(16):
            a32, b32, a16, b16 = wgf[wi % 2], wuf[wi % 2], wgb[wi % 2], wub[wi % 2]
            wi += 1
            p.dma(a32, p.dview("w_gate", IN["w_gate"].ap()[l, e, :, ft_ * 128:(ft_ + 1) * 128].rearrange("(kt q) f -> q kt f", q=128)))
            p.dma(b32, p.dview("w_up", IN["w_up"].ap()[l, e, :, ft_ * 128:(ft_ + 1) * 128].rearrange("(kt q) f -> q kt f", q=128)))
            o.cp(a16, a32, eng="pool")
            o.cp(b16, b32, eng="pool")
            for (tiles, cap, jts, yi) in SETS:
                pg = p.ps(4 + (ft_ % 2) * 2, cap); pu = p.ps(5 + (ft_ % 2) * 2, cap)
                for kt in range(8):
                    o.mm(pg, a16[:, kt, :], xsT[yi][:, kt, 0:cap], start=(kt == 0), stop=(kt == 7))
                for kt in range(8):
                    o.mm(pu, b16[:, kt, :], xsT[yi][:, kt, 0:cap], start=(kt == 0), stop=(kt == 7))
                s_ = sg[(ft_ + yi) % 2]
                o.act(s_[:, 0:cap], pg, AF.Silu)
                o.tt(hT[yi][:, ft_, 0:cap], s_[:, 0:cap], pu, ALU.mult)
        yt_ = ysb[e % 2]
        for half in range(2):
            hs_ = slice(half * 512, (half + 1) * 512)
            psd = {}
            for (tiles, cap, jts, yi) in SETS:
                for ji in range(len(jts)):
                    psd[(yi, ji)] = p.ps(ji if yi == 0 else 4, 512)
            for fp_ in range(8):
                d32, d16 = wdf[di % 2], wdb[di % 2]
                di += 1
                p.dma(d32, p.dview("w_down", IN["w_down"].ap()[l, e, fp_ * 256:(fp_ + 1) * 256, half * 512:(half + 1) * 512].rearrange("(a q) d -> q a d", q=128)))
                o.cp(d16, d32, eng="pool")
                for f2 in range(2):
                    ft_ = fp_ * 2 + f2
                    for (tiles, cap, jts, yi) in SETS:
                        for ji, (j0, nj) in enumerate(jts):
                            o.mm(psd[(yi, ji)][0:nj, :], hT[yi][:, ft_, j0:j0 + nj], d16[:, f2, :], start=(ft_ == 0), stop=(ft_ == 15))
            for (tiles, cap, jts, yi) in SETS:
                for ji, (j0, nj) in enumerate(jts):
                    yt = yt_[ji if yi == 0 else 4]
                    o.ts(yt[0:nj, hs_], psd[(yi, ji)][0:nj, :], gsel[yi][0:nj, ji:ji + 1], None, ALU.mult)
        for (tiles, cap, jts, yi) in SETS:
            for ji, (j0, nj) in enumerate(jts):
                yt = yt_[ji if yi == 0 else 4]
                p.dma(p.dview("YSD%d" % yi, k.YSD[yi].ap()[e, j0:j0 + nj, :]), yt[0:nj, :], q="pool")
    p.pop()
    p.pop()
    p.push()
    g2 = [p.sb([D]), p.sb([D])]
    for r in range(2):
        rowbc(k, g2[r], k.MOD.ap()[l, r:r + 1, 5120:6144], "MOD")
    lng = p.sb([D]); lnb = p.sb([D])
    rowbc(k, lng, IN["ln2_g"].ap()[l:l + 1, :], "ln2_g")
    rowbc(k, lnb, IN["ln2_b"].ap()[l:l + 1, :], "ln2_b")
    selT = [[p.sb([512], BF16) for _ in range(4)] for _ in range(16)]
    ysl = [p.sb([4, 128], BF16) for _ in range(4)]
    fT = p.sb([8, 512])
    hh = [p.sb([D]) for _ in range(2)]
    uu = [p.sb([D]) for _ in range(2)]
    s8 = p.sb([8]); junk2 = p.sb([D])
    groups = [(0, 256, SETS[1])] + [(256 + 512 * i, 512, SETS[0]) for i in range(8)]
    yl = 0
    for (t0, n, (tiles, cap, jts, yi)) in groups:
        for e in range(16):
            psr = p.ps(e % 2, n)
            o.mm(psr, C["OH%d" % e][0:16, :], posmT[0:16, t0:t0 + n])
            for ji, (j0, nj) in enumerate(jts):
                o.ts(selT[e][ji][0:nj, 0:n], psr[0:nj, :], IC[0:nj, ji:ji + 1], None, ALU.is_equal)
        for dt_ in range(8):
            psf = p.ps(2 + dt_ % 2, n)
            for e in range(16):
                y_ = ysl[yl % 4]; yl += 1
                nrow = jts[-1][0] + jts[-1][1]
                if yi == 0:
                    p.dma(y_, p.dview("YSD0", k.YSD[0].ap()[e, :, dt_ * 128:(dt_ + 1) * 128].rearrange("(jt q) d -> q jt d", q=128)))
                else:
                    p.dma(y_[0:32, 0, :], p.dview("YSD1", k.YSD[1].ap()[e, 0:32, dt_ * 128:(dt_ + 1) * 128]))
                for ji, (j0, nj) in enumerate(jts):
                    o.mm(psf, y_[0:nj, ji, :], selT[e][ji][0:nj, 0:n], start=(e == 0 and ji == 0), stop=(e == 15 and ji == len(jts) - 1))
            o.cp(fT[:, dt_, 0:n], psf, eng=("act" if dt_ % 2 else "dve"))
        for st_ in range(n // 128):
            t = t0 // 128 + st_
            r = 1 if t < NCTX else 0
            tok = slice(t * 128, (t + 1) * 128)
            h = hh[st_ % 2]; u = uu[st_ % 2]
            p.dma(h, p.dview("H", k.H.ap()[tok, :], t * 128 * D, (t + 1) * 128 * D))
            for half in range(2):
                ps = p.ps(4 + half)
                for q in range(4):
                    dt_ = half * 4 + q
                    o.tr(ps[:, q * 128:(q + 1) * 128], fT[:, dt_, st_ * 128:(st_ + 1) * 128], C["IDENT"])
                hs_ = slice(half * 512, (half + 1) * 512)
                if dbg_here(k, "f"):
                    o.cp(junk2[:, hs_], ps, eng="act")
                o.tt(u[:, hs_], ps, g2[r][:, hs_], ALU.mult)
            if dbg_here(k, "f"):
                p.dma(p.dview("y_out", k.YOUT.ap()[tok, :], t * 128 * D, (t + 1) * 128 * D), junk2, q="pool")
            o.stt(u, h, ALPHA, u, ALU.mult, ALU.add)
            layer_norm(k, u, h, lng, lnb, s8, junk2)
            p.dma(p.dview("H", k.H.ap()[tok, :], t * 128 * D, (t + 1) * 128 * D), h, q="pool")
    p.pop()
    p.pop()


_W_NAMES = ["w_mod", "b_mod", "ln1_g", "ln1_b", "ln2_g", "ln2_b", "w_router", "w_gate", "w_up", "w_down",
            "ev_w_in", "ev_w_out", "ev_conv", "ev_a_log", "ev_dt_bias", "ev_gdn_norm", "ev_ret_norm",
            "od_w_in", "od_w_out", "od_conv", "od_gate_bias", "od_mlstm_norm", "od_lam_re", "od_lam_im", "od_log_dt",
            "od_b_re", "od_b_im", "od_c_re", "od_c_im", "od_d_skip", "od_w_glu", "od_b_glu"]


def kernel(**inputs):
    nc = build(n_layers=4)
    cst = make_consts()
    retg = np.concatenate([ret_log_decay(0), ret_log_decay(1)])[None, :].astype(np.float32)
    in_maps = []
    B = inputs["x"].shape[0]
    for b in range(B):
        m = {}
        m["h0"] = np.ascontiguousarray(np.concatenate([inputs["ctx"][b], inputs["x"][b]], 0).astype(np.float32))
        m["cvec"] = np.ascontiguousarray(np.stack([inputs["c"][b], inputs["c_ctx"]], 0).astype(np.float32))
        m["cst"] = cst
        m["retg"] = retg
        for n in _W_NAMES:
            if n in nc.used_inputs:
                m[n] = np.ascontiguousarray(inputs[n], dtype=np.float32)
        in_maps.append({n: m[n] for n in nc.used_inputs})
    res = run_bass_kernel_spmd(nc, in_maps, core_ids=list(range(B)))
    out = np.stack([np.asarray(res.results[b]["y_out"])[256:] for b in range(B)], 0)
    return out.astype(np.float32)
```

```python
import bisect
import numpy as np
import concourse.bass as bass
import concourse.mybir as mybir
from concourse.bass_utils import run_bass_kernel_spmd

F32 = mybir.dt.float32
BF16 = mybir.dt.bfloat16
ALU = mybir.AluOpType
AF = mybir.ActivationFunctionType
AX = mybir.AxisListType

SEM_GEN = 30000
N_DMA_SEMS = 12


class IntervalMap:
    def __init__(self):
        self.bounds = [0]
        self.state = [(None, {})]

    def _split(self, x):
        i = bisect.bisect_right(self.bounds, x) - 1
        if self.bounds[i] == x:
            return i
        w, r = self.state[i]
        self.bounds.insert(i + 1, x)
        self.state.insert(i + 1, (w, dict(r)))
        return i + 1

    def segs(self, lo, hi):
        i0 = self._split(lo)
        i1 = self._split(hi)
        return range(i0, i1)


class T:
    def __init__(self, ap, key, lo, hi):
        self.ap = ap
        self.key = key
        self.lo = lo
        self.hi = hi

    def __getitem__(self, idx):
        return T(self.ap[idx], self.key, self.lo, self.hi)

    def v(self, ap):
        return T(ap, self.key, self.lo, self.hi)


class Prog:
    ENGS = ["pe", "act", "dve", "pool", "sp"]

    def __init__(self, nc, sb_bytes=206 * 1024):
        self.nc = nc
        self.ops = {e: [] for e in self.ENGS}
        self.cnt = {e: 0 for e in self.ENGS}
        self.maps = {}
        self.observed = {e: {} for e in self.ENGS}
        self.dma_cnt = {"sp": 0, "pool": 0, "act": 0}
        self.dma_sem_uses = {}
        self.sem_names = set()
        self.sb_bytes = sb_bytes
        self.sb_top = 0
        self.sb_stack = []
        self.ps_top = 0
        self.dram = {}
        self.arena = None
        self.psarena = None
        self.final_waits = []

    def setup_mem(self, es):
        nc = self.nc
        self.arena = es.enter_context(nc.sbuf_tensor("arena", [128, self.sb_bytes // 4], F32))
        self.psarena = es.enter_context(nc.psum_tensor("psarena", [128, 8 * 512], F32))

    def push(self):
        self.sb_stack.append(self.sb_top)

    def pop(self):
        self.sb_top = self.sb_stack.pop()

    def sb(self, shape, dtype=F32, name=None):
        esz = 4 if dtype == F32 else 2
        n = int(np.prod(shape))
        nbytes = (n * esz + 31) // 32 * 32
        off = self.sb_top
        self.sb_top += nbytes
        assert self.sb_top <= self.sb_bytes, f"SBUF overflow {self.sb_top}"
        ap = self.arena[:, off // 4:(off + nbytes) // 4]
        if dtype != F32:
            ap = ap.bitcast(dtype)
        ap = ap[:, 0:n]
        if len(shape) == 2:
            ap = ap.rearrange("p (a b) -> p a b", a=shape[0])
        elif len(shape) == 3:
            ap = ap.rearrange("p (a b c) -> p a b c", a=shape[0], b=shape[1])
        return T(ap, "sb", off, off + nbytes)

    def ps(self, bank, ncols=512, dtype=F32, col0=0):
        off = bank * 512 + col0
        ap = self.psarena[:, off:off + ncols]
        return T(ap, "ps", off * 4, (off + ncols) * 4)

    def dram_t(self, name, shape, dtype=F32, kind="Internal"):
        h = self.nc.dram_tensor(name, list(shape), dtype, kind=kind)
        self.dram[name] = h
        return h

    def dview(self, name, ap, lo=0, hi=1 << 40):
        return T(ap, "d:" + name, lo, hi)

    def _deps(self, eng, reads, writes, token):
        deps = set()
        for t in reads:
            m = self.maps.setdefault(t.key, IntervalMap())
            for i in m.segs(t.lo, t.hi):
                w, r = m.state[i]
                if w is not None:
                    deps.add(w)
        for t in writes:
            m = self.maps.setdefault(t.key, IntervalMap())
            for i in m.segs(t.lo, t.hi):
                w, r = m.state[i]
                if w is not None:
                    deps.add(w)
                for tok in r.values():
                    deps.add(tok)
        for t in reads:
            m = self.maps[t.key]
            for i in m.segs(t.lo, t.hi):
                m.state[i][1][token[0] if eng.startswith("dma") else eng] = token
        for t in writes:
            m = self.maps[t.key]
            for i in m.segs(t.lo, t.hi):
                m.state[i] = (token, {})
        return deps

    def _waits(self, eng, deps):
        obs = self.observed[eng]
        best = {}
        for (sem, val, src) in deps:
            if src == "pe" and eng == "pe":
                continue
            if obs.get(sem, 0) >= val:
                continue
            if best.get(sem, 0) < val:
                best[sem] = val
        for sem, val in best.items():
            obs[sem] = val
        return list(best.items())

    limit = None
    count = 0

    def _lim(self):
        if self.limit is not None:
            if self.count >= self.limit:
                return True
            self.count += 1
        return False

    capture = None

    def begin_capture(self):
        self.capture = []
        return self.capture

    def end_capture(self):
        c = self.capture
        self.capture = None
        return c

    def replay(self, lists):
        its = [iter(l) for l in lists]
        alive = list(its)
        while alive:
            nxt = []
            for it in alive:
                try:
                    item = next(it)
                except StopIteration:
                    continue
                if item[0] == "op":
                    self.op(*item[1:])
                else:
                    self.dma(*item[1:3], q=item[3], **item[4])
                nxt.append(it)
            alive = nxt

    def op(self, eng, fn, reads=(), writes=()):
        if self.capture is not None:
            self.capture.append(("op", eng, fn, list(reads), list(writes)))
            return
        if self._lim():
            return
        n = self.cnt[eng]
        gen, idx = divmod(n, SEM_GEN)
        sem = f"{eng}{gen}"
        self.sem_names.add(sem)
        token = (sem, idx + 1, eng)
        self.cnt[eng] = n + 1
        rd2, wr2 = [], []
        for t in reads:
            if t.key == "ps":
                wr2.append(T(t.ap, "ps", t.lo // 2048 * 2048, (t.hi + 2047) // 2048 * 2048))
            else:
                rd2.append(t)
        for t in writes:
            if t.key == "ps":
                wr2.append(T(t.ap, "ps", t.lo // 2048 * 2048, (t.hi + 2047) // 2048 * 2048))
            else:
                wr2.append(t)
        reads, writes = rd2, wr2
        deps = self._deps(eng, reads, writes, token)
        waits = self._waits(eng, deps)
        self.ops[eng].append((waits, fn, (sem, 1)))

    def dma(self, out, in_, q="sp", **kw):
        if self.capture is not None:
            self.capture.append(("dma", out, in_, q, kw))
            return
        if self._lim():
            return
        k = self.dma_cnt[q]
        self.dma_cnt[q] = k + 1
        sem = f"dma_{q}{k % N_DMA_SEMS}"
        self.sem_names.add(sem)
        uses = self.dma_sem_uses.get(sem, 0)
        self.dma_sem_uses[sem] = uses + 1
        token = (sem, 16 * (uses + 1), "dma")
        deps = self._deps("dma_" + q, [in_], [out], token)
        if uses > 0:
            deps.add((sem, 16 * uses, "dma"))
        waits = self._waits(q, deps)
        oap, iap = out.ap, in_.ap

        def fn(e, oap=oap, iap=iap, kw=kw):
            return e.dma_start(out=oap, in_=iap, allow_slow_non_contiguous=True, **kw)
        self.ops[q].append((waits, fn, (sem, 16)))
        return token

    def wait_all_dma(self, eng="sp"):
        deps = set()
        for sem, uses in self.dma_sem_uses.items():
            deps.add((sem, 16 * uses, "dma"))
        waits = self._waits(eng, deps)
        self.ops[eng].append((waits, None, None))

    def emit(self, es):
        nc = self.nc
        sems = {}
        for name in sorted(self.sem_names):
            sems[name] = es.enter_context(nc.semaphore(name))
        block = es.enter_context(nc.Block())
        ops = self.ops

        def run(e, lst):
            for waits, fn, inc in lst:
                for sem, val in waits:
                    e.wait_ge(sems[sem], val)
                if fn is not None:
                    ins = fn(e)
                    ins.then_inc(sems[inc[0]], inc[1])

        @block.sync
        def _(e):
            run(e, ops["sp"])

        @block.tensor
        def _(e):
            run(e, ops["pe"])

        @block.vector
        def _(e):
            run(e, ops["dve"])

        @block.scalar
        def _(e):
            run(e, ops["act"])

        @block.gpsimd
        def _(e):
            run(e, ops["pool"])


def _ap(x):
    return x.ap if isinstance(x, T) else x


def _ts(*xs):
    return [x for x in xs if isinstance(x, T)]


class Ops:
    def __init__(self, p):
        self.p = p

    def mm(self, out, lhsT, rhs, start=True, stop=True):
        self.p.op("pe", lambda e: e.matmul(out.ap, lhsT.ap, rhs.ap, start=start, stop=stop), [lhsT, rhs], [out])

    def tr(self, out, in_, ident):
        self.p.op("pe", lambda e: e.transpose(out.ap, in_.ap, ident.ap), [in_, ident], [out])

    def act(self, out, in_, func, bias=None, scale=None, accum=None, eng="act"):
        kw = {}
        if bias is not None:
            kw["bias"] = _ap(bias)
        if scale is not None:
            kw["scale"] = _ap(scale)
        if accum is not None:
            kw["accum_out"] = accum.ap
        self.p.op("act", lambda e: e.activation(out.ap, in_.ap, func, **kw),
                  _ts(in_, bias, scale), _ts(out, accum))

    def tt(self, out, a, b, op, eng="dve"):
        self.p.op(eng, lambda e: e.tensor_tensor(out.ap, a.ap, b.ap, op), [a, b], [out])

    def ts(self, out, a, s1, s2, op0, op1=None, accum=None, eng="dve"):
        kw = {}
        if accum is not None:
            kw["accum_out"] = accum.ap
        if op1 is None:
            fn = lambda e: e.tensor_scalar(out.ap, a.ap, _ap(s1), None, op0, **kw)
        else:
            fn = lambda e: e.tensor_scalar(out.ap, a.ap, _ap(s1), _ap(s2), op0, op1, **kw)
        self.p.op(eng, fn, _ts(a, s1, s2), _ts(out, accum))

    def stt(self, out, a, s, b, op0, op1, eng="dve"):
        self.p.op(eng, lambda e: e.scalar_tensor_tensor(out.ap, a.ap, _ap(s), b.ap, op0, op1), _ts(a, s, b), [out])

    def cp(self, out, in_, eng="dve"):
        if eng == "act":
            self.p.op("act", lambda e: e.copy(out.ap, in_.ap), [in_], [out])
        else:
            self.p.op(eng, lambda e: e.tensor_copy(out.ap, in_.ap), [in_], [out])

    def memset(self, out, val, eng="pool"):
        self.p.op(eng, lambda e: e.memset(out.ap, val), [], [out])

    def recip(self, out, in_):
        self.p.op("dve", lambda e: e.reciprocal(out.ap, in_.ap), [in_], [out])

from contextlib import ExitStack
import math

D = 1024
TT = 4352
NT = 34
NCTX = 2
ALPHA = 8 ** 0.25
EPS = 1e-5
NEG = -30000.0
EVEN_IN = 4112
ODD_IN = 2576

CONST_NAMES = ["IDENT", "ONES", "CSf", "CSb", "MTf", "MTb", "MSf", "MSb", "CSs", "MK", "IO0", "IO1", "IO2", "IO3", "IC"] + ["OH%d" % i for i in range(16)]


def make_consts():
    p = np.arange(128)[:, None]
    f = np.arange(128)[None, :]
    c = {}
    c["IDENT"] = (p == f)
    c["ONES"] = np.ones((128, 128))
    c["CSf"] = (p <= f)
    c["CSb"] = (p >= f)
    c["MTf"] = np.where(p <= f, 0.0, NEG)
    c["MTb"] = np.where(p >= f, 0.0, NEG)
    c["MSf"] = np.where(f < p, 0.0, NEG)
    c["MSb"] = np.where(f > p, 0.0, NEG)
    c["CSs"] = (p < f)
    mk = np.zeros((128, 128))
    mk[:64, 0] = 1; mk[64:, 1] = 1; mk[:16, 2] = 1; mk[16:32, 3] = 1
    for s4 in range(4):
        mk[s4 * 32:(s4 + 1) * 32, 4 + s4] = 1
    c["MK"] = mk
    for i in range(4):
        c["IO%d" % i] = np.broadcast_to(f + 128 * i, (128, 128))
    ic = np.zeros((128, 128))
    for i in range(4):
        ic[:, i] = np.arange(128) + 128 * i
    c["IC"] = ic
    for i in range(16):
        oh = np.zeros((128, 128)); oh[i, :] = 1
        c["OH%d" % i] = oh
    arr = np.concatenate([c[k].astype(np.float32) for k in CONST_NAMES], axis=1)
    return np.ascontiguousarray(arr)


def ret_log_decay(d):
    expo = 5.0 + 2.0 * np.arange(4, dtype=np.float32) + d
    return np.log1p(-np.exp2(-expo)).astype(np.float32)


class K:
    pass


def build(n_layers=4, dbg=(), stage=99, layers=None, dbg_layer=0):
    nc = bass.Bass("TRN2", target_bir_lowering=False)
    k = K()
    k.nc = nc
    k.dbg_on = set(dbg)
    k.stage = stage
    k.dbg_layer = dbg_layer
    k.cur = -1
    SHAPES = dict(h0=[TT, D], cvec=[2, D], cst=[128, 128 * len(CONST_NAMES)], retg=[1, 8],
                  w_mod=[4, D, 6 * D], b_mod=[4, 6 * D], ln1_g=[4, D], ln1_b=[4, D], ln2_g=[4, D], ln2_b=[4, D],
                  w_router=[4, D, 16], w_gate=[4, 16, D, 2048], w_up=[4, 16, D, 2048], w_down=[4, 16, 2048, D],
                  ev_w_in=[2, D, EVEN_IN], ev_w_out=[2, D, D], ev_conv=[2, 3, 3, 1536], ev_a_log=[2, 2, 4],
                  ev_dt_bias=[2, 2, 4], ev_gdn_norm=[2, 128], ev_ret_norm=[2, 512],
                  od_w_in=[2, D, ODD_IN], od_w_out=[2, D, D], od_conv=[2, 3, 3, 1024], od_gate_bias=[2, 2, 2, 4],
                  od_mlstm_norm=[2, 512], od_lam_re=[2, 2, 32, 64], od_lam_im=[2, 2, 32, 64], od_log_dt=[2, 2, 32],
                  od_b_re=[2, 32, 64, 16], od_b_im=[2, 32, 64, 16], od_c_re=[2, 32, 16, 64], od_c_im=[2, 32, 16, 64],
                  od_d_skip=[2, 512], od_w_glu=[2, 512, 512], od_b_glu=[2, 512])

    class LazyIn(dict):
        def __missing__(self, name):
            self[name] = nc.dram_tensor(name, list(SHAPES[name]), F32, kind="ExternalInput")
            return self[name]
    IN = LazyIn()
    k.IN = IN
    with ExitStack() as es:
        p = Prog(nc)
        p.setup_mem(es)
        o = Ops(p)
        k.p, k.o = p, o
        k.H = p.dram_t("H", [TT, D])
        k.MOD = p.dram_t("MOD", [4, 2, 6 * D])
        k.FM = p.dram_t("FM", [24, 128, TT])
        k.ZS = p.dram_t("ZS", [TT, D])
        k.GATES = p.dram_t("GATES", [TT, 16])
        k.OUTM = p.dram_t("OUTM", [8, 2, TT, 128])
        k.OUTO = p.dram_t("OUTO", [4, 2, TT, 132])
        k.YS5 = p.dram_t("YS5", [2, 4, 128, TT])
        k.YSD = [p.dram_t("YSDl", [16, 512, D], BF16), p.dram_t("YSDc", [16, 128, D], BF16)]
        k.YOUT = nc.dram_tensor("y_out", [TT, D], F32, kind="ExternalOutput")
        cst = p.sb([128 * len(CONST_NAMES)])
        p.dma(cst, p.dview("cst", IN["cst"].ap()))
        k.C = {n: cst[:, i * 128:(i + 1) * 128] for i, n in enumerate(CONST_NAMES)}
        i_io = CONST_NAMES.index('IO0')
        k.IOTA = cst[:, i_io * 128:(i_io + 4) * 128]
        eps_c = p.sb([4])
        o.memset(eps_c[:, 0:1], EPS); o.memset(eps_c[:, 1:2], 1e-6); o.memset(eps_c[:, 2:3], 1.0); o.memset(eps_c[:, 3:4], 0.0)
        k.eps = eps_c

        phase_mod(k)
        p.dma(p.dview("H", k.H.ap()), p.dview("h0", IN["h0"].ap()))
        for l in (layers if layers is not None else range(n_layers)):
            if k.stage <= 0:
                break
            k.cur = l
            mixer_layer(k, l)
        p.limit = None
        if not (k.dbg_on - {'none'}) and k.stage >= 5:
            p.dma(p.dview('y_out', k.YOUT.ap()), p.dview('H', k.H.ap()))
        if k.stage < 5:
            p.dma(p.dview('y_out', k.YOUT.ap()[0:8, :]), p.dview('MOD', k.MOD.ap().rearrange('l r (a f) -> (l r a) f', f=1024)[0:8, :]))
        p.wait_all_dma("sp")
        p.wait_all_dma("pool")
        p.emit(es)
    nc.used_inputs = list(IN.keys())
    return nc


def rowbc(k, dst, src_ap, name):
    k.p.dma(dst, k.p.dview(name, src_ap.partition_broadcast(128)))


def phase_mod(k):
    p, o, IN = k.p, k.o, k.IN
    p.push()
    cT = p.sb([2, 8]); sT = p.sb([2, 8])
    for r in range(2):
        p.dma(cT[:, r, :], p.dview("cvec", IN["cvec"].ap()[r].rearrange("(kt q) -> q kt", q=128)))
    o.act(sT, cT, AF.Silu)
    wt = [p.sb([8, 512]) for _ in range(2)]
    bm = p.sb([512]); res = [p.sb([512]) for _ in range(2)]
    i = 0
    for l in range(4):
        for cc in range(12):
            w = wt[i % 2]; r = res[i % 2]
            p.dma(w, p.dview("w_mod", IN["w_mod"].ap()[l, :, cc * 512:(cc + 1) * 512].rearrange("(kt q) f -> q kt f", q=128)))
            p.dma(bm[0:2, :], p.dview("b_mod", IN["b_mod"].ap()[l:l + 1, cc * 512:(cc + 1) * 512].partition_broadcast(2)))
            ps = p.ps(i % 2)
            for kt in range(8):
                o.mm(ps[0:2, :], sT[:, :, kt], w[:, kt, :], start=(kt == 0), stop=(kt == 7))
            o.tt(r[0:2, :], ps[0:2, :], bm[0:2, :], ALU.add)
            p.dma(p.dview("MOD", k.MOD.ap()[l, :, cc * 512:(cc + 1) * 512]), r[0:2, :], q="pool")
            i += 1
    p.pop()


def load_modT(k, l, off, plus1):
    p, o = k.p, k.o
    t = p.sb([2, 8])
    for r in range(2):
        p.dma(t[:, r, :], p.dview("MOD", k.MOD.ap()[l, r, off:off + D].rearrange("(kt q) -> q kt", q=128)))
    if plus1:
        o.ts(t, t, 1.0, None, ALU.add)
    return t


def build_inT(k, l, scT, shT, inT, extra=None):
    p, o = k.p, k.o
    p.push()
    ht = [p.sb([D]) for _ in range(2)]
    for t in range(NT):
        h = ht[t % 2]
        p.dma(h, p.dview("H", k.H.ap()[t * 128:(t + 1) * 128, :], t * 128 * D, (t + 1) * 128 * D))
        r = 1 if t < NCTX else 0
        for half in range(2):
            ps = p.ps(half)
            for q in range(4):
                kt = half * 4 + q
                o.tr(ps[:, q * 128:(q + 1) * 128], h[:, kt * 128:(kt + 1) * 128], k.C["IDENT"])
            for q in range(4):
                kt = half * 4 + q
                o.act(inT[:, kt, t * 128:(t + 1) * 128], ps[:, q * 128:(q + 1) * 128], AF.Identity,
                      bias=shT[:, r, kt:kt + 1], scale=scT[:, r, kt:kt + 1])
                if extra is not None:
                    extra(t, kt, ps[:, q * 128:(q + 1) * 128], r)
    p.pop()


TOKCH = [(i * 512, 512) for i in range(8)] + [(4096, 256)]


def dbg_here(k, name):
    return name in k.dbg_on and k.cur == k.dbg_layer


def mixer_layer(k, l):
    p, o, IN = k.p, k.o, k.IN
    odd = (l % 2 == 1)
    e = l // 2
    C = k.C
    wname = "od_w_in" if odd else "ev_w_in"
    cname = "od_conv" if odd else "ev_conv"
    p.push()
    scT = load_modT(k, l, 1024, True)
    shT = load_modT(k, l, 0, False)
    inT = p.sb([8, TT], BF16)
    build_inT(k, l, scT, shT, inT)
    if k.stage <= 1:
        p.pop(); return
    if not odd:
        specs = [(0 + 128 * h, h, "l2q") for h in range(4)] + [(512 + 128 * h, 4 + h, "l2k") for h in range(4)] + \
                [(1024 + 128 * h, 8 + h, None) for h in range(4)] + [(2064 + 128 * h, None, None) for h in range(4)] + \
                [(2576 + 128 * h, None, "scale") for h in range(4)] + [(3088 + 128 * h, None, None) for h in range(4)]
        nconv = 12
    else:
        specs = [(0 + 128 * h, h, None) for h in range(4)] + [(512 + 128 * h, 4 + h, "scale") for h in range(4)] + \
                [(1024 + 128 * h, None, None) for h in range(4)] + [(2064 + 128 * h, None, None) for h in range(4)]
        nconv = 8
    p.push()
    wconv = p.sb([nconv, 9])
    for t12 in range(nconv):
        p.dma(wconv[:, t12, :], p.dview(cname, IN[cname].ap()[e].rearrange("a b c -> c (a b)")[t12 * 128:(t12 + 1) * 128, :]))
    wf = [p.sb([8, 128]) for _ in range(2)]
    wb = [p.sb([8, 128], BF16) for _ in range(2)]
    ft = [p.sb([TT]) for _ in range(2)]
    yt = p.sb([TT])
    sq = p.sb([512]); rs = p.sb([512])
    for s, (col, cv, mode) in enumerate(specs):
        w32, w16, X = wf[s % 2], wb[s % 2], ft[s % 2]
        p.dma(w32, p.dview(wname, IN[wname].ap()[e, :, col:col + 128].rearrange("(kt q) f -> q kt f", q=128)))
        o.cp(w16, w32, eng="pool")
        for ci, (t0, n) in enumerate(TOKCH):
            ps = p.ps(2 + ci % 2, n)
            for kt in range(8):
                o.mm(ps, w16[:, kt, :], inT[:, kt, t0:t0 + n], start=(kt == 0), stop=(kt == 7))
            o.cp(X[:, t0:t0 + n], ps, eng=("act" if ci % 2 else "dve"))
        if cv is not None:
            wc = wconv[:, cv, :]
            Y = yt
            o.ts(Y[:, 0:256], X[:, 0:256], wc[:, 4:5], None, ALU.mult)
            o.stt(Y[:, 1:256], X[:, 0:255], wc[:, 3:4], Y[:, 1:256], ALU.mult, ALU.add)
            o.stt(Y[:, 0:255], X[:, 1:256], wc[:, 5:6], Y[:, 0:255], ALU.mult, ALU.add)
            Xg = X.v(X.ap[:, 256:TT].rearrange("q (r c) -> q r c", c=64))
            Yg = Y.v(Y.ap[:, 256:TT].rearrange("q (r c) -> q r c", c=64))
            o.ts(Y[:, 256:TT], X[:, 256:TT], wc[:, 4:5], None, ALU.mult)
            for dy in range(3):
                for dx in range(3):
                    if dy == 1 and dx == 1:
                        continue
                    oy, ox = dy - 1, dx - 1
                    r0, r1 = max(0, -oy), 64 - max(0, oy)
                    c0, c1 = max(0, -ox), 64 - max(0, ox)
                    o.stt(Yg[:, r0:r1, c0:c1], Xg[:, r0 + oy:r1 + oy, c0 + ox:c1 + ox], wc[:, dy * 3 + dx:dy * 3 + dx + 1],
                          Yg[:, r0:r1, c0:c1], ALU.mult, ALU.add)
            o.act(X, Y, AF.Silu)
        if mode in ("l2q", "l2k"):
            for ci, (t0, n) in enumerate(TOKCH):
                o.tt(sq[:, 0:n], X[:, t0:t0 + n], X[:, t0:t0 + n], ALU.mult)
                ps = p.ps(4 + ci % 2, n)
                o.mm(ps, C["ONES"], sq[:, 0:n])
                o.act(rs[:, 0:n], ps, AF.Sqrt, bias=k.eps[:, 1:2])
                o.recip(sq[:, 0:n], rs[:, 0:n])
                if mode == "l2q":
                    o.stt(X[:, t0:t0 + n], X[:, t0:t0 + n], 128 ** -0.5, sq[:, 0:n], ALU.mult, ALU.mult)
                else:
                    o.tt(X[:, t0:t0 + n], X[:, t0:t0 + n], sq[:, 0:n], ALU.mult)
        elif mode == "scale":
            o.ts(X, X, 128 ** -0.5, None, ALU.mult)
        p.dma(p.dview("FM", k.FM.ap()[s], s * 128 * TT, (s + 1) * 128 * TT), X, q="pool")
    p.pop()
    if k.stage <= 2:
        p.pop(); return
    p.push()
    wz = p.sb([8, 1040], BF16)
    wst = [p.sb([8, 260]) for _ in range(2)]
    zcols = [(1536, 512), (2048, 16), (3600, 512)] if not odd else [(1536, 512), (2048, 16), (2064, 512)]
    dst = 0
    i = 0
    for (c0, n) in zcols:
        for j in range(0, n, 260):
            m = min(260, n - j)
            w32 = wst[i % 2]
            p.dma(w32[:, :, 0:m], p.dview(wname, IN[wname].ap()[e, :, c0 + j:c0 + j + m].rearrange("(kt q) f -> q kt f", q=128)))
            o.cp(wz[:, :, dst:dst + m], w32[:, :, 0:m], eng="pool")
            dst += m
            i += 1
    if not odd:
        alog = p.sb([8]); dtb = p.sb([8]); nea = p.sb([8])
        rowbc(k, alog, IN["ev_a_log"].ap()[e:e + 1].rearrange("a d h -> a (d h)"), "ev_a_log")
        rowbc(k, dtb, IN["ev_dt_bias"].ap()[e:e + 1].rearrange("a d h -> a (d h)"), "ev_dt_bias")
        o.act(nea, alog, AF.Exp)
    else:
        gbi = p.sb([8]); gbf = p.sb([8])
        for d in range(2):
            rowbc(k, gbi[:, d * 4:(d + 1) * 4], IN["od_gate_bias"].ap()[e, d, 0:1, :], "od_gate_bias")
            rowbc(k, gbf[:, d * 4:(d + 1) * 4], IN["od_gate_bias"].ap()[e, d, 1:2, :], "od_gate_bias")
    zt = [p.sb([D]) for _ in range(2)]
    gt = [p.sb([16]) for _ in range(2)]
    tmp8 = p.sb([8]); tmp8b = p.sb([8])
    for t in range(NT):
        z, g = zt[t % 2], gt[t % 2]
        lt = lambda kt: inT[:, kt, t * 128:(t + 1) * 128]
        psa, psb, psg = p.ps(0), p.ps(1), p.ps(2, 16)
        for kt in range(8):
            o.mm(psa, lt(kt), wz[:, kt, 0:512], start=(kt == 0), stop=(kt == 7))
        for kt in range(8):
            o.mm(psb, lt(kt), wz[:, kt, 528:1040], start=(kt == 0), stop=(kt == 7))
        for kt in range(8):
            o.mm(psg, lt(kt), wz[:, kt, 512:528], start=(kt == 0), stop=(kt == 7))
        if not odd:
            o.act(z[:, 0:512], psa, AF.Silu)
            o.act(z[:, 512:1024], psb, AF.Silu)
            o.tt(tmp8, psg[:, 0:8], dtb, ALU.add)
            o.act(g[:, 0:8], tmp8, AF.Exp)
            o.act(tmp8, g[:, 0:8], AF.Ln, bias=k.eps[:, 2:3])
            o.stt(g[:, 0:8], tmp8, -1.0, nea, ALU.mult, ALU.mult)
            o.act(g[:, 8:16], psg[:, 8:16], AF.Sigmoid)
        else:
            o.act(z[:, 0:512], psa, AF.Sigmoid)
            o.cp(z[:, 512:1024], psb, eng="dve")
            o.tt(tmp8, psg[:, 8:16], gbf, ALU.add)
            o.act(tmp8b, tmp8, AF.Exp, scale=-1.0)
            o.act(tmp8, tmp8b, AF.Ln, bias=k.eps[:, 2:3])
            o.ts(g[:, 0:8], tmp8, -1.0, None, ALU.mult)
            o.tt(tmp8b, psg[:, 0:8], gbi, ALU.add)
            o.act(g[:, 8:16], tmp8b, AF.Exp)
        p.dma(p.dview("ZS", k.ZS.ap()[t * 128:(t + 1) * 128, :], t * 128 * D, (t + 1) * 128 * D), z, q="pool")
        p.dma(p.dview("GATES", k.GATES.ap()[t * 128:(t + 1) * 128, :], t * 128 * 16, (t + 1) * 128 * 16), g, q="pool")
    p.pop()
    p.pop()
    if k.stage <= 3:
        return
    if not odd:
        scan_layer(k, l, [0, 1])
    else:
        scan_layer(k, l, [2])
        if k.stage > 4:
            s5_scan(k, l)
    if k.stage <= 4:
        return
    if not odd:
        merge_even(k, l)
    else:
        merge_odd(k, l)
    if k.stage <= 5:
        return
    moe_layer2(k, l)


def chain_order(d):
    return list(range(NT)) if d == 0 else [1, 0] + list(range(NT - 1, 1, -1))


def scan_layer(k, l, types):
    p, o, IN, C = k.p, k.o, k.IN, k.C
    import os
    if 'OPLIMIT' in os.environ:
        p.limit = int(os.environ['OPLIMIT']); p.count = 0
    p.push()
    gates = p.sb([NT, 16])
    p.dma(gates, p.dview("GATES", k.GATES.ap().rearrange("(c q) g -> q c g", q=128)))
    retg = p.sb([8])
    negb = p.sb([NT, 8])
    if 1 in types:
        rowbc(k, retg, IN["retg"].ap(), "retg")
    if 0 in types:
        o.ts(negb, gates[:, :, 8:16], -1.0, None, ALU.mult)
    RING = 8
    W = lambda n=128: [p.sb([n]) for _ in range(RING)]
    names = ["qT", "kT", "vT", "G1", "cc", "tmp", "ET", "EXPR", "kgT", "qgT", "kend", "bv", "E", "N", "M", "N2", "M2",
             "TTa", "TTb", "AQ", "br", "vn", "o", "o2", "tmp2", "sc"]
    ring = {n: (W(132) if n in ("vn", "o") else W()) for n in names}
    chains = []
    for typ in types:
        for h in range(4):
            for d in range(2):
                chains.append(dict(typ=typ, h=h, d=d, S=[p.sb([132]), p.sb([132])], order=chain_order(d), dec=None))
    for ch in chains:
        o.memset(ch["S"][0], 0.0, eng="dve")
    step_i = [0]

    def decay(ch, gcol, slot):
        d = ch["d"]
        R = {n: ring[n][slot] for n in names}
        CS = C["CSf"] if d == 0 else C["CSb"]
        MT = C["MTf"] if d == 0 else C["MTb"]
        o.ts(R["G1"], C["ONES"], gcol, None, ALU.mult)
        pb = ch["pb"]
        psr = p.ps(pb + 0, 128)
        psc = p.ps(pb + 0, 256, col0=128)
        o.mm(psr, R["G1"], CS)
        o.mm(psc[:, 0:128], CS, R["G1"])
        o.mm(psc[:, 128:256], C["ONES"], R["G1"])
        cc = R["cc"]
        o.cp(cc[:, 0:1], psc[:, 0:1], eng="act")
        o.cp(cc[:, 1:2], psc[:, 128:129], eng="act")
        o.stt(R["tmp"], psr, cc[:, 0:1], MT, ALU.subtract, ALU.add)
        o.act(R["ET"], R["tmp"], AF.Exp)
        o.act(R["EXPR"], psr, AF.Exp)
        o.tt(cc[:, 4:5], cc[:, 1:2], cc[:, 0:1], ALU.subtract)
        o.act(cc[:, 2:3], cc[:, 4:5], AF.Exp)
        o.act(cc[:, 3:4], cc[:, 1:2], AF.Exp)
        return dict(ET=R["ET"], EXPR=R["EXPR"], cc=cc, psr=psr)

    def step(ch, c, si):
        typ, h, d = ch["typ"], ch["h"], ch["d"]
        slot = step_i[0] % RING
        step_i[0] += 1
        R = {n: ring[n][slot] for n in names}
        sq, sk, sv = (h, 4 + h, 8 + h) if typ != 1 else (12 + h, 16 + h, 20 + h)
        dv = 129 if typ == 2 else 128
        tok = slice(c * 128, (c + 1) * 128)
        for nm, s in (("qT", sq), ("kT", sk), ("vT", sv)):
            p.dma(R[nm], p.dview("FM", k.FM.ap()[s][:, tok], s * 128 * TT, (s + 1) * 128 * TT))
        qT, kT, vT = R["qT"], R["kT"], R["vT"]
        if typ != 1:
            gcol = gates[:, c, d * 4 + h:d * 4 + h + 1]
            dec = decay(ch, gcol, slot)
        else:
            if ch["dec"] is None:
                gcol = retg[:, d * 4 + h:d * 4 + h + 1]
                dslot = 0
                own = {n: p.sb([128]) for n in ["G1", "cc", "tmp", "ET", "EXPR"]}
                save = {n: ring[n][dslot] for n in own}
                for n in own:
                    ring[n][dslot] = own[n]
                ch["dec"] = decay(ch, gcol, dslot)
                for n in own:
                    ring[n][dslot] = save[n]
            dec = ch["dec"]
        cc = dec["cc"]
        pb = ch["pb"]
        pst = p.ps(pb + 1, 256)
        o.tr(pst[:, 0:128], kT, C["IDENT"])
        o.tr(pst[:, 128:256], vT, C["IDENT"])
        if typ == 2:
            ei = gates[:, c, 8 + d * 4 + h:8 + d * 4 + h + 1]
            o.ts(R["kend"], pst[:, 0:128], cc[:, 2:3], ei, ALU.mult, ALU.mult)
        else:
            o.ts(R["kend"], pst[:, 0:128], cc[:, 2:3], None, ALU.mult)
        o.tt(R["qgT"], qT, dec["EXPR"], ALU.mult)
        psq = p.ps(pb + 0, 128, col0=384)
        o.mm(psq, kT, qT)
        if typ == 2:
            o.stt(R["AQ"], psq, ei, dec["ET"], ALU.mult, ALU.mult)
        else:
            o.tt(R["AQ"], psq, dec["ET"], ALU.mult)
        S = ch["S"][si % 2][:, 0:dv]
        Sn = ch["S"][(si + 1) % 2][:, 0:dv]
        if typ == 0:
            MS = C["MSf"] if d == 0 else C["MSb"]
            nb = negb[:, c, d * 4 + h:d * 4 + h + 1]
            o.ts(R["bv"], pst[:, 128:256], gates[:, c, 8 + d * 4 + h:8 + d * 4 + h + 1], None, ALU.mult)
            o.tt(R["kgT"], kT, dec["EXPR"], ALU.mult)
            o.stt(R["tmp2"], dec["psr"], cc[:, 0:1], MS, ALU.subtract, ALU.subtract)
            o.act(R["E"], R["tmp2"], AF.Exp, scale=-1.0)
            psk = p.ps(pb + 1, 128, col0=256)
            o.mm(psk, kT, kT)
            o.stt(R["N"], psk, nb, R["E"], ALU.mult, ALU.mult)
            psm = p.ps(pb + 1, 128, col0=384)
            o.tr(psm, R["N"], C["IDENT"])
            o.cp(R["M"], psm, eng="act")
            o.tt(R["TTa"], C["IDENT"], R["M"], ALU.add)
            Nc, Mc, Nn, Mn = R["N"], R["M"], R["N2"], R["M2"]
            Tc, Tn = R["TTa"], R["TTb"]
            for lev in range(1, 7):
                ps1 = p.ps(pb + 1, 128, col0=256)
                o.mm(ps1, Mc, Nc)
                o.cp(Nn, ps1, eng="act")
                if lev < 6:
                    ps2 = p.ps(pb + 1, 128, col0=384)
                    o.mm(ps2, Nc, Mc)
                    o.cp(Mn, ps2, eng="dve")
                ps3 = p.ps(pb + 1, 128, col0=256)
                o.mm(ps3, Nn, Tc)
                o.tt(Tn, ps3, Tc, ALU.add)
                Nc, Nn = Nn, Nc
                Mc, Mn = Mn, Mc
                Tc, Tn = Tn, Tc
            psr2 = p.ps(pb + 1, 128, col0=256)
            o.mm(psr2, R["kgT"], S)
            o.stt(R["br"], psr2, nb, R["bv"], ALU.mult, ALU.add)
            psv = p.ps(pb + 1, 128, col0=256)
            o.mm(psv, Tc, R["br"])
            o.cp(R["vn"][:, 0:128], psv, eng="act")
        else:
            o.cp(R["vn"][:, 0:128], pst[:, 128:256], eng="act")
            if typ == 2:
                o.memset(R["vn"][:, 128:129], 1.0, eng="dve")
        vn = R["vn"][:, 0:dv]
        pso = p.ps(pb + 0, dv)
        o.mm(pso, R["qgT"], S, start=True, stop=False)
        o.mm(pso, R["AQ"], vn, start=False, stop=True)
        o.cp(R["o"][:, 0:dv], pso, eng="act")
        if typ == 2:
            p.dma(p.dview("OUTO%d_%d" % (h, d), k.OUTO.ap()[h, d, tok, 0:dv], c * 128 * 132, (c + 1) * 128 * 132), R["o"][:, 0:dv], q="pool")
        else:
            slot8 = typ * 4 + h
            p.dma(p.dview("OUTM%d_%d" % (slot8, d), k.OUTM.ap()[slot8, d, tok, :], c * 128 * 128, (c + 1) * 128 * 128), R["o"][:, 0:128], q="pool")
        pss = p.ps(pb + 1, dv)
        o.mm(pss, R["kend"], vn)
        o.stt(Sn, S, cc[:, 3:4], pss, ALU.mult, ALU.add)

    for si in range(NT):
        for c0 in range(0, len(chains), 4):
            caps = []
            for j, ch in enumerate(chains[c0:c0 + 4]):
                ch["pb"] = 2 * j
                p.begin_capture()
                step(ch, ch["order"][si], si)
                caps.append(p.end_capture())
            p.replay(caps)
    p.pop()


def merge_even(k, l):
    p, o, IN, C = k.p, k.o, k.IN, k.C
    e = l // 2
    p.push()
    wout = p.sb([8, D], BF16)
    wst = [p.sb([8, 256]) for _ in range(2)]
    for j in range(4):
        p.dma(wst[j % 2], p.dview("ev_w_out", IN["ev_w_out"].ap()[e, :, j * 256:(j + 1) * 256].rearrange("(kt q) f -> q kt f", q=128)))
        o.cp(wout[:, :, j * 256:(j + 1) * 256], wst[j % 2], eng="pool")
    gg = p.sb([128]); rg = p.sb([512])
    rowbc(k, gg, IN["ev_gdn_norm"].ap()[e:e + 1, :], "ev_gdn_norm")
    rowbc(k, rg, IN["ev_ret_norm"].ap()[e:e + 1, :], "ev_ret_norm")
    g1 = [p.sb([D]), p.sb([D])]
    for r in range(2):
        rowbc(k, g1[r], k.MOD.ap()[l, r:r + 1, 2048:3072], "MOD")
    lng = p.sb([D]); lnb = p.sb([D])
    rowbc(k, lng, IN["ln1_g"].ap()[l:l + 1, :], "ln1_g")
    rowbc(k, lnb, IN["ln1_b"].ap()[l:l + 1, :], "ln1_b")
    of = [p.sb([8, 128]) for _ in range(2)]
    ob = [p.sb([8, 128]) for _ in range(2)]
    zt = [p.sb([D]) for _ in range(2)]
    ht = [p.sb([D]) for _ in range(2)]
    Y = [p.sb([D]) for _ in range(2)]
    YT = [p.sb([8, 128], BF16) for _ in range(2)]
    st = [p.sb([32]) for _ in range(2)]
    junk = p.sb([D])
    U = [p.sb([D]) for _ in range(2)]
    for t in range(NT):
        r = 1 if t < NCTX else 0
        a, b, z, h, y, yT, s, u = of[t % 2], ob[t % 2], zt[t % 2], ht[t % 2], Y[t % 2], YT[t % 2], st[t % 2], U[t % 2]
        tok = slice(t * 128, (t + 1) * 128)
        for hs in range(8):
            p.dma(a[:, hs, :], p.dview("OUTM%d_0" % hs, k.OUTM.ap()[hs, 0, tok, :], t * 128 * 128, (t + 1) * 128 * 128))
            p.dma(b[:, hs, :], p.dview("OUTM%d_1" % hs, k.OUTM.ap()[hs, 1, tok, :], t * 128 * 128, (t + 1) * 128 * 128))
        p.dma(z, p.dview("ZS", k.ZS.ap()[tok, :], t * 128 * D, (t + 1) * 128 * D))
        p.dma(h, p.dview("H", k.H.ap()[tok, :], t * 128 * D, (t + 1) * 128 * D))
        o.tt(a, a, b, ALU.add)
        for hs in range(8):
            o.act(junk[:, 0:128], a[:, hs, :], AF.Square, accum=s[:, hs:hs + 1])
        for hs in range(4, 8):
            o.act(junk[:, 0:128], a[:, hs, :], AF.Identity, accum=s[:, 8 + hs:9 + hs])
        o.act(s[:, 16:20], s[:, 0:4], AF.Sqrt, bias=k.eps[:, 0:1], scale=1.0 / 128)
        o.recip(s[:, 20:24], s[:, 16:20])
        o.ts(s[:, 24:28], s[:, 12:16], 1.0 / 128, None, ALU.mult)
        o.tt(s[:, 28:32], s[:, 24:28], s[:, 24:28], ALU.mult)
        o.stt(s[:, 16:20], s[:, 4:8], 1.0 / 128, s[:, 28:32], ALU.mult, ALU.subtract)
        o.act(s[:, 28:32], s[:, 16:20], AF.Sqrt, bias=k.eps[:, 0:1])
        o.recip(s[:, 16:20], s[:, 28:32])
        for hs in range(4):
            o.stt(y[:, hs * 128:(hs + 1) * 128], a[:, hs, :], s[:, 20 + hs:21 + hs], gg, ALU.mult, ALU.mult)
        for hs in range(4):
            o.ts(junk[:, 0:128], a[:, 4 + hs, :], s[:, 24 + hs:25 + hs], s[:, 16 + hs:17 + hs], ALU.subtract, ALU.mult)
            o.tt(y[:, 512 + hs * 128:512 + (hs + 1) * 128], junk[:, 0:128], rg[:, hs * 128:(hs + 1) * 128], ALU.mult)
        o.tt(y, y, z, ALU.mult)
        for half in range(2):
            ps = p.ps(half)
            for q in range(4):
                kt = half * 4 + q
                o.tr(ps[:, q * 128:(q + 1) * 128], y[:, kt * 128:(kt + 1) * 128], C["IDENT"])
            o.cp(yT.v(yT.ap[:, half * 4:(half + 1) * 4, :].rearrange("q a b -> q (a b)")), ps, eng="act")
        for half in range(2):
            ps = p.ps(2 + half)
            for kt in range(8):
                o.mm(ps, yT[:, kt, :], wout[:, kt, half * 512:(half + 1) * 512], start=(kt == 0), stop=(kt == 7))
            hs_ = slice(half * 512, (half + 1) * 512)
            o.tt(u[:, hs_], ps, g1[r][:, hs_], ALU.mult)
        if dbg_here(k, "y"):
            p.dma(p.dview("y_out", k.YOUT.ap()[tok, :], t * 128 * D, (t + 1) * 128 * D), u, q="pool")
        o.stt(u, h, ALPHA, u, ALU.mult, ALU.add)
        layer_norm(k, u, h, lng, lnb, s, junk)
        p.dma(p.dview("H", k.H.ap()[tok, :], t * 128 * D, (t + 1) * 128 * D), h, q="pool")
        if dbg_here(k, "h1"):
            p.dma(p.dview("y_out", k.YOUT.ap()[tok, :], t * 128 * D, (t + 1) * 128 * D), h, q="pool")
    p.pop()


def layer_norm(k, u, out, g, b, s, junk):
    o = k.o
    o.act(junk, u, AF.Identity, accum=s[:, 0:1])
    o.act(junk, u, AF.Square, accum=s[:, 1:2])
    o.ts(s[:, 2:3], s[:, 0:1], 1.0 / D, None, ALU.mult)
    o.tt(s[:, 3:4], s[:, 2:3], s[:, 2:3], ALU.mult)
    o.stt(s[:, 4:5], s[:, 1:2], 1.0 / D, s[:, 3:4], ALU.mult, ALU.subtract)
    o.act(s[:, 5:6], s[:, 4:5], AF.Sqrt, bias=k.eps[:, 0:1])
    o.recip(s[:, 6:7], s[:, 5:6])
    o.ts(junk, u, s[:, 2:3], s[:, 6:7], ALU.subtract, ALU.mult)
    o.tt(junk, junk, g, ALU.mult)
    o.tt(out, junk, b, ALU.add)


def moe_layer(k, l, update_ctx=True):
    p, o, IN, C = k.p, k.o, k.IN, k.C
    p.push()
    g2 = [p.sb([D]), p.sb([D])]
    for r in range(2):
        rowbc(k, g2[r], k.MOD.ap()[l, r:r + 1, 5120:6144], "MOD")
    lng = p.sb([D]); lnb = p.sb([D])
    rowbc(k, lng, IN["ln2_g"].ap()[l:l + 1, :], "ln2_g")
    rowbc(k, lnb, IN["ln2_b"].ap()[l:l + 1, :], "ln2_b")
    wr = p.sb([8, 16])
    p.dma(wr, p.dview("w_router", IN["w_router"].ap()[l].rearrange("(kt q) e -> q kt e", q=128)))
    in2T = p.sb([8, TT], BF16)
    aff = p.sb([NT, 16]); gw = p.sb([NT, 16])
    p.push()
    affT = p.sb([TT])
    sc2 = [p.sb([D]), p.sb([D])]; sh2 = [p.sb([D]), p.sb([D])]
    for r in range(2):
        rowbc(k, sc2[r], k.MOD.ap()[l, r:r + 1, 4096:5120], "MOD")
        o.ts(sc2[r], sc2[r], 1.0, None, ALU.add)
        rowbc(k, sh2[r], k.MOD.ap()[l, r:r + 1, 3072:4096], "MOD")
    p.push()
    ht = [p.sb([D]) for _ in range(2)]
    x2 = [p.sb([D]) for _ in range(2)]
    xTf = [p.sb([8, 128]) for _ in range(2)]
    sm = [p.sb([40]) for _ in range(2)]
    for t in range(NT):
        r = 1 if t < NCTX else 0
        h, x, xf, s = ht[t % 2], x2[t % 2], xTf[t % 2], sm[t % 2]
        tok = slice(t * 128, (t + 1) * 128)
        p.dma(h, p.dview("H", k.H.ap()[tok, :], t * 128 * D, (t + 1) * 128 * D))
        o.tt(x, h, sc2[r], ALU.mult)
        o.tt(x, x, sh2[r], ALU.add)
        for half in range(2):
            ps = p.ps(half)
            for q in range(4):
                kt = half * 4 + q
                o.tr(ps[:, q * 128:(q + 1) * 128], x[:, kt * 128:(kt + 1) * 128], C["IDENT"])
            ps3 = ps.v(ps.ap.rearrange("q (a b) -> q a b", a=4))
            o.cp(xf[:, half * 4:(half + 1) * 4, :], ps3, eng="act")
            o.cp(in2T[:, half * 4:(half + 1) * 4, tok], ps3, eng="dve")
        psl = p.ps(2, 16)
        for kt in range(8):
            o.mm(psl, xf[:, kt, :], wr[:, kt, :], start=(kt == 0), stop=(kt == 7))
        p.op("dve", lambda e, s=s, psl=psl: e.reduce_max(s.ap[:, 0:1], psl.ap, AX.X), [psl], [s])
        o.ts(s[:, 1:2], s[:, 0:1], -1.0, None, ALU.mult)
        o.act(s[:, 8:24], psl, AF.Exp, bias=s[:, 1:2], accum=s[:, 2:3])
        o.recip(s[:, 3:4], s[:, 2:3])
        o.ts(aff[:, t, :], s[:, 8:24], s[:, 3:4], None, ALU.mult)
        pst = p.ps(3, 128)
        o.tr(pst[0:16, :], aff[:, t, :], C["IDENT"])
        o.cp(affT[0:16, tok], pst[0:16, :], eng="act")
    p.pop()
    p.push()
    st = p.sb([16]); junk = p.sb([4096]); thr = [p.sb([16]), p.sb([16])]; dt = p.sb([16])
    sets = [(0, 256, 32.0, 1), (256, TT, 512.0, 0)]
    for (c0, c1, cap, r) in sets:
        lo, hi, mid, cnt, ge, d1 = [st[0:16, i:i + 1] for i in range(6)]
        o.memset(lo, 0.0, eng="dve"); o.memset(hi, 1.0, eng="dve")
        for it in range(32):
            o.tt(mid, lo, hi, ALU.add)
            o.ts(mid, mid, 0.5, None, ALU.mult)
            o.ts(junk[0:16, 0:c1 - c0], affT[0:16, c0:c1], mid, None, ALU.is_ge, ALU.add, accum=cnt)
            o.ts(ge, cnt, cap - 0.5, None, ALU.is_ge)
            o.tt(d1, mid, lo, ALU.subtract)
            o.stt(lo, d1, ge, lo, ALU.mult, ALU.add)
            o.tt(d1, hi, mid, ALU.subtract)
            o.stt(hi, d1, ge, mid, ALU.mult, ALU.add)
        o.ts(dt[0:16, 0:16], C["IDENT"][0:16, 0:16], lo, None, ALU.mult)
        pth = p.ps(2, 16)
        o.mm(pth, C["ONES"][0:16, :], dt[0:16, 0:16])
        o.cp(thr[r], pth, eng="act")
    for t in range(NT):
        r = 1 if t < NCTX else 0
        o.tt(gw[:, t, :], aff[:, t, :], thr[r], ALU.is_ge)
        o.tt(gw[:, t, :], gw[:, t, :], aff[:, t, :], ALU.mult)
    p.pop()
    p.pop()
    p.push()
    wgf = [p.sb([8, 256]) for _ in range(2)]; wuf = [p.sb([8, 256]) for _ in range(2)]
    wgb = [p.sb([8, 256], BF16) for _ in range(2)]; wub = [p.sb([8, 256], BF16) for _ in range(2)]
    wdf = [p.sb([2, 512]) for _ in range(2)]; wdb = [p.sb([2, 512], BF16) for _ in range(2)]
    hT = p.sb([16, 512], BF16)
    sg = [p.sb([512]) for _ in range(2)]
    acc = [p.sb([D]) for _ in range(4)]
    hh = [p.sb([D])] * 2
    s8 = p.sb([8]); junk2 = p.sb([D])
    wi = 0
    di = 0
    import os
    nexp = int(os.environ.get("MOE_NEXP", 16))
    for (t0, n) in TOKCH:
        nst = n // 128
        for e in range(nexp):
            for fg in range(8):
                a32, b32, a16, b16 = wgf[wi % 2], wuf[wi % 2], wgb[wi % 2], wub[wi % 2]
                wi += 1
                p.dma(a32, p.dview("w_gate", IN["w_gate"].ap()[l, e, :, fg * 256:(fg + 1) * 256].rearrange("(kt q) f -> q kt f", q=128)))
                p.dma(b32, p.dview("w_up", IN["w_up"].ap()[l, e, :, fg * 256:(fg + 1) * 256].rearrange("(kt q) f -> q kt f", q=128)))
                o.cp(a16, a32, eng="pool")
                o.cp(b16, b32, eng="pool")
                for f2 in range(2):
                    ft = fg * 2 + f2
                    psg = p.ps(4 + (ft % 2) * 2, n)
                    psu = p.ps(5 + (ft % 2) * 2, n)
                    for kt in range(8):
                        o.mm(psg, a16[:, kt, f2 * 128:(f2 + 1) * 128], in2T[:, kt, t0:t0 + n], start=(kt == 0), stop=(kt == 7))
                    for kt in range(8):
                        o.mm(psu, b16[:, kt, f2 * 128:(f2 + 1) * 128], in2T[:, kt, t0:t0 + n], start=(kt == 0), stop=(kt == 7))
                    s_ = sg[ft % 2]
                    o.act(s_[:, 0:n], psg, AF.Silu)
                    o.tt(hT[:, ft, 0:n], s_[:, 0:n], psu, ALU.mult)
            for half in range(2):
                psd = [p.ps(st_, 512) for st_ in range(nst)]
                for fp_ in range(8):
                    d32, d16 = wdf[di % 2], wdb[di % 2]
                    di += 1
                    p.dma(d32, p.dview("w_down", IN["w_down"].ap()[l, e, fp_ * 256:(fp_ + 1) * 256, half * 512:(half + 1) * 512].rearrange("(a q) d -> q a d", q=128)))
                    o.cp(d16, d32, eng="pool")
                    for f2 in range(2):
                        ft = fp_ * 2 + f2
                        for st_ in range(nst):
                            o.mm(psd[st_], hT[:, ft, st_ * 128:(st_ + 1) * 128], d16[:, f2, :], start=(ft == 0), stop=(ft == 15))
                for st_ in range(nst):
                    t = t0 // 128 + st_
                    a_ = acc[st_][:, half * 512:(half + 1) * 512]
                    if e == 0:
                        o.ts(a_, psd[st_], gw[:, t, e:e + 1], None, ALU.mult)
                    else:
                        o.stt(a_, psd[st_], gw[:, t, e:e + 1], a_, ALU.mult, ALU.add)
        for st_ in range(nst):
            t = t0 // 128 + st_
            r = 1 if t < NCTX else 0
            tok = slice(t * 128, (t + 1) * 128)
            if dbg_here(k, "f"):
                p.dma(p.dview("y_out", k.YOUT.ap()[tok, :], t * 128 * D, (t + 1) * 128 * D), acc[st_], q="pool")
            h = hh[st_ % 2]
            p.dma(h, p.dview("H", k.H.ap()[tok, :], t * 128 * D, (t + 1) * 128 * D))
            u = acc[st_]
            o.tt(u, u, g2[r], ALU.mult)
            o.stt(u, h, ALPHA, u, ALU.mult, ALU.add)
            layer_norm(k, u, h, lng, lnb, s8, junk2)
            if update_ctx or r == 0:
                p.dma(p.dview("H", k.H.ap()[tok, :], t * 128 * D, (t + 1) * 128 * D), h, q="pool")
    p.pop()
    p.pop()


def rev(t, n):
    a = t.ap
    st = a.ap[-1][0]
    return t.v(bass.AP(a.tensor, a.offset + (n - 1) * st, [list(a.ap[0]), [-st, n]]))


def bcast_cols(t, n):
    a = t.ap
    return t.v(bass.AP(a.tensor, a.offset, [list(a.ap[0]), [0, n]]))


SIN_C = [-1.0 / 6, 1.0 / 120, -1.0 / 5040, 1.0 / 362880, -1.0 / 39916800]
COS_C = [-0.5, 1.0 / 24, -1.0 / 720, 1.0 / 40320, -1.0 / 3628800, 1.0 / 479001600]


def s5_scan(k, l):
    p, o, IN, C = k.p, k.o, k.IN, k.C
    e = l // 2
    MK = C["MK"]
    p.push()
    WCf = [[p.sb([128]) for _ in range(16)] for _ in range(2)]
    WB = [[[p.sb([128]) for _ in range(16)] for _ in range(2)] for _ in range(2)]
    mag = [p.sb([16]) for _ in range(2)]
    CLv = [p.sb([16, 10]) for _ in range(2)]; SLv = [p.sb([16, 10]) for _ in range(2)]; NSLv = [p.sb([16, 10]) for _ in range(2)]
    p.push()
    BR = p.sb([16, 16]); BI = p.sb([16, 16])
    CLr = p.sb([16, 64]); CLi = p.sb([16, 64])
    for gl in range(2):
        pr = slice(gl * 64, (gl + 1) * 64)
        p.dma(BR[pr], p.dview("od_b_re", IN["od_b_re"].ap()[e].rearrange("(st gl) q h -> gl q st h", gl=2)[gl]))
        p.dma(BI[pr], p.dview("od_b_im", IN["od_b_im"].ap()[e].rearrange("(st gl) q h -> gl q st h", gl=2)[gl]))
        pc = slice(gl * 16, (gl + 1) * 16)
        p.dma(CLr[pc], p.dview("od_c_re", IN["od_c_re"].ap()[e].rearrange("(st gl) h q -> gl h st q", gl=2)[gl]))
        p.dma(CLi[pc], p.dview("od_c_im", IN["od_c_im"].ap()[e].rearrange("(st gl) h q -> gl h st q", gl=2)[gl]))
    X = [p.sb([128]) for _ in range(2)]
    for ri, CLx in enumerate((CLr, CLi)):
        for st in range(16):
            s4 = st % 4
            x = X[st % 2]
            for gl2 in range(2):
                o.ts(x[0:32, gl2 * 64:(gl2 + 1) * 64], CLx[0:32, st, :], MK[0:32, 2 + gl2:3 + gl2], None, ALU.mult)
            ps = p.ps(st % 2, 32)
            o.tr(ps, x[0:32, :], C["IDENT"][0:32, 0:32])
            w = WCf[ri][st]
            o.memset(w, 0.0, eng="pool")
            if ri == 0:
                o.cp(w[:, s4 * 32:(s4 + 1) * 32], ps, eng="act")
            else:
                o.ts(w[:, s4 * 32:(s4 + 1) * 32], ps, -1.0, None, ALU.mult)
    LR = p.sb([16]); LI = p.sb([16]); DT = p.sb([16])
    tl = {n: p.sb([16]) for n in ["lr", "dt", "a", "th", "x", "z", "q", "s", "c", "cc", "ss", "cs", "abr", "abi", "xr", "den",
                                  "t1", "t2", "fre", "fim", "nfim"]}
    fm = {n: p.sb([16]) for n in ["fre0", "fre1", "fim0", "fim1", "nfim0", "nfim1"]}
    INre = [p.sb([128]) for _ in range(4)]; INim = [p.sb([128]) for _ in range(4)]
    tb = p.sb([16])
    for d in range(2):
        for gl in range(2):
            pr = slice(gl * 64, (gl + 1) * 64)
            p.dma(LR[pr], p.dview("od_lam_re", IN["od_lam_re"].ap()[e, d].rearrange("(st gl) q -> gl q st", gl=2)[gl]))
            p.dma(LI[pr], p.dview("od_lam_im", IN["od_lam_im"].ap()[e, d].rearrange("(st gl) q -> gl q st", gl=2)[gl]))
            p.dma(DT[pr], p.dview("od_log_dt", IN["od_log_dt"].ap()[e, d:d + 1, :].rearrange("a (st gl) -> a gl st", gl=2)[:, gl, :].partition_broadcast(64)))
        T_ = tl
        o.ts(T_["lr"], LR, -1e-4, None, ALU.min)
        o.act(T_["dt"], DT, AF.Exp)
        o.tt(T_["a"], T_["lr"], T_["dt"], ALU.mult)
        o.act(mag[d], T_["a"], AF.Exp)
        o.tt(T_["th"], LI, T_["dt"], ALU.mult)
        o.ts(T_["x"], T_["th"], 1.0 / 16, None, ALU.mult)
        o.tt(T_["z"], T_["x"], T_["x"], ALU.mult)
        o.ts(T_["q"], T_["z"], SIN_C[4], None, ALU.mult)
        for a_ in (SIN_C[3], SIN_C[2], SIN_C[1], SIN_C[0]):
            o.stt(T_["q"], T_["q"], a_, T_["z"], ALU.add, ALU.mult)
        o.stt(T_["s"], T_["q"], 1.0, T_["x"], ALU.add, ALU.mult)
        o.ts(T_["q"], T_["z"], COS_C[5], None, ALU.mult)
        for a_ in (COS_C[4], COS_C[3], COS_C[2], COS_C[1], COS_C[0]):
            o.stt(T_["q"], T_["q"], a_, T_["z"], ALU.add, ALU.mult)
        o.ts(T_["c"], T_["q"], 1.0, None, ALU.add)
        for _ in range(4):
            o.tt(T_["cc"], T_["c"], T_["c"], ALU.mult)
            o.tt(T_["ss"], T_["s"], T_["s"], ALU.mult)
            o.tt(T_["cs"], T_["c"], T_["s"], ALU.mult)
            o.tt(T_["c"], T_["cc"], T_["ss"], ALU.subtract)
            o.ts(T_["s"], T_["cs"], 2.0, None, ALU.mult)
        o.cp(CLv[d][:, :, 0], T_["c"]); o.cp(SLv[d][:, :, 0], T_["s"])
        for kk in range(1, 10):
            o.tt(T_["cc"], CLv[d][:, :, kk - 1], CLv[d][:, :, kk - 1], ALU.mult)
            o.tt(T_["ss"], SLv[d][:, :, kk - 1], SLv[d][:, :, kk - 1], ALU.mult)
            o.tt(T_["cs"], CLv[d][:, :, kk - 1], SLv[d][:, :, kk - 1], ALU.mult)
            o.tt(CLv[d][:, :, kk], T_["cc"], T_["ss"], ALU.subtract)
            o.ts(SLv[d][:, :, kk], T_["cs"], 2.0, None, ALU.mult)
        o.ts(NSLv[d], SLv[d], -1.0, None, ALU.mult)
        o.tt(T_["abr"], mag[d], T_["c"], ALU.mult)
        o.tt(T_["abi"], mag[d], T_["s"], ALU.mult)
        o.ts(T_["xr"], T_["abr"], -1.0, None, ALU.add)
        o.tt(T_["t1"], T_["lr"], T_["lr"], ALU.mult)
        o.tt(T_["t2"], LI, LI, ALU.mult)
        o.tt(T_["den"], T_["t1"], T_["t2"], ALU.add)
        o.recip(T_["den"], T_["den"])
        o.tt(T_["t1"], T_["xr"], T_["lr"], ALU.mult)
        o.tt(T_["t2"], T_["abi"], LI, ALU.mult)
        o.tt(T_["t1"], T_["t1"], T_["t2"], ALU.add)
        o.tt(T_["fre"], T_["t1"], T_["den"], ALU.mult)
        o.tt(T_["t1"], T_["abi"], T_["lr"], ALU.mult)
        o.tt(T_["t2"], T_["xr"], LI, ALU.mult)
        o.tt(T_["t1"], T_["t1"], T_["t2"], ALU.subtract)
        o.tt(T_["fim"], T_["t1"], T_["den"], ALU.mult)
        o.ts(T_["nfim"], T_["fim"], -1.0, None, ALU.mult)
        for gl in range(2):
            o.ts(fm["fre%d" % gl], T_["fre"], MK[:, gl:gl + 1], None, ALU.mult)
            o.ts(fm["fim%d" % gl], T_["fim"], MK[:, gl:gl + 1], None, ALU.mult)
            o.ts(fm["nfim%d" % gl], T_["nfim"], MK[:, gl:gl + 1], None, ALU.mult)
        for st in range(16):
            ft_, s4 = divmod(st, 4)
            for gl in range(2):
                cs_ = slice(s4 * 32 + gl * 16, s4 * 32 + gl * 16 + 16)
                fre, fim, nfim = fm["fre%d" % gl][:, st:st + 1], fm["fim%d" % gl][:, st:st + 1], fm["nfim%d" % gl][:, st:st + 1]
                o.ts(tb, BR[:, st, :], fre, None, ALU.mult)
                o.stt(INre[ft_][:, cs_], BI[:, st, :], nfim, tb, ALU.mult, ALU.add)
                o.ts(tb, BR[:, st, :], fim, None, ALU.mult)
                o.stt(INim[ft_][:, cs_], BI[:, st, :], fre, tb, ALU.mult, ALU.add)
        for ft_ in range(4):
            for ri, INx in enumerate((INre, INim)):
                ps = p.ps(2 + ri, 128)
                o.tr(ps, INx[ft_], C["IDENT"])
                for s4 in range(4):
                    o.ts(WB[d][ri][ft_ * 4 + s4], ps, MK[:, 4 + s4:5 + s4], None, ALU.mult)
    p.pop()
    NSEG = [(0, 256)] + [(256 + 512 * i, 512) for i in range(8)]
    cosT = [p.sb([516]) for _ in range(2)]; sinT = [p.sb([516]) for _ in range(2)]
    tAs = [p.sb([256]) for _ in range(2)]; tBs = [p.sb([256]) for _ in range(2)]
    uF = p.sb([TT]); yaccs = [p.sb([TT]) for _ in range(2)]
    wk = {n: [p.sb([512]) for _ in range(2)] for n in ["t1", "t2", "t3", "t4", "wre", "wim", "gre", "gim", "hre", "him"]}
    inis = [[p.sb([8]) for _ in range(2)] for _ in range(2)]
    crs = [p.sb([8]) for _ in range(2)]

    def stream(d, ft_, sidx, s4list, segs):
        cosv, sinv, tA, tB, yacc, cr = cosT[sidx], sinT[sidx], tAs[sidx], tBs[sidx], yaccs[sidx], crs[sidx]
        W_ = {nm: wk[nm][sidx] for nm in wk}
        ini = inis[sidx]
        bk_r, bk_i, bk_y = (4, 5, 0) if sidx == 0 else (6, 7, 1)
        for s4 in s4list:
            st = ft_ * 4 + s4
            o.memset(cosv[:, 0:1], 1.0, eng="dve"); o.memset(sinv[:, 0:1], 0.0, eng="dve")
            for kk in range(9):
                w = 1 << kk
                c_, s_, ns_ = CLv[d][:, st, kk:kk + 1], SLv[d][:, st, kk:kk + 1], NSLv[d][:, st, kk:kk + 1]
                o.ts(tA[:, 0:w], cosv[:, 0:w], c_, None, ALU.mult)
                o.ts(tB[:, 0:w], sinv[:, 0:w], c_, None, ALU.mult)
                o.stt(tA[:, 0:w], sinv[:, 0:w], ns_, tA[:, 0:w], ALU.mult, ALU.add)
                o.stt(tB[:, 0:w], cosv[:, 0:w], s_, tB[:, 0:w], ALU.mult, ALU.add)
                o.cp(cosv[:, w:2 * w], tA[:, 0:w]); o.cp(sinv[:, w:2 * w], tB[:, 0:w])
            o.cp(cosv[:, 512:513], CLv[d][:, st, 9:10]); o.cp(sinv[:, 512:513], SLv[d][:, st, 9:10])
            cur = ini[0]
            o.memset(cur[:, 0:2], 0.0, eng="dve")
            for si, (t0, n) in enumerate(segs):
                psr = p.ps(bk_r, n); psi = p.ps(bk_i, n)
                o.mm(psr, WB[d][0][st], uF[:, t0:t0 + n])
                o.mm(psi, WB[d][1][st], uF[:, t0:t0 + n])
                V = (lambda t_: rev(t_, n)) if d == 1 else (lambda t_: t_[:, 0:n])
                cs_n, sn_n = cosv[:, 0:n], sinv[:, 0:n]
                o.tt(W_["t1"][:, 0:n], V(psr), cs_n, ALU.mult)
                o.tt(W_["t2"][:, 0:n], V(psi), sn_n, ALU.mult)
                o.tt(W_["wre"][:, 0:n], W_["t1"][:, 0:n], W_["t2"][:, 0:n], ALU.add)
                o.tt(W_["t3"][:, 0:n], V(psi), cs_n, ALU.mult)
                o.tt(W_["t4"][:, 0:n], V(psr), sn_n, ALU.mult)
                o.tt(W_["wim"][:, 0:n], W_["t3"][:, 0:n], W_["t4"][:, 0:n], ALU.subtract)
                gre, gim = W_["gre"], W_["gim"]
                mb = bcast_cols(mag[d][:, st:st + 1], n)
                p.op("dve", lambda e_, gre=gre, mb=mb, w=W_["wre"], cur=cur, n=n: e_.tensor_tensor_scan(
                    gre.ap[:, 0:n], mb.ap, w.ap[:, 0:n], cur.ap[:, 0:1], ALU.mult, ALU.add), [mb, W_["wre"], cur], [gre])
                p.op("dve", lambda e_, gim=gim, mb=mb, w=W_["wim"], cur=cur, n=n: e_.tensor_tensor_scan(
                    gim.ap[:, 0:n], mb.ap, w.ap[:, 0:n], cur.ap[:, 1:2], ALU.mult, ALU.add), [mb, W_["wim"], cur], [gim])
                o.tt(W_["t1"][:, 0:n], gre[:, 0:n], cs_n, ALU.mult)
                o.tt(W_["t2"][:, 0:n], gim[:, 0:n], sn_n, ALU.mult)
                o.tt(V(W_["hre"]), W_["t1"][:, 0:n], W_["t2"][:, 0:n], ALU.subtract)
                o.tt(W_["t3"][:, 0:n], gim[:, 0:n], cs_n, ALU.mult)
                o.tt(W_["t4"][:, 0:n], gre[:, 0:n], sn_n, ALU.mult)
                o.tt(V(W_["him"]), W_["t3"][:, 0:n], W_["t4"][:, 0:n], ALU.add)
                nxt = ini[(si + 1) % 2]
                o.ts(cr[:, 0:1], gre[:, n - 1:n], cosv[:, n:n + 1], None, ALU.mult)
                o.ts(cr[:, 1:2], gim[:, n - 1:n], sinv[:, n:n + 1], None, ALU.mult)
                o.tt(nxt[:, 0:1], cr[:, 0:1], cr[:, 1:2], ALU.subtract)
                o.ts(cr[:, 2:3], gim[:, n - 1:n], cosv[:, n:n + 1], None, ALU.mult)
                o.stt(nxt[:, 1:2], gre[:, n - 1:n], sinv[:, n:n + 1], cr[:, 2:3], ALU.mult, ALU.add)
                cur = nxt
                psy = p.ps(bk_y, n)
                o.mm(psy, WCf[0][st], W_["hre"][:, 0:n], start=True, stop=False)
                o.mm(psy, WCf[1][st], W_["him"][:, 0:n], start=False, stop=True)
                if s4 == s4list[0]:
                    o.cp(yacc[:, t0:t0 + n], psy, eng="act")
                else:
                    o.tt(yacc[:, t0:t0 + n], psy, yacc[:, t0:t0 + n], ALU.add)

    for d in range(2):
        segs = NSEG if d == 0 else [NSEG[0]] + NSEG[:0:-1]
        for ft_ in range(4):
            p.dma(uF, p.dview("FM", k.FM.ap()[12 + ft_], (12 + ft_) * 128 * TT, (13 + ft_) * 128 * TT))
            caps = []
            for sidx, s4list in enumerate(((0, 2), (1, 3))):
                p.begin_capture()
                stream(d, ft_, sidx, s4list, segs)
                caps.append(p.end_capture())
            p.replay(caps)
            o.tt(yaccs[0], yaccs[0], yaccs[1], ALU.add)
            p.dma(p.dview("YS5", k.YS5.ap()[d, ft_], (d * 4 + ft_) * 128 * TT, (d * 4 + ft_ + 1) * 128 * TT), yaccs[0], q="pool")
    p.pop()


def merge_odd(k, l):
    p, o, IN, C = k.p, k.o, k.IN, k.C
    e = l // 2
    p.push()
    wout = p.sb([8, D], BF16)
    wst = [p.sb([8, 256]) for _ in range(2)]
    for j in range(4):
        p.dma(wst[j % 2], p.dview("od_w_out", IN["od_w_out"].ap()[e, :, j * 256:(j + 1) * 256].rearrange("(kt q) f -> q kt f", q=128)))
        o.cp(wout[:, :, j * 256:(j + 1) * 256], wst[j % 2], eng="pool")
    wglu = p.sb([4, 512], BF16)
    for j in range(2):
        w32 = wst[j % 2]
        w3 = w32.v(w32.ap.rearrange("q a b -> q (a b)")[:, 0:1024].rearrange("q (a b) -> q a b", a=4))
        p.dma(w3, p.dview("od_w_glu", IN["od_w_glu"].ap()[e, :, j * 256:(j + 1) * 256].rearrange("(kt q) f -> q kt f", q=128)))
        o.cp(wglu[:, :, j * 256:(j + 1) * 256], w3, eng="pool")
    mg = p.sb([512]); dsk = p.sb([512]); bgl = p.sb([512])
    rowbc(k, mg, IN["od_mlstm_norm"].ap()[e:e + 1, :], "od_mlstm_norm")
    rowbc(k, dsk, IN["od_d_skip"].ap()[e:e + 1, :], "od_d_skip")
    rowbc(k, bgl, IN["od_b_glu"].ap()[e:e + 1, :], "od_b_glu")
    g1 = [p.sb([D]), p.sb([D])]
    for r in range(2):
        rowbc(k, g1[r], k.MOD.ap()[l, r:r + 1, 2048:3072], "MOD")
    lng = p.sb([D]); lnb = p.sb([D])
    rowbc(k, lng, IN["ln1_g"].ap()[l:l + 1, :], "ln1_g")
    rowbc(k, lnb, IN["ln1_b"].ap()[l:l + 1, :], "ln1_b")
    A2 = [p.sb([2, 4, 132]) for _ in range(2)]
    YS = [p.sb([2, 4, 128]) for _ in range(2)]
    zt = [p.sb([D]) for _ in range(2)]
    ht = [p.sb([D]) for _ in range(2)]
    Y = [p.sb([D]) for _ in range(2)]
    YT = [p.sb([8, 128], BF16) for _ in range(2)]
    st_ = [p.sb([48]) for _ in range(2)]
    junk = p.sb([D])
    U = [p.sb([D]) for _ in range(2)]
    HM = [p.sb([4, 128]) for _ in range(2)]
    YG = [p.sb([512]) for _ in range(2)]
    YGT = [p.sb([4, 128], BF16) for _ in range(2)]
    for t in range(NT):
        r = 1 if t < NCTX else 0
        a2, ys, z, h, y, yT, s, u, hm, yg, ygT = (A2[t % 2], YS[t % 2], zt[t % 2], ht[t % 2], Y[t % 2], YT[t % 2], st_[t % 2],
                                                  U[t % 2], HM[t % 2], YG[t % 2], YGT[t % 2])
        tok = slice(t * 128, (t + 1) * 128)
        for d in range(2):
            for hh in range(4):
                p.dma(a2[:, d, hh, 0:129], p.dview("OUTO%d_%d" % (hh, d), k.OUTO.ap()[hh, d, tok, 0:129], t * 128 * 132, (t + 1) * 128 * 132))
                fi = d * 4 + hh
                p.dma(ys[:, d, hh, :], p.dview("YS5", k.YS5.ap()[d, hh][:, tok], fi * 128 * TT, (fi + 1) * 128 * TT))
        p.dma(z, p.dview("ZS", k.ZS.ap()[tok, :], t * 128 * D, (t + 1) * 128 * D))
        p.dma(h, p.dview("H", k.H.ap()[tok, :], t * 128 * D, (t + 1) * 128 * D))
        den = a2[:, :, :, 128]
        s3 = lambda c0: s.v(s.ap[:, c0:c0 + 8].rearrange("q (a b) -> q a b", a=2))
        o.ts(s3(0), den, -1.0, None, ALU.mult)
        o.tt(s3(0), s3(0), den, ALU.max)
        o.ts(s3(0), s3(0), 1.0, None, ALU.max)
        o.recip(s[:, 8:16], s[:, 0:8])
        for hh in range(4):
            o.ts(junk[:, 0:128], a2[:, 0, hh, 0:128], s[:, 8 + hh:9 + hh], None, ALU.mult)
            o.stt(hm[:, hh, :], a2[:, 1, hh, 0:128], s[:, 12 + hh:13 + hh], junk[:, 0:128], ALU.mult, ALU.add)
        for hh in range(4):
            o.act(junk[:, 0:128], hm[:, hh, :], AF.Square, accum=s[:, 16 + hh:17 + hh])
            o.act(junk[:, 128:256], hm[:, hh, :], AF.Identity, accum=s[:, 20 + hh:21 + hh])
        o.ts(s[:, 24:28], s[:, 20:24], 1.0 / 128, None, ALU.mult)
        o.tt(s[:, 28:32], s[:, 24:28], s[:, 24:28], ALU.mult)
        o.stt(s[:, 32:36], s[:, 16:20], 1.0 / 128, s[:, 28:32], ALU.mult, ALU.subtract)
        o.act(s[:, 36:40], s[:, 32:36], AF.Sqrt, bias=k.eps[:, 0:1])
        o.recip(s[:, 40:44], s[:, 36:40])
        for hh in range(4):
            o.ts(junk[:, 0:128], hm[:, hh, :], s[:, 24 + hh:25 + hh], s[:, 40 + hh:41 + hh], ALU.subtract, ALU.mult)
            o.tt(y[:, hh * 128:(hh + 1) * 128], junk[:, 0:128], mg[:, hh * 128:(hh + 1) * 128], ALU.mult)
        o.tt(y[:, 0:512], y[:, 0:512], z[:, 0:512], ALU.mult)
        o.tt(ys[:, 0], ys[:, 0], ys[:, 1], ALU.add)
        ps = p.ps(4)
        for ft_ in range(4):
            o.tr(ps[:, ft_ * 128:(ft_ + 1) * 128], ys[:, 0, ft_, :], C["IDENT"])
        o.tt(junk[:, 0:512], z[:, 512:1024], dsk, ALU.mult)
        o.tt(junk[:, 0:512], junk[:, 0:512], ps, ALU.add)
        xg = junk[:, 0:512]
        o.tt(junk[:, 512:1024], xg, xg, ALU.mult)
        o.ts(junk[:, 512:1024], junk[:, 512:1024], 0.044715, 1.0, ALU.mult, ALU.add)
        o.tt(junk[:, 512:1024], junk[:, 512:1024], xg, ALU.mult)
        o.act(yg, junk[:, 512:1024], AF.Tanh, scale=math.sqrt(2.0 / math.pi))
        o.stt(yg, yg, 1.0, xg, ALU.add, ALU.mult)
        o.ts(yg, yg, 0.5, None, ALU.mult)
        ps2 = p.ps(5)
        for ft_ in range(4):
            o.tr(ps2[:, ft_ * 128:(ft_ + 1) * 128], yg[:, ft_ * 128:(ft_ + 1) * 128], C["IDENT"])
        o.cp(ygT.v(ygT.ap.rearrange("q a b -> q (a b)")), ps2, eng="act")
        ps3 = p.ps(6)
        for kt in range(4):
            o.mm(ps3, ygT[:, kt, :], wglu[:, kt, :], start=(kt == 0), stop=(kt == 3))
        o.tt(junk[:, 0:512], ps3, bgl, ALU.add)
        o.act(junk[:, 512:1024], junk[:, 0:512], AF.Sigmoid)
        o.tt(y[:, 512:1024], yg, junk[:, 512:1024], ALU.mult)
        for half in range(2):
            ps = p.ps(half)
            for q in range(4):
                kt = half * 4 + q
                o.tr(ps[:, q * 128:(q + 1) * 128], y[:, kt * 128:(kt + 1) * 128], C["IDENT"])
            o.cp(yT.v(yT.ap[:, half * 4:(half + 1) * 4, :].rearrange("q a b -> q (a b)")), ps, eng="act")
        for half in range(2):
            ps = p.ps(2 + half)
            for kt in range(8):
                o.mm(ps, yT[:, kt, :], wout[:, kt, half * 512:(half + 1) * 512], start=(kt == 0), stop=(kt == 7))
            hs_ = slice(half * 512, (half + 1) * 512)
            o.tt(u[:, hs_], ps, g1[r][:, hs_], ALU.mult)
        if dbg_here(k, "y"):
            p.dma(p.dview("y_out", k.YOUT.ap()[tok, :], t * 128 * D, (t + 1) * 128 * D), u, q="pool")
        o.stt(u, h, ALPHA, u, ALU.mult, ALU.add)
        layer_norm(k, u, h, lng, lnb, s, junk)
        p.dma(p.dview("H", k.H.ap()[tok, :], t * 128 * D, (t + 1) * 128 * D), h, q="pool")
        if dbg_here(k, "h1"):
            p.dma(p.dview("y_out", k.YOUT.ap()[tok, :], t * 128 * D, (t + 1) * 128 * D), h, q="pool")
    p.pop()


def moe_layer2(k, l):
    p, o, IN, C = k.p, k.o, k.IN, k.C
    IOTA = k.IOTA
    IC = C["IC"]
    SETS = [(list(range(NCTX, NT)), 512, [(0, 128), (128, 128), (256, 128), (384, 128)], 0),
            (list(range(0, NCTX)), 32, [(0, 32)], 1)]
    p.push()
    wr = p.sb([8, 16])
    p.dma(wr, p.dview("w_router", IN["w_router"].ap()[l].rearrange("(kt q) e -> q kt e", q=128)))
    aff = p.sb([NT, 16]); gw = p.sb([NT, 16]); msk = p.sb([NT, 16]); posm = p.sb([NT, 16])
    posmT = p.sb([TT])
    ghl = p.sb([NT, 16, 2], BF16)
    p.push()
    in2tm = p.sb([NT, D], BF16)
    p.push()
    affT = p.sb([TT])
    sc2 = [p.sb([D]), p.sb([D])]; sh2 = [p.sb([D]), p.sb([D])]
    for r in range(2):
        rowbc(k, sc2[r], k.MOD.ap()[l, r:r + 1, 4096:5120], "MOD")
        o.ts(sc2[r], sc2[r], 1.0, None, ALU.add)
        rowbc(k, sh2[r], k.MOD.ap()[l, r:r + 1, 3072:4096], "MOD")
    ht = [p.sb([D]) for _ in range(2)]
    x2 = [p.sb([D]) for _ in range(2)]
    xTf = [p.sb([8, 128]) for _ in range(2)]
    sm = [p.sb([40]) for _ in range(2)]
    for t in range(NT):
        r = 1 if t < NCTX else 0
        h, x, xf, s = ht[t % 2], x2[t % 2], xTf[t % 2], sm[t % 2]
        tok = slice(t * 128, (t + 1) * 128)
        p.dma(h, p.dview("H", k.H.ap()[tok, :], t * 128 * D, (t + 1) * 128 * D))
        o.tt(x, h, sc2[r], ALU.mult)
        o.tt(x, x, sh2[r], ALU.add)
        o.cp(in2tm[:, t, :], x, eng="pool")
        for half in range(2):
            ps = p.ps(half)
            for q in range(4):
                kt = half * 4 + q
                o.tr(ps[:, q * 128:(q + 1) * 128], x[:, kt * 128:(kt + 1) * 128], C["IDENT"])
            ps3 = ps.v(ps.ap.rearrange("q (a b) -> q a b", a=4))
            o.cp(xf[:, half * 4:(half + 1) * 4, :], ps3, eng="act")
        psl = p.ps(2, 16)
        for kt in range(8):
            o.mm(psl, xf[:, kt, :], wr[:, kt, :], start=(kt == 0), stop=(kt == 7))
        p.op("dve", lambda e, s=s, psl=psl: e.reduce_max(s.ap[:, 0:1], psl.ap, AX.X), [psl], [s])
        o.ts(s[:, 1:2], s[:, 0:1], -1.0, None, ALU.mult)
        o.act(s[:, 8:24], psl, AF.Exp, bias=s[:, 1:2], accum=s[:, 2:3])
        o.recip(s[:, 3:4], s[:, 2:3])
        o.ts(aff[:, t, :], s[:, 8:24], s[:, 3:4], None, ALU.mult)
        pst = p.ps(3, 128)
        o.tr(pst[0:16, :], aff[:, t, :], C["IDENT"])
        o.cp(affT[0:16, tok], pst[0:16, :], eng="act")
    st = p.sb([16]); junk = p.sb([4096]); thr = [p.sb([16]), p.sb([16])]; dt = p.sb([16])
    sets = [(0, 256, 32.0, 1), (256, TT, 512.0, 0)]
    for (c0, c1, cap, r) in sets:
        lo, hi, mid, cnt, ge, d1 = [st[0:16, i:i + 1] for i in range(6)]
        o.memset(lo, 0.0, eng="dve"); o.memset(hi, 1.0, eng="dve")
        for it in range(32):
            o.tt(mid, lo, hi, ALU.add)
            o.ts(mid, mid, 0.5, None, ALU.mult)
            o.ts(junk[0:16, 0:c1 - c0], affT[0:16, c0:c1], mid, None, ALU.is_ge, ALU.add, accum=cnt)
            o.ts(ge, cnt, cap - 0.5, None, ALU.is_ge)
            o.tt(d1, mid, lo, ALU.subtract)
            o.stt(lo, d1, ge, lo, ALU.mult, ALU.add)
            o.tt(d1, hi, mid, ALU.subtract)
            o.stt(hi, d1, ge, mid, ALU.mult, ALU.add)
        o.ts(dt[0:16, 0:16], C["IDENT"][0:16, 0:16], lo, None, ALU.mult)
        pth = p.ps(2, 16)
        o.mm(pth, C["ONES"][0:16, :], dt[0:16, 0:16])
        o.cp(thr[r], pth, eng="act")
    tot = p.sb([16]); tmp16 = p.sb([16])
    for (tiles, cap, jts, yi) in SETS:
        r = 1 if tiles[0] < NCTX else 0
        o.memset(tot, 0.0, eng="dve")
        for t in tiles:
            tok = slice(t * 128, (t + 1) * 128)
            o.tt(msk[:, t, :], aff[:, t, :], thr[r], ALU.is_ge)
            o.tt(gw[:, t, :], msk[:, t, :], aff[:, t, :], ALU.mult)
            ps = p.ps(0, 16); ps2 = p.ps(1, 16)
            o.mm(ps, C["CSs"], msk[:, t, :])
            o.mm(ps2, C["ONES"], msk[:, t, :])
            o.tt(tmp16, ps, tot, ALU.add)
            o.tt(tot, tot, ps2, ALU.add)
            o.stt(tmp16, tmp16, 1.0, msk[:, t, :], ALU.add, ALU.mult)
            o.ts(posm[:, t, :], tmp16, -1.0, None, ALU.add)
            pst = p.ps(3, 128)
            o.tr(pst[0:16, :], posm[:, t, :], C["IDENT"])
            o.cp(posmT[0:16, tok], pst[0:16, :], eng="act")
    hi16 = p.sb([NT, 16], BF16); hi32 = p.sb([NT, 16]); lo32 = p.sb([NT, 16])
    o.cp(hi16, gw); o.cp(hi32, hi16); o.tt(lo32, gw, hi32, ALU.subtract)
    o.cp(ghl[:, :, :, 0], hi32); o.cp(ghl[:, :, :, 1], lo32)
    p.pop()
    p.push()
    wgf = [p.sb([8, 128]) for _ in range(2)]; wuf = [p.sb([8, 128]) for _ in range(2)]
    wgb = [p.sb([8, 128], BF16) for _ in range(2)]; wub = [p.sb([8, 128], BF16) for _ in range(2)]
    wdf = [p.sb([2, 512]) for _ in range(2)]; wdb = [p.sb([2, 512], BF16) for _ in range(2)]
    xsT = [p.sb([8, 512], BF16), p.sb([8, 32], BF16)]
    hT = [p.sb([16, 512], BF16), p.sb([16, 32], BF16)]
    sg = [p.sb([512]) for _ in range(2)]
    sel = [p.sb([512], BF16) for _ in range(3)]
    gsel = [p.sb([4]), p.sb([4])]
    ysb = [[p.sb([D], BF16) for _ in range(5)] for _ in range(2)]
    wi = 0; di = 0; si = 0
    for e in range(16):
        for (tiles, cap, jts, yi) in SETS:
            for ps_ in range(2):
                psx = [p.ps(q, cap) for q in range(4)]
                psg = [p.ps(4 + ji, 2) for ji in range(len(jts))]
                for t in tiles:
                    sl = sel[si % 3]; si += 1
                    o.ts(sl[:, 0:cap], IOTA[:, 0:cap], posm[:, t, e:e + 1], None, ALU.is_equal)
                    for q in range(4):
                        kt = ps_ * 4 + q
                        o.mm(psx[q], in2tm[:, t, kt * 128:(kt + 1) * 128], sl[:, 0:cap], start=(t == tiles[0]), stop=(t == tiles[-1]))
                    if ps_ == 0:
                        for ji, (j0, nj) in enumerate(jts):
                            o.mm(psg[ji][0:nj, :], sl[:, j0:j0 + nj], ghl[:, t, e, :], start=(t == tiles[0]), stop=(t == tiles[-1]))
                for q in range(4):
                    kt = ps_ * 4 + q
                    o.cp(xsT[yi][:, kt, 0:cap], psx[q], eng=("act" if q % 2 else "dve"))
                if ps_ == 0:
                    for ji, (j0, nj) in enumerate(jts):
                        gs_, pg_ = gsel[yi][0:nj, ji:ji + 1], psg[ji][0:nj, 0:2]
                        p.op("dve", lambda e_, gs_=gs_, pg_=pg_: e_.reduce_sum(gs_.ap, pg_.ap, AX.X), [pg_], [gs_])
        for ft_ in range(16):
            a32, b32, a16, b16 = wgf[wi % 2], wuf[wi % 2], wgb[wi % 2], wub[wi % 2]
            wi += 1
            p.dma(a32, p.dview("w_gate", IN["w_gate"].ap()[l, e, :, ft_ * 128:(ft_ + 1) * 128].rearrange("(kt q) f -> q kt f", q=128)))
            p.dma(b32, p.dview("w_up", IN["w_up"].ap()[l, e, :, ft_ * 128:(ft_ + 1) * 128].rearrange("(kt q) f -> q kt f", q=128)))
            o.cp(a16, a32, eng="pool")
            o.cp(b16, b32, eng="pool")
            for (tiles, cap, jts, yi) in SETS:
                pg = p.ps(4 + (ft_ % 2) * 2, cap); pu = p.ps(5 + (ft_ % 2) * 2, cap)
                for kt in range(8):
                    o.mm(pg, a16[:, kt, :], xsT[yi][:, kt, 0:cap], start=(kt == 0), stop=(kt == 7))
                for kt in range(8):
                    o.mm(pu, b16[:, kt, :], xsT[yi][:, kt, 0:cap], start=(kt == 0), stop=(kt == 7))
                s_ = sg[(ft_ + yi) % 2]
                o.act(s_[:, 0:cap], pg, AF.Silu)
                o.tt(hT[yi][:, ft_, 0:cap], s_[:, 0:cap], pu, ALU.mult)
        yt_ = ysb[e % 2]
        for half in range(2):
            hs_ = slice(half * 512, (half + 1) * 512)
            psd = {}
            for (tiles, cap, jts, yi) in SETS:
                for ji in range(len(jts)):
                    psd[(yi, ji)] = p.ps(ji if yi == 0 else 4, 512)
            for fp_ in range(8):
                d32, d16 = wdf[di % 2], wdb[di % 2]
                di += 1
                p.dma(d32, p.dview("w_down", IN["w_down"].ap()[l, e, fp_ * 256:(fp_ + 1) * 256, half * 512:(half + 1) * 512].rearrange("(a q) d -> q a d", q=128)))
                o.cp(d16, d32, eng="pool")
                for f2 in range(2):
                    ft_ = fp_ * 2 + f2
                    for (tiles, cap, jts, yi) in SETS:
                        for ji, (j0, nj) in enumerate(jts):
                            o.mm(psd[(yi, ji)][0:nj, :], hT[yi][:, ft_, j0:j0 + nj], d16[:, f2, :], start=(ft_ == 0), stop=(ft_ == 15))
            for (tiles, cap, jts, yi) in SETS:
                for ji, (j0, nj) in enumerate(jts):
                    yt = yt_[ji if yi == 0 else 4]
                    o.ts(yt[0:nj, hs_], psd[(yi, ji)][0:nj, :], gsel[yi][0:nj, ji:ji + 1], None, ALU.mult)
        for (tiles, cap, jts, yi) in SETS:
            for ji, (j0, nj) in enumerate(jts):
                yt = yt_[ji if yi == 0 else 4]
                p.dma(p.dview("YSD%d" % yi, k.YSD[yi].ap()[e, j0:j0 + nj, :]), yt[0:nj, :], q="pool")
    p.pop()
    p.pop()
    p.push()
    g2 = [p.sb([D]), p.sb([D])]
    for r in range(2):
        rowbc(k, g2[r], k.MOD.ap()[l, r:r + 1, 5120:6144], "MOD")
    lng = p.sb([D]); lnb = p.sb([D])
    rowbc(k, lng, IN["ln2_g"].ap()[l:l + 1, :], "ln2_g")
    rowbc(k, lnb, IN["ln2_b"].ap()[l:l + 1, :], "ln2_b")
    selT = [[p.sb([512], BF16) for _ in range(4)] for _ in range(16)]
    ysl = [p.sb([4, 128], BF16) for _ in range(4)]
    fT = p.sb([8, 512])
    hh = [p.sb([D]) for _ in range(2)]
    uu = [p.sb([D]) for _ in range(2)]
    s8 = p.sb([8]); junk2 = p.sb([D])
    groups = [(0, 256, SETS[1])] + [(256 + 512 * i, 512, SETS[0]) for i in range(8)]
    yl = 0
    for (t0, n, (tiles, cap, jts, yi)) in groups:
        for e in range(16):
            psr = p.ps(e % 2, n)
            o.mm(psr, C["OH%d" % e][0:16, :], posmT[0:16, t0:t0 + n])
            for ji, (j0, nj) in enumerate(jts):
                o.ts(selT[e][ji][0:nj, 0:n], psr[0:nj, :], IC[0:nj, ji:ji + 1], None, ALU.is_equal)
        for dt_ in range(8):
            psf = p.ps(2 + dt_ % 2, n)
            for e in range(16):
                y_ = ysl[yl % 4]; yl += 1
                nrow = jts[-1][0] + jts[-1][1]
                if yi == 0:
                    p.dma(y_, p.dview("YSD0", k.YSD[0].ap()[e, :, dt_ * 128:(dt_ + 1) * 128].rearrange("(jt q) d -> q jt d", q=128)))
                else:
                    p.dma(y_[0:32, 0, :], p.dview("YSD1", k.YSD[1].ap()[e, 0:32, dt_ * 128:(dt_ + 1) * 128]))
                for ji, (j0, nj) in enumerate(jts):
                    o.mm(psf, y_[0:nj, ji, :], selT[e][ji][0:nj, 0:n], start=(e == 0 and ji == 0), stop=(e == 15 and ji == len(jts) - 1))
            o.cp(fT[:, dt_, 0:n], psf, eng=("act" if dt_ % 2 else "dve"))
        for st_ in range(n // 128):
            t = t0 // 128 + st_
            r = 1 if t < NCTX else 0
            tok = slice(t * 128, (t + 1) * 128)
            h = hh[st_ % 2]; u = uu[st_ % 2]
            p.dma(h, p.dview("H", k.H.ap()[tok, :], t * 128 * D, (t + 1) * 128 * D))
            for half in range(2):
                ps = p.ps(4 + half)
                for q in range(4):
                    dt_ = half * 4 + q
                    o.tr(ps[:, q * 128:(q + 1) * 128], fT[:, dt_, st_ * 128:(st_ + 1) * 128], C["IDENT"])
                hs_ = slice(half * 512, (half + 1) * 512)
                if dbg_here(k, "f"):
                    o.cp(junk2[:, hs_], ps, eng="act")
                o.tt(u[:, hs_], ps, g2[r][:, hs_], ALU.mult)
            if dbg_here(k, "f"):
                p.dma(p.dview("y_out", k.YOUT.ap()[tok, :], t * 128 * D, (t + 1) * 128 * D), junk2, q="pool")
            o.stt(u, h, ALPHA, u, ALU.mult, ALU.add)
            layer_norm(k, u, h, lng, lnb, s8, junk2)
            p.dma(p.dview("H", k.H.ap()[tok, :], t * 128 * D, (t + 1) * 128 * D), h, q="pool")
    p.pop()
    p.pop()


_W_NAMES = ["w_mod", "b_mod", "ln1_g", "ln1_b", "ln2_g", "ln2_b", "w_router", "w_gate", "w_up", "w_down",
            "ev_w_in", "ev_w_out", "ev_conv", "ev_a_log", "ev_dt_bias", "ev_gdn_norm", "ev_ret_norm",
            "od_w_in", "od_w_out", "od_conv", "od_gate_bias", "od_mlstm_norm", "od_lam_re", "od_lam_im", "od_log_dt",
            "od_b_re", "od_b_im", "od_c_re", "od_c_im", "od_d_skip", "od_w_glu", "od_b_glu"]


def kernel(**inputs):
    nc = build(n_layers=4)
    cst = make_consts()
    retg = np.concatenate([ret_log_decay(0), ret_log_decay(1)])[None, :].astype(np.float32)
    in_maps = []
    B = inputs["x"].shape[0]
    for b in range(B):
        m = {}
        m["h0"] = np.ascontiguousarray(np.concatenate([inputs["ctx"][b], inputs["x"][b]], 0).astype(np.float32))
        m["cvec"] = np.ascontiguousarray(np.stack([inputs["c"][b], inputs["c_ctx"]], 0).astype(np.float32))
        m["cst"] = cst
        m["retg"] = retg
        for n in _W_NAMES:
            if n in nc.used_inputs:
                m[n] = np.ascontiguousarray(inputs[n], dtype=np.float32)
        in_maps.append({n: m[n] for n in nc.used_inputs})
    res = run_bass_kernel_spmd(nc, in_maps, core_ids=list(range(B)))
    out = np.stack([np.asarray(res.results[b]["y_out"])[256:] for b in range(B)], 0)
    return out.astype(np.float32)
```

```python
import bisect
import numpy as np
import concourse.bass as bass
import concourse.mybir as mybir
from concourse.bass_utils import run_bass_kernel_spmd

F32 = mybir.dt.float32
BF16 = mybir.dt.bfloat16
ALU = mybir.AluOpType
AF = mybir.ActivationFunctionType
AX = mybir.AxisListType

SEM_GEN = 30000
N_DMA_SEMS = 12


class IntervalMap:
    def __init__(self):
        self.bounds = [0]
        self.state = [(None, {})]

    def _split(self, x):
        i = bisect.bisect_right(self.bounds, x) - 1
        if self.bounds[i] == x:
            return i
        w, r = self.state[i]
        self.bounds.insert(i + 1, x)
        self.state.insert(i + 1, (w, dict(r)))
        return i + 1

    def segs(self, lo, hi):
        i0 = self._split(lo)
        i1 = self._split(hi)
        return range(i0, i1)


class T:
    def __init__(self, ap, key, lo, hi):
        self.ap = ap
        self.key = key
        self.lo = lo
        self.hi = hi

    def __getitem__(self, idx):
        return T(self.ap[idx], self.key, self.lo, self.hi)

    def v(self, ap):
        return T(ap, self.key, self.lo, self.hi)


class Prog:
    ENGS = ["pe", "act", "dve", "pool", "sp"]

    def __init__(self, nc, sb_bytes=206 * 1024):
        self.nc = nc
        self.ops = {e: [] for e in self.ENGS}
        self.cnt = {e: 0 for e in self.ENGS}
        self.maps = {}
        self.observed = {e: {} for e in self.ENGS}
        self.dma_cnt = {"sp": 0, "pool": 0, "act": 0}
        self.dma_sem_uses = {}
        self.sem_names = set()
        self.sb_bytes = sb_bytes
        self.sb_top = 0
        self.sb_stack = []
        self.ps_top = 0
        self.dram = {}
        self.arena = None
        self.psarena = None
        self.final_waits = []

    def setup_mem(self, es):
        nc = self.nc
        self.arena = es.enter_context(nc.sbuf_tensor("arena", [128, self.sb_bytes // 4], F32))
        self.psarena = es.enter_context(nc.psum_tensor("psarena", [128, 8 * 512], F32))

    def push(self):
        self.sb_stack.append(self.sb_top)

    def pop(self):
        self.sb_top = self.sb_stack.pop()

    def sb(self, shape, dtype=F32, name=None):
        esz = 4 if dtype == F32 else 2
        n = int(np.prod(shape))
        nbytes = (n * esz + 31) // 32 * 32
        off = self.sb_top
        self.sb_top += nbytes
        assert self.sb_top <= self.sb_bytes, f"SBUF overflow {self.sb_top}"
        ap = self.arena[:, off // 4:(off + nbytes) // 4]
        if dtype != F32:
            ap = ap.bitcast(dtype)
        ap = ap[:, 0:n]
        if len(shape) == 2:
            ap = ap.rearrange("p (a b) -> p a b", a=shape[0])
        elif len(shape) == 3:
            ap = ap.rearrange("p (a b c) -> p a b c", a=shape[0], b=shape[1])
        return T(ap, "sb", off, off + nbytes)

    def ps(self, bank, ncols=512, dtype=F32, col0=0):
        off = bank * 512 + col0
        ap = self.psarena[:, off:off + ncols]
        return T(ap, "ps", off * 4, (off + ncols) * 4)

    def dram_t(self, name, shape, dtype=F32, kind="Internal"):
        h = self.nc.dram_tensor(name, list(shape), dtype, kind=kind)
        self.dram[name] = h
        return h

    def dview(self, name, ap, lo=0, hi=1 << 40):
        return T(ap, "d:" + name, lo, hi)

    def _deps(self, eng, reads, writes, token):
        deps = set()
        for t in reads:
            m = self.maps.setdefault(t.key, IntervalMap())
            for i in m.segs(t.lo, t.hi):
                w, r = m.state[i]
                if w is not None:
                    deps.add(w)
        for t in writes:
            m = self.maps.setdefault(t.key, IntervalMap())
            for i in m.segs(t.lo, t.hi):
                w, r = m.state[i]
                if w is not None:
                    deps.add(w)
                for tok in r.values():
                    deps.add(tok)
        for t in reads:
            m = self.maps[t.key]
            for i in m.segs(t.lo, t.hi):
                m.state[i][1][token[0] if eng.startswith("dma") else eng] = token
        for t in writes:
            m = self.maps[t.key]
            for i in m.segs(t.lo, t.hi):
                m.state[i] = (token, {})
        return deps

    def _waits(self, eng, deps):
        obs = self.observed[eng]
        best = {}
        for (sem, val, src) in deps:
            if src == "pe" and eng == "pe":
                continue
            if obs.get(sem, 0) >= val:
                continue
            if best.get(sem, 0) < val:
                best[sem] = val
        for sem, val in best.items():
            obs[sem] = val
        return list(best.items())

    limit = None
    count = 0

    def _lim(self):
        if self.limit is not None:
            if self.count >= self.limit:
                return True
            self.count += 1
        return False

    capture = None

    def begin_capture(self):
        self.capture = []
        return self.capture

    def end_capture(self):
        c = self.capture
        self.capture = None
        return c

    def replay(self, lists):
        its = [iter(l) for l in lists]
        alive = list(its)
        while alive:
            nxt = []
            for it in alive:
                try:
                    item = next(it)
                except StopIteration:
                    continue
                if item[0] == "op":
                    self.op(*item[1:])
                else:
                    self.dma(*item[1:3], q=item[3], **item[4])
                nxt.append(it)
            alive = nxt

    def op(self, eng, fn, reads=(), writes=()):
        if self.capture is not None:
            self.capture.append(("op", eng, fn, list(reads), list(writes)))
            return
        if self._lim():
            return
        n = self.cnt[eng]
        gen, idx = divmod(n, SEM_GEN)
        sem = f"{eng}{gen}"
        self.sem_names.add(sem)
        token = (sem, idx + 1, eng)
        self.cnt[eng] = n + 1
        rd2, wr2 = [], []
        for t in reads:
            if t.key == "ps":
                wr2.append(T(t.ap, "ps", t.lo // 2048 * 2048, (t.hi + 2047) // 2048 * 2048))
            else:
                rd2.append(t)
        for t in writes:
            if t.key == "ps":
                wr2.append(T(t.ap, "ps", t.lo // 2048 * 2048, (t.hi + 2047) // 2048 * 2048))
            else:
                wr2.append(t)
        reads, writes = rd2, wr2
        deps = self._deps(eng, reads, writes, token)
        waits = self._waits(eng, deps)
        self.ops[eng].append((waits, fn, (sem, 1)))

    def dma(self, out, in_, q="sp", **kw):
        if self.capture is not None:
            self.capture.append(("dma", out, in_, q, kw))
            return
        if self._lim():
            return
        k = self.dma_cnt[q]
        self.dma_cnt[q] = k + 1
        sem = f"dma_{q}{k % N_DMA_SEMS}"
        self.sem_names.add(sem)
        uses = self.dma_sem_uses.get(sem, 0)
        self.dma_sem_uses[sem] = uses + 1
        token = (sem, 16 * (uses + 1), "dma")
        deps = self._deps("dma_" + q, [in_], [out], token)
        if uses > 0:
            deps.add((sem, 16 * uses, "dma"))
        waits = self._waits(q, deps)
        oap, iap = out.ap, in_.ap

        def fn(e, oap=oap, iap=iap, kw=kw):
            return e.dma_start(out=oap, in_=iap, allow_slow_non_contiguous=True, **kw)
        self.ops[q].append((waits, fn, (sem, 16)))
        return token

    def wait_all_dma(self, eng="sp"):
        deps = set()
        for sem, uses in self.dma_sem_uses.items():
            deps.add((sem, 16 * uses, "dma"))
        waits = self._waits(eng, deps)
        self.ops[eng].append((waits, None, None))

    def emit(self, es):
        nc = self.nc
        sems = {}
        for name in sorted(self.sem_names):
            sems[name] = es.enter_context(nc.semaphore(name))
        block = es.enter_context(nc.Block())
        ops = self.ops

        def run(e, lst):
            for waits, fn, inc in lst:
                for sem, val in waits:
                    e.wait_ge(sems[sem], val)
                if fn is not None:
                    ins = fn(e)
                    ins.then_inc(sems[inc[0]], inc[1])

        @block.sync
        def _(e):
            run(e, ops["sp"])

        @block.tensor
        def _(e):
            run(e, ops["pe"])

        @block.vector
        def _(e):
            run(e, ops["dve"])

        @block.scalar
        def _(e):
            run(e, ops["act"])

        @block.gpsimd
        def _(e):
            run(e, ops["pool"])


def _ap(x):
    return x.ap if isinstance(x, T) else x


def _ts(*xs):
    return [x for x in xs if isinstance(x, T)]


class Ops:
    def __init__(self, p):
        self.p = p

    def mm(self, out, lhsT, rhs, start=True, stop=True):
        self.p.op("pe", lambda e: e.matmul(out.ap, lhsT.ap, rhs.ap, start=start, stop=stop), [lhsT, rhs], [out])

    def tr(self, out, in_, ident):
        self.p.op("pe", lambda e: e.transpose(out.ap, in_.ap, ident.ap), [in_, ident], [out])

    def act(self, out, in_, func, bias=None, scale=None, accum=None, eng="act"):
        kw = {}
        if bias is not None:
            kw["bias"] = _ap(bias)
        if scale is not None:
            kw["scale"] = _ap(scale)
        if accum is not None:
            kw["accum_out"] = accum.ap
        self.p.op("act", lambda e: e.activation(out.ap, in_.ap, func, **kw),
                  _ts(in_, bias, scale), _ts(out, accum))

    def tt(self, out, a, b, op, eng="dve"):
        self.p.op(eng, lambda e: e.tensor_tensor(out.ap, a.ap, b.ap, op), [a, b], [out])

    def ts(self, out, a, s1, s2, op0, op1=None, accum=None, eng="dve"):
        kw = {}
        if accum is not None:
            kw["accum_out"] = accum.ap
        if op1 is None:
            fn = lambda e: e.tensor_scalar(out.ap, a.ap, _ap(s1), None, op0, **kw)
        else:
            fn = lambda e: e.tensor_scalar(out.ap, a.ap, _ap(s1), _ap(s2), op0, op1, **kw)
        self.p.op(eng, fn, _ts(a, s1, s2), _ts(out, accum))

    def stt(self, out, a, s, b, op0, op1, eng="dve"):
        self.p.op(eng, lambda e: e.scalar_tensor_tensor(out.ap, a.ap, _ap(s), b.ap, op0, op1), _ts(a, s, b), [out])

    def cp(self, out, in_, eng="dve"):
        if eng == "act":
            self.p.op("act", lambda e: e.copy(out.ap, in_.ap), [in_], [out])
        else:
            self.p.op(eng, lambda e: e.tensor_copy(out.ap, in_.ap), [in_], [out])

    def memset(self, out, val, eng="pool"):
        self.p.op(eng, lambda e: e.memset(out.ap, val), [], [out])

    def recip(self, out, in_):
        self.p.op("dve", lambda e: e.reciprocal(out.ap, in_.ap), [in_], [out])

from contextlib import ExitStack
import math

D = 1024
TT = 4352
NT = 34
NCTX = 2
ALPHA = 8 ** 0.25
EPS = 1e-5
NEG = -30000.0
EVEN_IN = 4112
ODD_IN = 2576

CONST_NAMES = ["IDENT", "ONES", "CSf", "CSb", "MTf", "MTb", "MSf", "MSb", "CSs", "MK", "IO0", "IO1", "IO2", "IO3", "IC"] + ["OH%d" % i for i in range(16)]


def make_consts():
    p = np.arange(128)[:, None]
    f = np.arange(128)[None, :]
    c = {}
    c["IDENT"] = (p == f)
    c["ONES"] = np.ones((128, 128))
    c["CSf"] = (p <= f)
    c["CSb"] = (p >= f)
    c["MTf"] = np.where(p <= f, 0.0, NEG)
    c["MTb"] = np.where(p >= f, 0.0, NEG)
    c["MSf"] = np.where(f < p, 0.0, NEG)
    c["MSb"] = np.where(f > p, 0.0, NEG)
    c["CSs"] = (p < f)
    mk = np.zeros((128, 128))
    mk[:64, 0] = 1; mk[64:, 1] = 1; mk[:16, 2] = 1; mk[16:32, 3] = 1
    for s4 in range(4):
        mk[s4 * 32:(s4 + 1) * 32, 4 + s4] = 1
    c["MK"] = mk
    for i in range(4):
        c["IO%d" % i] = np.broadcast_to(f + 128 * i, (128, 128))
    ic = np.zeros((128, 128))
    for i in range(4):
        ic[:, i] = np.arange(128) + 128 * i
    c["IC"] = ic
    for i in range(16):
        oh = np.zeros((128, 128)); oh[i, :] = 1
        c["OH%d" % i] = oh
    arr = np.concatenate([c[k].astype(np.float32) for k in CONST_NAMES], axis=1)
    return np.ascontiguousarray(arr)


def ret_log_decay(d):
    expo = 5.0 + 2.0 * np.arange(4, dtype=np.float32) + d
    return np.log1p(-np.exp2(-expo)).astype(np.float32)


class K:
    pass


def build(n_layers=4, dbg=(), stage=99, layers=None, dbg_layer=0):
    nc = bass.Bass("TRN2", target_bir_lowering=False)
    k = K()
    k.nc = nc
    k.dbg_on = set(dbg)
    k.stage = stage
    k.dbg_layer = dbg_layer
    k.cur = -1
    SHAPES = dict(h0=[TT, D], cvec=[2, D], cst=[128, 128 * len(CONST_NAMES)], retg=[1, 8],
                  w_mod=[4, D, 6 * D], b_mod=[4, 6 * D], ln1_g=[4, D], ln1_b=[4, D], ln2_g=[4, D], ln2_b=[4, D],
                  w_router=[4, D, 16], w_gate=[4, 16, D, 2048], w_up=[4, 16, D, 2048], w_down=[4, 16, 2048, D],
                  ev_w_in=[2, D, EVEN_IN], ev_w_out=[2, D, D], ev_conv=[2, 3, 3, 1536], ev_a_log=[2, 2, 4],
                  ev_dt_bias=[2, 2, 4], ev_gdn_norm=[2, 128], ev_ret_norm=[2, 512],
                  od_w_in=[2, D, ODD_IN], od_w_out=[2, D, D], od_conv=[2, 3, 3, 1024], od_gate_bias=[2, 2, 2, 4],
                  od_mlstm_norm=[2, 512], od_lam_re=[2, 2, 32, 64], od_lam_im=[2, 2, 32, 64], od_log_dt=[2, 2, 32],
                  od_b_re=[2, 32, 64, 16], od_b_im=[2, 32, 64, 16], od_c_re=[2, 32, 16, 64], od_c_im=[2, 32, 16, 64],
                  od_d_skip=[2, 512], od_w_glu=[2, 512, 512], od_b_glu=[2, 512])

    class LazyIn(dict):
        def __missing__(self, name):
            self[name] = nc.dram_tensor(name, list(SHAPES[name]), F32, kind="ExternalInput")
            return self[name]
    IN = LazyIn()
    k.IN = IN
    with ExitStack() as es:
        p = Prog(nc)
        p.setup_mem(es)
        o = Ops(p)
        k.p, k.o = p, o
        k.H = p.dram_t("H", [TT, D])
        k.MOD = p.dram_t("MOD", [4, 2, 6 * D])
        k.FM = p.dram_t("FM", [24, 128, TT])
        k.ZS = p.dram_t("ZS", [TT, D])
        k.GATES = p.dram_t("GATES", [TT, 16])
        k.OUTM = p.dram_t("OUTM", [8, 2, TT, 128])
        k.OUTO = p.dram_t("OUTO", [4, 2, TT, 132])
        k.YS5 = p.dram_t("YS5", [2, 4, 128, TT])
        k.YSD = [p.dram_t("YSDl", [16, 512, D], BF16), p.dram_t("YSDc", [16, 128, D], BF16)]
        k.YOUT = nc.dram_tensor("y_out", [TT, D], F32, kind="ExternalOutput")
        cst = p.sb([128 * len(CONST_NAMES)])
        p.dma(cst, p.dview("cst", IN["cst"].ap()))
        k.C = {n: cst[:, i * 128:(i + 1) * 128] for i, n in enumerate(CONST_NAMES)}
        i_io = CONST_NAMES.index('IO0')
        k.IOTA = cst[:, i_io * 128:(i_io + 4) * 128]
        eps_c = p.sb([4])
        o.memset(eps_c[:, 0:1], EPS); o.memset(eps_c[:, 1:2], 1e-6); o.memset(eps_c[:, 2:3], 1.0); o.memset(eps_c[:, 3:4], 0.0)
        k.eps = eps_c

        phase_mod(k)
        p.dma(p.dview("H", k.H.ap()), p.dview("h0", IN["h0"].ap()))
        for l in (layers if layers is not None else range(n_layers)):
            if k.stage <= 0:
                break
            k.cur = l
            mixer_layer(k, l)
        p.limit = None
        if not (k.dbg_on - {'none'}) and k.stage >= 5:
            p.dma(p.dview('y_out', k.YOUT.ap()), p.dview('H', k.H.ap()))
        if k.stage < 5:
            p.dma(p.dview('y_out', k.YOUT.ap()[0:8, :]), p.dview('MOD', k.MOD.ap().rearrange('l r (a f) -> (l r a) f', f=1024)[0:8, :]))
        p.wait_all_dma("sp")
        p.wait_all_dma("pool")
        p.emit(es)
    nc.used_inputs = list(IN.keys())
    return nc


def rowbc(k, dst, src_ap, name):
    k.p.dma(dst, k.p.dview(name, src_ap.partition_broadcast(128)))


def phase_mod(k):
    p, o, IN = k.p, k.o, k.IN
    p.push()
    cT = p.sb([2, 8]); sT = p.sb([2, 8])
    for r in range(2):
        p.dma(cT[:, r, :], p.dview("cvec", IN["cvec"].ap()[r].rearrange("(kt q) -> q kt", q=128)))
    o.act(sT, cT, AF.Silu)
    wt = [p.sb([8, 512]) for _ in range(2)]
    bm = p.sb([512]); res = [p.sb([512]) for _ in range(2)]
    i = 0
    for l in range(4):
        for cc in range(12):
            w = wt[i % 2]; r = res[i % 2]
            p.dma(w, p.dview("w_mod", IN["w_mod"].ap()[l, :, cc * 512:(cc + 1) * 512].rearrange("(kt q) f -> q kt f", q=128)))
            p.dma(bm[0:2, :], p.dview("b_mod", IN["b_mod"].ap()[l:l + 1, cc * 512:(cc + 1) * 512].partition_broadcast(2)))
            ps = p.ps(i % 2)
            for kt in range(8):
                o.mm(ps[0:2, :], sT[:, :, kt], w[:, kt, :], start=(kt == 0), stop=(kt == 7))
            o.tt(r[0:2, :], ps[0:2, :], bm[0:2, :], ALU.add)
            p.dma(p.dview("MOD", k.MOD.ap()[l, :, cc * 512:(cc + 1) * 512]), r[0:2, :], q="pool")
            i += 1
    p.pop()


def load_modT(k, l, off, plus1):
    p, o = k.p, k.o
    t = p.sb([2, 8])
    for r in range(2):
        p.dma(t[:, r, :], p.dview("MOD", k.MOD.ap()[l, r, off:off + D].rearrange("(kt q) -> q kt", q=128)))
    if plus1:
        o.ts(t, t, 1.0, None, ALU.add)
    return t


def build_inT(k, l, scT, shT, inT, extra=None):
    p, o = k.p, k.o
    p.push()
    ht = [p.sb([D]) for _ in range(2)]
    for t in range(NT):
        h = ht[t % 2]
        p.dma(h, p.dview("H", k.H.ap()[t * 128:(t + 1) * 128, :], t * 128 * D, (t + 1) * 128 * D))
        r = 1 if t < NCTX else 0
        for half in range(2):
            ps = p.ps(half)
            for q in range(4):
                kt = half * 4 + q
                o.tr(ps[:, q * 128:(q + 1) * 128], h[:, kt * 128:(kt + 1) * 128], k.C["IDENT"])
            for q in range(4):
                kt = half * 4 + q
                o.act(inT[:, kt, t * 128:(t + 1) * 128], ps[:, q * 128:(q + 1) * 128], AF.Identity,
                      bias=shT[:, r, kt:kt + 1], scale=scT[:, r, kt:kt + 1])
                if extra is not None:
                    extra(t, kt, ps[:, q * 128:(q + 1) * 128], r)
    p.pop()


TOKCH = [(i * 512, 512) for i in range(8)] + [(4096, 256)]


def dbg_here(k, name):
    return name in k.dbg_on and k.cur == k.dbg_layer


def mixer_layer(k, l):
    p, o, IN = k.p, k.o, k.IN
    odd = (l % 2 == 1)
    e = l // 2
    C = k.C
    wname = "od_w_in" if odd else "ev_w_in"
    cname = "od_conv" if odd else "ev_conv"
    p.push()
    scT = load_modT(k, l, 1024, True)
    shT = load_modT(k, l, 0, False)
    inT = p.sb([8, TT], BF16)
    build_inT(k, l, scT, shT, inT)
    if k.stage <= 1:
        p.pop(); return
    if not odd:
        specs = [(0 + 128 * h, h, "l2q") for h in range(4)] + [(512 + 128 * h, 4 + h, "l2k") for h in range(4)] + \
                [(1024 + 128 * h, 8 + h, None) for h in range(4)] + [(2064 + 128 * h, None, None) for h in range(4)] + \
                [(2576 + 128 * h, None, "scale") for h in range(4)] + [(3088 + 128 * h, None, None) for h in range(4)]
        nconv = 12
    else:
        specs = [(0 + 128 * h, h, None) for h in range(4)] + [(512 + 128 * h, 4 + h, "scale") for h in range(4)] + \
                [(1024 + 128 * h, None, None) for h in range(4)] + [(2064 + 128 * h, None, None) for h in range(4)]
        nconv = 8
    p.push()
    p.begin_capture()
    wconv = p.sb([nconv, 9])
    for t12 in range(nconv):
        p.dma(wconv[:, t12, :], p.dview(cname, IN[cname].ap()[e].rearrange("a b c -> c (a b)")[t12 * 128:(t12 + 1) * 128, :]))
    wf = [p.sb([8, 128]) for _ in range(2)]
    wb = [p.sb([8, 128], BF16) for _ in range(2)]
    ft = [p.sb([TT]) for _ in range(2)]
    yt = p.sb([TT])
    sq = p.sb([512]); rs = p.sb([512])
    for s, (col, cv, mode) in enumerate(specs):
        w32, w16, X = wf[s % 2], wb[s % 2], ft[s % 2]
        p.dma(w32, p.dview(wname, IN[wname].ap()[e, :, col:col + 128].rearrange("(kt q) f -> q kt f", q=128)))
        o.cp(w16, w32, eng="pool")
        for ci, (t0, n) in enumerate(TOKCH):
            ps = p.ps(2 + ci % 2, n)
            for kt in range(8):
                o.mm(ps, w16[:, kt, :], inT[:, kt, t0:t0 + n], start=(kt == 0), stop=(kt == 7))
            o.cp(X[:, t0:t0 + n], ps, eng=("act" if ci % 2 else "dve"))
        if cv is not None:
            wc = wconv[:, cv, :]
            Y = yt
            o.ts(Y[:, 0:256], X[:, 0:256], wc[:, 4:5], None, ALU.mult)
            o.stt(Y[:, 1:256], X[:, 0:255], wc[:, 3:4], Y[:, 1:256], ALU.mult, ALU.add)
            o.stt(Y[:, 0:255], X[:, 1:256], wc[:, 5:6], Y[:, 0:255], ALU.mult, ALU.add)
            Xg = X.v(X.ap[:, 256:TT].rearrange("q (r c) -> q r c", c=64))
            Yg = Y.v(Y.ap[:, 256:TT].rearrange("q (r c) -> q r c", c=64))
            o.ts(Y[:, 256:TT], X[:, 256:TT], wc[:, 4:5], None, ALU.mult)
            for dy in range(3):
                for dx in range(3):
                    if dy == 1 and dx == 1:
                        continue
                    oy, ox = dy - 1, dx - 1
                    r0, r1 = max(0, -oy), 64 - max(0, oy)
                    c0, c1 = max(0, -ox), 64 - max(0, ox)
                    o.stt(Yg[:, r0:r1, c0:c1], Xg[:, r0 + oy:r1 + oy, c0 + ox:c1 + ox], wc[:, dy * 3 + dx:dy * 3 + dx + 1],
                          Yg[:, r0:r1, c0:c1], ALU.mult, ALU.add)
            o.act(X, Y, AF.Silu)
        if mode in ("l2q", "l2k"):
            for ci, (t0, n) in enumerate(TOKCH):
                o.tt(sq[:, 0:n], X[:, t0:t0 + n], X[:, t0:t0 + n], ALU.mult)
                ps = p.ps(4 + ci % 2, n)
                o.mm(ps, C["ONES"], sq[:, 0:n])
                o.act(rs[:, 0:n], ps, AF.Sqrt, bias=k.eps[:, 1:2])
                o.recip(sq[:, 0:n], rs[:, 0:n])
                if mode == "l2q":
                    o.stt(X[:, t0:t0 + n], X[:, t0:t0 + n], 128 ** -0.5, sq[:, 0:n], ALU.mult, ALU.mult)
                else:
                    o.tt(X[:, t0:t0 + n], X[:, t0:t0 + n], sq[:, 0:n], ALU.mult)
        elif mode == "scale":
            o.ts(X, X, 128 ** -0.5, None, ALU.mult)
        p.dma(p.dview("FM", k.FM.ap()[s], s * 128 * TT, (s + 1) * 128 * TT), X, q="pool")
    cap_b = p.end_capture()
    p.push()
    p.begin_capture()
    wz = p.sb([8, 1040], BF16)
    wst = [p.sb([8, 260]) for _ in range(2)]
    zcols = [(1536, 512), (2048, 16), (3600, 512)] if not odd else [(1536, 512), (2048, 16), (2064, 512)]
    dst = 0
    i = 0
    for (c0, n) in zcols:
        for j in range(0, n, 260):
            m = min(260, n - j)
            w32 = wst[i % 2]
            p.dma(w32[:, :, 0:m], p.dview(wname, IN[wname].ap()[e, :, c0 + j:c0 + j + m].rearrange("(kt q) f -> q kt f", q=128)))
            o.cp(wz[:, :, dst:dst + m], w32[:, :, 0:m], eng="pool")
            dst += m
            i += 1
    if not odd:
        alog = p.sb([8]); dtb = p.sb([8]); nea = p.sb([8])
        rowbc(k, alog, IN["ev_a_log"].ap()[e:e + 1].rearrange("a d h -> a (d h)"), "ev_a_log")
        rowbc(k, dtb, IN["ev_dt_bias"].ap()[e:e + 1].rearrange("a d h -> a (d h)"), "ev_dt_bias")
        o.act(nea, alog, AF.Exp)
    else:
        gbi = p.sb([8]); gbf = p.sb([8])
        for d in range(2):
            rowbc(k, gbi[:, d * 4:(d + 1) * 4], IN["od_gate_bias"].ap()[e, d, 0:1, :], "od_gate_bias")
            rowbc(k, gbf[:, d * 4:(d + 1) * 4], IN["od_gate_bias"].ap()[e, d, 1:2, :], "od_gate_bias")
    zt = [p.sb([D]) for _ in range(2)]
    gt = [p.sb([16]) for _ in range(2)]
    tmp8 = p.sb([8]); tmp8b = p.sb([8])
    for t in range(NT):
        z, g = zt[t % 2], gt[t % 2]
        lt = lambda kt: inT[:, kt, t * 128:(t + 1) * 128]
        psa, psb, psg = p.ps(0), p.ps(1), p.ps(6, 16)
        for kt in range(8):
            o.mm(psa, lt(kt), wz[:, kt, 0:512], start=(kt == 0), stop=(kt == 7))
        for kt in range(8):
            o.mm(psb, lt(kt), wz[:, kt, 528:1040], start=(kt == 0), stop=(kt == 7))
        for kt in range(8):
            o.mm(psg, lt(kt), wz[:, kt, 512:528], start=(kt == 0), stop=(kt == 7))
        if not odd:
            o.act(z[:, 0:512], psa, AF.Silu)
            o.act(z[:, 512:1024], psb, AF.Silu)
            o.tt(tmp8, psg[:, 0:8], dtb, ALU.add)
            o.act(g[:, 0:8], tmp8, AF.Exp)
            o.act(tmp8, g[:, 0:8], AF.Ln, bias=k.eps[:, 2:3])
            o.stt(g[:, 0:8], tmp8, -1.0, nea, ALU.mult, ALU.mult)
            o.act(g[:, 8:16], psg[:, 8:16], AF.Sigmoid)
        else:
            o.act(z[:, 0:512], psa, AF.Sigmoid)
            o.cp(z[:, 512:1024], psb, eng="dve")
            o.tt(tmp8, psg[:, 8:16], gbf, ALU.add)
            o.act(tmp8b, tmp8, AF.Exp, scale=-1.0)
            o.act(tmp8, tmp8b, AF.Ln, bias=k.eps[:, 2:3])
            o.ts(g[:, 0:8], tmp8, -1.0, None, ALU.mult)
            o.tt(tmp8b, psg[:, 0:8], gbi, ALU.add)
            o.act(g[:, 8:16], tmp8b, AF.Exp)
        p.dma(p.dview("ZS", k.ZS.ap()[t * 128:(t + 1) * 128, :], t * 128 * D, (t + 1) * 128 * D), z, q="pool")
        p.dma(p.dview("GATES", k.GATES.ap()[t * 128:(t + 1) * 128, :], t * 128 * 16, (t + 1) * 128 * 16), g, q="pool")
    cap_c = p.end_capture()
    p.replay([cap_b, cap_c])
    p.pop()
    p.pop()
    p.pop()
    if k.stage <= 3:
        return
    if not odd:
        scan_layer(k, l, [0, 1])
    else:
        scan_layer(k, l, [2])
        if k.stage > 4:
            s5_scan(k, l)
    if k.stage <= 4:
        return
    if not odd:
        merge_even(k, l)
    else:
        merge_odd(k, l)
    if k.stage <= 5:
        return
    moe_layer2(k, l)


def chain_order(d):
    return list(range(NT)) if d == 0 else [1, 0] + list(range(NT - 1, 1, -1))


def scan_layer(k, l, types):
    p, o, IN, C = k.p, k.o, k.IN, k.C
    import os
    if 'OPLIMIT' in os.environ:
        p.limit = int(os.environ['OPLIMIT']); p.count = 0
    p.push()
    gates = p.sb([NT, 16])
    p.dma(gates, p.dview("GATES", k.GATES.ap().rearrange("(c q) g -> q c g", q=128)))
    retg = p.sb([8])
    negb = p.sb([NT, 8])
    if 1 in types:
        rowbc(k, retg, IN["retg"].ap(), "retg")
    if 0 in types:
        o.ts(negb, gates[:, :, 8:16], -1.0, None, ALU.mult)
    RING = 8
    W = lambda n=128: [p.sb([n]) for _ in range(RING)]
    names = ["qT", "kT", "vT", "G1", "cc", "tmp", "ET", "EXPR", "kgT", "qgT", "kend", "bv", "E", "N", "M", "N2", "M2",
             "TTa", "TTb", "AQ", "br", "vn", "o", "o2", "tmp2", "sc"]
    ring = {n: (W(132) if n in ("vn", "o") else W()) for n in names}
    chains = []
    for typ in types:
        for h in range(4):
            for d in range(2):
                chains.append(dict(typ=typ, h=h, d=d, S=[p.sb([132]), p.sb([132])], order=chain_order(d), dec=None))
    for ch in chains:
        o.memset(ch["S"][0], 0.0, eng="dve")
    step_i = [0]

    def decay(ch, gcol, slot):
        d = ch["d"]
        R = {n: ring[n][slot] for n in names}
        CS = C["CSf"] if d == 0 else C["CSb"]
        MT = C["MTf"] if d == 0 else C["MTb"]
        o.ts(R["G1"], C["ONES"], gcol, None, ALU.mult)
        pb = ch["pb"]
        psr = p.ps(pb + 0, 128)
        psc = p.ps(pb + 0, 256, col0=128)
        o.mm(psr, R["G1"], CS)
        o.mm(psc[:, 0:128], CS, R["G1"])
        o.mm(psc[:, 128:256], C["ONES"], R["G1"])
        cc = R["cc"]
        o.cp(cc[:, 0:1], psc[:, 0:1], eng="act")
        o.cp(cc[:, 1:2], psc[:, 128:129], eng="act")
        o.stt(R["tmp"], psr, cc[:, 0:1], MT, ALU.subtract, ALU.add)
        o.act(R["ET"], R["tmp"], AF.Exp)
        o.act(R["EXPR"], psr, AF.Exp)
        o.tt(cc[:, 4:5], cc[:, 1:2], cc[:, 0:1], ALU.subtract)
        o.act(cc[:, 2:3], cc[:, 4:5], AF.Exp)
        o.act(cc[:, 3:4], cc[:, 1:2], AF.Exp)
        return dict(ET=R["ET"], EXPR=R["EXPR"], cc=cc, psr=psr)

    def step(ch, c, si):
        typ, h, d = ch["typ"], ch["h"], ch["d"]
        slot = step_i[0] % RING
        step_i[0] += 1
        R = {n: ring[n][slot] for n in names}
        sq, sk, sv = (h, 4 + h, 8 + h) if typ != 1 else (12 + h, 16 + h, 20 + h)
        dv = 129 if typ == 2 else 128
        tok = slice(c * 128, (c + 1) * 128)
        for nm, s in (("qT", sq), ("kT", sk), ("vT", sv)):
            p.dma(R[nm], p.dview("FM", k.FM.ap()[s][:, tok], s * 128 * TT, (s + 1) * 128 * TT))
        qT, kT, vT = R["qT"], R["kT"], R["vT"]
        if typ != 1:
            gcol = gates[:, c, d * 4 + h:d * 4 + h + 1]
            dec = decay(ch, gcol, slot)
        else:
            if ch["dec"] is None:
                gcol = retg[:, d * 4 + h:d * 4 + h + 1]
                dslot = 0
                own = {n: p.sb([128]) for n in ["G1", "cc", "tmp", "ET", "EXPR"]}
                save = {n: ring[n][dslot] for n in own}
                for n in own:
                    ring[n][dslot] = own[n]
                ch["dec"] = decay(ch, gcol, dslot)
                for n in own:
                    ring[n][dslot] = save[n]
            dec = ch["dec"]
        cc = dec["cc"]
        pb = ch["pb"]
        pst = p.ps(pb + 1, 256)
        o.tr(pst[:, 0:128], kT, C["IDENT"])
        o.tr(pst[:, 128:256], vT, C["IDENT"])
        if typ == 2:
            ei = gates[:, c, 8 + d * 4 + h:8 + d * 4 + h + 1]
            o.ts(R["kend"], pst[:, 0:128], cc[:, 2:3], ei, ALU.mult, ALU.mult)
        else:
            o.ts(R["kend"], pst[:, 0:128], cc[:, 2:3], None, ALU.mult)
        o.tt(R["qgT"], qT, dec["EXPR"], ALU.mult)
        psq = p.ps(pb + 0, 128, col0=384)
        o.mm(psq, kT, qT)
        if typ == 2:
            o.stt(R["AQ"], psq, ei, dec["ET"], ALU.mult, ALU.mult)
        else:
            o.tt(R["AQ"], psq, dec["ET"], ALU.mult)
        S = ch["S"][si % 2][:, 0:dv]
        Sn = ch["S"][(si + 1) % 2][:, 0:dv]
        if typ == 0:
            MS = C["MSf"] if d == 0 else C["MSb"]
            nb = negb[:, c, d * 4 + h:d * 4 + h + 1]
            o.ts(R["bv"], pst[:, 128:256], gates[:, c, 8 + d * 4 + h:8 + d * 4 + h + 1], None, ALU.mult)
            o.tt(R["kgT"], kT, dec["EXPR"], ALU.mult)
            o.stt(R["tmp2"], dec["psr"], cc[:, 0:1], MS, ALU.subtract, ALU.subtract)
            o.act(R["E"], R["tmp2"], AF.Exp, scale=-1.0)
            psk = p.ps(pb + 1, 128, col0=256)
            o.mm(psk, kT, kT)
            o.stt(R["N"], psk, nb, R["E"], ALU.mult, ALU.mult)
            psm = p.ps(pb + 1, 128, col0=384)
            o.tr(psm, R["N"], C["IDENT"])
            o.cp(R["M"], psm, eng="act")
            o.tt(R["TTa"], C["IDENT"], R["M"], ALU.add)
            Nc, Mc, Nn, Mn = R["N"], R["M"], R["N2"], R["M2"]
            Tc, Tn = R["TTa"], R["TTb"]
            for lev in range(1, 7):
                ps1 = p.ps(pb + 1, 128, col0=256)
                o.mm(ps1, Mc, Nc)
                o.cp(Nn, ps1, eng="act")
                if lev < 6:
                    ps2 = p.ps(pb + 1, 128, col0=384)
                    o.mm(ps2, Nc, Mc)
                    o.cp(Mn, ps2, eng="dve")
                ps3 = p.ps(pb + 1, 128, col0=256)
                o.mm(ps3, Nn, Tc)
                o.tt(Tn, ps3, Tc, ALU.add)
                Nc, Nn = Nn, Nc
                Mc, Mn = Mn, Mc
                Tc, Tn = Tn, Tc
            psr2 = p.ps(pb + 1, 128, col0=256)
            o.mm(psr2, R["kgT"], S)
            o.stt(R["br"], psr2, nb, R["bv"], ALU.mult, ALU.add)
            psv = p.ps(pb + 1, 128, col0=256)
            o.mm(psv, Tc, R["br"])
            o.cp(R["vn"][:, 0:128], psv, eng="act")
        else:
            o.cp(R["vn"][:, 0:128], pst[:, 128:256], eng="act")
            if typ == 2:
                o.memset(R["vn"][:, 128:129], 1.0, eng="dve")
        vn = R["vn"][:, 0:dv]
        pso = p.ps(pb + 0, dv)
        o.mm(pso, R["qgT"], S, start=True, stop=False)
        o.mm(pso, R["AQ"], vn, start=False, stop=True)
        o.cp(R["o"][:, 0:dv], pso, eng="act")
        if typ == 2:
            p.dma(p.dview("OUTO%d_%d" % (h, d), k.OUTO.ap()[h, d, tok, 0:dv], c * 128 * 132, (c + 1) * 128 * 132), R["o"][:, 0:dv], q="pool")
        else:
            slot8 = typ * 4 + h
            p.dma(p.dview("OUTM%d_%d" % (slot8, d), k.OUTM.ap()[slot8, d, tok, :], c * 128 * 128, (c + 1) * 128 * 128), R["o"][:, 0:128], q="pool")
        pss = p.ps(pb + 1, dv)
        o.mm(pss, R["kend"], vn)
        o.stt(Sn, S, cc[:, 3:4], pss, ALU.mult, ALU.add)

    for si in range(NT):
        for c0 in range(0, len(chains), 4):
            caps = []
            for j, ch in enumerate(chains[c0:c0 + 4]):
                ch["pb"] = 2 * j
                p.begin_capture()
                step(ch, ch["order"][si], si)
                caps.append(p.end_capture())
            p.replay(caps)
    p.pop()


def merge_even(k, l):
    p, o, IN, C = k.p, k.o, k.IN, k.C
    e = l // 2
    p.push()
    wout = p.sb([8, D], BF16)
    wst = [p.sb([8, 256]) for _ in range(2)]
    for j in range(4):
        p.dma(wst[j % 2], p.dview("ev_w_out", IN["ev_w_out"].ap()[e, :, j * 256:(j + 1) * 256].rearrange("(kt q) f -> q kt f", q=128)))
        o.cp(wout[:, :, j * 256:(j + 1) * 256], wst[j % 2], eng="pool")
    gg = p.sb([128]); rg = p.sb([512])
    rowbc(k, gg, IN["ev_gdn_norm"].ap()[e:e + 1, :], "ev_gdn_norm")
    rowbc(k, rg, IN["ev_ret_norm"].ap()[e:e + 1, :], "ev_ret_norm")
    g1 = [p.sb([D]), p.sb([D])]
    for r in range(2):
        rowbc(k, g1[r], k.MOD.ap()[l, r:r + 1, 2048:3072], "MOD")
    lng = p.sb([D]); lnb = p.sb([D])
    rowbc(k, lng, IN["ln1_g"].ap()[l:l + 1, :], "ln1_g")
    rowbc(k, lnb, IN["ln1_b"].ap()[l:l + 1, :], "ln1_b")
    of = [p.sb([8, 128]) for _ in range(2)]
    ob = [p.sb([8, 128]) for _ in range(2)]
    zt = [p.sb([D]) for _ in range(2)]
    ht = [p.sb([D]) for _ in range(2)]
    Y = [p.sb([D]) for _ in range(2)]
    YT = [p.sb([8, 128], BF16) for _ in range(2)]
    st = [p.sb([32]) for _ in range(2)]
    junk = p.sb([D])
    U = [p.sb([D]) for _ in range(2)]
    for t in range(NT):
        r = 1 if t < NCTX else 0
        a, b, z, h, y, yT, s, u = of[t % 2], ob[t % 2], zt[t % 2], ht[t % 2], Y[t % 2], YT[t % 2], st[t % 2], U[t % 2]
        tok = slice(t * 128, (t + 1) * 128)
        for hs in range(8):
            p.dma(a[:, hs, :], p.dview("OUTM%d_0" % hs, k.OUTM.ap()[hs, 0, tok, :], t * 128 * 128, (t + 1) * 128 * 128))
            p.dma(b[:, hs, :], p.dview("OUTM%d_1" % hs, k.OUTM.ap()[hs, 1, tok, :], t * 128 * 128, (t + 1) * 128 * 128))
        p.dma(z, p.dview("ZS", k.ZS.ap()[tok, :], t * 128 * D, (t + 1) * 128 * D))
        p.dma(h, p.dview("H", k.H.ap()[tok, :], t * 128 * D, (t + 1) * 128 * D))
        o.tt(a, a, b, ALU.add)
        for hs in range(8):
            o.act(junk[:, 0:128], a[:, hs, :], AF.Square, accum=s[:, hs:hs + 1])
        for hs in range(4, 8):
            o.act(junk[:, 0:128], a[:, hs, :], AF.Identity, accum=s[:, 8 + hs:9 + hs])
        o.act(s[:, 16:20], s[:, 0:4], AF.Sqrt, bias=k.eps[:, 0:1], scale=1.0 / 128)
        o.recip(s[:, 20:24], s[:, 16:20])
        o.ts(s[:, 24:28], s[:, 12:16], 1.0 / 128, None, ALU.mult)
        o.tt(s[:, 28:32], s[:, 24:28], s[:, 24:28], ALU.mult)
        o.stt(s[:, 16:20], s[:, 4:8], 1.0 / 128, s[:, 28:32], ALU.mult, ALU.subtract)
        o.act(s[:, 28:32], s[:, 16:20], AF.Sqrt, bias=k.eps[:, 0:1])
        o.recip(s[:, 16:20], s[:, 28:32])
        for hs in range(4):
            o.stt(y[:, hs * 128:(hs + 1) * 128], a[:, hs, :], s[:, 20 + hs:21 + hs], gg, ALU.mult, ALU.mult)
        for hs in range(4):
            o.ts(junk[:, 0:128], a[:, 4 + hs, :], s[:, 24 + hs:25 + hs], s[:, 16 + hs:17 + hs], ALU.subtract, ALU.mult)
            o.tt(y[:, 512 + hs * 128:512 + (hs + 1) * 128], junk[:, 0:128], rg[:, hs * 128:(hs + 1) * 128], ALU.mult)
        o.tt(y, y, z, ALU.mult)
        for half in range(2):
            ps = p.ps(half)
            for q in range(4):
                kt = half * 4 + q
                o.tr(ps[:, q * 128:(q + 1) * 128], y[:, kt * 128:(kt + 1) * 128], C["IDENT"])
            o.cp(yT.v(yT.ap[:, half * 4:(half + 1) * 4, :].rearrange("q a b -> q (a b)")), ps, eng="act")
        for half in range(2):
            ps = p.ps(2 + half)
            for kt in range(8):
                o.mm(ps, yT[:, kt, :], wout[:, kt, half * 512:(half + 1) * 512], start=(kt == 0), stop=(kt == 7))
            hs_ = slice(half * 512, (half + 1) * 512)
            o.tt(u[:, hs_], ps, g1[r][:, hs_], ALU.mult)
        if dbg_here(k, "y"):
            p.dma(p.dview("y_out", k.YOUT.ap()[tok, :], t * 128 * D, (t + 1) * 128 * D), u, q="pool")
        o.stt(u, h, ALPHA, u, ALU.mult, ALU.add)
        layer_norm(k, u, h, lng, lnb, s, junk)
        p.dma(p.dview("H", k.H.ap()[tok, :], t * 128 * D, (t + 1) * 128 * D), h, q="pool")
        if dbg_here(k, "h1"):
            p.dma(p.dview("y_out", k.YOUT.ap()[tok, :], t * 128 * D, (t + 1) * 128 * D), h, q="pool")
    p.pop()


def layer_norm(k, u, out, g, b, s, junk):
    o = k.o
    o.act(junk, u, AF.Identity, accum=s[:, 0:1])
    o.act(junk, u, AF.Square, accum=s[:, 1:2])
    o.ts(s[:, 2:3], s[:, 0:1], 1.0 / D, None, ALU.mult)
    o.tt(s[:, 3:4], s[:, 2:3], s[:, 2:3], ALU.mult)
    o.stt(s[:, 4:5], s[:, 1:2], 1.0 / D, s[:, 3:4], ALU.mult, ALU.subtract)
    o.act(s[:, 5:6], s[:, 4:5], AF.Sqrt, bias=k.eps[:, 0:1])
    o.recip(s[:, 6:7], s[:, 5:6])
    o.ts(junk, u, s[:, 2:3], s[:, 6:7], ALU.subtract, ALU.mult)
    o.tt(junk, junk, g, ALU.mult)
    o.tt(out, junk, b, ALU.add)


def moe_layer(k, l, update_ctx=True):
    p, o, IN, C = k.p, k.o, k.IN, k.C
    p.push()
    g2 = [p.sb([D]), p.sb([D])]
    for r in range(2):
        rowbc(k, g2[r], k.MOD.ap()[l, r:r + 1, 5120:6144], "MOD")
    lng = p.sb([D]); lnb = p.sb([D])
    rowbc(k, lng, IN["ln2_g"].ap()[l:l + 1, :], "ln2_g")
    rowbc(k, lnb, IN["ln2_b"].ap()[l:l + 1, :], "ln2_b")
    wr = p.sb([8, 16])
    p.dma(wr, p.dview("w_router", IN["w_router"].ap()[l].rearrange("(kt q) e -> q kt e", q=128)))
    in2T = p.sb([8, TT], BF16)
    aff = p.sb([NT, 16]); gw = p.sb([NT, 16])
    p.push()
    affT = p.sb([TT])
    sc2 = [p.sb([D]), p.sb([D])]; sh2 = [p.sb([D]), p.sb([D])]
    for r in range(2):
        rowbc(k, sc2[r], k.MOD.ap()[l, r:r + 1, 4096:5120], "MOD")
        o.ts(sc2[r], sc2[r], 1.0, None, ALU.add)
        rowbc(k, sh2[r], k.MOD.ap()[l, r:r + 1, 3072:4096], "MOD")
    p.push()
    ht = [p.sb([D]) for _ in range(2)]
    x2 = [p.sb([D]) for _ in range(2)]
    xTf = [p.sb([8, 128]) for _ in range(2)]
    sm = [p.sb([40]) for _ in range(2)]
    for t in range(NT):
        r = 1 if t < NCTX else 0
        h, x, xf, s = ht[t % 2], x2[t % 2], xTf[t % 2], sm[t % 2]
        tok = slice(t * 128, (t + 1) * 128)
        p.dma(h, p.dview("H", k.H.ap()[tok, :], t * 128 * D, (t + 1) * 128 * D))
        o.tt(x, h, sc2[r], ALU.mult)
        o.tt(x, x, sh2[r], ALU.add)
        for half in range(2):
            ps = p.ps(half)
            for q in range(4):
                kt = half * 4 + q
                o.tr(ps[:, q * 128:(q + 1) * 128], x[:, kt * 128:(kt + 1) * 128], C["IDENT"])
            ps3 = ps.v(ps.ap.rearrange("q (a b) -> q a b", a=4))
            o.cp(xf[:, half * 4:(half + 1) * 4, :], ps3, eng="act")
            o.cp(in2T[:, half * 4:(half + 1) * 4, tok], ps3, eng="dve")
        psl = p.ps(2, 16)
        for kt in range(8):
            o.mm(psl, xf[:, kt, :], wr[:, kt, :], start=(kt == 0), stop=(kt == 7))
        p.op("dve", lambda e, s=s, psl=psl: e.reduce_max(s.ap[:, 0:1], psl.ap, AX.X), [psl], [s])
        o.ts(s[:, 1:2], s[:, 0:1], -1.0, None, ALU.mult)
        o.act(s[:, 8:24], psl, AF.Exp, bias=s[:, 1:2], accum=s[:, 2:3])
        o.recip(s[:, 3:4], s[:, 2:3])
        o.ts(aff[:, t, :], s[:, 8:24], s[:, 3:4], None, ALU.mult)
        pst = p.ps(3, 128)
        o.tr(pst[0:16, :], aff[:, t, :], C["IDENT"])
        o.cp(affT[0:16, tok], pst[0:16, :], eng="act")
    p.pop()
    p.push()
    st = p.sb([16]); junk = p.sb([4096]); thr = [p.sb([16]), p.sb([16])]; dt = p.sb([16])
    sets = [(0, 256, 32.0, 1), (256, TT, 512.0, 0)]
    for (c0, c1, cap, r) in sets:
        lo, hi, mid, cnt, ge, d1 = [st[0:16, i:i + 1] for i in range(6)]
        o.memset(lo, 0.0, eng="dve"); o.memset(hi, 1.0, eng="dve")
        for it in range(32):
            o.tt(mid, lo, hi, ALU.add)
            o.ts(mid, mid, 0.5, None, ALU.mult)
            o.ts(junk[0:16, 0:c1 - c0], affT[0:16, c0:c1], mid, None, ALU.is_ge, ALU.add, accum=cnt)
            o.ts(ge, cnt, cap - 0.5, None, ALU.is_ge)
            o.tt(d1, mid, lo, ALU.subtract)
            o.stt(lo, d1, ge, lo, ALU.mult, ALU.add)
            o.tt(d1, hi, mid, ALU.subtract)
            o.stt(hi, d1, ge, mid, ALU.mult, ALU.add)
        o.ts(dt[0:16, 0:16], C["IDENT"][0:16, 0:16], lo, None, ALU.mult)
        pth = p.ps(2, 16)
        o.mm(pth, C["ONES"][0:16, :], dt[0:16, 0:16])
        o.cp(thr[r], pth, eng="act")
    for t in range(NT):
        r = 1 if t < NCTX else 0
        o.tt(gw[:, t, :], aff[:, t, :], thr[r], ALU.is_ge)
        o.tt(gw[:, t, :], gw[:, t, :], aff[:, t, :], ALU.mult)
    p.pop()
    p.pop()
    p.push()
    wgf = [p.sb([8, 256]) for _ in range(2)]; wuf = [p.sb([8, 256]) for _ in range(2)]
    wgb = [p.sb([8, 256], BF16) for _ in range(2)]; wub = [p.sb([8, 256], BF16) for _ in range(2)]
    wdf = [p.sb([2, 512]) for _ in range(2)]; wdb = [p.sb([2, 512], BF16) for _ in range(2)]
    hT = p.sb([16, 512], BF16)
    sg = [p.sb([512]) for _ in range(2)]
    acc = [p.sb([D]) for _ in range(4)]
    hh = [p.sb([D])] * 2
    s8 = p.sb([8]); junk2 = p.sb([D])
    wi = 0
    di = 0
    import os
    nexp = int(os.environ.get("MOE_NEXP", 16))
    for (t0, n) in TOKCH:
        nst = n // 128
        for e in range(nexp):
            for fg in range(8):
                a32, b32, a16, b16 = wgf[wi % 2], wuf[wi % 2], wgb[wi % 2], wub[wi % 2]
                wi += 1
                p.dma(a32, p.dview("w_gate", IN["w_gate"].ap()[l, e, :, fg * 256:(fg + 1) * 256].rearrange("(kt q) f -> q kt f", q=128)))
                p.dma(b32, p.dview("w_up", IN["w_up"].ap()[l, e, :, fg * 256:(fg + 1) * 256].rearrange("(kt q) f -> q kt f", q=128)))
                o.cp(a16, a32, eng="pool")
                o.cp(b16, b32, eng="pool")
                for f2 in range(2):
                    ft = fg * 2 + f2
                    psg = p.ps(4 + (ft % 2) * 2, n)
                    psu = p.ps(5 + (ft % 2) * 2, n)
                    for kt in range(8):
                        o.mm(psg, a16[:, kt, f2 * 128:(f2 + 1) * 128], in2T[:, kt, t0:t0 + n], start=(kt == 0), stop=(kt == 7))
                    for kt in range(8):
                        o.mm(psu, b16[:, kt, f2 * 128:(f2 + 1) * 128], in2T[:, kt, t0:t0 + n], start=(kt == 0), stop=(kt == 7))
                    s_ = sg[ft % 2]
                    o.act(s_[:, 0:n], psg, AF.Silu)
                    o.tt(hT[:, ft, 0:n], s_[:, 0:n], psu, ALU.mult)
            for half in range(2):
                psd = [p.ps(st_, 512) for st_ in range(nst)]
                for fp_ in range(8):
                    d32, d16 = wdf[di % 2], wdb[di % 2]
                    di += 1
                    p.dma(d32, p.dview("w_down", IN["w_down"].ap()[l, e, fp_ * 256:(fp_ + 1) * 256, half * 512:(half + 1) * 512].rearrange("(a q) d -> q a d", q=128)))
                    o.cp(d16, d32, eng="pool")
                    for f2 in range(2):
                        ft = fp_ * 2 + f2
                        for st_ in range(nst):
                            o.mm(psd[st_], hT[:, ft, st_ * 128:(st_ + 1) * 128], d16[:, f2, :], start=(ft == 0), stop=(ft == 15))
                for st_ in range(nst):
                    t = t0 // 128 + st_
                    a_ = acc[st_][:, half * 512:(half + 1) * 512]
                    if e == 0:
                        o.ts(a_, psd[st_], gw[:, t, e:e + 1], None, ALU.mult)
                    else:
                        o.stt(a_, psd[st_], gw[:, t, e:e + 1], a_, ALU.mult, ALU.add)
        for st_ in range(nst):
            t = t0 // 128 + st_
            r = 1 if t < NCTX else 0
            tok = slice(t * 128, (t + 1) * 128)
            if dbg_here(k, "f"):
                p.dma(p.dview("y_out", k.YOUT.ap()[tok, :], t * 128 * D, (t + 1) * 128 * D), acc[st_], q="pool")
            h = hh[st_ % 2]
            p.dma(h, p.dview("H", k.H.ap()[tok, :], t * 128 * D, (t + 1) * 128 * D))
            u = acc[st_]
            o.tt(u, u, g2[r], ALU.mult)
            o.stt(u, h, ALPHA, u, ALU.mult, ALU.add)
            layer_norm(k, u, h, lng, lnb, s8, junk2)
            if update_ctx or r == 0:
                p.dma(p.dview("H", k.H.ap()[tok, :], t * 128 * D, (t + 1) * 128 * D), h, q="pool")
    p.pop()
    p.pop()


def rev(t, n):
    a = t.ap
    st = a.ap[-1][0]
    return t.v(bass.AP(a.tensor, a.offset + (n - 1) * st, [list(a.ap[0]), [-st, n]]))


def bcast_cols(t, n):
    a = t.ap
    return t.v(bass.AP(a.tensor, a.offset, [list(a.ap[0]), [0, n]]))


SIN_C = [-1.0 / 6, 1.0 / 120, -1.0 / 5040, 1.0 / 362880, -1.0 / 39916800]
COS_C = [-0.5, 1.0 / 24, -1.0 / 720, 1.0 / 40320, -1.0 / 3628800, 1.0 / 479001600]


def s5_scan(k, l):
    p, o, IN, C = k.p, k.o, k.IN, k.C
    e = l // 2
    MK = C["MK"]
    p.push()
    WCf = [[p.sb([128]) for _ in range(16)] for _ in range(2)]
    WB = [[[p.sb([128]) for _ in range(16)] for _ in range(2)] for _ in range(2)]
    mag = [p.sb([16]) for _ in range(2)]
    CLv = [p.sb([16, 10]) for _ in range(2)]; SLv = [p.sb([16, 10]) for _ in range(2)]; NSLv = [p.sb([16, 10]) for _ in range(2)]
    p.push()
    BR = p.sb([16, 16]); BI = p.sb([16, 16])
    CLr = p.sb([16, 64]); CLi = p.sb([16, 64])
    for gl in range(2):
        pr = slice(gl * 64, (gl + 1) * 64)
        p.dma(BR[pr], p.dview("od_b_re", IN["od_b_re"].ap()[e].rearrange("(st gl) q h -> gl q st h", gl=2)[gl]))
        p.dma(BI[pr], p.dview("od_b_im", IN["od_b_im"].ap()[e].rearrange("(st gl) q h -> gl q st h", gl=2)[gl]))
        pc = slice(gl * 16, (gl + 1) * 16)
        p.dma(CLr[pc], p.dview("od_c_re", IN["od_c_re"].ap()[e].rearrange("(st gl) h q -> gl h st q", gl=2)[gl]))
        p.dma(CLi[pc], p.dview("od_c_im", IN["od_c_im"].ap()[e].rearrange("(st gl) h q -> gl h st q", gl=2)[gl]))
    X = [p.sb([128]) for _ in range(2)]
    for ri, CLx in enumerate((CLr, CLi)):
        for st in range(16):
            s4 = st % 4
            x = X[st % 2]
            for gl2 in range(2):
                o.ts(x[0:32, gl2 * 64:(gl2 + 1) * 64], CLx[0:32, st, :], MK[0:32, 2 + gl2:3 + gl2], None, ALU.mult)
            ps = p.ps(st % 2, 32)
            o.tr(ps, x[0:32, :], C["IDENT"][0:32, 0:32])
            w = WCf[ri][st]
            o.memset(w, 0.0, eng="pool")
            if ri == 0:
                o.cp(w[:, s4 * 32:(s4 + 1) * 32], ps, eng="act")
            else:
                o.ts(w[:, s4 * 32:(s4 + 1) * 32], ps, -1.0, None, ALU.mult)
    LR = p.sb([16]); LI = p.sb([16]); DT = p.sb([16])
    tl = {n: p.sb([16]) for n in ["lr", "dt", "a", "th", "x", "z", "q", "s", "c", "cc", "ss", "cs", "abr", "abi", "xr", "den",
                                  "t1", "t2", "fre", "fim", "nfim"]}
    fm = {n: p.sb([16]) for n in ["fre0", "fre1", "fim0", "fim1", "nfim0", "nfim1"]}
    INre = [p.sb([128]) for _ in range(4)]; INim = [p.sb([128]) for _ in range(4)]
    tb = p.sb([16])
    for d in range(2):
        for gl in range(2):
            pr = slice(gl * 64, (gl + 1) * 64)
            p.dma(LR[pr], p.dview("od_lam_re", IN["od_lam_re"].ap()[e, d].rearrange("(st gl) q -> gl q st", gl=2)[gl]))
            p.dma(LI[pr], p.dview("od_lam_im", IN["od_lam_im"].ap()[e, d].rearrange("(st gl) q -> gl q st", gl=2)[gl]))
            p.dma(DT[pr], p.dview("od_log_dt", IN["od_log_dt"].ap()[e, d:d + 1, :].rearrange("a (st gl) -> a gl st", gl=2)[:, gl, :].partition_broadcast(64)))
        T_ = tl
        o.ts(T_["lr"], LR, -1e-4, None, ALU.min)
        o.act(T_["dt"], DT, AF.Exp)
        o.tt(T_["a"], T_["lr"], T_["dt"], ALU.mult)
        o.act(mag[d], T_["a"], AF.Exp)
        o.tt(T_["th"], LI, T_["dt"], ALU.mult)
        o.ts(T_["x"], T_["th"], 1.0 / 16, None, ALU.mult)
        o.tt(T_["z"], T_["x"], T_["x"], ALU.mult)
        o.ts(T_["q"], T_["z"], SIN_C[4], None, ALU.mult)
        for a_ in (SIN_C[3], SIN_C[2], SIN_C[1], SIN_C[0]):
            o.stt(T_["q"], T_["q"], a_, T_["z"], ALU.add, ALU.mult)
        o.stt(T_["s"], T_["q"], 1.0, T_["x"], ALU.add, ALU.mult)
        o.ts(T_["q"], T_["z"], COS_C[5], None, ALU.mult)
        for a_ in (COS_C[4], COS_C[3], COS_C[2], COS_C[1], COS_C[0]):
            o.stt(T_["q"], T_["q"], a_, T_["z"], ALU.add, ALU.mult)
        o.ts(T_["c"], T_["q"], 1.0, None, ALU.add)
        for _ in range(4):
            o.tt(T_["cc"], T_["c"], T_["c"], ALU.mult)
            o.tt(T_["ss"], T_["s"], T_["s"], ALU.mult)
            o.tt(T_["cs"], T_["c"], T_["s"], ALU.mult)
            o.tt(T_["c"], T_["cc"], T_["ss"], ALU.subtract)
            o.ts(T_["s"], T_["cs"], 2.0, None, ALU.mult)
        o.cp(CLv[d][:, :, 0], T_["c"]); o.cp(SLv[d][:, :, 0], T_["s"])
        for kk in range(1, 10):
            o.tt(T_["cc"], CLv[d][:, :, kk - 1], CLv[d][:, :, kk - 1], ALU.mult)
            o.tt(T_["ss"], SLv[d][:, :, kk - 1], SLv[d][:, :, kk - 1], ALU.mult)
            o.tt(T_["cs"], CLv[d][:, :, kk - 1], SLv[d][:, :, kk - 1], ALU.mult)
            o.tt(CLv[d][:, :, kk], T_["cc"], T_["ss"], ALU.subtract)
            o.ts(SLv[d][:, :, kk], T_["cs"], 2.0, None, ALU.mult)
        o.ts(NSLv[d], SLv[d], -1.0, None, ALU.mult)
        o.tt(T_["abr"], mag[d], T_["c"], ALU.mult)
        o.tt(T_["abi"], mag[d], T_["s"], ALU.mult)
        o.ts(T_["xr"], T_["abr"], -1.0, None, ALU.add)
        o.tt(T_["t1"], T_["lr"], T_["lr"], ALU.mult)
        o.tt(T_["t2"], LI, LI, ALU.mult)
        o.tt(T_["den"], T_["t1"], T_["t2"], ALU.add)
        o.recip(T_["den"], T_["den"])
        o.tt(T_["t1"], T_["xr"], T_["lr"], ALU.mult)
        o.tt(T_["t2"], T_["abi"], LI, ALU.mult)
        o.tt(T_["t1"], T_["t1"], T_["t2"], ALU.add)
        o.tt(T_["fre"], T_["t1"], T_["den"], ALU.mult)
        o.tt(T_["t1"], T_["abi"], T_["lr"], ALU.mult)
        o.tt(T_["t2"], T_["xr"], LI, ALU.mult)
        o.tt(T_["t1"], T_["t1"], T_["t2"], ALU.subtract)
        o.tt(T_["fim"], T_["t1"], T_["den"], ALU.mult)
        o.ts(T_["nfim"], T_["fim"], -1.0, None, ALU.mult)
        for gl in range(2):
            o.ts(fm["fre%d" % gl], T_["fre"], MK[:, gl:gl + 1], None, ALU.mult)
            o.ts(fm["fim%d" % gl], T_["fim"], MK[:, gl:gl + 1], None, ALU.mult)
            o.ts(fm["nfim%d" % gl], T_["nfim"], MK[:, gl:gl + 1], None, ALU.mult)
        for st in range(16):
            ft_, s4 = divmod(st, 4)
            for gl in range(2):
                cs_ = slice(s4 * 32 + gl * 16, s4 * 32 + gl * 16 + 16)
                fre, fim, nfim = fm["fre%d" % gl][:, st:st + 1], fm["fim%d" % gl][:, st:st + 1], fm["nfim%d" % gl][:, st:st + 1]
                o.ts(tb, BR[:, st, :], fre, None, ALU.mult)
                o.stt(INre[ft_][:, cs_], BI[:, st, :], nfim, tb, ALU.mult, ALU.add)
                o.ts(tb, BR[:, st, :], fim, None, ALU.mult)
                o.stt(INim[ft_][:, cs_], BI[:, st, :], fre, tb, ALU.mult, ALU.add)
        for ft_ in range(4):
            for ri, INx in enumerate((INre, INim)):
                ps = p.ps(2 + ri, 128)
                o.tr(ps, INx[ft_], C["IDENT"])
                for s4 in range(4):
                    o.ts(WB[d][ri][ft_ * 4 + s4], ps, MK[:, 4 + s4:5 + s4], None, ALU.mult)
    p.pop()
    NSEG = [(0, 256)] + [(256 + 512 * i, 512) for i in range(8)]
    cosT = [p.sb([516]) for _ in range(2)]; sinT = [p.sb([516]) for _ in range(2)]
    tAs = [p.sb([256]) for _ in range(2)]; tBs = [p.sb([256]) for _ in range(2)]
    uF = p.sb([TT]); yaccs = [p.sb([TT]) for _ in range(2)]
    wk = {n: [p.sb([512]) for _ in range(2)] for n in ["t1", "t2", "t3", "t4", "wre", "wim", "gre", "gim", "hre", "him"]}
    inis = [[p.sb([8]) for _ in range(2)] for _ in range(2)]
    crs = [p.sb([8]) for _ in range(2)]

    def stream(d, ft_, sidx, s4list, segs):
        cosv, sinv, tA, tB, yacc, cr = cosT[sidx], sinT[sidx], tAs[sidx], tBs[sidx], yaccs[sidx], crs[sidx]
        W_ = {nm: wk[nm][sidx] for nm in wk}
        ini = inis[sidx]
        bk_r, bk_i, bk_y = (4, 5, 0) if sidx == 0 else (6, 7, 1)
        for s4 in s4list:
            st = ft_ * 4 + s4
            o.memset(cosv[:, 0:1], 1.0, eng="dve"); o.memset(sinv[:, 0:1], 0.0, eng="dve")
            for kk in range(9):
                w = 1 << kk
                c_, s_, ns_ = CLv[d][:, st, kk:kk + 1], SLv[d][:, st, kk:kk + 1], NSLv[d][:, st, kk:kk + 1]
                o.ts(tA[:, 0:w], cosv[:, 0:w], c_, None, ALU.mult)
                o.ts(tB[:, 0:w], sinv[:, 0:w], c_, None, ALU.mult)
                o.stt(tA[:, 0:w], sinv[:, 0:w], ns_, tA[:, 0:w], ALU.mult, ALU.add)
                o.stt(tB[:, 0:w], cosv[:, 0:w], s_, tB[:, 0:w], ALU.mult, ALU.add)
                o.cp(cosv[:, w:2 * w], tA[:, 0:w]); o.cp(sinv[:, w:2 * w], tB[:, 0:w])
            o.cp(cosv[:, 512:513], CLv[d][:, st, 9:10]); o.cp(sinv[:, 512:513], SLv[d][:, st, 9:10])
            cur = ini[0]
            o.memset(cur[:, 0:2], 0.0, eng="dve")
            for si, (t0, n) in enumerate(segs):
                psr = p.ps(bk_r, n); psi = p.ps(bk_i, n)
                o.mm(psr, WB[d][0][st], uF[:, t0:t0 + n])
                o.mm(psi, WB[d][1][st], uF[:, t0:t0 + n])
                V = (lambda t_: rev(t_, n)) if d == 1 else (lambda t_: t_[:, 0:n])
                cs_n, sn_n = cosv[:, 0:n], sinv[:, 0:n]
                o.tt(W_["t1"][:, 0:n], V(psr), cs_n, ALU.mult)
                o.tt(W_["t2"][:, 0:n], V(psi), sn_n, ALU.mult)
                o.tt(W_["wre"][:, 0:n], W_["t1"][:, 0:n], W_["t2"][:, 0:n], ALU.add)
                o.tt(W_["t3"][:, 0:n], V(psi), cs_n, ALU.mult)
                o.tt(W_["t4"][:, 0:n], V(psr), sn_n, ALU.mult)
                o.tt(W_["wim"][:, 0:n], W_["t3"][:, 0:n], W_["t4"][:, 0:n], ALU.subtract)
                gre, gim = W_["gre"], W_["gim"]
                mb = bcast_cols(mag[d][:, st:st + 1], n)
                p.op("dve", lambda e_, gre=gre, mb=mb, w=W_["wre"], cur=cur, n=n: e_.tensor_tensor_scan(
                    gre.ap[:, 0:n], mb.ap, w.ap[:, 0:n], cur.ap[:, 0:1], ALU.mult, ALU.add), [mb, W_["wre"], cur], [gre])
                p.op("dve", lambda e_, gim=gim, mb=mb, w=W_["wim"], cur=cur, n=n: e_.tensor_tensor_scan(
                    gim.ap[:, 0:n], mb.ap, w.ap[:, 0:n], cur.ap[:, 1:2], ALU.mult, ALU.add), [mb, W_["wim"], cur], [gim])
                o.tt(W_["t1"][:, 0:n], gre[:, 0:n], cs_n, ALU.mult)
                o.tt(W_["t2"][:, 0:n], gim[:, 0:n], sn_n, ALU.mult)
                o.tt(V(W_["hre"]), W_["t1"][:, 0:n], W_["t2"][:, 0:n], ALU.subtract)
                o.tt(W_["t3"][:, 0:n], gim[:, 0:n], cs_n, ALU.mult)
                o.tt(W_["t4"][:, 0:n], gre[:, 0:n], sn_n, ALU.mult)
                o.tt(V(W_["him"]), W_["t3"][:, 0:n], W_["t4"][:, 0:n], ALU.add)
                nxt = ini[(si + 1) % 2]
                o.ts(cr[:, 0:1], gre[:, n - 1:n], cosv[:, n:n + 1], None, ALU.mult)
                o.ts(cr[:, 1:2], gim[:, n - 1:n], sinv[:, n:n + 1], None, ALU.mult)
                o.tt(nxt[:, 0:1], cr[:, 0:1], cr[:, 1:2], ALU.subtract)
                o.ts(cr[:, 2:3], gim[:, n - 1:n], cosv[:, n:n + 1], None, ALU.mult)
                o.stt(nxt[:, 1:2], gre[:, n - 1:n], sinv[:, n:n + 1], cr[:, 2:3], ALU.mult, ALU.add)
                cur = nxt
                psy = p.ps(bk_y, n)
                o.mm(psy, WCf[0][st], W_["hre"][:, 0:n], start=True, stop=False)
                o.mm(psy, WCf[1][st], W_["him"][:, 0:n], start=False, stop=True)
                if s4 == s4list[0]:
                    o.cp(yacc[:, t0:t0 + n], psy, eng="act")
                else:
                    o.tt(yacc[:, t0:t0 + n], psy, yacc[:, t0:t0 + n], ALU.add)

    for d in range(2):
        segs = NSEG if d == 0 else [NSEG[0]] + NSEG[:0:-1]
        for ft_ in range(4):
            p.dma(uF, p.dview("FM", k.FM.ap()[12 + ft_], (12 + ft_) * 128 * TT, (13 + ft_) * 128 * TT))
            caps = []
            for sidx, s4list in enumerate(((0, 2), (1, 3))):
                p.begin_capture()
                stream(d, ft_, sidx, s4list, segs)
                caps.append(p.end_capture())
            p.replay(caps)
            o.tt(yaccs[0], yaccs[0], yaccs[1], ALU.add)
            p.dma(p.dview("YS5", k.YS5.ap()[d, ft_], (d * 4 + ft_) * 128 * TT, (d * 4 + ft_ + 1) * 128 * TT), yaccs[0], q="pool")
    p.pop()


def merge_odd(k, l):
    p, o, IN, C = k.p, k.o, k.IN, k.C
    e = l // 2
    p.push()
    wout = p.sb([8, D], BF16)
    wst = [p.sb([8, 256]) for _ in range(2)]
    for j in range(4):
        p.dma(wst[j % 2], p.dview("od_w_out", IN["od_w_out"].ap()[e, :, j * 256:(j + 1) * 256].rearrange("(kt q) f -> q kt f", q=128)))
        o.cp(wout[:, :, j * 256:(j + 1) * 256], wst[j % 2], eng="pool")
    wglu = p.sb([4, 512], BF16)
    for j in range(2):
        w32 = wst[j % 2]
        w3 = w32.v(w32.ap.rearrange("q a b -> q (a b)")[:, 0:1024].rearrange("q (a b) -> q a b", a=4))
        p.dma(w3, p.dview("od_w_glu", IN["od_w_glu"].ap()[e, :, j * 256:(j + 1) * 256].rearrange("(kt q) f -> q kt f", q=128)))
        o.cp(wglu[:, :, j * 256:(j + 1) * 256], w3, eng="pool")
    mg = p.sb([512]); dsk = p.sb([512]); bgl = p.sb([512])
    rowbc(k, mg, IN["od_mlstm_norm"].ap()[e:e + 1, :], "od_mlstm_norm")
    rowbc(k, dsk, IN["od_d_skip"].ap()[e:e + 1, :], "od_d_skip")
    rowbc(k, bgl, IN["od_b_glu"].ap()[e:e + 1, :], "od_b_glu")
    g1 = [p.sb([D]), p.sb([D])]
    for r in range(2):
        rowbc(k, g1[r], k.MOD.ap()[l, r:r + 1, 2048:3072], "MOD")
    lng = p.sb([D]); lnb = p.sb([D])
    rowbc(k, lng, IN["ln1_g"].ap()[l:l + 1, :], "ln1_g")
    rowbc(k, lnb, IN["ln1_b"].ap()[l:l + 1, :], "ln1_b")
    A2 = [p.sb([2, 4, 132]) for _ in range(2)]
    YS = [p.sb([2, 4, 128]) for _ in range(2)]
    zt = [p.sb([D]) for _ in range(2)]
    ht = [p.sb([D]) for _ in range(2)]
    Y = [p.sb([D]) for _ in range(2)]
    YT = [p.sb([8, 128], BF16) for _ in range(2)]
    st_ = [p.sb([48]) for _ in range(2)]
    junk = p.sb([D])
    U = [p.sb([D]) for _ in range(2)]
    HM = [p.sb([4, 128]) for _ in range(2)]
    YG = [p.sb([512]) for _ in range(2)]
    YGT = [p.sb([4, 128], BF16) for _ in range(2)]
    for t in range(NT):
        r = 1 if t < NCTX else 0
        a2, ys, z, h, y, yT, s, u, hm, yg, ygT = (A2[t % 2], YS[t % 2], zt[t % 2], ht[t % 2], Y[t % 2], YT[t % 2], st_[t % 2],
                                                  U[t % 2], HM[t % 2], YG[t % 2], YGT[t % 2])
        tok = slice(t * 128, (t + 1) * 128)
        for d in range(2):
            for hh in range(4):
                p.dma(a2[:, d, hh, 0:129], p.dview("OUTO%d_%d" % (hh, d), k.OUTO.ap()[hh, d, tok, 0:129], t * 128 * 132, (t + 1) * 128 * 132))
                fi = d * 4 + hh
                p.dma(ys[:, d, hh, :], p.dview("YS5", k.YS5.ap()[d, hh][:, tok], fi * 128 * TT, (fi + 1) * 128 * TT))
        p.dma(z, p.dview("ZS", k.ZS.ap()[tok, :], t * 128 * D, (t + 1) * 128 * D))
        p.dma(h, p.dview("H", k.H.ap()[tok, :], t * 128 * D, (t + 1) * 128 * D))
        den = a2[:, :, :, 128]
        s3 = lambda c0: s.v(s.ap[:, c0:c0 + 8].rearrange("q (a b) -> q a b", a=2))
        o.ts(s3(0), den, -1.0, None, ALU.mult)
        o.tt(s3(0), s3(0), den, ALU.max)
        o.ts(s3(0), s3(0), 1.0, None, ALU.max)
        o.recip(s[:, 8:16], s[:, 0:8])
        for hh in range(4):
            o.ts(junk[:, 0:128], a2[:, 0, hh, 0:128], s[:, 8 + hh:9 + hh], None, ALU.mult)
            o.stt(hm[:, hh, :], a2[:, 1, hh, 0:128], s[:, 12 + hh:13 + hh], junk[:, 0:128], ALU.mult, ALU.add)
        for hh in range(4):
            o.act(junk[:, 0:128], hm[:, hh, :], AF.Square, accum=s[:, 16 + hh:17 + hh])
            o.act(junk[:, 128:256], hm[:, hh, :], AF.Identity, accum=s[:, 20 + hh:21 + hh])
        o.ts(s[:, 24:28], s[:, 20:24], 1.0 / 128, None, ALU.mult)
        o.tt(s[:, 28:32], s[:, 24:28], s[:, 24:28], ALU.mult)
        o.stt(s[:, 32:36], s[:, 16:20], 1.0 / 128, s[:, 28:32], ALU.mult, ALU.subtract)
        o.act(s[:, 36:40], s[:, 32:36], AF.Sqrt, bias=k.eps[:, 0:1])
        o.recip(s[:, 40:44], s[:, 36:40])
        for hh in range(4):
            o.ts(junk[:, 0:128], hm[:, hh, :], s[:, 24 + hh:25 + hh], s[:, 40 + hh:41 + hh], ALU.subtract, ALU.mult)
            o.tt(y[:, hh * 128:(hh + 1) * 128], junk[:, 0:128], mg[:, hh * 128:(hh + 1) * 128], ALU.mult)
        o.tt(y[:, 0:512], y[:, 0:512], z[:, 0:512], ALU.mult)
        o.tt(ys[:, 0], ys[:, 0], ys[:, 1], ALU.add)
        ps = p.ps(4)
        for ft_ in range(4):
            o.tr(ps[:, ft_ * 128:(ft_ + 1) * 128], ys[:, 0, ft_, :], C["IDENT"])
        o.tt(junk[:, 0:512], z[:, 512:1024], dsk, ALU.mult)
        o.tt(junk[:, 0:512], junk[:, 0:512], ps, ALU.add)
        xg = junk[:, 0:512]
        o.tt(junk[:, 512:1024], xg, xg, ALU.mult)
        o.ts(junk[:, 512:1024], junk[:, 512:1024], 0.044715, 1.0, ALU.mult, ALU.add)
        o.tt(junk[:, 512:1024], junk[:, 512:1024], xg, ALU.mult)
        o.act(yg, junk[:, 512:1024], AF.Tanh, scale=math.sqrt(2.0 / math.pi))
        o.stt(yg, yg, 1.0, xg, ALU.add, ALU.mult)
        o.ts(yg, yg, 0.5, None, ALU.mult)
        ps2 = p.ps(5)
        for ft_ in range(4):
            o.tr(ps2[:, ft_ * 128:(ft_ + 1) * 128], yg[:, ft_ * 128:(ft_ + 1) * 128], C["IDENT"])
        o.cp(ygT.v(ygT.ap.rearrange("q a b -> q (a b)")), ps2, eng="act")
        ps3 = p.ps(6)
        for kt in range(4):
            o.mm(ps3, ygT[:, kt, :], wglu[:, kt, :], start=(kt == 0), stop=(kt == 3))
        o.tt(junk[:, 0:512], ps3, bgl, ALU.add)
        o.act(junk[:, 512:1024], junk[:, 0:512], AF.Sigmoid)
        o.tt(y[:, 512:1024], yg, junk[:, 512:1024], ALU.mult)
        for half in range(2):
            ps = p.ps(half)
            for q in range(4):
                kt = half * 4 + q
                o.tr(ps[:, q * 128:(q + 1) * 128], y[:, kt * 128:(kt + 1) * 128], C["IDENT"])
            o.cp(yT.v(yT.ap[:, half * 4:(half + 1) * 4, :].rearrange("q a b -> q (a b)")), ps, eng="act")
        for half in range(2):
            ps = p.ps(2 + half)
            for kt in range(8):
                o.mm(ps, yT[:, kt, :], wout[:, kt, half * 512:(half + 1) * 512], start=(kt == 0), stop=(kt == 7))
            hs_ = slice(half * 512, (half + 1) * 512)
            o.tt(u[:, hs_], ps, g1[r][:, hs_], ALU.mult)
        if dbg_here(k, "y"):
            p.dma(p.dview("y_out", k.YOUT.ap()[tok, :], t * 128 * D, (t + 1) * 128 * D), u, q="pool")
        o.stt(u, h, ALPHA, u, ALU.mult, ALU.add)
        layer_norm(k, u, h, lng, lnb, s, junk)
        p.dma(p.dview("H", k.H.ap()[tok, :], t * 128 * D, (t + 1) * 128 * D), h, q="pool")
        if dbg_here(k, "h1"):
            p.dma(p.dview("y_out", k.YOUT.ap()[tok, :], t * 128 * D, (t + 1) * 128 * D), h, q="pool")
    p.pop()


def moe_layer2(k, l):
    p, o, IN, C = k.p, k.o, k.IN, k.C
    IOTA = k.IOTA
    IC = C["IC"]
    SETS = [(list(range(NCTX, NT)), 512, [(0, 128), (128, 128), (256, 128), (384, 128)], 0),
            (list(range(0, NCTX)), 32, [(0, 32)], 1)]
    p.push()
    wr = p.sb([8, 16])
    p.dma(wr, p.dview("w_router", IN["w_router"].ap()[l].rearrange("(kt q) e -> q kt e", q=128)))
    aff = p.sb([NT, 16]); gw = p.sb([NT, 16]); msk = p.sb([NT, 16]); posm = p.sb([NT, 16])
    posmT = p.sb([TT])
    ghl = p.sb([NT, 16, 2], BF16)
    p.push()
    in2tm = p.sb([NT, D], BF16)
    p.push()
    affT = p.sb([TT])
    sc2 = [p.sb([D]), p.sb([D])]; sh2 = [p.sb([D]), p.sb([D])]
    for r in range(2):
        rowbc(k, sc2[r], k.MOD.ap()[l, r:r + 1, 4096:5120], "MOD")
        o.ts(sc2[r], sc2[r], 1.0, None, ALU.add)
        rowbc(k, sh2[r], k.MOD.ap()[l, r:r + 1, 3072:4096], "MOD")
    ht = [p.sb([D]) for _ in range(2)]
    x2 = [p.sb([D]) for _ in range(2)]
    xTf = [p.sb([8, 128]) for _ in range(2)]
    sm = [p.sb([40]) for _ in range(2)]
    for t in range(NT):
        r = 1 if t < NCTX else 0
        h, x, xf, s = ht[t % 2], x2[t % 2], xTf[t % 2], sm[t % 2]
        tok = slice(t * 128, (t + 1) * 128)
        p.dma(h, p.dview("H", k.H.ap()[tok, :], t * 128 * D, (t + 1) * 128 * D))
        o.tt(x, h, sc2[r], ALU.mult)
        o.tt(x, x, sh2[r], ALU.add)
        o.cp(in2tm[:, t, :], x, eng="pool")
        for half in range(2):
            ps = p.ps(half)
            for q in range(4):
                kt = half * 4 + q
                o.tr(ps[:, q * 128:(q + 1) * 128], x[:, kt * 128:(kt + 1) * 128], C["IDENT"])
            ps3 = ps.v(ps.ap.rearrange("q (a b) -> q a b", a=4))
            o.cp(xf[:, half * 4:(half + 1) * 4, :], ps3, eng="act")
        psl = p.ps(2, 16)
        for kt in range(8):
            o.mm(psl, xf[:, kt, :], wr[:, kt, :], start=(kt == 0), stop=(kt == 7))
        p.op("dve", lambda e, s=s, psl=psl: e.reduce_max(s.ap[:, 0:1], psl.ap, AX.X), [psl], [s])
        o.ts(s[:, 1:2], s[:, 0:1], -1.0, None, ALU.mult)
        o.act(s[:, 8:24], psl, AF.Exp, bias=s[:, 1:2], accum=s[:, 2:3])
        o.recip(s[:, 3:4], s[:, 2:3])
        o.ts(aff[:, t, :], s[:, 8:24], s[:, 3:4], None, ALU.mult)
        pst = p.ps(3, 128)
        o.tr(pst[0:16, :], aff[:, t, :], C["IDENT"])
        o.cp(affT[0:16, tok], pst[0:16, :], eng="act")
    sts = [p.sb([16]), p.sb([16])]; junks = [p.sb([4096]), p.sb([256])]; thr = [p.sb([16]), p.sb([16])]; dts = [p.sb([16]), p.sb([16])]
    sets = [(256, TT, 512.0, 0), (0, 256, 32.0, 1)]
    caps_ = []
    for si_, (c0, c1, cap, r) in enumerate(sets):
        p.begin_capture()
        st, junk, dt = sts[si_], junks[si_], dts[si_]
        lo, hi, mid, cnt, ge, d1, d2 = [st[0:16, i:i + 1] for i in range(7)]
        o.memset(lo, 0.0, eng="dve"); o.memset(hi, 1.0, eng="dve")
        for it in range(32):
            o.tt(mid, lo, hi, ALU.add)
            o.ts(mid, mid, 0.5, None, ALU.mult)
            o.ts(junk[0:16, 0:c1 - c0], affT[0:16, c0:c1], mid, None, ALU.is_ge, ALU.add, accum=cnt)
            o.ts(ge, cnt, cap - 0.5, None, ALU.is_ge)
            o.tt(d1, mid, lo, ALU.subtract)
            o.tt(d2, hi, mid, ALU.subtract)
            o.stt(lo, d1, ge, lo, ALU.mult, ALU.add)
            o.stt(hi, d2, ge, mid, ALU.mult, ALU.add)
        o.ts(dt[0:16, 0:16], C["IDENT"][0:16, 0:16], lo, None, ALU.mult)
        pth = p.ps(2 + si_, 16)
        o.mm(pth, C["ONES"][0:16, :], dt[0:16, 0:16])
        o.cp(thr[r], pth, eng="act")
        caps_.append(p.end_capture())
    p.replay(caps_)
    tot = p.sb([16]); tmp16 = p.sb([16])
    for (tiles, cap, jts, yi) in SETS:
        r = 1 if tiles[0] < NCTX else 0
        o.memset(tot, 0.0, eng="dve")
        for t in tiles:
            tok = slice(t * 128, (t + 1) * 128)
            o.tt(msk[:, t, :], aff[:, t, :], thr[r], ALU.is_ge)
            o.tt(gw[:, t, :], msk[:, t, :], aff[:, t, :], ALU.mult)
            ps = p.ps(0, 16); ps2 = p.ps(1, 16)
            o.mm(ps, C["CSs"], msk[:, t, :])
            o.mm(ps2, C["ONES"], msk[:, t, :])
            o.tt(tmp16, ps, tot, ALU.add)
            o.tt(tot, tot, ps2, ALU.add)
            o.stt(tmp16, tmp16, 1.0, msk[:, t, :], ALU.add, ALU.mult)
            o.ts(posm[:, t, :], tmp16, -1.0, None, ALU.add)
            pst = p.ps(3, 128)
            o.tr(pst[0:16, :], posm[:, t, :], C["IDENT"])
            o.cp(posmT[0:16, tok], pst[0:16, :], eng="act")
    hi16 = p.sb([NT, 16], BF16); hi32 = p.sb([NT, 16]); lo32 = p.sb([NT, 16])
    o.cp(hi16, gw); o.cp(hi32, hi16); o.tt(lo32, gw, hi32, ALU.subtract)
    o.cp(ghl[:, :, :, 0], hi32); o.cp(ghl[:, :, :, 1], lo32)
    p.pop()
    p.push()
    wgf = [p.sb([8, 128]) for _ in range(2)]; wuf = [p.sb([8, 128]) for _ in range(2)]
    wgb = [p.sb([8, 128], BF16) for _ in range(2)]; wub = [p.sb([8, 128], BF16) for _ in range(2)]
    wdf = [p.sb([2, 512]) for _ in range(2)]; wdb = [p.sb([2, 512], BF16) for _ in range(2)]
    xsT = [p.sb([8, 512], BF16), p.sb([8, 32], BF16)]
    hT = [p.sb([16, 512], BF16), p.sb([16, 32], BF16)]
    sg = [p.sb([512]) for _ in range(2)]
    sel = [p.sb([512], BF16) for _ in range(3)]
    gsel = [p.sb([4]), p.sb([4])]
    ysb = [[p.sb([D], BF16) for _ in range(5)] for _ in range(2)]
    wi = 0; di = 0; si = 0
    for e in range(16):
        for (tiles, cap, jts, yi) in SETS:
            for ps_ in range(2):
                psx = [p.ps(q, cap) for q in range(4)]
                psg = [p.ps(4 + ji, 2) for ji in range(len(jts))]
                for t in tiles:
                    sl = sel[si % 3]; si += 1
                    o.ts(sl[:, 0:cap], IOTA[:, 0:cap], posm[:, t, e:e + 1], None, ALU.is_equal)
                    for q in range(4):
                        kt = ps_ * 4 + q
                        o.mm(psx[q], in2tm[:, t, kt * 128:(kt + 1) * 128], sl[:, 0:cap], start=(t == tiles[0]), stop=(t == tiles[-1]))
                    if ps_ == 0:
                        for ji, (j0, nj) in enumerate(jts):
                            o.mm(psg[ji][0:nj, :], sl[:, j0:j0 + nj], ghl[:, t, e, :], start=(t == tiles[0]), stop=(t == tiles[-1]))
                for q in range(4):
                    kt = ps_ * 4 + q
                    o.cp(xsT[yi][:, kt, 0:cap], psx[q], eng=("act" if q % 2 else "dve"))
                if ps_ == 0:
                    for ji, (j0, nj) in enumerate(jts):
                        gs_, pg_ = gsel[yi][0:nj, ji:ji + 1], psg[ji][0:nj, 0:2]
                        p.op("dve", lambda e_, gs_=gs_, pg_=pg_: e_.reduce_sum(gs_.ap, pg_.ap, AX.X), [pg_], [gs_])
        for ft_ in range(16):
            a32, b32, a16, b16 = wgf[wi % 2], wuf[wi % 2], wgb[wi % 2], wub[wi % 2]
            wi += 1
            p.dma(a32, p.dview("w_gate", IN["w_gate"].ap()[l, e, :, ft_ * 128:(ft_ + 1) * 128].rearrange("(kt q) f -> q kt f", q=128)))
            p.dma(b32, p.dview("w_up", IN["w_up"].ap()[l, e, :, ft_ * 128:(ft_ + 1) * 128].rearrange("(kt q) f -> q kt f", q=128)))
            o.cp(a16, a32, eng="pool")
            o.cp(b16, b32, eng="pool")
            for (tiles, cap, jts, yi) in SETS:
                pg = p.ps(4 + (ft_ % 2) * 2, cap); pu = p.ps(5 + (ft_ % 2) * 2, cap)
                for kt in range(8):
                    o.mm(pg, a16[:, kt, :], xsT[yi][:, kt, 0:cap], start=(kt == 0), stop=(kt == 7))
                for kt in range(8):
                    o.mm(pu, b16[:, kt, :], xsT[yi][:, kt, 0:cap], start=(kt == 0), stop=(kt == 7))
                s_ = sg[(ft_ + yi) % 2]
                o.act(s_[:, 0:cap], pg, AF.Silu)
                o.tt(hT[yi][:, ft_, 0:cap], s_[:, 0:cap], pu, ALU.mult)
        yt_ = ysb[e % 2]
        for half in range(2):
            hs_ = slice(half * 512, (half + 1) * 512)
            psd = {}
            for (tiles, cap, jts, yi) in SETS:
                for ji in range(len(jts)):
                    psd[(yi, ji)] = p.ps(ji if yi == 0 else 4, 512)
            for fp_ in range(8):
                d32, d16 = wdf[di % 2], wdb[di % 2]
                di += 1
                p.dma(d32, p.dview("w_down", IN["w_down"].ap()[l, e, fp_ * 256:(fp_ + 1) * 256, half * 512:(half + 1) * 512].rearrange("(a q) d -> q a d", q=128)))
                o.cp(d16, d32, eng="pool")
                for f2 in range(2):
                    ft_ = fp_ * 2 + f2
                    for (tiles, cap, jts, yi) in SETS:
                        for ji, (j0, nj) in enumerate(jts):
                            o.mm(psd[(yi, ji)][0:nj, :], hT[yi][:, ft_, j0:j0 + nj], d16[:, f2, :], start=(ft_ == 0), stop=(ft_ == 15))
            for (tiles, cap, jts, yi) in SETS:
                for ji, (j0, nj) in enumerate(jts):
                    yt = yt_[ji if yi == 0 else 4]
                    o.ts(yt[0:nj, hs_], psd[(yi, ji)][0:nj, :], gsel[yi][0:nj, ji:ji + 1], None, ALU.mult)
        for (tiles, cap, jts, yi) in SETS:
            for ji, (j0, nj) in enumerate(jts):
                yt = yt_[ji if yi == 0 else 4]
                p.dma(p.dview("YSD%d" % yi, k.YSD[yi].ap()[e, j0:j0 + nj, :]), yt[0:nj, :], q="pool")
    p.pop()
    p.pop()
    p.push()
    g2 = [p.sb([D]), p.sb([D])]
    for r in range(2):
        rowbc(k, g2[r], k.MOD.ap()[l, r:r + 1, 5120:6144], "MOD")
    lng = p.sb([D]); lnb = p.sb([D])
    rowbc(k, lng, IN["ln2_g"].ap()[l:l + 1, :], "ln2_g")
    rowbc(k, lnb, IN["ln2_b"].ap()[l:l + 1, :], "ln2_b")
    selT = [[p.sb([512], BF16) for _ in range(4)] for _ in range(16)]
    ysl = [p.sb([4, 128], BF16) for _ in range(4)]
    fT = p.sb([8, 512])
    hh = [p.sb([D]) for _ in range(2)]
    uu = [p.sb([D]) for _ in range(2)]
    s8 = p.sb([8]); junk2 = p.sb([D])
    groups = [(0, 256, SETS[1])] + [(256 + 512 * i, 512, SETS[0]) for i in range(8)]
    yl = 0
    for (t0, n, (tiles, cap, jts, yi)) in groups:
        for e in range(16):
            psr = p.ps(e % 2, n)
            o.mm(psr, C["OH%d" % e][0:16, :], posmT[0:16, t0:t0 + n])
            for ji, (j0, nj) in enumerate(jts):
                o.ts(selT[e][ji][0:nj, 0:n], psr[0:nj, :], IC[0:nj, ji:ji + 1], None, ALU.is_equal)
        for dt_ in range(8):
            psf = p.ps(2 + dt_ % 2, n)
            for e in range(16):
                y_ = ysl[yl % 4]; yl += 1
                nrow = jts[-1][0] + jts[-1][1]
                if yi == 0:
                    p.dma(y_, p.dview("YSD0", k.YSD[0].ap()[e, :, dt_ * 128:(dt_ + 1) * 128].rearrange("(jt q) d -> q jt d", q=128)))
                else:
                    p.dma(y_[0:32, 0, :], p.dview("YSD1", k.YSD[1].ap()[e, 0:32, dt_ * 128:(dt_ + 1) * 128]))
                for ji, (j0, nj) in enumerate(jts):
                    o.mm(psf, y_[0:nj, ji, :], selT[e][ji][0:nj, 0:n], start=(e == 0 and ji == 0), stop=(e == 15 and ji == len(jts) - 1))
            o.cp(fT[:, dt_, 0:n], psf, eng=("act" if dt_ % 2 else "dve"))
        for st_ in range(n // 128):
            t = t0 // 128 + st_
            r = 1 if t < NCTX else 0
            tok = slice(t * 128, (t + 1) * 128)
            h = hh[st_ % 2]; u = uu[st_ % 2]
            p.dma(h, p.dview("H", k.H.ap()[tok, :], t * 128 * D, (t + 1) * 128 * D))
            for half in range(2):
                ps = p.ps(4 + half)
                for q in range(4):
                    dt_ = half * 4 + q
                    o.tr(ps[:, q * 128:(q + 1) * 128], fT[:, dt_, st_ * 128:(st_ + 1) * 128], C["IDENT"])
                hs_ = slice(half * 512, (half + 1) * 512)
                if dbg_here(k, "f"):
                    o.cp(junk2[:, hs_], ps, eng="act")
                o.tt(u[:, hs_], ps, g2[r][:, hs_], ALU.mult)
            if dbg_here(k, "f"):
                p.dma(p.dview("y_out", k.YOUT.ap()[tok, :], t * 128 * D, (t + 1) * 128 * D), junk2, q="pool")
            o.stt(u, h, ALPHA, u, ALU.mult, ALU.add)
            layer_norm(k, u, h, lng, lnb, s8, junk2)
            p.dma(p.dview("H", k.H.ap()[tok, :], t * 128 * D, (t + 1) * 128 * D), h, q="pool")
    p.pop()
    p.pop()


_W_NAMES = ["w_mod", "b_mod", "ln1_g", "ln1_b", "ln2_g", "ln2_b", "w_router", "w_gate", "w_up", "w_down",
            "ev_w_in", "ev_w_out", "ev_conv", "ev_a_log", "ev_dt_bias", "ev_gdn_norm", "ev_ret_norm",
            "od_w_in", "od_w_out", "od_conv", "od_gate_bias", "od_mlstm_norm", "od_lam_re", "od_lam_im", "od_log_dt",
            "od_b_re", "od_b_im", "od_c_re", "od_c_im", "od_d_skip", "od_w_glu", "od_b_glu"]


def kernel(**inputs):
    nc = build(n_layers=4)
    cst = make_consts()
    retg = np.concatenate([ret_log_decay(0), ret_log_decay(1)])[None, :].astype(np.float32)
    in_maps = []
    B = inputs["x"].shape[0]
    for b in range(B):
        m = {}
        m["h0"] = np.ascontiguousarray(np.concatenate([inputs["ctx"][b], inputs["x"][b]], 0).astype(np.float32))
        m["cvec"] = np.ascontiguousarray(np.stack([inputs["c"][b], inputs["c_ctx"]], 0).astype(np.float32))
        m["cst"] = cst
        m["retg"] = retg
        for n in _W_NAMES:
            if n in nc.used_inputs:
                m[n] = np.ascontiguousarray(inputs[n], dtype=np.float32)
        in_maps.append({n: m[n] for n in nc.used_inputs})
    res = run_bass_kernel_spmd(nc, in_maps, core_ids=list(range(B)))
    out = np.stack([np.asarray(res.results[b]["y_out"])[256:] for b in range(B)], 0)
    return out.astype(np.float32)
```

```python
import bisect
import numpy as np
import concourse.bass as bass
import concourse.mybir as mybir
from concourse.bass_utils import run_bass_kernel_spmd

F32 = mybir.dt.float32
BF16 = mybir.dt.bfloat16
ALU = mybir.AluOpType
AF = mybir.ActivationFunctionType
AX = mybir.AxisListType

SEM_GEN = 30000
N_DMA_SEMS = 12


class IntervalMap:
    def __init__(self):
        self.bounds = [0]
        self.state = [(None, {})]

    def _split(self, x):
        i = bisect.bisect_right(self.bounds, x) - 1
        if self.bounds[i] == x:
            return i
        w, r = self.state[i]
        self.bounds.insert(i + 1, x)
        self.state.insert(i + 1, (w, dict(r)))
        return i + 1

    def segs(self, lo, hi):
        i0 = self._split(lo)
        i1 = self._split(hi)
        return range(i0, i1)


class T:
    def __init__(self, ap, key, lo, hi):
        self.ap = ap
        self.key = key
        self.lo = lo
        self.hi = hi

    def __getitem__(self, idx):
        return T(self.ap[idx], self.key, self.lo, self.hi)

    def v(self, ap):
        return T(ap, self.key, self.lo, self.hi)


class Prog:
    ENGS = ["pe", "act", "dve", "pool", "sp"]

    def __init__(self, nc, sb_bytes=206 * 1024):
        self.nc = nc
        self.ops = {e: [] for e in self.ENGS}
        self.cnt = {e: 0 for e in self.ENGS}
        self.maps = {}
        self.observed = {e: {} for e in self.ENGS}
        self.dma_cnt = {"sp": 0, "pool": 0, "act": 0}
        self.dma_sem_uses = {}
        self.sem_names = set()
        self.sb_bytes = sb_bytes
        self.sb_top = 0
        self.sb_stack = []
        self.ps_top = 0
        self.dram = {}
        self.arena = None
        self.psarena = None
        self.final_waits = []

    def setup_mem(self, es):
        nc = self.nc
        self.arena = es.enter_context(nc.sbuf_tensor("arena", [128, self.sb_bytes // 4], F32))
        self.psarena = es.enter_context(nc.psum_tensor("psarena", [128, 8 * 512], F32))

    def push(self):
        self.sb_stack.append(self.sb_top)

    def pop(self):
        self.sb_top = self.sb_stack.pop()

    def sb(self, shape, dtype=F32, name=None):
        esz = 4 if dtype == F32 else 2
        n = int(np.prod(shape))
        nbytes = (n * esz + 31) // 32 * 32
        off = self.sb_top
        self.sb_top += nbytes
        assert self.sb_top <= self.sb_bytes, f"SBUF overflow {self.sb_top}"
        ap = self.arena[:, off // 4:(off + nbytes) // 4]
        if dtype != F32:
            ap = ap.bitcast(dtype)
        ap = ap[:, 0:n]
        if len(shape) == 2:
            ap = ap.rearrange("p (a b) -> p a b", a=shape[0])
        elif len(shape) == 3:
            ap = ap.rearrange("p (a b c) -> p a b c", a=shape[0], b=shape[1])
        return T(ap, "sb", off, off + nbytes)

    def ps(self, bank, ncols=512, dtype=F32, col0=0):
        off = bank * 512 + col0
        ap = self.psarena[:, off:off + ncols]
        return T(ap, "ps", off * 4, (off + ncols) * 4)

    def dram_t(self, name, shape, dtype=F32, kind="Internal"):
        h = self.nc.dram_tensor(name, list(shape), dtype, kind=kind)
        self.dram[name] = h
        return h

    def dview(self, name, ap, lo=0, hi=1 << 40):
        return T(ap, "d:" + name, lo, hi)

    def _deps(self, eng, reads, writes, token):
        deps = set()
        for t in reads:
            m = self.maps.setdefault(t.key, IntervalMap())
            for i in m.segs(t.lo, t.hi):
                w, r = m.state[i]
                if w is not None:
                    deps.add(w)
        for t in writes:
            m = self.maps.setdefault(t.key, IntervalMap())
            for i in m.segs(t.lo, t.hi):
                w, r = m.state[i]
                if w is not None:
                    deps.add(w)
                for tok in r.values():
                    deps.add(tok)
        for t in reads:
            m = self.maps[t.key]
            for i in m.segs(t.lo, t.hi):
                m.state[i][1][token[0] if eng.startswith("dma") else eng] = token
        for t in writes:
            m = self.maps[t.key]
            for i in m.segs(t.lo, t.hi):
                m.state[i] = (token, {})
        return deps

    def _waits(self, eng, deps):
        obs = self.observed[eng]
        best = {}
        for (sem, val, src) in deps:
            if src == "pe" and eng == "pe":
                continue
            if obs.get(sem, 0) >= val:
                continue
            if best.get(sem, 0) < val:
                best[sem] = val
        for sem, val in best.items():
            obs[sem] = val
        return list(best.items())

    limit = None
    count = 0

    def _lim(self):
        if self.limit is not None:
            if self.count >= self.limit:
                return True
            self.count += 1
        return False

    capture = None

    def begin_capture(self):
        self.capture = []
        return self.capture

    def end_capture(self):
        c = self.capture
        self.capture = None
        return c

    def replay(self, lists):
        its = [iter(l) for l in lists]
        alive = list(its)
        while alive:
            nxt = []
            for it in alive:
                try:
                    item = next(it)
                except StopIteration:
                    continue
                if item[0] == "op":
                    self.op(*item[1:])
                else:
                    self.dma(*item[1:3], q=item[3], **item[4])
                nxt.append(it)
            alive = nxt

    def op(self, eng, fn, reads=(), writes=()):
        if self.capture is not None:
            self.capture.append(("op", eng, fn, list(reads), list(writes)))
            return
        if self._lim():
            return
        n = self.cnt[eng]
        gen, idx = divmod(n, SEM_GEN)
        sem = f"{eng}{gen}"
        self.sem_names.add(sem)
        token = (sem, idx + 1, eng)
        self.cnt[eng] = n + 1
        rd2, wr2 = [], []
        for t in reads:
            if t.key == "ps":
                wr2.append(T(t.ap, "ps", t.lo // 2048 * 2048, (t.hi + 2047) // 2048 * 2048))
            else:
                rd2.append(t)
        for t in writes:
            if t.key == "ps":
                wr2.append(T(t.ap, "ps", t.lo // 2048 * 2048, (t.hi + 2047) // 2048 * 2048))
            else:
                wr2.append(t)
        reads, writes = rd2, wr2
        deps = self._deps(eng, reads, writes, token)
        waits = self._waits(eng, deps)
        self.ops[eng].append((waits, fn, (sem, 1)))

    def dma(self, out, in_, q="sp", **kw):
        if self.capture is not None:
            self.capture.append(("dma", out, in_, q, kw))
            return
        if self._lim():
            return
        k = self.dma_cnt[q]
        self.dma_cnt[q] = k + 1
        sem = f"dma_{q}{k % N_DMA_SEMS}"
        self.sem_names.add(sem)
        uses = self.dma_sem_uses.get(sem, 0)
        self.dma_sem_uses[sem] = uses + 1
        token = (sem, 16 * (uses + 1), "dma")
        deps = self._deps("dma_" + q, [in_], [out], token)
        if uses > 0:
            deps.add((sem, 16 * uses, "dma"))
        waits = self._waits(q, deps)
        oap, iap = out.ap, in_.ap

        def fn(e, oap=oap, iap=iap, kw=kw):
            return e.dma_start(out=oap, in_=iap, allow_slow_non_contiguous=True, **kw)
        self.ops[q].append((waits, fn, (sem, 16)))
        return token

    def wait_all_dma(self, eng="sp"):
        deps = set()
        for sem, uses in self.dma_sem_uses.items():
            deps.add((sem, 16 * uses, "dma"))
        waits = self._waits(eng, deps)
        self.ops[eng].append((waits, None, None))

    def emit(self, es):
        nc = self.nc
        sems = {}
        for name in sorted(self.sem_names):
            sems[name] = es.enter_context(nc.semaphore(name))
        block = es.enter_context(nc.Block())
        ops = self.ops

        def run(e, lst):
            for waits, fn, inc in lst:
                for sem, val in waits:
                    e.wait_ge(sems[sem], val)
                if fn is not None:
                    ins = fn(e)
                    ins.then_inc(sems[inc[0]], inc[1])

        @block.sync
        def _(e):
            run(e, ops["sp"])

        @block.tensor
        def _(e):
            run(e, ops["pe"])

        @block.vector
        def _(e):
            run(e, ops["dve"])

        @block.scalar
        def _(e):
            run(e, ops["act"])

        @block.gpsimd
        def _(e):
            run(e, ops["pool"])


def _ap(x):
    return x.ap if isinstance(x, T) else x


def _ts(*xs):
    return [x for x in xs if isinstance(x, T)]


class Ops:
    def __init__(self, p):
        self.p = p

    def mm(self, out, lhsT, rhs, start=True, stop=True):
        self.p.op("pe", lambda e: e.matmul(out.ap, lhsT.ap, rhs.ap, start=start, stop=stop), [lhsT, rhs], [out])

    def tr(self, out, in_, ident):
        self.p.op("pe", lambda e: e.transpose(out.ap, in_.ap, ident.ap), [in_, ident], [out])

    def act(self, out, in_, func, bias=None, scale=None, accum=None, eng="act"):
        kw = {}
        if bias is not None:
            kw["bias"] = _ap(bias)
        if scale is not None:
            kw["scale"] = _ap(scale)
        if accum is not None:
            kw["accum_out"] = accum.ap
        self.p.op("act", lambda e: e.activation(out.ap, in_.ap, func, **kw),
                  _ts(in_, bias, scale), _ts(out, accum))

    def tt(self, out, a, b, op, eng="dve"):
        self.p.op(eng, lambda e: e.tensor_tensor(out.ap, a.ap, b.ap, op), [a, b], [out])

    def ts(self, out, a, s1, s2, op0, op1=None, accum=None, eng="dve"):
        kw = {}
        if accum is not None:
            kw["accum_out"] = accum.ap
        if op1 is None:
            fn = lambda e: e.tensor_scalar(out.ap, a.ap, _ap(s1), None, op0, **kw)
        else:
            fn = lambda e: e.tensor_scalar(out.ap, a.ap, _ap(s1), _ap(s2), op0, op1, **kw)
        self.p.op(eng, fn, _ts(a, s1, s2), _ts(out, accum))

    def stt(self, out, a, s, b, op0, op1, eng="dve"):
        self.p.op(eng, lambda e: e.scalar_tensor_tensor(out.ap, a.ap, _ap(s), b.ap, op0, op1), _ts(a, s, b), [out])

    def cp(self, out, in_, eng="dve"):
        if eng == "act":
            self.p.op("act", lambda e: e.copy(out.ap, in_.ap), [in_], [out])
        else:
            self.p.op(eng, lambda e: e.tensor_copy(out.ap, in_.ap), [in_], [out])

    def memset(self, out, val, eng="pool"):
        self.p.op(eng, lambda e: e.memset(out.ap, val), [], [out])

    def recip(self, out, in_):
        self.p.op("dve", lambda e: e.reciprocal(out.ap, in_.ap), [in_], [out])

from contextlib import ExitStack
import math

D = 1024
TT = 4352
NT = 34
NCTX = 2
ALPHA = 8 ** 0.25
EPS = 1e-5
NEG = -30000.0
EVEN_IN = 4112
ODD_IN = 2576

CONST_NAMES = ["IDENT", "ONES", "CSf", "CSb", "MTf", "MTb", "MSf", "MSb", "CSs", "MK", "IO0", "IO1", "IO2", "IO3", "IC"] + ["OH%d" % i for i in range(16)]


def make_consts():
    p = np.arange(128)[:, None]
    f = np.arange(128)[None, :]
    c = {}
    c["IDENT"] = (p == f)
    c["ONES"] = np.ones((128, 128))
    c["CSf"] = (p <= f)
    c["CSb"] = (p >= f)
    c["MTf"] = np.where(p <= f, 0.0, NEG)
    c["MTb"] = np.where(p >= f, 0.0, NEG)
    c["MSf"] = np.where(f < p, 0.0, NEG)
    c["MSb"] = np.where(f > p, 0.0, NEG)
    c["CSs"] = (p < f)
    mk = np.zeros((128, 128))
    mk[:64, 0] = 1; mk[64:, 1] = 1; mk[:16, 2] = 1; mk[16:32, 3] = 1
    for s4 in range(4):
        mk[s4 * 32:(s4 + 1) * 32, 4 + s4] = 1
    c["MK"] = mk
    for i in range(4):
        c["IO%d" % i] = np.broadcast_to(f + 128 * i, (128, 128))
    ic = np.zeros((128, 128))
    for i in range(4):
        ic[:, i] = np.arange(128) + 128 * i
    c["IC"] = ic
    for i in range(16):
        oh = np.zeros((128, 128)); oh[i, :] = 1
        c["OH%d" % i] = oh
    arr = np.concatenate([c[k].astype(np.float32) for k in CONST_NAMES], axis=1)
    return np.ascontiguousarray(arr)


def ret_log_decay(d):
    expo = 5.0 + 2.0 * np.arange(4, dtype=np.float32) + d
    return np.log1p(-np.exp2(-expo)).astype(np.float32)


class K:
    pass


def build(n_layers=4, dbg=(), stage=99, layers=None, dbg_layer=0):
    nc = bass.Bass("TRN2", target_bir_lowering=False)
    k = K()
    k.nc = nc
    k.dbg_on = set(dbg)
    k.stage = stage
    k.dbg_layer = dbg_layer
    k.cur = -1
    SHAPES = dict(h0=[TT, D], cvec=[2, D], cst=[128, 128 * len(CONST_NAMES)], retg=[1, 8],
                  w_mod=[4, D, 6 * D], b_mod=[4, 6 * D], ln1_g=[4, D], ln1_b=[4, D], ln2_g=[4, D], ln2_b=[4, D],
                  w_router=[4, D, 16], w_gate=[4, 16, D, 2048], w_up=[4, 16, D, 2048], w_down=[4, 16, 2048, D],
                  ev_w_in=[2, D, EVEN_IN], ev_w_out=[2, D, D], ev_conv=[2, 3, 3, 1536], ev_a_log=[2, 2, 4],
                  ev_dt_bias=[2, 2, 4], ev_gdn_norm=[2, 128], ev_ret_norm=[2, 512],
                  od_w_in=[2, D, ODD_IN], od_w_out=[2, D, D], od_conv=[2, 3, 3, 1024], od_gate_bias=[2, 2, 2, 4],
                  od_mlstm_norm=[2, 512], od_lam_re=[2, 2, 32, 64], od_lam_im=[2, 2, 32, 64], od_log_dt=[2, 2, 32],
                  od_b_re=[2, 32, 64, 16], od_b_im=[2, 32, 64, 16], od_c_re=[2, 32, 16, 64], od_c_im=[2, 32, 16, 64],
                  od_d_skip=[2, 512], od_w_glu=[2, 512, 512], od_b_glu=[2, 512])

    class LazyIn(dict):
        def __missing__(self, name):
            self[name] = nc.dram_tensor(name, list(SHAPES[name]), F32, kind="ExternalInput")
            return self[name]
    IN = LazyIn()
    k.IN = IN
    with ExitStack() as es:
        p = Prog(nc)
        p.setup_mem(es)
        o = Ops(p)
        k.p, k.o = p, o
        k.H = p.dram_t("H", [TT, D])
        k.MOD = p.dram_t("MOD", [4, 2, 6 * D])
        k.FM = p.dram_t("FM", [24, 128, TT])
        k.ZS = p.dram_t("ZS", [TT, D])
        k.GATES = p.dram_t("GATES", [TT, 16])
        k.OUTM = p.dram_t("OUTM", [8, 2, TT, 128])
        k.OUTO = p.dram_t("OUTO", [4, 2, TT, 132])
        k.YS5 = p.dram_t("YS5", [2, 4, 128, TT])
        k.YSD = [p.dram_t("YSDl", [16, 512, D], BF16), p.dram_t("YSDc", [16, 128, D], BF16)]
        k.YOUT = nc.dram_tensor("y_out", [TT, D], F32, kind="ExternalOutput")
        cst = p.sb([128 * len(CONST_NAMES)])
        p.dma(cst, p.dview("cst", IN["cst"].ap()))
        k.C = {n: cst[:, i * 128:(i + 1) * 128] for i, n in enumerate(CONST_NAMES)}
        i_io = CONST_NAMES.index('IO0')
        k.IOTA = cst[:, i_io * 128:(i_io + 4) * 128]
        eps_c = p.sb([4])
        o.memset(eps_c[:, 0:1], EPS); o.memset(eps_c[:, 1:2], 1e-6); o.memset(eps_c[:, 2:3], 1.0); o.memset(eps_c[:, 3:4], 0.0)
        k.eps = eps_c

        phase_mod(k)
        p.dma(p.dview("H", k.H.ap()), p.dview("h0", IN["h0"].ap()))
        for l in (layers if layers is not None else range(n_layers)):
            if k.stage <= 0:
                break
            k.cur = l
            mixer_layer(k, l)
        p.limit = None
        if not (k.dbg_on - {'none'}) and k.stage >= 5:
            p.dma(p.dview('y_out', k.YOUT.ap()), p.dview('H', k.H.ap()))
        if k.stage < 5:
            p.dma(p.dview('y_out', k.YOUT.ap()[0:8, :]), p.dview('MOD', k.MOD.ap().rearrange('l r (a f) -> (l r a) f', f=1024)[0:8, :]))
        p.wait_all_dma("sp")
        p.wait_all_dma("pool")
        p.emit(es)
    nc.used_inputs = list(IN.keys())
    return nc


def rowbc(k, dst, src_ap, name):
    k.p.dma(dst, k.p.dview(name, src_ap.partition_broadcast(128)))


def phase_mod(k):
    p, o, IN = k.p, k.o, k.IN
    p.push()
    cT = p.sb([2, 8]); sT = p.sb([2, 8])
    for r in range(2):
        p.dma(cT[:, r, :], p.dview("cvec", IN["cvec"].ap()[r].rearrange("(kt q) -> q kt", q=128)))
    o.act(sT, cT, AF.Silu)
    wt = [p.sb([8, 512]) for _ in range(2)]
    bm = p.sb([512]); res = [p.sb([512]) for _ in range(2)]
    i = 0
    for l in range(4):
        for cc in range(12):
            w = wt[i % 2]; r = res[i % 2]
            p.dma(w, p.dview("w_mod", IN["w_mod"].ap()[l, :, cc * 512:(cc + 1) * 512].rearrange("(kt q) f -> q kt f", q=128)))
            p.dma(bm[0:2, :], p.dview("b_mod", IN["b_mod"].ap()[l:l + 1, cc * 512:(cc + 1) * 512].partition_broadcast(2)))
            ps = p.ps(i % 2)
            for kt in range(8):
                o.mm(ps[0:2, :], sT[:, :, kt], w[:, kt, :], start=(kt == 0), stop=(kt == 7))
            o.tt(r[0:2, :], ps[0:2, :], bm[0:2, :], ALU.add)
            p.dma(p.dview("MOD", k.MOD.ap()[l, :, cc * 512:(cc + 1) * 512]), r[0:2, :], q="pool")
            i += 1
    p.pop()


def load_modT(k, l, off, plus1):
    p, o = k.p, k.o
    t = p.sb([2, 8])
    for r in range(2):
        p.dma(t[:, r, :], p.dview("MOD", k.MOD.ap()[l, r, off:off + D].rearrange("(kt q) -> q kt", q=128)))
    if plus1:
        o.ts(t, t, 1.0, None, ALU.add)
    return t


def build_inT(k, l, scT, shT, inT, extra=None):
    p, o = k.p, k.o
    p.push()
    ht = [p.sb([D]) for _ in range(2)]
    caps_ = []
    for t in range(NT):
        par = t % 2
        p.begin_capture()
        h = ht[t % 2]
        p.dma(h, p.dview("H", k.H.ap()[t * 128:(t + 1) * 128, :], t * 128 * D, (t + 1) * 128 * D))
        r = 1 if t < NCTX else 0
        for half in range(2):
            ps = p.ps(half + 2 * par)
            for q in range(4):
                kt = half * 4 + q
                o.tr(ps[:, q * 128:(q + 1) * 128], h[:, kt * 128:(kt + 1) * 128], k.C["IDENT"])
            for q in range(4):
                kt = half * 4 + q
                o.act(inT[:, kt, t * 128:(t + 1) * 128], ps[:, q * 128:(q + 1) * 128], AF.Identity,
                      bias=shT[:, r, kt:kt + 1], scale=scT[:, r, kt:kt + 1])
                if extra is not None:
                    extra(t, kt, ps[:, q * 128:(q + 1) * 128], r)
        caps_.append(p.end_capture())
        if par == 1 or t == NT - 1:
            p.replay(caps_)
            caps_ = []
    p.pop()


TOKCH = [(i * 512, 512) for i in range(8)] + [(4096, 256)]


def dbg_here(k, name):
    return name in k.dbg_on and k.cur == k.dbg_layer


def mixer_layer(k, l):
    p, o, IN = k.p, k.o, k.IN
    odd = (l % 2 == 1)
    e = l // 2
    C = k.C
    wname = "od_w_in" if odd else "ev_w_in"
    cname = "od_conv" if odd else "ev_conv"
    p.push()
    scT = load_modT(k, l, 1024, True)
    shT = load_modT(k, l, 0, False)
    inT = p.sb([8, TT], BF16)
    build_inT(k, l, scT, shT, inT)
    if k.stage <= 1:
        p.pop(); return
    if not odd:
        specs = [(0 + 128 * h, h, "l2q") for h in range(4)] + [(512 + 128 * h, 4 + h, "l2k") for h in range(4)] + \
                [(1024 + 128 * h, 8 + h, None) for h in range(4)] + [(2064 + 128 * h, None, None) for h in range(4)] + \
                [(2576 + 128 * h, None, "scale") for h in range(4)] + [(3088 + 128 * h, None, None) for h in range(4)]
        nconv = 12
    else:
        specs = [(0 + 128 * h, h, None) for h in range(4)] + [(512 + 128 * h, 4 + h, "scale") for h in range(4)] + \
                [(1024 + 128 * h, None, None) for h in range(4)] + [(2064 + 128 * h, None, None) for h in range(4)]
        nconv = 8
    p.push()
    p.begin_capture()
    wconv = p.sb([nconv, 9])
    for t12 in range(nconv):
        p.dma(wconv[:, t12, :], p.dview(cname, IN[cname].ap()[e].rearrange("a b c -> c (a b)")[t12 * 128:(t12 + 1) * 128, :]))
    wf = [p.sb([8, 128]) for _ in range(2)]
    wb = [p.sb([8, 128], BF16) for _ in range(2)]
    ft = [p.sb([TT]) for _ in range(2)]
    yt = p.sb([TT])
    sq = p.sb([512]); rs = p.sb([512])
    for s, (col, cv, mode) in enumerate(specs):
        w32, w16, X = wf[s % 2], wb[s % 2], ft[s % 2]
        p.dma(w32, p.dview(wname, IN[wname].ap()[e, :, col:col + 128].rearrange("(kt q) f -> q kt f", q=128)))
        o.cp(w16, w32, eng="pool")
        for ci, (t0, n) in enumerate(TOKCH):
            ps = p.ps(2 + ci % 2, n)
            for kt in range(8):
                o.mm(ps, w16[:, kt, :], inT[:, kt, t0:t0 + n], start=(kt == 0), stop=(kt == 7))
            o.cp(X[:, t0:t0 + n], ps, eng=("act" if ci % 2 else "dve"))
        if cv is not None:
            wc = wconv[:, cv, :]
            Y = yt
            o.ts(Y[:, 0:256], X[:, 0:256], wc[:, 4:5], None, ALU.mult)
            o.stt(Y[:, 1:256], X[:, 0:255], wc[:, 3:4], Y[:, 1:256], ALU.mult, ALU.add)
            o.stt(Y[:, 0:255], X[:, 1:256], wc[:, 5:6], Y[:, 0:255], ALU.mult, ALU.add)
            Xg = X.v(X.ap[:, 256:TT].rearrange("q (r c) -> q r c", c=64))
            Yg = Y.v(Y.ap[:, 256:TT].rearrange("q (r c) -> q r c", c=64))
            o.ts(Y[:, 256:TT], X[:, 256:TT], wc[:, 4:5], None, ALU.mult)
            for dy in range(3):
                for dx in range(3):
                    if dy == 1 and dx == 1:
                        continue
                    oy, ox = dy - 1, dx - 1
                    r0, r1 = max(0, -oy), 64 - max(0, oy)
                    c0, c1 = max(0, -ox), 64 - max(0, ox)
                    o.stt(Yg[:, r0:r1, c0:c1], Xg[:, r0 + oy:r1 + oy, c0 + ox:c1 + ox], wc[:, dy * 3 + dx:dy * 3 + dx + 1],
                          Yg[:, r0:r1, c0:c1], ALU.mult, ALU.add)
            o.act(X, Y, AF.Silu)
        if mode in ("l2q", "l2k"):
            for ci, (t0, n) in enumerate(TOKCH):
                o.tt(sq[:, 0:n], X[:, t0:t0 + n], X[:, t0:t0 + n], ALU.mult)
                ps = p.ps(4 + ci % 2, n)
                o.mm(ps, C["ONES"], sq[:, 0:n])
                o.act(rs[:, 0:n], ps, AF.Sqrt, bias=k.eps[:, 1:2])
                o.recip(sq[:, 0:n], rs[:, 0:n])
                if mode == "l2q":
                    o.stt(X[:, t0:t0 + n], X[:, t0:t0 + n], 128 ** -0.5, sq[:, 0:n], ALU.mult, ALU.mult)
                else:
                    o.tt(X[:, t0:t0 + n], X[:, t0:t0 + n], sq[:, 0:n], ALU.mult)
        elif mode == "scale":
            o.ts(X, X, 128 ** -0.5, None, ALU.mult)
        p.dma(p.dview("FM", k.FM.ap()[s], s * 128 * TT, (s + 1) * 128 * TT), X, q="pool")
    cap_b = p.end_capture()
    p.push()
    p.begin_capture()
    wz = p.sb([8, 1040], BF16)
    wst = [p.sb([8, 260]) for _ in range(2)]
    zcols = [(1536, 512), (2048, 16), (3600, 512)] if not odd else [(1536, 512), (2048, 16), (2064, 512)]
    dst = 0
    i = 0
    for (c0, n) in zcols:
        for j in range(0, n, 260):
            m = min(260, n - j)
            w32 = wst[i % 2]
            p.dma(w32[:, :, 0:m], p.dview(wname, IN[wname].ap()[e, :, c0 + j:c0 + j + m].rearrange("(kt q) f -> q kt f", q=128)))
            o.cp(wz[:, :, dst:dst + m], w32[:, :, 0:m], eng="pool")
            dst += m
            i += 1
    if not odd:
        alog = p.sb([8]); dtb = p.sb([8]); nea = p.sb([8])
        rowbc(k, alog, IN["ev_a_log"].ap()[e:e + 1].rearrange("a d h -> a (d h)"), "ev_a_log")
        rowbc(k, dtb, IN["ev_dt_bias"].ap()[e:e + 1].rearrange("a d h -> a (d h)"), "ev_dt_bias")
        o.act(nea, alog, AF.Exp)
    else:
        gbi = p.sb([8]); gbf = p.sb([8])
        for d in range(2):
            rowbc(k, gbi[:, d * 4:(d + 1) * 4], IN["od_gate_bias"].ap()[e, d, 0:1, :], "od_gate_bias")
            rowbc(k, gbf[:, d * 4:(d + 1) * 4], IN["od_gate_bias"].ap()[e, d, 1:2, :], "od_gate_bias")
    zt = [p.sb([D]) for _ in range(2)]
    gt = [p.sb([16]) for _ in range(2)]
    tmp8 = p.sb([8]); tmp8b = p.sb([8])
    for t in range(NT):
        z, g = zt[t % 2], gt[t % 2]
        lt = lambda kt: inT[:, kt, t * 128:(t + 1) * 128]
        psa, psb, psg = p.ps(0), p.ps(1), p.ps(6, 16)
        for kt in range(8):
            o.mm(psa, lt(kt), wz[:, kt, 0:512], start=(kt == 0), stop=(kt == 7))
        for kt in range(8):
            o.mm(psb, lt(kt), wz[:, kt, 528:1040], start=(kt == 0), stop=(kt == 7))
        for kt in range(8):
            o.mm(psg, lt(kt), wz[:, kt, 512:528], start=(kt == 0), stop=(kt == 7))
        if not odd:
            o.act(z[:, 0:512], psa, AF.Silu)
            o.act(z[:, 512:1024], psb, AF.Silu)
            o.tt(tmp8, psg[:, 0:8], dtb, ALU.add)
            o.act(g[:, 0:8], tmp8, AF.Exp)
            o.act(tmp8, g[:, 0:8], AF.Ln, bias=k.eps[:, 2:3])
            o.stt(g[:, 0:8], tmp8, -1.0, nea, ALU.mult, ALU.mult)
            o.act(g[:, 8:16], psg[:, 8:16], AF.Sigmoid)
        else:
            o.act(z[:, 0:512], psa, AF.Sigmoid)
            o.cp(z[:, 512:1024], psb, eng="dve")
            o.tt(tmp8, psg[:, 8:16], gbf, ALU.add)
            o.act(tmp8b, tmp8, AF.Exp, scale=-1.0)
            o.act(tmp8, tmp8b, AF.Ln, bias=k.eps[:, 2:3])
            o.ts(g[:, 0:8], tmp8, -1.0, None, ALU.mult)
            o.tt(tmp8b, psg[:, 0:8], gbi, ALU.add)
            o.act(g[:, 8:16], tmp8b, AF.Exp)
        p.dma(p.dview("ZS", k.ZS.ap()[t * 128:(t + 1) * 128, :], t * 128 * D, (t + 1) * 128 * D), z, q="pool")
        p.dma(p.dview("GATES", k.GATES.ap()[t * 128:(t + 1) * 128, :], t * 128 * 16, (t + 1) * 128 * 16), g, q="pool")
    cap_c = p.end_capture()
    p.replay([cap_b, cap_c])
    p.pop()
    p.pop()
    p.pop()
    if k.stage <= 3:
        return
    if not odd:
        scan_layer(k, l, [0, 1])
    else:
        scan_layer(k, l, [2])
        if k.stage > 4:
            s5_scan(k, l)
    if k.stage <= 4:
        return
    if not odd:
        merge_even(k, l)
    else:
        merge_odd(k, l)
    if k.stage <= 5:
        return
    moe_layer2(k, l)


def chain_order(d):
    return list(range(NT)) if d == 0 else [1, 0] + list(range(NT - 1, 1, -1))


def scan_layer(k, l, types):
    p, o, IN, C = k.p, k.o, k.IN, k.C
    import os
    if 'OPLIMIT' in os.environ:
        p.limit = int(os.environ['OPLIMIT']); p.count = 0
    p.push()
    gates = p.sb([NT, 16])
    p.dma(gates, p.dview("GATES", k.GATES.ap().rearrange("(c q) g -> q c g", q=128)))
    retg = p.sb([8])
    negb = p.sb([NT, 8])
    if 1 in types:
        rowbc(k, retg, IN["retg"].ap(), "retg")
    if 0 in types:
        o.ts(negb, gates[:, :, 8:16], -1.0, None, ALU.mult)
    RING = 8
    W = lambda n=128: [p.sb([n]) for _ in range(RING)]
    names = ["qT", "kT", "vT", "G1", "cc", "tmp", "ET", "EXPR", "kgT", "qgT", "kend", "bv", "E", "N", "M", "N2", "M2",
             "TTa", "TTb", "AQ", "br", "vn", "o", "o2", "tmp2", "sc"]
    ring = {n: (W(132) if n in ("vn", "o") else W()) for n in names}
    chains = []
    for typ in types:
        for h in range(4):
            for d in range(2):
                chains.append(dict(typ=typ, h=h, d=d, S=[p.sb([132]), p.sb([132])], order=chain_order(d), dec=None))
    for ch in chains:
        o.memset(ch["S"][0], 0.0, eng="dve")
    step_i = [0]

    def decay(ch, gcol, slot):
        d = ch["d"]
        R = {n: ring[n][slot] for n in names}
        CS = C["CSf"] if d == 0 else C["CSb"]
        MT = C["MTf"] if d == 0 else C["MTb"]
        o.ts(R["G1"], C["ONES"], gcol, None, ALU.mult)
        pb = ch["pb"]
        psr = p.ps(pb + 0, 128)
        psc = p.ps(pb + 0, 256, col0=128)
        o.mm(psr, R["G1"], CS)
        o.mm(psc[:, 0:128], CS, R["G1"])
        o.mm(psc[:, 128:256], C["ONES"], R["G1"])
        cc = R["cc"]
        o.cp(cc[:, 0:1], psc[:, 0:1], eng="act")
        o.cp(cc[:, 1:2], psc[:, 128:129], eng="act")
        o.stt(R["tmp"], psr, cc[:, 0:1], MT, ALU.subtract, ALU.add)
        o.act(R["ET"], R["tmp"], AF.Exp)
        o.act(R["EXPR"], psr, AF.Exp)
        o.tt(cc[:, 4:5], cc[:, 1:2], cc[:, 0:1], ALU.subtract)
        o.act(cc[:, 2:3], cc[:, 4:5], AF.Exp)
        o.act(cc[:, 3:4], cc[:, 1:2], AF.Exp)
        return dict(ET=R["ET"], EXPR=R["EXPR"], cc=cc, psr=psr)

    def step(ch, c, si):
        typ, h, d = ch["typ"], ch["h"], ch["d"]
        slot = step_i[0] % RING
        step_i[0] += 1
        R = {n: ring[n][slot] for n in names}
        sq, sk, sv = (h, 4 + h, 8 + h) if typ != 1 else (12 + h, 16 + h, 20 + h)
        dv = 129 if typ == 2 else 128
        tok = slice(c * 128, (c + 1) * 128)
        for nm, s in (("qT", sq), ("kT", sk), ("vT", sv)):
            p.dma(R[nm], p.dview("FM", k.FM.ap()[s][:, tok], s * 128 * TT, (s + 1) * 128 * TT))
        qT, kT, vT = R["qT"], R["kT"], R["vT"]
        if typ != 1:
            gcol = gates[:, c, d * 4 + h:d * 4 + h + 1]
            dec = decay(ch, gcol, slot)
        else:
            if ch["dec"] is None:
                gcol = retg[:, d * 4 + h:d * 4 + h + 1]
                dslot = 0
                own = {n: p.sb([128]) for n in ["G1", "cc", "tmp", "ET", "EXPR"]}
                save = {n: ring[n][dslot] for n in own}
                for n in own:
                    ring[n][dslot] = own[n]
                ch["dec"] = decay(ch, gcol, dslot)
                for n in own:
                    ring[n][dslot] = save[n]
            dec = ch["dec"]
        cc = dec["cc"]
        pb = ch["pb"]
        pst = p.ps(pb + 1, 256)
        o.tr(pst[:, 0:128], kT, C["IDENT"])
        o.tr(pst[:, 128:256], vT, C["IDENT"])
        if typ == 2:
            ei = gates[:, c, 8 + d * 4 + h:8 + d * 4 + h + 1]
            o.ts(R["kend"], pst[:, 0:128], cc[:, 2:3], ei, ALU.mult, ALU.mult)
        else:
            o.ts(R["kend"], pst[:, 0:128], cc[:, 2:3], None, ALU.mult)
        o.tt(R["qgT"], qT, dec["EXPR"], ALU.mult)
        psq = p.ps(pb + 0, 128, col0=384)
        o.mm(psq, kT, qT)
        if typ == 2:
            o.stt(R["AQ"], psq, ei, dec["ET"], ALU.mult, ALU.mult)
        else:
            o.tt(R["AQ"], psq, dec["ET"], ALU.mult)
        S = ch["S"][si % 2][:, 0:dv]
        Sn = ch["S"][(si + 1) % 2][:, 0:dv]
        if typ == 0:
            MS = C["MSf"] if d == 0 else C["MSb"]
            nb = negb[:, c, d * 4 + h:d * 4 + h + 1]
            o.ts(R["bv"], pst[:, 128:256], gates[:, c, 8 + d * 4 + h:8 + d * 4 + h + 1], None, ALU.mult)
            o.tt(R["kgT"], kT, dec["EXPR"], ALU.mult)
            o.stt(R["tmp2"], dec["psr"], cc[:, 0:1], MS, ALU.subtract, ALU.subtract)
            o.act(R["E"], R["tmp2"], AF.Exp, scale=-1.0)
            psk = p.ps(pb + 1, 128, col0=256)
            o.mm(psk, kT, kT)
            o.stt(R["N"], psk, nb, R["E"], ALU.mult, ALU.mult)
            psm = p.ps(pb + 1, 128, col0=384)
            o.tr(psm, R["N"], C["IDENT"])
            o.cp(R["M"], psm, eng="act")
            o.tt(R["TTa"], C["IDENT"], R["M"], ALU.add)
            Nc, Mc, Nn, Mn = R["N"], R["M"], R["N2"], R["M2"]
            Tc, Tn = R["TTa"], R["TTb"]
            for lev in range(1, 7):
                ps1 = p.ps(pb + 1, 128, col0=256)
                o.mm(ps1, Mc, Nc)
                o.cp(Nn, ps1, eng="act")
                if lev < 6:
                    ps2 = p.ps(pb + 1, 128, col0=384)
                    o.mm(ps2, Nc, Mc)
                    o.cp(Mn, ps2, eng="dve")
                ps3 = p.ps(pb + 1, 128, col0=256)
                o.mm(ps3, Nn, Tc)
                o.tt(Tn, ps3, Tc, ALU.add)
                Nc, Nn = Nn, Nc
                Mc, Mn = Mn, Mc
                Tc, Tn = Tn, Tc
            psr2 = p.ps(pb + 1, 128, col0=256)
            o.mm(psr2, R["kgT"], S)
            o.stt(R["br"], psr2, nb, R["bv"], ALU.mult, ALU.add)
            psv = p.ps(pb + 1, 128, col0=256)
            o.mm(psv, Tc, R["br"])
            o.cp(R["vn"][:, 0:128], psv, eng="act")
        else:
            o.cp(R["vn"][:, 0:128], pst[:, 128:256], eng="act")
            if typ == 2:
                o.memset(R["vn"][:, 128:129], 1.0, eng="dve")
        vn = R["vn"][:, 0:dv]
        pso = p.ps(pb + 0, dv)
        o.mm(pso, R["qgT"], S, start=True, stop=False)
        o.mm(pso, R["AQ"], vn, start=False, stop=True)
        o.cp(R["o"][:, 0:dv], pso, eng="act")
        if typ == 2:
            p.dma(p.dview("OUTO%d_%d" % (h, d), k.OUTO.ap()[h, d, tok, 0:dv], c * 128 * 132, (c + 1) * 128 * 132), R["o"][:, 0:dv], q="pool")
        else:
            slot8 = typ * 4 + h
            p.dma(p.dview("OUTM%d_%d" % (slot8, d), k.OUTM.ap()[slot8, d, tok, :], c * 128 * 128, (c + 1) * 128 * 128), R["o"][:, 0:128], q="pool")
        pss = p.ps(pb + 1, dv)
        o.mm(pss, R["kend"], vn)
        o.stt(Sn, S, cc[:, 3:4], pss, ALU.mult, ALU.add)

    for si in range(NT):
        for c0 in range(0, len(chains), 4):
            caps = []
            for j, ch in enumerate(chains[c0:c0 + 4]):
                ch["pb"] = 2 * j
                p.begin_capture()
                step(ch, ch["order"][si], si)
                caps.append(p.end_capture())
            p.replay(caps)
    p.pop()


def merge_even(k, l):
    p, o, IN, C = k.p, k.o, k.IN, k.C
    e = l // 2
    p.push()
    wout = p.sb([8, D], BF16)
    wst = [p.sb([8, 256]) for _ in range(2)]
    for j in range(4):
        p.dma(wst[j % 2], p.dview("ev_w_out", IN["ev_w_out"].ap()[e, :, j * 256:(j + 1) * 256].rearrange("(kt q) f -> q kt f", q=128)))
        o.cp(wout[:, :, j * 256:(j + 1) * 256], wst[j % 2], eng="pool")
    gg = p.sb([128]); rg = p.sb([512])
    rowbc(k, gg, IN["ev_gdn_norm"].ap()[e:e + 1, :], "ev_gdn_norm")
    rowbc(k, rg, IN["ev_ret_norm"].ap()[e:e + 1, :], "ev_ret_norm")
    g1 = [p.sb([D]), p.sb([D])]
    for r in range(2):
        rowbc(k, g1[r], k.MOD.ap()[l, r:r + 1, 2048:3072], "MOD")
    lng = p.sb([D]); lnb = p.sb([D])
    rowbc(k, lng, IN["ln1_g"].ap()[l:l + 1, :], "ln1_g")
    rowbc(k, lnb, IN["ln1_b"].ap()[l:l + 1, :], "ln1_b")
    of = [p.sb([8, 128]) for _ in range(2)]
    ob = [p.sb([8, 128]) for _ in range(2)]
    zt = [p.sb([D]) for _ in range(2)]
    ht = [p.sb([D]) for _ in range(2)]
    Y = [p.sb([D]) for _ in range(2)]
    YT = [p.sb([8, 128], BF16) for _ in range(2)]
    st = [p.sb([32]) for _ in range(2)]
    junks = [p.sb([D]) for _ in range(2)]
    U = [p.sb([D]) for _ in range(2)]
    caps_ = []
    for t in range(NT):
        par = t % 2
        junk = junks[par]
        p.begin_capture()
        r = 1 if t < NCTX else 0
        a, b, z, h, y, yT, s, u = of[t % 2], ob[t % 2], zt[t % 2], ht[t % 2], Y[t % 2], YT[t % 2], st[t % 2], U[t % 2]
        tok = slice(t * 128, (t + 1) * 128)
        for hs in range(8):
            p.dma(a[:, hs, :], p.dview("OUTM%d_0" % hs, k.OUTM.ap()[hs, 0, tok, :], t * 128 * 128, (t + 1) * 128 * 128))
            p.dma(b[:, hs, :], p.dview("OUTM%d_1" % hs, k.OUTM.ap()[hs, 1, tok, :], t * 128 * 128, (t + 1) * 128 * 128))
        p.dma(z, p.dview("ZS", k.ZS.ap()[tok, :], t * 128 * D, (t + 1) * 128 * D))
        p.dma(h, p.dview("H", k.H.ap()[tok, :], t * 128 * D, (t + 1) * 128 * D))
        o.tt(a, a, b, ALU.add)
        for hs in range(8):
            o.act(junk[:, 0:128], a[:, hs, :], AF.Square, accum=s[:, hs:hs + 1])
        for hs in range(4, 8):
            o.act(junk[:, 0:128], a[:, hs, :], AF.Identity, accum=s[:, 8 + hs:9 + hs])
        o.act(s[:, 16:20], s[:, 0:4], AF.Sqrt, bias=k.eps[:, 0:1], scale=1.0 / 128)
        o.recip(s[:, 20:24], s[:, 16:20])
        o.ts(s[:, 24:28], s[:, 12:16], 1.0 / 128, None, ALU.mult)
        o.tt(s[:, 28:32], s[:, 24:28], s[:, 24:28], ALU.mult)
        o.stt(s[:, 16:20], s[:, 4:8], 1.0 / 128, s[:, 28:32], ALU.mult, ALU.subtract)
        o.act(s[:, 28:32], s[:, 16:20], AF.Sqrt, bias=k.eps[:, 0:1])
        o.recip(s[:, 16:20], s[:, 28:32])
        for hs in range(4):
            o.stt(y[:, hs * 128:(hs + 1) * 128], a[:, hs, :], s[:, 20 + hs:21 + hs], gg, ALU.mult, ALU.mult)
        for hs in range(4):
            o.ts(junk[:, 0:128], a[:, 4 + hs, :], s[:, 24 + hs:25 + hs], s[:, 16 + hs:17 + hs], ALU.subtract, ALU.mult)
            o.tt(y[:, 512 + hs * 128:512 + (hs + 1) * 128], junk[:, 0:128], rg[:, hs * 128:(hs + 1) * 128], ALU.mult)
        o.tt(y, y, z, ALU.mult)
        for half in range(2):
            ps = p.ps(half + 4 * par)
            for q in range(4):
                kt = half * 4 + q
                o.tr(ps[:, q * 128:(q + 1) * 128], y[:, kt * 128:(kt + 1) * 128], C["IDENT"])
            o.cp(yT.v(yT.ap[:, half * 4:(half + 1) * 4, :].rearrange("q a b -> q (a b)")), ps, eng="act")
        for half in range(2):
            ps = p.ps(2 + half + 4 * par)
            for kt in range(8):
                o.mm(ps, yT[:, kt, :], wout[:, kt, half * 512:(half + 1) * 512], start=(kt == 0), stop=(kt == 7))
            hs_ = slice(half * 512, (half + 1) * 512)
            o.tt(u[:, hs_], ps, g1[r][:, hs_], ALU.mult)
        if dbg_here(k, "y"):
            p.dma(p.dview("y_out", k.YOUT.ap()[tok, :], t * 128 * D, (t + 1) * 128 * D), u, q="pool")
        o.stt(u, h, ALPHA, u, ALU.mult, ALU.add)
        layer_norm(k, u, h, lng, lnb, s, junk)
        p.dma(p.dview("H", k.H.ap()[tok, :], t * 128 * D, (t + 1) * 128 * D), h, q="pool")
        if dbg_here(k, "h1"):
            p.dma(p.dview("y_out", k.YOUT.ap()[tok, :], t * 128 * D, (t + 1) * 128 * D), h, q="pool")
        caps_.append(p.end_capture())
        if par == 1 or t == NT - 1:
            p.replay(caps_)
            caps_ = []
    p.pop()


def layer_norm(k, u, out, g, b, s, junk):
    o = k.o
    o.act(junk, u, AF.Identity, accum=s[:, 0:1])
    o.act(junk, u, AF.Square, accum=s[:, 1:2])
    o.ts(s[:, 2:3], s[:, 0:1], 1.0 / D, None, ALU.mult)
    o.tt(s[:, 3:4], s[:, 2:3], s[:, 2:3], ALU.mult)
    o.stt(s[:, 4:5], s[:, 1:2], 1.0 / D, s[:, 3:4], ALU.mult, ALU.subtract)
    o.act(s[:, 5:6], s[:, 4:5], AF.Sqrt, bias=k.eps[:, 0:1])
    o.recip(s[:, 6:7], s[:, 5:6])
    o.ts(junk, u, s[:, 2:3], s[:, 6:7], ALU.subtract, ALU.mult)
    o.tt(junk, junk, g, ALU.mult)
    o.tt(out, junk, b, ALU.add)


def moe_layer(k, l, update_ctx=True):
    p, o, IN, C = k.p, k.o, k.IN, k.C
    p.push()
    g2 = [p.sb([D]), p.sb([D])]
    for r in range(2):
        rowbc(k, g2[r], k.MOD.ap()[l, r:r + 1, 5120:6144], "MOD")
    lng = p.sb([D]); lnb = p.sb([D])
    rowbc(k, lng, IN["ln2_g"].ap()[l:l + 1, :], "ln2_g")
    rowbc(k, lnb, IN["ln2_b"].ap()[l:l + 1, :], "ln2_b")
    wr = p.sb([8, 16])
    p.dma(wr, p.dview("w_router", IN["w_router"].ap()[l].rearrange("(kt q) e -> q kt e", q=128)))
    in2T = p.sb([8, TT], BF16)
    aff = p.sb([NT, 16]); gw = p.sb([NT, 16])
    p.push()
    affT = p.sb([TT])
    sc2 = [p.sb([D]), p.sb([D])]; sh2 = [p.sb([D]), p.sb([D])]
    for r in range(2):
        rowbc(k, sc2[r], k.MOD.ap()[l, r:r + 1, 4096:5120], "MOD")
        o.ts(sc2[r], sc2[r], 1.0, None, ALU.add)
        rowbc(k, sh2[r], k.MOD.ap()[l, r:r + 1, 3072:4096], "MOD")
    p.push()
    ht = [p.sb([D]) for _ in range(2)]
    x2 = [p.sb([D]) for _ in range(2)]
    xTf = [p.sb([8, 128]) for _ in range(2)]
    sm = [p.sb([40]) for _ in range(2)]
    for t in range(NT):
        r = 1 if t < NCTX else 0
        h, x, xf, s = ht[t % 2], x2[t % 2], xTf[t % 2], sm[t % 2]
        tok = slice(t * 128, (t + 1) * 128)
        p.dma(h, p.dview("H", k.H.ap()[tok, :], t * 128 * D, (t + 1) * 128 * D))
        o.tt(x, h, sc2[r], ALU.mult)
        o.tt(x, x, sh2[r], ALU.add)
        for half in range(2):
            ps = p.ps(half)
            for q in range(4):
                kt = half * 4 + q
                o.tr(ps[:, q * 128:(q + 1) * 128], x[:, kt * 128:(kt + 1) * 128], C["IDENT"])
            ps3 = ps.v(ps.ap.rearrange("q (a b) -> q a b", a=4))
            o.cp(xf[:, half * 4:(half + 1) * 4, :], ps3, eng="act")
            o.cp(in2T[:, half * 4:(half + 1) * 4, tok], ps3, eng="dve")
        psl = p.ps(2, 16)
        for kt in range(8):
            o.mm(psl, xf[:, kt, :], wr[:, kt, :], start=(kt == 0), stop=(kt == 7))
        p.op("dve", lambda e, s=s, psl=psl: e.reduce_max(s.ap[:, 0:1], psl.ap, AX.X), [psl], [s])
        o.ts(s[:, 1:2], s[:, 0:1], -1.0, None, ALU.mult)
        o.act(s[:, 8:24], psl, AF.Exp, bias=s[:, 1:2], accum=s[:, 2:3])
        o.recip(s[:, 3:4], s[:, 2:3])
        o.ts(aff[:, t, :], s[:, 8:24], s[:, 3:4], None, ALU.mult)
        pst = p.ps(3, 128)
        o.tr(pst[0:16, :], aff[:, t, :], C["IDENT"])
        o.cp(affT[0:16, tok], pst[0:16, :], eng="act")
    p.pop()
    p.push()
    st = p.sb([16]); junk = p.sb([4096]); thr = [p.sb([16]), p.sb([16])]; dt = p.sb([16])
    sets = [(0, 256, 32.0, 1), (256, TT, 512.0, 0)]
    for (c0, c1, cap, r) in sets:
        lo, hi, mid, cnt, ge, d1 = [st[0:16, i:i + 1] for i in range(6)]
        o.memset(lo, 0.0, eng="dve"); o.memset(hi, 1.0, eng="dve")
        for it in range(32):
            o.tt(mid, lo, hi, ALU.add)
            o.ts(mid, mid, 0.5, None, ALU.mult)
            o.ts(junk[0:16, 0:c1 - c0], affT[0:16, c0:c1], mid, None, ALU.is_ge, ALU.add, accum=cnt)
            o.ts(ge, cnt, cap - 0.5, None, ALU.is_ge)
            o.tt(d1, mid, lo, ALU.subtract)
            o.stt(lo, d1, ge, lo, ALU.mult, ALU.add)
            o.tt(d1, hi, mid, ALU.subtract)
            o.stt(hi, d1, ge, mid, ALU.mult, ALU.add)
        o.ts(dt[0:16, 0:16], C["IDENT"][0:16, 0:16], lo, None, ALU.mult)
        pth = p.ps(2, 16)
        o.mm(pth, C["ONES"][0:16, :], dt[0:16, 0:16])
        o.cp(thr[r], pth, eng="act")
    for t in range(NT):
        r = 1 if t < NCTX else 0
        o.tt(gw[:, t, :], aff[:, t, :], thr[r], ALU.is_ge)
        o.tt(gw[:, t, :], gw[:, t, :], aff[:, t, :], ALU.mult)
    p.pop()
    p.pop()
    p.push()
    wgf = [p.sb([8, 256]) for _ in range(2)]; wuf = [p.sb([8, 256]) for _ in range(2)]
    wgb = [p.sb([8, 256], BF16) for _ in range(2)]; wub = [p.sb([8, 256], BF16) for _ in range(2)]
    wdf = [p.sb([2, 512]) for _ in range(2)]; wdb = [p.sb([2, 512], BF16) for _ in range(2)]
    hT = p.sb([16, 512], BF16)
    sg = [p.sb([512]) for _ in range(2)]
    acc = [p.sb([D]) for _ in range(4)]
    hh = [p.sb([D])] * 2
    s8 = p.sb([8]); junk2 = p.sb([D])
    wi = 0
    di = 0
    import os
    nexp = int(os.environ.get("MOE_NEXP", 16))
    for (t0, n) in TOKCH:
        nst = n // 128
        for e in range(nexp):
            for fg in range(8):
                a32, b32, a16, b16 = wgf[wi % 2], wuf[wi % 2], wgb[wi % 2], wub[wi % 2]
                wi += 1
                p.dma(a32, p.dview("w_gate", IN["w_gate"].ap()[l, e, :, fg * 256:(fg + 1) * 256].rearrange("(kt q) f -> q kt f", q=128)))
                p.dma(b32, p.dview("w_up", IN["w_up"].ap()[l, e, :, fg * 256:(fg + 1) * 256].rearrange("(kt q) f -> q kt f", q=128)))
                o.cp(a16, a32, eng="pool")
                o.cp(b16, b32, eng="pool")
                for f2 in range(2):
                    ft = fg * 2 + f2
                    psg = p.ps(4 + (ft % 2) * 2, n)
                    psu = p.ps(5 + (ft % 2) * 2, n)
                    for kt in range(8):
                        o.mm(psg, a16[:, kt, f2 * 128:(f2 + 1) * 128], in2T[:, kt, t0:t0 + n], start=(kt == 0), stop=(kt == 7))
                    for kt in range(8):
                        o.mm(psu, b16[:, kt, f2 * 128:(f2 + 1) * 128], in2T[:, kt, t0:t0 + n], start=(kt == 0), stop=(kt == 7))
                    s_ = sg[ft % 2]
                    o.act(s_[:, 0:n], psg, AF.Silu)
                    o.tt(hT[:, ft, 0:n], s_[:, 0:n], psu, ALU.mult)
            for half in range(2):
                psd = [p.ps(st_, 512) for st_ in range(nst)]
                for fp_ in range(8):
                    d32, d16 = wdf[di % 2], wdb[di % 2]
                    di += 1
                    p.dma(d32, p.dview("w_down", IN["w_down"].ap()[l, e, fp_ * 256:(fp_ + 1) * 256, half * 512:(half + 1) * 512].rearrange("(a q) d -> q a d", q=128)))
                    o.cp(d16, d32, eng="pool")
                    for f2 in range(2):
                        ft = fp_ * 2 + f2
                        for st_ in range(nst):
                            o.mm(psd[st_], hT[:, ft, st_ * 128:(st_ + 1) * 128], d16[:, f2, :], start=(ft == 0), stop=(ft == 15))
                for st_ in range(nst):
                    t = t0 // 128 + st_
                    a_ = acc[st_][:, half * 512:(half + 1) * 512]
                    if e == 0:
                        o.ts(a_, psd[st_], gw[:, t, e:e + 1], None, ALU.mult)
                    else:
                        o.stt(a_, psd[st_], gw[:, t, e:e + 1], a_, ALU.mult, ALU.add)
        for st_ in range(nst):
            t = t0 // 128 + st_
            r = 1 if t < NCTX else 0
            tok = slice(t * 128, (t + 1) * 128)
            if dbg_here(k, "f"):
                p.dma(p.dview("y_out", k.YOUT.ap()[tok, :], t * 128 * D, (t + 1) * 128 * D), acc[st_], q="pool")
            h = hh[st_ % 2]
            p.dma(h, p.dview("H", k.H.ap()[tok, :], t * 128 * D, (t + 1) * 128 * D))
            u = acc[st_]
            o.tt(u, u, g2[r], ALU.mult)
            o.stt(u, h, ALPHA, u, ALU.mult, ALU.add)
            layer_norm(k, u, h, lng, lnb, s8, junk2)
            if update_ctx or r == 0:
                p.dma(p.dview("H", k.H.ap()[tok, :], t * 128 * D, (t + 1) * 128 * D), h, q="pool")
    p.pop()
    p.pop()


def rev(t, n):
    a = t.ap
    st = a.ap[-1][0]
    return t.v(bass.AP(a.tensor, a.offset + (n - 1) * st, [list(a.ap[0]), [-st, n]]))


def bcast_cols(t, n):
    a = t.ap
    return t.v(bass.AP(a.tensor, a.offset, [list(a.ap[0]), [0, n]]))


SIN_C = [-1.0 / 6, 1.0 / 120, -1.0 / 5040, 1.0 / 362880, -1.0 / 39916800]
COS_C = [-0.5, 1.0 / 24, -1.0 / 720, 1.0 / 40320, -1.0 / 3628800, 1.0 / 479001600]


def s5_scan(k, l):
    p, o, IN, C = k.p, k.o, k.IN, k.C
    e = l // 2
    MK = C["MK"]
    p.push()
    WCf = [[p.sb([128]) for _ in range(16)] for _ in range(2)]
    WB = [[[p.sb([128]) for _ in range(16)] for _ in range(2)] for _ in range(2)]
    mag = [p.sb([16]) for _ in range(2)]
    CLv = [p.sb([16, 10]) for _ in range(2)]; SLv = [p.sb([16, 10]) for _ in range(2)]; NSLv = [p.sb([16, 10]) for _ in range(2)]
    p.push()
    BR = p.sb([16, 16]); BI = p.sb([16, 16])
    CLr = p.sb([16, 64]); CLi = p.sb([16, 64])
    for gl in range(2):
        pr = slice(gl * 64, (gl + 1) * 64)
        p.dma(BR[pr], p.dview("od_b_re", IN["od_b_re"].ap()[e].rearrange("(st gl) q h -> gl q st h", gl=2)[gl]))
        p.dma(BI[pr], p.dview("od_b_im", IN["od_b_im"].ap()[e].rearrange("(st gl) q h -> gl q st h", gl=2)[gl]))
        pc = slice(gl * 16, (gl + 1) * 16)
        p.dma(CLr[pc], p.dview("od_c_re", IN["od_c_re"].ap()[e].rearrange("(st gl) h q -> gl h st q", gl=2)[gl]))
        p.dma(CLi[pc], p.dview("od_c_im", IN["od_c_im"].ap()[e].rearrange("(st gl) h q -> gl h st q", gl=2)[gl]))
    X = [p.sb([128]) for _ in range(2)]
    for ri, CLx in enumerate((CLr, CLi)):
        for st in range(16):
            s4 = st % 4
            x = X[st % 2]
            for gl2 in range(2):
                o.ts(x[0:32, gl2 * 64:(gl2 + 1) * 64], CLx[0:32, st, :], MK[0:32, 2 + gl2:3 + gl2], None, ALU.mult)
            ps = p.ps(st % 2, 32)
            o.tr(ps, x[0:32, :], C["IDENT"][0:32, 0:32])
            w = WCf[ri][st]
            o.memset(w, 0.0, eng="pool")
            if ri == 0:
                o.cp(w[:, s4 * 32:(s4 + 1) * 32], ps, eng="act")
            else:
                o.ts(w[:, s4 * 32:(s4 + 1) * 32], ps, -1.0, None, ALU.mult)
    LR = p.sb([16]); LI = p.sb([16]); DT = p.sb([16])
    tl = {n: p.sb([16]) for n in ["lr", "dt", "a", "th", "x", "z", "q", "s", "c", "cc", "ss", "cs", "abr", "abi", "xr", "den",
                                  "t1", "t2", "fre", "fim", "nfim"]}
    fm = {n: p.sb([16]) for n in ["fre0", "fre1", "fim0", "fim1", "nfim0", "nfim1"]}
    INre = [p.sb([128]) for _ in range(4)]; INim = [p.sb([128]) for _ in range(4)]
    tb = p.sb([16])
    for d in range(2):
        for gl in range(2):
            pr = slice(gl * 64, (gl + 1) * 64)
            p.dma(LR[pr], p.dview("od_lam_re", IN["od_lam_re"].ap()[e, d].rearrange("(st gl) q -> gl q st", gl=2)[gl]))
            p.dma(LI[pr], p.dview("od_lam_im", IN["od_lam_im"].ap()[e, d].rearrange("(st gl) q -> gl q st", gl=2)[gl]))
            p.dma(DT[pr], p.dview("od_log_dt", IN["od_log_dt"].ap()[e, d:d + 1, :].rearrange("a (st gl) -> a gl st", gl=2)[:, gl, :].partition_broadcast(64)))
        T_ = tl
        o.ts(T_["lr"], LR, -1e-4, None, ALU.min)
        o.act(T_["dt"], DT, AF.Exp)
        o.tt(T_["a"], T_["lr"], T_["dt"], ALU.mult)
        o.act(mag[d], T_["a"], AF.Exp)
        o.tt(T_["th"], LI, T_["dt"], ALU.mult)
        o.ts(T_["x"], T_["th"], 1.0 / 16, None, ALU.mult)
        o.tt(T_["z"], T_["x"], T_["x"], ALU.mult)
        o.ts(T_["q"], T_["z"], SIN_C[4], None, ALU.mult)
        for a_ in (SIN_C[3], SIN_C[2], SIN_C[1], SIN_C[0]):
            o.stt(T_["q"], T_["q"], a_, T_["z"], ALU.add, ALU.mult)
        o.stt(T_["s"], T_["q"], 1.0, T_["x"], ALU.add, ALU.mult)
        o.ts(T_["q"], T_["z"], COS_C[5], None, ALU.mult)
        for a_ in (COS_C[4], COS_C[3], COS_C[2], COS_C[1], COS_C[0]):
            o.stt(T_["q"], T_["q"], a_, T_["z"], ALU.add, ALU.mult)
        o.ts(T_["c"], T_["q"], 1.0, None, ALU.add)
        for _ in range(4):
            o.tt(T_["cc"], T_["c"], T_["c"], ALU.mult)
            o.tt(T_["ss"], T_["s"], T_["s"], ALU.mult)
            o.tt(T_["cs"], T_["c"], T_["s"], ALU.mult)
            o.tt(T_["c"], T_["cc"], T_["ss"], ALU.subtract)
            o.ts(T_["s"], T_["cs"], 2.0, None, ALU.mult)
        o.cp(CLv[d][:, :, 0], T_["c"]); o.cp(SLv[d][:, :, 0], T_["s"])
        for kk in range(1, 10):
            o.tt(T_["cc"], CLv[d][:, :, kk - 1], CLv[d][:, :, kk - 1], ALU.mult)
            o.tt(T_["ss"], SLv[d][:, :, kk - 1], SLv[d][:, :, kk - 1], ALU.mult)
            o.tt(T_["cs"], CLv[d][:, :, kk - 1], SLv[d][:, :, kk - 1], ALU.mult)
            o.tt(CLv[d][:, :, kk], T_["cc"], T_["ss"], ALU.subtract)
            o.ts(SLv[d][:, :, kk], T_["cs"], 2.0, None, ALU.mult)
        o.ts(NSLv[d], SLv[d], -1.0, None, ALU.mult)
        o.tt(T_["abr"], mag[d], T_["c"], ALU.mult)
        o.tt(T_["abi"], mag[d], T_["s"], ALU.mult)
        o.ts(T_["xr"], T_["abr"], -1.0, None, ALU.add)
        o.tt(T_["t1"], T_["lr"], T_["lr"], ALU.mult)
        o.tt(T_["t2"], LI, LI, ALU.mult)
        o.tt(T_["den"], T_["t1"], T_["t2"], ALU.add)
        o.recip(T_["den"], T_["den"])
        o.tt(T_["t1"], T_["xr"], T_["lr"], ALU.mult)
        o.tt(T_["t2"], T_["abi"], LI, ALU.mult)
        o.tt(T_["t1"], T_["t1"], T_["t2"], ALU.add)
        o.tt(T_["fre"], T_["t1"], T_["den"], ALU.mult)
        o.tt(T_["t1"], T_["abi"], T_["lr"], ALU.mult)
        o.tt(T_["t2"], T_["xr"], LI, ALU.mult)
        o.tt(T_["t1"], T_["t1"], T_["t2"], ALU.subtract)
        o.tt(T_["fim"], T_["t1"], T_["den"], ALU.mult)
        o.ts(T_["nfim"], T_["fim"], -1.0, None, ALU.mult)
        for gl in range(2):
            o.ts(fm["fre%d" % gl], T_["fre"], MK[:, gl:gl + 1], None, ALU.mult)
            o.ts(fm["fim%d" % gl], T_["fim"], MK[:, gl:gl + 1], None, ALU.mult)
            o.ts(fm["nfim%d" % gl], T_["nfim"], MK[:, gl:gl + 1], None, ALU.mult)
        for st in range(16):
            ft_, s4 = divmod(st, 4)
            for gl in range(2):
                cs_ = slice(s4 * 32 + gl * 16, s4 * 32 + gl * 16 + 16)
                fre, fim, nfim = fm["fre%d" % gl][:, st:st + 1], fm["fim%d" % gl][:, st:st + 1], fm["nfim%d" % gl][:, st:st + 1]
                o.ts(tb, BR[:, st, :], fre, None, ALU.mult)
                o.stt(INre[ft_][:, cs_], BI[:, st, :], nfim, tb, ALU.mult, ALU.add)
                o.ts(tb, BR[:, st, :], fim, None, ALU.mult)
                o.stt(INim[ft_][:, cs_], BI[:, st, :], fre, tb, ALU.mult, ALU.add)
        for ft_ in range(4):
            for ri, INx in enumerate((INre, INim)):
                ps = p.ps(2 + ri, 128)
                o.tr(ps, INx[ft_], C["IDENT"])
                for s4 in range(4):
                    o.ts(WB[d][ri][ft_ * 4 + s4], ps, MK[:, 4 + s4:5 + s4], None, ALU.mult)
    p.pop()
    NSEG = [(0, 256)] + [(256 + 512 * i, 512) for i in range(8)]
    cosT = [p.sb([516]) for _ in range(2)]; sinT = [p.sb([516]) for _ in range(2)]
    tAs = [p.sb([256]) for _ in range(2)]; tBs = [p.sb([256]) for _ in range(2)]
    uF = p.sb([TT]); yaccs = [p.sb([TT]) for _ in range(2)]
    wk = {n: [p.sb([512]) for _ in range(2)] for n in ["t1", "t2", "t3", "t4", "wre", "wim", "gre", "gim", "hre", "him"]}
    inis = [[p.sb([8]) for _ in range(2)] for _ in range(2)]
    crs = [p.sb([8]) for _ in range(2)]

    def stream(d, ft_, sidx, s4list, segs):
        cosv, sinv, tA, tB, yacc, cr = cosT[sidx], sinT[sidx], tAs[sidx], tBs[sidx], yaccs[sidx], crs[sidx]
        W_ = {nm: wk[nm][sidx] for nm in wk}
        ini = inis[sidx]
        bk_r, bk_i, bk_y = (4, 5, 0) if sidx == 0 else (6, 7, 1)
        for s4 in s4list:
            st = ft_ * 4 + s4
            o.memset(cosv[:, 0:1], 1.0, eng="dve"); o.memset(sinv[:, 0:1], 0.0, eng="dve")
            for kk in range(9):
                w = 1 << kk
                c_, s_, ns_ = CLv[d][:, st, kk:kk + 1], SLv[d][:, st, kk:kk + 1], NSLv[d][:, st, kk:kk + 1]
                o.ts(tA[:, 0:w], cosv[:, 0:w], c_, None, ALU.mult)
                o.ts(tB[:, 0:w], sinv[:, 0:w], c_, None, ALU.mult)
                o.stt(tA[:, 0:w], sinv[:, 0:w], ns_, tA[:, 0:w], ALU.mult, ALU.add)
                o.stt(tB[:, 0:w], cosv[:, 0:w], s_, tB[:, 0:w], ALU.mult, ALU.add)
                o.cp(cosv[:, w:2 * w], tA[:, 0:w]); o.cp(sinv[:, w:2 * w], tB[:, 0:w])
            o.cp(cosv[:, 512:513], CLv[d][:, st, 9:10]); o.cp(sinv[:, 512:513], SLv[d][:, st, 9:10])
            cur = ini[0]
            o.memset(cur[:, 0:2], 0.0, eng="dve")
            for si, (t0, n) in enumerate(segs):
                psr = p.ps(bk_r, n); psi = p.ps(bk_i, n)
                o.mm(psr, WB[d][0][st], uF[:, t0:t0 + n])
                o.mm(psi, WB[d][1][st], uF[:, t0:t0 + n])
                V = (lambda t_: rev(t_, n)) if d == 1 else (lambda t_: t_[:, 0:n])
                cs_n, sn_n = cosv[:, 0:n], sinv[:, 0:n]
                o.tt(W_["t1"][:, 0:n], V(psr), cs_n, ALU.mult)
                o.tt(W_["t2"][:, 0:n], V(psi), sn_n, ALU.mult)
                o.tt(W_["wre"][:, 0:n], W_["t1"][:, 0:n], W_["t2"][:, 0:n], ALU.add)
                o.tt(W_["t3"][:, 0:n], V(psi), cs_n, ALU.mult)
                o.tt(W_["t4"][:, 0:n], V(psr), sn_n, ALU.mult)
                o.tt(W_["wim"][:, 0:n], W_["t3"][:, 0:n], W_["t4"][:, 0:n], ALU.subtract)
                gre, gim = W_["gre"], W_["gim"]
                mb = bcast_cols(mag[d][:, st:st + 1], n)
                p.op("dve", lambda e_, gre=gre, mb=mb, w=W_["wre"], cur=cur, n=n: e_.tensor_tensor_scan(
                    gre.ap[:, 0:n], mb.ap, w.ap[:, 0:n], cur.ap[:, 0:1], ALU.mult, ALU.add), [mb, W_["wre"], cur], [gre])
                p.op("dve", lambda e_, gim=gim, mb=mb, w=W_["wim"], cur=cur, n=n: e_.tensor_tensor_scan(
                    gim.ap[:, 0:n], mb.ap, w.ap[:, 0:n], cur.ap[:, 1:2], ALU.mult, ALU.add), [mb, W_["wim"], cur], [gim])
                o.tt(W_["t1"][:, 0:n], gre[:, 0:n], cs_n, ALU.mult)
                o.tt(W_["t2"][:, 0:n], gim[:, 0:n], sn_n, ALU.mult)
                o.tt(V(W_["hre"]), W_["t1"][:, 0:n], W_["t2"][:, 0:n], ALU.subtract)
                o.tt(W_["t3"][:, 0:n], gim[:, 0:n], cs_n, ALU.mult)
                o.tt(W_["t4"][:, 0:n], gre[:, 0:n], sn_n, ALU.mult)
                o.tt(V(W_["him"]), W_["t3"][:, 0:n], W_["t4"][:, 0:n], ALU.add)
                nxt = ini[(si + 1) % 2]
                o.ts(cr[:, 0:1], gre[:, n - 1:n], cosv[:, n:n + 1], None, ALU.mult)
                o.ts(cr[:, 1:2], gim[:, n - 1:n], sinv[:, n:n + 1], None, ALU.mult)
                o.tt(nxt[:, 0:1], cr[:, 0:1], cr[:, 1:2], ALU.subtract)
                o.ts(cr[:, 2:3], gim[:, n - 1:n], cosv[:, n:n + 1], None, ALU.mult)
                o.stt(nxt[:, 1:2], gre[:, n - 1:n], sinv[:, n:n + 1], cr[:, 2:3], ALU.mult, ALU.add)
                cur = nxt
                psy = p.ps(bk_y, n)
                o.mm(psy, WCf[0][st], W_["hre"][:, 0:n], start=True, stop=False)
                o.mm(psy, WCf[1][st], W_["him"][:, 0:n], start=False, stop=True)
                if s4 == s4list[0]:
                    o.cp(yacc[:, t0:t0 + n], psy, eng="act")
                else:
                    o.tt(yacc[:, t0:t0 + n], psy, yacc[:, t0:t0 + n], ALU.add)

    for d in range(2):
        segs = NSEG if d == 0 else [NSEG[0]] + NSEG[:0:-1]
        for ft_ in range(4):
            p.dma(uF, p.dview("FM", k.FM.ap()[12 + ft_], (12 + ft_) * 128 * TT, (13 + ft_) * 128 * TT))
            caps = []
            for sidx, s4list in enumerate(((0, 2), (1, 3))):
                p.begin_capture()
                stream(d, ft_, sidx, s4list, segs)
                caps.append(p.end_capture())
            p.replay(caps)
            o.tt(yaccs[0], yaccs[0], yaccs[1], ALU.add)
            p.dma(p.dview("YS5", k.YS5.ap()[d, ft_], (d * 4 + ft_) * 128 * TT, (d * 4 + ft_ + 1) * 128 * TT), yaccs[0], q="pool")
    p.pop()


def merge_odd(k, l):
    p, o, IN, C = k.p, k.o, k.IN, k.C
    e = l // 2
    p.push()
    wout = p.sb([8, D], BF16)
    wst = [p.sb([8, 256]) for _ in range(2)]
    for j in range(4):
        p.dma(wst[j % 2], p.dview("od_w_out", IN["od_w_out"].ap()[e, :, j * 256:(j + 1) * 256].rearrange("(kt q) f -> q kt f", q=128)))
        o.cp(wout[:, :, j * 256:(j + 1) * 256], wst[j % 2], eng="pool")
    wglu = p.sb([4, 512], BF16)
    for j in range(2):
        w32 = wst[j % 2]
        w3 = w32.v(w32.ap.rearrange("q a b -> q (a b)")[:, 0:1024].rearrange("q (a b) -> q a b", a=4))
        p.dma(w3, p.dview("od_w_glu", IN["od_w_glu"].ap()[e, :, j * 256:(j + 1) * 256].rearrange("(kt q) f -> q kt f", q=128)))
        o.cp(wglu[:, :, j * 256:(j + 1) * 256], w3, eng="pool")
    mg = p.sb([512]); dsk = p.sb([512]); bgl = p.sb([512])
    rowbc(k, mg, IN["od_mlstm_norm"].ap()[e:e + 1, :], "od_mlstm_norm")
    rowbc(k, dsk, IN["od_d_skip"].ap()[e:e + 1, :], "od_d_skip")
    rowbc(k, bgl, IN["od_b_glu"].ap()[e:e + 1, :], "od_b_glu")
    g1 = [p.sb([D]), p.sb([D])]
    for r in range(2):
        rowbc(k, g1[r], k.MOD.ap()[l, r:r + 1, 2048:3072], "MOD")
    lng = p.sb([D]); lnb = p.sb([D])
    rowbc(k, lng, IN["ln1_g"].ap()[l:l + 1, :], "ln1_g")
    rowbc(k, lnb, IN["ln1_b"].ap()[l:l + 1, :], "ln1_b")
    A2 = [p.sb([2, 4, 132]) for _ in range(2)]
    YS = [p.sb([2, 4, 128]) for _ in range(2)]
    zt = [p.sb([D]) for _ in range(2)]
    ht = [p.sb([D]) for _ in range(2)]
    Y = [p.sb([D]) for _ in range(2)]
    YT = [p.sb([8, 128], BF16) for _ in range(2)]
    st_ = [p.sb([48]) for _ in range(2)]
    junks = [p.sb([D]) for _ in range(2)]
    U = [p.sb([D]) for _ in range(2)]
    HM = [p.sb([4, 128]) for _ in range(2)]
    YG = [p.sb([512]) for _ in range(2)]
    YGT = [p.sb([4, 128], BF16) for _ in range(2)]
    caps_ = []
    for t in range(NT):
        par = t % 2
        junk = junks[par]
        p.begin_capture()
        r = 1 if t < NCTX else 0
        a2, ys, z, h, y, yT, s, u, hm, yg, ygT = (A2[t % 2], YS[t % 2], zt[t % 2], ht[t % 2], Y[t % 2], YT[t % 2], st_[t % 2],
                                                  U[t % 2], HM[t % 2], YG[t % 2], YGT[t % 2])
        tok = slice(t * 128, (t + 1) * 128)
        for d in range(2):
            for hh in range(4):
                p.dma(a2[:, d, hh, 0:129], p.dview("OUTO%d_%d" % (hh, d), k.OUTO.ap()[hh, d, tok, 0:129], t * 128 * 132, (t + 1) * 128 * 132))
                fi = d * 4 + hh
                p.dma(ys[:, d, hh, :], p.dview("YS5", k.YS5.ap()[d, hh][:, tok], fi * 128 * TT, (fi + 1) * 128 * TT))
        p.dma(z, p.dview("ZS", k.ZS.ap()[tok, :], t * 128 * D, (t + 1) * 128 * D))
        p.dma(h, p.dview("H", k.H.ap()[tok, :], t * 128 * D, (t + 1) * 128 * D))
        den = a2[:, :, :, 128]
        s3 = lambda c0: s.v(s.ap[:, c0:c0 + 8].rearrange("q (a b) -> q a b", a=2))
        o.ts(s3(0), den, -1.0, None, ALU.mult)
        o.tt(s3(0), s3(0), den, ALU.max)
        o.ts(s3(0), s3(0), 1.0, None, ALU.max)
        o.recip(s[:, 8:16], s[:, 0:8])
        for hh in range(4):
            o.ts(junk[:, 0:128], a2[:, 0, hh, 0:128], s[:, 8 + hh:9 + hh], None, ALU.mult)
            o.stt(hm[:, hh, :], a2[:, 1, hh, 0:128], s[:, 12 + hh:13 + hh], junk[:, 0:128], ALU.mult, ALU.add)
        for hh in range(4):
            o.act(junk[:, 0:128], hm[:, hh, :], AF.Square, accum=s[:, 16 + hh:17 + hh])
            o.act(junk[:, 128:256], hm[:, hh, :], AF.Identity, accum=s[:, 20 + hh:21 + hh])
        o.ts(s[:, 24:28], s[:, 20:24], 1.0 / 128, None, ALU.mult)
        o.tt(s[:, 28:32], s[:, 24:28], s[:, 24:28], ALU.mult)
        o.stt(s[:, 32:36], s[:, 16:20], 1.0 / 128, s[:, 28:32], ALU.mult, ALU.subtract)
        o.act(s[:, 36:40], s[:, 32:36], AF.Sqrt, bias=k.eps[:, 0:1])
        o.recip(s[:, 40:44], s[:, 36:40])
        for hh in range(4):
            o.ts(junk[:, 0:128], hm[:, hh, :], s[:, 24 + hh:25 + hh], s[:, 40 + hh:41 + hh], ALU.subtract, ALU.mult)
            o.tt(y[:, hh * 128:(hh + 1) * 128], junk[:, 0:128], mg[:, hh * 128:(hh + 1) * 128], ALU.mult)
        o.tt(y[:, 0:512], y[:, 0:512], z[:, 0:512], ALU.mult)
        o.tt(ys[:, 0], ys[:, 0], ys[:, 1], ALU.add)
        ps = p.ps(0 + 4 * par)
        for ft_ in range(4):
            o.tr(ps[:, ft_ * 128:(ft_ + 1) * 128], ys[:, 0, ft_, :], C["IDENT"])
        o.tt(junk[:, 0:512], z[:, 512:1024], dsk, ALU.mult)
        o.tt(junk[:, 0:512], junk[:, 0:512], ps, ALU.add)
        xg = junk[:, 0:512]
        o.tt(junk[:, 512:1024], xg, xg, ALU.mult)
        o.ts(junk[:, 512:1024], junk[:, 512:1024], 0.044715, 1.0, ALU.mult, ALU.add)
        o.tt(junk[:, 512:1024], junk[:, 512:1024], xg, ALU.mult)
        o.act(yg, junk[:, 512:1024], AF.Tanh, scale=math.sqrt(2.0 / math.pi))
        o.stt(yg, yg, 1.0, xg, ALU.add, ALU.mult)
        o.ts(yg, yg, 0.5, None, ALU.mult)
        ps2 = p.ps(1 + 4 * par)
        for ft_ in range(4):
            o.tr(ps2[:, ft_ * 128:(ft_ + 1) * 128], yg[:, ft_ * 128:(ft_ + 1) * 128], C["IDENT"])
        o.cp(ygT.v(ygT.ap.rearrange("q a b -> q (a b)")), ps2, eng="act")
        ps3 = p.ps(2 + 4 * par)
        for kt in range(4):
            o.mm(ps3, ygT[:, kt, :], wglu[:, kt, :], start=(kt == 0), stop=(kt == 3))
        o.tt(junk[:, 0:512], ps3, bgl, ALU.add)
        o.act(junk[:, 512:1024], junk[:, 0:512], AF.Sigmoid)
        o.tt(y[:, 512:1024], yg, junk[:, 512:1024], ALU.mult)
        for half in range(2):
            ps = p.ps(half + 4 * par)
            for q in range(4):
                kt = half * 4 + q
                o.tr(ps[:, q * 128:(q + 1) * 128], y[:, kt * 128:(kt + 1) * 128], C["IDENT"])
            o.cp(yT.v(yT.ap[:, half * 4:(half + 1) * 4, :].rearrange("q a b -> q (a b)")), ps, eng="act")
        for half in range(2):
            ps = p.ps(2 + half + 4 * par)
            for kt in range(8):
                o.mm(ps, yT[:, kt, :], wout[:, kt, half * 512:(half + 1) * 512], start=(kt == 0), stop=(kt == 7))
            hs_ = slice(half * 512, (half + 1) * 512)
            o.tt(u[:, hs_], ps, g1[r][:, hs_], ALU.mult)
        if dbg_here(k, "y"):
            p.dma(p.dview("y_out", k.YOUT.ap()[tok, :], t * 128 * D, (t + 1) * 128 * D), u, q="pool")
        o.stt(u, h, ALPHA, u, ALU.mult, ALU.add)
        layer_norm(k, u, h, lng, lnb, s, junk)
        p.dma(p.dview("H", k.H.ap()[tok, :], t * 128 * D, (t + 1) * 128 * D), h, q="pool")
        if dbg_here(k, "h1"):
            p.dma(p.dview("y_out", k.YOUT.ap()[tok, :], t * 128 * D, (t + 1) * 128 * D), h, q="pool")
        caps_.append(p.end_capture())
        if par == 1 or t == NT - 1:
            p.replay(caps_)
            caps_ = []
    p.pop()


def moe_layer2(k, l):
    p, o, IN, C = k.p, k.o, k.IN, k.C
    IOTA = k.IOTA
    IC = C["IC"]
    SETS = [(list(range(NCTX, NT)), 512, [(0, 128), (128, 128), (256, 128), (384, 128)], 0),
            (list(range(0, NCTX)), 32, [(0, 32)], 1)]
    p.push()
    wr = p.sb([8, 16])
    p.dma(wr, p.dview("w_router", IN["w_router"].ap()[l].rearrange("(kt q) e -> q kt e", q=128)))
    aff = p.sb([NT, 16]); gw = p.sb([NT, 16]); msk = p.sb([NT, 16]); posm = p.sb([NT, 16])
    posmT = p.sb([TT])
    ghl = p.sb([NT, 16, 2], BF16)
    p.push()
    in2tm = p.sb([NT, D], BF16)
    p.push()
    affT = p.sb([TT])
    sc2 = [p.sb([D]), p.sb([D])]; sh2 = [p.sb([D]), p.sb([D])]
    for r in range(2):
        rowbc(k, sc2[r], k.MOD.ap()[l, r:r + 1, 4096:5120], "MOD")
        o.ts(sc2[r], sc2[r], 1.0, None, ALU.add)
        rowbc(k, sh2[r], k.MOD.ap()[l, r:r + 1, 3072:4096], "MOD")
    ht = [p.sb([D]) for _ in range(2)]
    x2 = [p.sb([D]) for _ in range(2)]
    xTf = [p.sb([8, 128]) for _ in range(2)]
    sm = [p.sb([40]) for _ in range(2)]
    for t in range(NT):
        r = 1 if t < NCTX else 0
        h, x, xf, s = ht[t % 2], x2[t % 2], xTf[t % 2], sm[t % 2]
        tok = slice(t * 128, (t + 1) * 128)
        p.dma(h, p.dview("H", k.H.ap()[tok, :], t * 128 * D, (t + 1) * 128 * D))
        o.tt(x, h, sc2[r], ALU.mult)
        o.tt(x, x, sh2[r], ALU.add)
        o.cp(in2tm[:, t, :], x, eng="pool")
        for half in range(2):
            ps = p.ps(half)
            for q in range(4):
                kt = half * 4 + q
                o.tr(ps[:, q * 128:(q + 1) * 128], x[:, kt * 128:(kt + 1) * 128], C["IDENT"])
            ps3 = ps.v(ps.ap.rearrange("q (a b) -> q a b", a=4))
            o.cp(xf[:, half * 4:(half + 1) * 4, :], ps3, eng="act")
        psl = p.ps(2, 16)
        for kt in range(8):
            o.mm(psl, xf[:, kt, :], wr[:, kt, :], start=(kt == 0), stop=(kt == 7))
        p.op("dve", lambda e, s=s, psl=psl: e.reduce_max(s.ap[:, 0:1], psl.ap, AX.X), [psl], [s])
        o.ts(s[:, 1:2], s[:, 0:1], -1.0, None, ALU.mult)
        o.act(s[:, 8:24], psl, AF.Exp, bias=s[:, 1:2], accum=s[:, 2:3])
        o.recip(s[:, 3:4], s[:, 2:3])
        o.ts(aff[:, t, :], s[:, 8:24], s[:, 3:4], None, ALU.mult)
        pst = p.ps(3, 128)
        o.tr(pst[0:16, :], aff[:, t, :], C["IDENT"])
        o.cp(affT[0:16, tok], pst[0:16, :], eng="act")
    sts = [p.sb([16]), p.sb([16])]; junks = [p.sb([4096]), p.sb([256])]; thr = [p.sb([16]), p.sb([16])]; dts = [p.sb([16]), p.sb([16])]
    sets = [(256, TT, 512.0, 0), (0, 256, 32.0, 1)]
    caps_ = []
    for si_, (c0, c1, cap, r) in enumerate(sets):
        p.begin_capture()
        st, junk, dt = sts[si_], junks[si_], dts[si_]
        lo, hi, mid, cnt, ge, d1, d2 = [st[0:16, i:i + 1] for i in range(7)]
        o.memset(lo, 0.0, eng="dve"); o.memset(hi, 1.0, eng="dve")
        for it in range(32):
            o.tt(mid, lo, hi, ALU.add)
            o.ts(mid, mid, 0.5, None, ALU.mult)
            o.ts(junk[0:16, 0:c1 - c0], affT[0:16, c0:c1], mid, None, ALU.is_ge, ALU.add, accum=cnt)
            o.ts(ge, cnt, cap - 0.5, None, ALU.is_ge)
            o.tt(d1, mid, lo, ALU.subtract)
            o.tt(d2, hi, mid, ALU.subtract)
            o.stt(lo, d1, ge, lo, ALU.mult, ALU.add)
            o.stt(hi, d2, ge, mid, ALU.mult, ALU.add)
        o.ts(dt[0:16, 0:16], C["IDENT"][0:16, 0:16], lo, None, ALU.mult)
        pth = p.ps(2 + si_, 16)
        o.mm(pth, C["ONES"][0:16, :], dt[0:16, 0:16])
        o.cp(thr[r], pth, eng="act")
        caps_.append(p.end_capture())
    p.replay(caps_)
    tot = p.sb([16]); tmp16 = p.sb([16])
    for (tiles, cap, jts, yi) in SETS:
        r = 1 if tiles[0] < NCTX else 0
        o.memset(tot, 0.0, eng="dve")
        for t in tiles:
            tok = slice(t * 128, (t + 1) * 128)
            o.tt(msk[:, t, :], aff[:, t, :], thr[r], ALU.is_ge)
            o.tt(gw[:, t, :], msk[:, t, :], aff[:, t, :], ALU.mult)
            ps = p.ps(0, 16); ps2 = p.ps(1, 16)
            o.mm(ps, C["CSs"], msk[:, t, :])
            o.mm(ps2, C["ONES"], msk[:, t, :])
            o.tt(tmp16, ps, tot, ALU.add)
            o.tt(tot, tot, ps2, ALU.add)
            o.stt(tmp16, tmp16, 1.0, msk[:, t, :], ALU.add, ALU.mult)
            o.ts(posm[:, t, :], tmp16, -1.0, None, ALU.add)
            pst = p.ps(3, 128)
            o.tr(pst[0:16, :], posm[:, t, :], C["IDENT"])
            o.cp(posmT[0:16, tok], pst[0:16, :], eng="act")
    hi16 = p.sb([NT, 16], BF16); hi32 = p.sb([NT, 16]); lo32 = p.sb([NT, 16])
    o.cp(hi16, gw); o.cp(hi32, hi16); o.tt(lo32, gw, hi32, ALU.subtract)
    o.cp(ghl[:, :, :, 0], hi32); o.cp(ghl[:, :, :, 1], lo32)
    p.pop()
    p.push()
    wgf = [p.sb([8, 128]) for _ in range(2)]; wuf = [p.sb([8, 128]) for _ in range(2)]
    wgb = [p.sb([8, 128], BF16) for _ in range(2)]; wub = [p.sb([8, 128], BF16) for _ in range(2)]
    wdf = [p.sb([2, 512]) for _ in range(2)]; wdb = [p.sb([2, 512], BF16) for _ in range(2)]
    xsT = [p.sb([8, 512], BF16), p.sb([8, 32], BF16)]
    hT = [p.sb([16, 512], BF16), p.sb([16, 32], BF16)]
    sg = [p.sb([512]) for _ in range(2)]
    sel = [p.sb([512], BF16) for _ in range(3)]
    gsel = [p.sb([4]), p.sb([4])]
    ysb = [[p.sb([D], BF16) for _ in range(5)] for _ in range(2)]
    wi = 0; di = 0; si = 0
    for e in range(16):
        for (tiles, cap, jts, yi) in SETS:
            for ps_ in range(2):
                psx = [p.ps(q, cap) for q in range(4)]
                psg = [p.ps(4 + ji, 2) for ji in range(len(jts))]
                for t in tiles:
                    sl = sel[si % 3]; si += 1
                    o.ts(sl[:, 0:cap], IOTA[:, 0:cap], posm[:, t, e:e + 1], None, ALU.is_equal)
                    for q in range(4):
                        kt = ps_ * 4 + q
                        o.mm(psx[q], in2tm[:, t, kt * 128:(kt + 1) * 128], sl[:, 0:cap], start=(t == tiles[0]), stop=(t == tiles[-1]))
                    if ps_ == 0:
                        for ji, (j0, nj) in enumerate(jts):
                            o.mm(psg[ji][0:nj, :], sl[:, j0:j0 + nj], ghl[:, t, e, :], start=(t == tiles[0]), stop=(t == tiles[-1]))
                for q in range(4):
                    kt = ps_ * 4 + q
                    o.cp(xsT[yi][:, kt, 0:cap], psx[q], eng=("act" if q % 2 else "dve"))
                if ps_ == 0:
                    for ji, (j0, nj) in enumerate(jts):
                        gs_, pg_ = gsel[yi][0:nj, ji:ji + 1], psg[ji][0:nj, 0:2]
                        p.op("dve", lambda e_, gs_=gs_, pg_=pg_: e_.reduce_sum(gs_.ap, pg_.ap, AX.X), [pg_], [gs_])
        for ft_ in range(16):
            a32, b32, a16, b16 = wgf[wi % 2], wuf[wi % 2], wgb[wi % 2], wub[wi % 2]
            wi += 1
            p.dma(a32, p.dview("w_gate", IN["w_gate"].ap()[l, e, :, ft_ * 128:(ft_ + 1) * 128].rearrange("(kt q) f -> q kt f", q=128)))
            p.dma(b32, p.dview("w_up", IN["w_up"].ap()[l, e, :, ft_ * 128:(ft_ + 1) * 128].rearrange("(kt q) f -> q kt f", q=128)))
            o.cp(a16, a32, eng="pool")
            o.cp(b16, b32, eng="pool")
            for (tiles, cap, jts, yi) in SETS:
                pg = p.ps(4 + (ft_ % 2) * 2, cap); pu = p.ps(5 + (ft_ % 2) * 2, cap)
                for kt in range(8):
                    o.mm(pg, a16[:, kt, :], xsT[yi][:, kt, 0:cap], start=(kt == 0), stop=(kt == 7))
                for kt in range(8):
                    o.mm(pu, b16[:, kt, :], xsT[yi][:, kt, 0:cap], start=(kt == 0), stop=(kt == 7))
                s_ = sg[(ft_ + yi) % 2]
                o.act(s_[:, 0:cap], pg, AF.Silu)
                o.tt(hT[yi][:, ft_, 0:cap], s_[:, 0:cap], pu, ALU.mult)
        yt_ = ysb[e % 2]
        for half in range(2):
            hs_ = slice(half * 512, (half + 1) * 512)
            psd = {}
            for (tiles, cap, jts, yi) in SETS:
                for ji in range(len(jts)):
                    psd[(yi, ji)] = p.ps(ji if yi == 0 else 4, 512)
            for fp_ in range(8):
                d32, d16 = wdf[di % 2], wdb[di % 2]
                di += 1
                p.dma(d32, p.dview("w_down", IN["w_down"].ap()[l, e, fp_ * 256:(fp_ + 1) * 256, half * 512:(half + 1) * 512].rearrange("(a q) d -> q a d", q=128)))
                o.cp(d16, d32, eng="pool")
                for f2 in range(2):
                    ft_ = fp_ * 2 + f2
                    for (tiles, cap, jts, yi) in SETS:
                        for ji, (j0, nj) in enumerate(jts):
                            o.mm(psd[(yi, ji)][0:nj, :], hT[yi][:, ft_, j0:j0 + nj], d16[:, f2, :], start=(ft_ == 0), stop=(ft_ == 15))
            for (tiles, cap, jts, yi) in SETS:
                for ji, (j0, nj) in enumerate(jts):
                    yt = yt_[ji if yi == 0 else 4]
                    o.ts(yt[0:nj, hs_], psd[(yi, ji)][0:nj, :], gsel[yi][0:nj, ji:ji + 1], None, ALU.mult)
        for (tiles, cap, jts, yi) in SETS:
            for ji, (j0, nj) in enumerate(jts):
                yt = yt_[ji if yi == 0 else 4]
                p.dma(p.dview("YSD%d" % yi, k.YSD[yi].ap()[e, j0:j0 + nj, :]), yt[0:nj, :], q="pool")
    p.pop()
    p.pop()
    p.push()
    g2 = [p.sb([D]), p.sb([D])]
    for r in range(2):
        rowbc(k, g2[r], k.MOD.ap()[l, r:r + 1, 5120:6144], "MOD")
    lng = p.sb([D]); lnb = p.sb([D])
    rowbc(k, lng, IN["ln2_g"].ap()[l:l + 1, :], "ln2_g")
    rowbc(k, lnb, IN["ln2_b"].ap()[l:l + 1, :], "ln2_b")
    selT = [[p.sb([512], BF16) for _ in range(4)] for _ in range(16)]
    ysl = [p.sb([4, 128], BF16) for _ in range(4)]
    fT = p.sb([8, 512])
    hh = [p.sb([D]) for _ in range(2)]
    uu = [p.sb([D]) for _ in range(2)]
    s8 = p.sb([8]); junk2 = p.sb([D])
    groups = [(0, 256, SETS[1])] + [(256 + 512 * i, 512, SETS[0]) for i in range(8)]
    yl = 0
    for (t0, n, (tiles, cap, jts, yi)) in groups:
        for e in range(16):
            psr = p.ps(e % 2, n)
            o.mm(psr, C["OH%d" % e][0:16, :], posmT[0:16, t0:t0 + n])
            for ji, (j0, nj) in enumerate(jts):
                o.ts(selT[e][ji][0:nj, 0:n], psr[0:nj, :], IC[0:nj, ji:ji + 1], None, ALU.is_equal)
        for dt_ in range(8):
            psf = p.ps(2 + dt_ % 2, n)
            for e in range(16):
                y_ = ysl[yl % 4]; yl += 1
                nrow = jts[-1][0] + jts[-1][1]
                if yi == 0:
                    p.dma(y_, p.dview("YSD0", k.YSD[0].ap()[e, :, dt_ * 128:(dt_ + 1) * 128].rearrange("(jt q) d -> q jt d", q=128)))
                else:
                    p.dma(y_[0:32, 0, :], p.dview("YSD1", k.YSD[1].ap()[e, 0:32, dt_ * 128:(dt_ + 1) * 128]))
                for ji, (j0, nj) in enumerate(jts):
                    o.mm(psf, y_[0:nj, ji, :], selT[e][ji][0:nj, 0:n], start=(e == 0 and ji == 0), stop=(e == 15 and ji == len(jts) - 1))
            o.cp(fT[:, dt_, 0:n], psf, eng=("act" if dt_ % 2 else "dve"))
        for st_ in range(n // 128):
            t = t0 // 128 + st_
            r = 1 if t < NCTX else 0
            tok = slice(t * 128, (t + 1) * 128)
            h = hh[st_ % 2]; u = uu[st_ % 2]
            p.dma(h, p.dview("H", k.H.ap()[tok, :], t * 128 * D, (t + 1) * 128 * D))
            for half in range(2):
                ps = p.ps(4 + half)
                for q in range(4):
                    dt_ = half * 4 + q
                    o.tr(ps[:, q * 128:(q + 1) * 128], fT[:, dt_, st_ * 128:(st_ + 1) * 128], C["IDENT"])
                hs_ = slice(half * 512, (half + 1) * 512)
                if dbg_here(k, "f"):
                    o.cp(junk2[:, hs_], ps, eng="act")
                o.tt(u[:, hs_], ps, g2[r][:, hs_], ALU.mult)
            if dbg_here(k, "f"):
                p.dma(p.dview("y_out", k.YOUT.ap()[tok, :], t * 128 * D, (t + 1) * 128 * D), junk2, q="pool")
            o.stt(u, h, ALPHA, u, ALU.mult, ALU.add)
            layer_norm(k, u, h, lng, lnb, s8, junk2)
            p.dma(p.dview("H", k.H.ap()[tok, :], t * 128 * D, (t + 1) * 128 * D), h, q="pool")
    p.pop()
    p.pop()


_W_NAMES = ["w_mod", "b_mod", "ln1_g", "ln1_b", "ln2_g", "ln2_b", "w_router", "w_gate", "w_up", "w_down",
            "ev_w_in", "ev_w_out", "ev_conv", "ev_a_log", "ev_dt_bias", "ev_gdn_norm", "ev_ret_norm",
            "od_w_in", "od_w_out", "od_conv", "od_gate_bias", "od_mlstm_norm", "od_lam_re", "od_lam_im", "od_log_dt",
            "od_b_re", "od_b_im", "od_c_re", "od_c_im", "od_d_skip", "od_w_glu", "od_b_glu"]


def kernel(**inputs):
    nc = build(n_layers=4)
    cst = make_consts()
    retg = np.concatenate([ret_log_decay(0), ret_log_decay(1)])[None, :].astype(np.float32)
    in_maps = []
    B = inputs["x"].shape[0]
    for b in range(B):
        m = {}
        m["h0"] = np.ascontiguousarray(np.concatenate([inputs["ctx"][b], inputs["x"][b]], 0).astype(np.float32))
        m["cvec"] = np.ascontiguousarray(np.stack([inputs["c"][b], inputs["c_ctx"]], 0).astype(np.float32))
        m["cst"] = cst
        m["retg"] = retg
        for n in _W_NAMES:
            if n in nc.used_inputs:
                m[n] = np.ascontiguousarray(inputs[n], dtype=np.float32)
        in_maps.append({n: m[n] for n in nc.used_inputs})
    res = run_bass_kernel_spmd(nc, in_maps, core_ids=list(range(B)))
    out = np.stack([np.asarray(res.results[b]["y_out"])[256:] for b in range(B)], 0)
    return out.astype(np.float32)
```

```python
import bisect
import numpy as np
import concourse.bass as bass
import concourse.mybir as mybir
from concourse.bass_utils import run_bass_kernel_spmd

F32 = mybir.dt.float32
BF16 = mybir.dt.bfloat16
ALU = mybir.AluOpType
AF = mybir.ActivationFunctionType
AX = mybir.AxisListType

SEM_GEN = 30000
N_DMA_SEMS = 12


class IntervalMap:
    def __init__(self):
        self.bounds = [0]
        self.state = [(None, {})]

    def _split(self, x):
        i = bisect.bisect_right(self.bounds, x) - 1
        if self.bounds[i] == x:
            return i
        w, r = self.state[i]
        self.bounds.insert(i + 1, x)
        self.state.insert(i + 1, (w, dict(r)))
        return i + 1

    def segs(self, lo, hi):
        i0 = self._split(lo)
        i1 = self._split(hi)
        return range(i0, i1)


class T:
    def __init__(self, ap, key, lo, hi):
        self.ap = ap
        self.key = key
        self.lo = lo
        self.hi = hi

    def __getitem__(self, idx):
        return T(self.ap[idx], self.key, self.lo, self.hi)

    def v(self, ap):
        return T(ap, self.key, self.lo, self.hi)


class Prog:
    ENGS = ["pe", "act", "dve", "pool", "sp"]

    def __init__(self, nc, sb_bytes=206 * 1024):
        self.nc = nc
        self.ops = {e: [] for e in self.ENGS}
        self.cnt = {e: 0 for e in self.ENGS}
        self.maps = {}
        self.observed = {e: {} for e in self.ENGS}
        self.dma_cnt = {"sp": 0, "pool": 0, "act": 0}
        self.dma_sem_uses = {}
        self.sem_names = set()
        self.sb_bytes = sb_bytes
        self.sb_top = 0
        self.sb_stack = []
        self.ps_top = 0
        self.dram = {}
        self.arena = None
        self.psarena = None
        self.final_waits = []

    def setup_mem(self, es):
        nc = self.nc
        self.arena = es.enter_context(nc.sbuf_tensor("arena", [128, self.sb_bytes // 4], F32))
        self.psarena = es.enter_context(nc.psum_tensor("psarena", [128, 8 * 512], F32))

    def push(self):
        self.sb_stack.append(self.sb_top)

    def pop(self):
        self.sb_top = self.sb_stack.pop()

    def sb(self, shape, dtype=F32, name=None):
        esz = 4 if dtype == F32 else 2
        n = int(np.prod(shape))
        nbytes = (n * esz + 31) // 32 * 32
        off = self.sb_top
        self.sb_top += nbytes
        assert self.sb_top <= self.sb_bytes, f"SBUF overflow {self.sb_top}"
        ap = self.arena[:, off // 4:(off + nbytes) // 4]
        if dtype != F32:
            ap = ap.bitcast(dtype)
        ap = ap[:, 0:n]
        if len(shape) == 2:
            ap = ap.rearrange("p (a b) -> p a b", a=shape[0])
        elif len(shape) == 3:
            ap = ap.rearrange("p (a b c) -> p a b c", a=shape[0], b=shape[1])
        return T(ap, "sb", off, off + nbytes)

    def ps(self, bank, ncols=512, dtype=F32, col0=0):
        off = bank * 512 + col0
        ap = self.psarena[:, off:off + ncols]
        return T(ap, "ps", off * 4, (off + ncols) * 4)

    def dram_t(self, name, shape, dtype=F32, kind="Internal"):
        h = self.nc.dram_tensor(name, list(shape), dtype, kind=kind)
        self.dram[name] = h
        return h

    def dview(self, name, ap, lo=0, hi=1 << 40):
        return T(ap, "d:" + name, lo, hi)

    def _deps(self, eng, reads, writes, token):
        deps = set()
        for t in reads:
            m = self.maps.setdefault(t.key, IntervalMap())
            for i in m.segs(t.lo, t.hi):
                w, r = m.state[i]
                if w is not None:
                    deps.add(w)
        for t in writes:
            m = self.maps.setdefault(t.key, IntervalMap())
            for i in m.segs(t.lo, t.hi):
                w, r = m.state[i]
                if w is not None:
                    deps.add(w)
                for tok in r.values():
                    deps.add(tok)
        for t in reads:
            m = self.maps[t.key]
            for i in m.segs(t.lo, t.hi):
                m.state[i][1][token[0] if eng.startswith("dma") else eng] = token
        for t in writes:
            m = self.maps[t.key]
            for i in m.segs(t.lo, t.hi):
                m.state[i] = (token, {})
        return deps

    def _waits(self, eng, deps):
        obs = self.observed[eng]
        best = {}
        for (sem, val, src) in deps:
            if src == "pe" and eng == "pe":
                continue
            if obs.get(sem, 0) >= val:
                continue
            if best.get(sem, 0) < val:
                best[sem] = val
        for sem, val in best.items():
            obs[sem] = val
        return list(best.items())

    limit = None
    count = 0

    def _lim(self):
        if self.limit is not None:
            if self.count >= self.limit:
                return True
            self.count += 1
        return False

    capture = None

    def begin_capture(self):
        self.capture = []
        return self.capture

    def end_capture(self):
        c = self.capture
        self.capture = None
        return c

    def replay(self, lists):
        its = [iter(l) for l in lists]
        alive = list(its)
        while alive:
            nxt = []
            for it in alive:
                try:
                    item = next(it)
                except StopIteration:
                    continue
                if item[0] == "op":
                    self.op(*item[1:])
                else:
                    self.dma(*item[1:3], q=item[3], **item[4])
                nxt.append(it)
            alive = nxt

    def op(self, eng, fn, reads=(), writes=()):
        if self.capture is not None:
            self.capture.append(("op", eng, fn, list(reads), list(writes)))
            return
        if self._lim():
            return
        n = self.cnt[eng]
        gen, idx = divmod(n, SEM_GEN)
        sem = f"{eng}{gen}"
        self.sem_names.add(sem)
        token = (sem, idx + 1, eng)
        self.cnt[eng] = n + 1
        rd2, wr2 = [], []
        for t in reads:
            if t.key == "ps":
                wr2.append(T(t.ap, "ps", t.lo // 2048 * 2048, (t.hi + 2047) // 2048 * 2048))
            else:
                rd2.append(t)
        for t in writes:
            if t.key == "ps":
                wr2.append(T(t.ap, "ps", t.lo // 2048 * 2048, (t.hi + 2047) // 2048 * 2048))
            else:
                wr2.append(t)
        reads, writes = rd2, wr2
        deps = self._deps(eng, reads, writes, token)
        waits = self._waits(eng, deps)
        self.ops[eng].append((waits, fn, (sem, 1)))

    def dma(self, out, in_, q="sp", **kw):
        if self.capture is not None:
            self.capture.append(("dma", out, in_, q, kw))
            return
        if self._lim():
            return
        k = self.dma_cnt[q]
        self.dma_cnt[q] = k + 1
        sem = f"dma_{q}{k % N_DMA_SEMS}"
        self.sem_names.add(sem)
        uses = self.dma_sem_uses.get(sem, 0)
        self.dma_sem_uses[sem] = uses + 1
        token = (sem, 16 * (uses + 1), "dma")
        deps = self._deps("dma_" + q, [in_], [out], token)
        if uses > 0:
            deps.add((sem, 16 * uses, "dma"))
        waits = self._waits(q, deps)
        oap, iap = out.ap, in_.ap

        def fn(e, oap=oap, iap=iap, kw=kw):
            return e.dma_start(out=oap, in_=iap, allow_slow_non_contiguous=True, **kw)
        self.ops[q].append((waits, fn, (sem, 16)))
        return token

    def wait_all_dma(self, eng="sp"):
        deps = set()
        for sem, uses in self.dma_sem_uses.items():
            deps.add((sem, 16 * uses, "dma"))
        waits = self._waits(eng, deps)
        self.ops[eng].append((waits, None, None))

    def emit(self, es):
        nc = self.nc
        sems = {}
        for name in sorted(self.sem_names):
            sems[name] = es.enter_context(nc.semaphore(name))
        block = es.enter_context(nc.Block())
        ops = self.ops

        def run(e, lst):
            for waits, fn, inc in lst:
                for sem, val in waits:
                    e.wait_ge(sems[sem], val)
                if fn is not None:
                    ins = fn(e)
                    ins.then_inc(sems[inc[0]], inc[1])

        @block.sync
        def _(e):
            run(e, ops["sp"])

        @block.tensor
        def _(e):
            run(e, ops["pe"])

        @block.vector
        def _(e):
            run(e, ops["dve"])

        @block.scalar
        def _(e):
            run(e, ops["act"])

        @block.gpsimd
        def _(e):
            run(e, ops["pool"])


def _ap(x):
    return x.ap if isinstance(x, T) else x


def _ts(*xs):
    return [x for x in xs if isinstance(x, T)]


class Ops:
    def __init__(self, p):
        self.p = p

    def mm(self, out, lhsT, rhs, start=True, stop=True):
        self.p.op("pe", lambda e: e.matmul(out.ap, lhsT.ap, rhs.ap, start=start, stop=stop), [lhsT, rhs], [out])

    def tr(self, out, in_, ident):
        self.p.op("pe", lambda e: e.transpose(out.ap, in_.ap, ident.ap), [in_, ident], [out])

    def act(self, out, in_, func, bias=None, scale=None, accum=None, eng="act"):
        kw = {}
        if bias is not None:
            kw["bias"] = _ap(bias)
        if scale is not None:
            kw["scale"] = _ap(scale)
        if accum is not None:
            kw["accum_out"] = accum.ap
        self.p.op("act", lambda e: e.activation(out.ap, in_.ap, func, **kw),
                  _ts(in_, bias, scale), _ts(out, accum))

    def tt(self, out, a, b, op, eng="dve"):
        self.p.op(eng, lambda e: e.tensor_tensor(out.ap, a.ap, b.ap, op), [a, b], [out])

    def ts(self, out, a, s1, s2, op0, op1=None, accum=None, eng="dve"):
        kw = {}
        if accum is not None:
            kw["accum_out"] = accum.ap
        if op1 is None:
            fn = lambda e: e.tensor_scalar(out.ap, a.ap, _ap(s1), None, op0, **kw)
        else:
            fn = lambda e: e.tensor_scalar(out.ap, a.ap, _ap(s1), _ap(s2), op0, op1, **kw)
        self.p.op(eng, fn, _ts(a, s1, s2), _ts(out, accum))

    def stt(self, out, a, s, b, op0, op1, eng="dve"):
        self.p.op(eng, lambda e: e.scalar_tensor_tensor(out.ap, a.ap, _ap(s), b.ap, op0, op1), _ts(a, s, b), [out])

    def cp(self, out, in_, eng="dve"):
        if eng == "act":
            self.p.op("act", lambda e: e.copy(out.ap, in_.ap), [in_], [out])
        else:
            self.p.op(eng, lambda e: e.tensor_copy(out.ap, in_.ap), [in_], [out])

    def memset(self, out, val, eng="pool"):
        self.p.op(eng, lambda e: e.memset(out.ap, val), [], [out])

    def recip(self, out, in_):
        self.p.op("dve", lambda e: e.reciprocal(out.ap, in_.ap), [in_], [out])

from contextlib import ExitStack
import math

D = 1024
TT = 4352
NT = 34
NCTX = 2
ALPHA = 8 ** 0.25
EPS = 1e-5
NEG = -30000.0
EVEN_IN = 4112
ODD_IN = 2576

CONST_NAMES = ["IDENT", "ONES", "CSf", "CSb", "MTf", "MTb", "MSf", "MSb", "CSs", "MK", "IO0", "IO1", "IO2", "IO3", "IC"] + ["OH%d" % i for i in range(16)]


def make_consts():
    p = np.arange(128)[:, None]
    f = np.arange(128)[None, :]
    c = {}
    c["IDENT"] = (p == f)
    c["ONES"] = np.ones((128, 128))
    c["CSf"] = (p <= f)
    c["CSb"] = (p >= f)
    c["MTf"] = np.where(p <= f, 0.0, NEG)
    c["MTb"] = np.where(p >= f, 0.0, NEG)
    c["MSf"] = np.where(f < p, 0.0, NEG)
    c["MSb"] = np.where(f > p, 0.0, NEG)
    c["CSs"] = (p < f)
    mk = np.zeros((128, 128))
    mk[:64, 0] = 1; mk[64:, 1] = 1; mk[:16, 2] = 1; mk[16:32, 3] = 1
    for s4 in range(4):
        mk[s4 * 32:(s4 + 1) * 32, 4 + s4] = 1
    c["MK"] = mk
    for i in range(4):
        c["IO%d" % i] = np.broadcast_to(f + 128 * i, (128, 128))
    ic = np.zeros((128, 128))
    for i in range(4):
        ic[:, i] = np.arange(128) + 128 * i
    c["IC"] = ic
    for i in range(16):
        oh = np.zeros((128, 128)); oh[i, :] = 1
        c["OH%d" % i] = oh
    arr = np.concatenate([c[k].astype(np.float32) for k in CONST_NAMES], axis=1)
    return np.ascontiguousarray(arr)


def ret_log_decay(d):
    expo = 5.0 + 2.0 * np.arange(4, dtype=np.float32) + d
    return np.log1p(-np.exp2(-expo)).astype(np.float32)


class K:
    pass


def build(n_layers=4, dbg=(), stage=99, layers=None, dbg_layer=0):
    nc = bass.Bass("TRN2", target_bir_lowering=False)
    k = K()
    k.nc = nc
    k.dbg_on = set(dbg)
    k.stage = stage
    k.dbg_layer = dbg_layer
    k.cur = -1
    SHAPES = dict(h0=[TT, D], cvec=[2, D], cst=[128, 128 * len(CONST_NAMES)], retg=[1, 8],
                  w_mod=[4, D, 6 * D], b_mod=[4, 6 * D], ln1_g=[4, D], ln1_b=[4, D], ln2_g=[4, D], ln2_b=[4, D],
                  w_router=[4, D, 16], w_gate=[4, 16, D, 2048], w_up=[4, 16, D, 2048], w_down=[4, 16, 2048, D],
                  ev_w_in=[2, D, EVEN_IN], ev_w_out=[2, D, D], ev_conv=[2, 3, 3, 1536], ev_a_log=[2, 2, 4],
                  ev_dt_bias=[2, 2, 4], ev_gdn_norm=[2, 128], ev_ret_norm=[2, 512],
                  od_w_in=[2, D, ODD_IN], od_w_out=[2, D, D], od_conv=[2, 3, 3, 1024], od_gate_bias=[2, 2, 2, 4],
                  od_mlstm_norm=[2, 512], od_lam_re=[2, 2, 32, 64], od_lam_im=[2, 2, 32, 64], od_log_dt=[2, 2, 32],
                  od_b_re=[2, 32, 64, 16], od_b_im=[2, 32, 64, 16], od_c_re=[2, 32, 16, 64], od_c_im=[2, 32, 16, 64],
                  od_d_skip=[2, 512], od_w_glu=[2, 512, 512], od_b_glu=[2, 512])

    class LazyIn(dict):
        def __missing__(self, name):
            self[name] = nc.dram_tensor(name, list(SHAPES[name]), F32, kind="ExternalInput")
            return self[name]
    IN = LazyIn()
    k.IN = IN
    with ExitStack() as es:
        p = Prog(nc)
        p.setup_mem(es)
        o = Ops(p)
        k.p, k.o = p, o
        k.H = p.dram_t("H", [TT, D])
        k.MOD = p.dram_t("MOD", [4, 2, 6 * D])
        k.FM = p.dram_t("FM", [24, 128, TT])
        k.ZS = p.dram_t("ZS", [TT, D])
        k.GATES = p.dram_t("GATES", [TT, 16])
        k.OUTM = p.dram_t("OUTM", [8, 2, TT, 128])
        k.OUTO = p.dram_t("OUTO", [4, 2, TT, 132])
        k.YS5 = p.dram_t("YS5", [2, 4, 128, TT])
        k.YSD = [p.dram_t("YSDl", [16, 512, D], BF16), p.dram_t("YSDc", [16, 128, D], BF16)]
        k.YOUT = nc.dram_tensor("y_out", [TT, D], F32, kind="ExternalOutput")
        cst = p.sb([128 * len(CONST_NAMES)])
        p.dma(cst, p.dview("cst", IN["cst"].ap()))
        k.C = {n: cst[:, i * 128:(i + 1) * 128] for i, n in enumerate(CONST_NAMES)}
        i_io = CONST_NAMES.index('IO0')
        k.IOTA = cst[:, i_io * 128:(i_io + 4) * 128]
        eps_c = p.sb([4])
        o.memset(eps_c[:, 0:1], EPS); o.memset(eps_c[:, 1:2], 1e-6); o.memset(eps_c[:, 2:3], 1.0); o.memset(eps_c[:, 3:4], 0.0)
        k.eps = eps_c

        phase_mod(k)
        p.dma(p.dview("H", k.H.ap()), p.dview("h0", IN["h0"].ap()))
        for l in (layers if layers is not None else range(n_layers)):
            if k.stage <= 0:
                break
            k.cur = l
            mixer_layer(k, l)
        p.limit = None
        if not (k.dbg_on - {'none'}) and k.stage >= 5:
            p.dma(p.dview('y_out', k.YOUT.ap()), p.dview('H', k.H.ap()))
        if k.stage < 5:
            p.dma(p.dview('y_out', k.YOUT.ap()[0:8, :]), p.dview('MOD', k.MOD.ap().rearrange('l r (a f) -> (l r a) f', f=1024)[0:8, :]))
        p.wait_all_dma("sp")
        p.wait_all_dma("pool")
        p.emit(es)
    nc.used_inputs = list(IN.keys())
    return nc


def rowbc(k, dst, src_ap, name):
    k.p.dma(dst, k.p.dview(name, src_ap.partition_broadcast(128)))


def phase_mod(k):
    p, o, IN = k.p, k.o, k.IN
    p.push()
    cT = p.sb([2, 8]); sT = p.sb([2, 8])
    for r in range(2):
        p.dma(cT[:, r, :], p.dview("cvec", IN["cvec"].ap()[r].rearrange("(kt q) -> q kt", q=128)))
    o.act(sT, cT, AF.Silu)
    wt = [p.sb([8, 512]) for _ in range(2)]
    bm = p.sb([512]); res = [p.sb([512]) for _ in range(2)]
    i = 0
    for l in range(4):
        for cc in range(12):
            w = wt[i % 2]; r = res[i % 2]
            p.dma(w, p.dview("w_mod", IN["w_mod"].ap()[l, :, cc * 512:(cc + 1) * 512].rearrange("(kt q) f -> q kt f", q=128)))
            p.dma(bm[0:2, :], p.dview("b_mod", IN["b_mod"].ap()[l:l + 1, cc * 512:(cc + 1) * 512].partition_broadcast(2)))
            ps = p.ps(i % 2)
            for kt in range(8):
                o.mm(ps[0:2, :], sT[:, :, kt], w[:, kt, :], start=(kt == 0), stop=(kt == 7))
            o.tt(r[0:2, :], ps[0:2, :], bm[0:2, :], ALU.add)
            p.dma(p.dview("MOD", k.MOD.ap()[l, :, cc * 512:(cc + 1) * 512]), r[0:2, :], q="pool")
            i += 1
    p.pop()


def load_modT(k, l, off, plus1):
    p, o = k.p, k.o
    t = p.sb([2, 8])
    for r in range(2):
        p.dma(t[:, r, :], p.dview("MOD", k.MOD.ap()[l, r, off:off + D].rearrange("(kt q) -> q kt", q=128)))
    if plus1:
        o.ts(t, t, 1.0, None, ALU.add)
    return t


def build_inT(k, l, scT, shT, inT, extra=None):
    p, o = k.p, k.o
    p.push()
    ht = [p.sb([D]) for _ in range(2)]
    caps_ = []
    for t in range(NT):
        par = t % 2
        p.begin_capture()
        h = ht[t % 2]
        p.dma(h, p.dview("H", k.H.ap()[t * 128:(t + 1) * 128, :], t * 128 * D, (t + 1) * 128 * D))
        r = 1 if t < NCTX else 0
        for half in range(2):
            ps = p.ps(half + 2 * par)
            for q in range(4):
                kt = half * 4 + q
                o.tr(ps[:, q * 128:(q + 1) * 128], h[:, kt * 128:(kt + 1) * 128], k.C["IDENT"])
            for q in range(4):
                kt = half * 4 + q
                o.act(inT[:, kt, t * 128:(t + 1) * 128], ps[:, q * 128:(q + 1) * 128], AF.Identity,
                      bias=shT[:, r, kt:kt + 1], scale=scT[:, r, kt:kt + 1])
                if extra is not None:
                    extra(t, kt, ps[:, q * 128:(q + 1) * 128], r)
        caps_.append(p.end_capture())
        if par == 1 or t == NT - 1:
            p.replay(caps_)
            caps_ = []
    p.pop()


TOKCH = [(i * 512, 512) for i in range(8)] + [(4096, 256)]


def dbg_here(k, name):
    return name in k.dbg_on and k.cur == k.dbg_layer


def mixer_layer(k, l):
    p, o, IN = k.p, k.o, k.IN
    odd = (l % 2 == 1)
    e = l // 2
    C = k.C
    wname = "od_w_in" if odd else "ev_w_in"
    cname = "od_conv" if odd else "ev_conv"
    p.push()
    scT = load_modT(k, l, 1024, True)
    shT = load_modT(k, l, 0, False)
    inT = p.sb([8, TT], BF16)
    build_inT(k, l, scT, shT, inT)
    if k.stage <= 1:
        p.pop(); return
    if not odd:
        specs = [(0 + 128 * h, h, "l2q") for h in range(4)] + [(512 + 128 * h, 4 + h, "l2k") for h in range(4)] + \
                [(1024 + 128 * h, 8 + h, None) for h in range(4)] + [(2064 + 128 * h, None, None) for h in range(4)] + \
                [(2576 + 128 * h, None, "scale") for h in range(4)] + [(3088 + 128 * h, None, None) for h in range(4)]
        nconv = 12
    else:
        specs = [(0 + 128 * h, h, None) for h in range(4)] + [(512 + 128 * h, 4 + h, "scale") for h in range(4)] + \
                [(1024 + 128 * h, None, None) for h in range(4)] + [(2064 + 128 * h, None, None) for h in range(4)]
        nconv = 8
    p.push()
    p.begin_capture()
    wconv = p.sb([nconv, 9])
    for t12 in range(nconv):
        p.dma(wconv[:, t12, :], p.dview(cname, IN[cname].ap()[e].rearrange("a b c -> c (a b)")[t12 * 128:(t12 + 1) * 128, :]))
    wf = [p.sb([8, 128]) for _ in range(2)]
    wb = [p.sb([8, 128], BF16) for _ in range(2)]
    ft = [p.sb([TT]) for _ in range(2)]
    yt = p.sb([TT])
    sq = p.sb([512]); rs = p.sb([512])
    for s, (col, cv, mode) in enumerate(specs):
        w32, w16, X = wf[s % 2], wb[s % 2], ft[s % 2]
        p.dma(w32, p.dview(wname, IN[wname].ap()[e, :, col:col + 128].rearrange("(kt q) f -> q kt f", q=128)))
        o.cp(w16, w32, eng="pool")
        for ci, (t0, n) in enumerate(TOKCH):
            ps = p.ps(2 + ci % 2, n)
            for kt in range(8):
                o.mm(ps, w16[:, kt, :], inT[:, kt, t0:t0 + n], start=(kt == 0), stop=(kt == 7))
            o.cp(X[:, t0:t0 + n], ps, eng=("act" if ci % 2 else "dve"))
        if cv is not None:
            wc = wconv[:, cv, :]
            Y = yt
            o.ts(Y[:, 0:256], X[:, 0:256], wc[:, 4:5], None, ALU.mult)
            o.stt(Y[:, 1:256], X[:, 0:255], wc[:, 3:4], Y[:, 1:256], ALU.mult, ALU.add)
            o.stt(Y[:, 0:255], X[:, 1:256], wc[:, 5:6], Y[:, 0:255], ALU.mult, ALU.add)
            Xg = X.v(X.ap[:, 256:TT].rearrange("q (r c) -> q r c", c=64))
            Yg = Y.v(Y.ap[:, 256:TT].rearrange("q (r c) -> q r c", c=64))
            o.ts(Y[:, 256:TT], X[:, 256:TT], wc[:, 4:5], None, ALU.mult)
            for dy in range(3):
                for dx in range(3):
                    if dy == 1 and dx == 1:
                        continue
                    oy, ox = dy - 1, dx - 1
                    r0, r1 = max(0, -oy), 64 - max(0, oy)
                    c0, c1 = max(0, -ox), 64 - max(0, ox)
                    o.stt(Yg[:, r0:r1, c0:c1], Xg[:, r0 + oy:r1 + oy, c0 + ox:c1 + ox], wc[:, dy * 3 + dx:dy * 3 + dx + 1],
                          Yg[:, r0:r1, c0:c1], ALU.mult, ALU.add)
            o.act(X, Y, AF.Silu)
        if mode in ("l2q", "l2k"):
            for ci, (t0, n) in enumerate(TOKCH):
                o.tt(sq[:, 0:n], X[:, t0:t0 + n], X[:, t0:t0 + n], ALU.mult)
                ps = p.ps(4 + ci % 2, n)
                o.mm(ps, C["ONES"], sq[:, 0:n])
                o.act(rs[:, 0:n], ps, AF.Sqrt, bias=k.eps[:, 1:2])
                o.recip(sq[:, 0:n], rs[:, 0:n])
                if mode == "l2q":
                    o.stt(X[:, t0:t0 + n], X[:, t0:t0 + n], 128 ** -0.5, sq[:, 0:n], ALU.mult, ALU.mult)
                else:
                    o.tt(X[:, t0:t0 + n], X[:, t0:t0 + n], sq[:, 0:n], ALU.mult)
        elif mode == "scale":
            o.ts(X, X, 128 ** -0.5, None, ALU.mult)
        p.dma(p.dview("FM", k.FM.ap()[s], s * 128 * TT, (s + 1) * 128 * TT), X, q="pool")
    cap_b = p.end_capture()
    p.push()
    p.begin_capture()
    wz = p.sb([8, 1040], BF16)
    wst = [p.sb([8, 260]) for _ in range(2)]
    zcols = [(1536, 512), (2048, 16), (3600, 512)] if not odd else [(1536, 512), (2048, 16), (2064, 512)]
    dst = 0
    i = 0
    for (c0, n) in zcols:
        for j in range(0, n, 260):
            m = min(260, n - j)
            w32 = wst[i % 2]
            p.dma(w32[:, :, 0:m], p.dview(wname, IN[wname].ap()[e, :, c0 + j:c0 + j + m].rearrange("(kt q) f -> q kt f", q=128)))
            o.cp(wz[:, :, dst:dst + m], w32[:, :, 0:m], eng="pool")
            dst += m
            i += 1
    if not odd:
        alog = p.sb([8]); dtb = p.sb([8]); nea = p.sb([8])
        rowbc(k, alog, IN["ev_a_log"].ap()[e:e + 1].rearrange("a d h -> a (d h)"), "ev_a_log")
        rowbc(k, dtb, IN["ev_dt_bias"].ap()[e:e + 1].rearrange("a d h -> a (d h)"), "ev_dt_bias")
        o.act(nea, alog, AF.Exp)
    else:
        gbi = p.sb([8]); gbf = p.sb([8])
        for d in range(2):
            rowbc(k, gbi[:, d * 4:(d + 1) * 4], IN["od_gate_bias"].ap()[e, d, 0:1, :], "od_gate_bias")
            rowbc(k, gbf[:, d * 4:(d + 1) * 4], IN["od_gate_bias"].ap()[e, d, 1:2, :], "od_gate_bias")
    zt = [p.sb([D]) for _ in range(2)]
    gt = [p.sb([16]) for _ in range(2)]
    tmp8 = p.sb([8]); tmp8b = p.sb([8])
    for t in range(NT):
        z, g = zt[t % 2], gt[t % 2]
        lt = lambda kt: inT[:, kt, t * 128:(t + 1) * 128]
        psa, psb, psg = p.ps(0), p.ps(1), p.ps(6, 16)
        for kt in range(8):
            o.mm(psa, lt(kt), wz[:, kt, 0:512], start=(kt == 0), stop=(kt == 7))
        for kt in range(8):
            o.mm(psb, lt(kt), wz[:, kt, 528:1040], start=(kt == 0), stop=(kt == 7))
        for kt in range(8):
            o.mm(psg, lt(kt), wz[:, kt, 512:528], start=(kt == 0), stop=(kt == 7))
        if not odd:
            o.act(z[:, 0:512], psa, AF.Silu)
            o.act(z[:, 512:1024], psb, AF.Silu)
            o.tt(tmp8, psg[:, 0:8], dtb, ALU.add)
            o.act(g[:, 0:8], tmp8, AF.Exp)
            o.act(tmp8, g[:, 0:8], AF.Ln, bias=k.eps[:, 2:3])
            o.stt(g[:, 0:8], tmp8, -1.0, nea, ALU.mult, ALU.mult)
            o.act(g[:, 8:16], psg[:, 8:16], AF.Sigmoid)
        else:
            o.act(z[:, 0:512], psa, AF.Sigmoid)
            o.cp(z[:, 512:1024], psb, eng="dve")
            o.tt(tmp8, psg[:, 8:16], gbf, ALU.add)
            o.act(tmp8b, tmp8, AF.Exp, scale=-1.0)
            o.act(tmp8, tmp8b, AF.Ln, bias=k.eps[:, 2:3])
            o.ts(g[:, 0:8], tmp8, -1.0, None, ALU.mult)
            o.tt(tmp8b, psg[:, 0:8], gbi, ALU.add)
            o.act(g[:, 8:16], tmp8b, AF.Exp)
        p.dma(p.dview("ZS", k.ZS.ap()[t * 128:(t + 1) * 128, :], t * 128 * D, (t + 1) * 128 * D), z, q="pool")
        p.dma(p.dview("GATES", k.GATES.ap()[t * 128:(t + 1) * 128, :], t * 128 * 16, (t + 1) * 128 * 16), g, q="pool")
    cap_c = p.end_capture()
    p.replay([cap_b, cap_c])
    p.pop()
    p.pop()
    p.pop()
    if k.stage <= 3:
        return
    if not odd:
        scan_layer(k, l, [0, 1])
    else:
        scan_layer(k, l, [2])
        if k.stage > 4:
            s5_scan(k, l)
    if k.stage <= 4:
        return
    if not odd:
        merge_even(k, l)
    else:
        merge_odd(k, l)
    if k.stage <= 5:
        return
    moe_layer2(k, l)


def chain_order(d):
    return list(range(NT)) if d == 0 else [1, 0] + list(range(NT - 1, 1, -1))


def scan_layer(k, l, types):
    p, o, IN, C = k.p, k.o, k.IN, k.C
    import os
    if 'OPLIMIT' in os.environ:
        p.limit = int(os.environ['OPLIMIT']); p.count = 0
    p.push()
    gates = p.sb([NT, 16])
    p.dma(gates, p.dview("GATES", k.GATES.ap().rearrange("(c q) g -> q c g", q=128)))
    retg = p.sb([8])
    negb = p.sb([NT, 8])
    if 1 in types:
        rowbc(k, retg, IN["retg"].ap(), "retg")
    if 0 in types:
        o.ts(negb, gates[:, :, 8:16], -1.0, None, ALU.mult)
    RING = 8
    W = lambda n=128: [p.sb([n]) for _ in range(RING)]
    names = ["qT", "kT", "vT", "G1", "cc", "tmp", "ET", "EXPR", "kgT", "qgT", "kend", "bv", "E", "N", "M", "N2", "M2",
             "TTa", "TTb", "AQ", "br", "vn", "o", "o2", "tmp2", "sc"]
    ring = {n: (W(132) if n in ("vn", "o") else W()) for n in names}
    chains = []
    for typ in types:
        for h in range(4):
            for d in range(2):
                chains.append(dict(typ=typ, h=h, d=d, S=[p.sb([132]), p.sb([132])], order=chain_order(d), dec=None))
    for ch in chains:
        o.memset(ch["S"][0], 0.0, eng="dve")
    step_i = [0]

    def decay(ch, gcol, slot):
        d = ch["d"]
        R = {n: ring[n][slot] for n in names}
        CS = C["CSf"] if d == 0 else C["CSb"]
        MT = C["MTf"] if d == 0 else C["MTb"]
        o.ts(R["G1"], C["ONES"], gcol, None, ALU.mult)
        pb = ch["pb"]
        psr = p.ps(pb + 0, 128)
        psc = p.ps(pb + 0, 256, col0=128)
        o.mm(psr, R["G1"], CS)
        o.mm(psc[:, 0:128], CS, R["G1"])
        o.mm(psc[:, 128:256], C["ONES"], R["G1"])
        cc = R["cc"]
        o.cp(cc[:, 0:1], psc[:, 0:1], eng="act")
        o.cp(cc[:, 1:2], psc[:, 128:129], eng="act")
        o.stt(R["tmp"], psr, cc[:, 0:1], MT, ALU.subtract, ALU.add)
        o.act(R["ET"], R["tmp"], AF.Exp)
        o.act(R["EXPR"], psr, AF.Exp)
        o.tt(cc[:, 4:5], cc[:, 1:2], cc[:, 0:1], ALU.subtract)
        o.act(cc[:, 2:3], cc[:, 4:5], AF.Exp)
        o.act(cc[:, 3:4], cc[:, 1:2], AF.Exp)
        return dict(ET=R["ET"], EXPR=R["EXPR"], cc=cc, psr=psr)

    def step(ch, c, si):
        typ, h, d = ch["typ"], ch["h"], ch["d"]
        slot = step_i[0] % RING
        step_i[0] += 1
        R = {n: ring[n][slot] for n in names}
        sq, sk, sv = (h, 4 + h, 8 + h) if typ != 1 else (12 + h, 16 + h, 20 + h)
        dv = 129 if typ == 2 else 128
        tok = slice(c * 128, (c + 1) * 128)
        for nm, s in (("qT", sq), ("kT", sk), ("vT", sv)):
            p.dma(R[nm], p.dview("FM", k.FM.ap()[s][:, tok], s * 128 * TT, (s + 1) * 128 * TT))
        qT, kT, vT = R["qT"], R["kT"], R["vT"]
        if typ != 1:
            gcol = gates[:, c, d * 4 + h:d * 4 + h + 1]
            dec = decay(ch, gcol, slot)
        else:
            if ch["dec"] is None:
                gcol = retg[:, d * 4 + h:d * 4 + h + 1]
                dslot = 0
                own = {n: p.sb([128]) for n in ["G1", "cc", "tmp", "ET", "EXPR"]}
                save = {n: ring[n][dslot] for n in own}
                for n in own:
                    ring[n][dslot] = own[n]
                ch["dec"] = decay(ch, gcol, dslot)
                for n in own:
                    ring[n][dslot] = save[n]
            dec = ch["dec"]
        cc = dec["cc"]
        pb = ch["pb"]
        pst = p.ps(pb + 1, 256)
        o.tr(pst[:, 0:128], kT, C["IDENT"])
        o.tr(pst[:, 128:256], vT, C["IDENT"])
        if typ == 2:
            ei = gates[:, c, 8 + d * 4 + h:8 + d * 4 + h + 1]
            o.ts(R["kend"], pst[:, 0:128], cc[:, 2:3], ei, ALU.mult, ALU.mult)
        else:
            o.ts(R["kend"], pst[:, 0:128], cc[:, 2:3], None, ALU.mult)
        o.tt(R["qgT"], qT, dec["EXPR"], ALU.mult)
        psq = p.ps(pb + 0, 128, col0=384)
        o.mm(psq, kT, qT)
        if typ == 2:
            o.stt(R["AQ"], psq, ei, dec["ET"], ALU.mult, ALU.mult)
        else:
            o.tt(R["AQ"], psq, dec["ET"], ALU.mult)
        S = ch["S"][si % 2][:, 0:dv]
        Sn = ch["S"][(si + 1) % 2][:, 0:dv]
        if typ == 0:
            MS = C["MSf"] if d == 0 else C["MSb"]
            nb = negb[:, c, d * 4 + h:d * 4 + h + 1]
            o.ts(R["bv"], pst[:, 128:256], gates[:, c, 8 + d * 4 + h:8 + d * 4 + h + 1], None, ALU.mult)
            o.tt(R["kgT"], kT, dec["EXPR"], ALU.mult)
            o.stt(R["tmp2"], dec["psr"], cc[:, 0:1], MS, ALU.subtract, ALU.subtract)
            o.act(R["E"], R["tmp2"], AF.Exp, scale=-1.0)
            psk = p.ps(pb + 1, 128, col0=256)
            o.mm(psk, kT, kT)
            o.stt(R["N"], psk, nb, R["E"], ALU.mult, ALU.mult)
            psm = p.ps(pb + 1, 128, col0=384)
            o.tr(psm, R["N"], C["IDENT"])
            o.cp(R["M"], psm, eng="act")
            o.tt(R["TTa"], C["IDENT"], R["M"], ALU.add)
            Nc, Mc, Nn, Mn = R["N"], R["M"], R["N2"], R["M2"]
            Tc, Tn = R["TTa"], R["TTb"]
            for lev in range(1, 7):
                ps1 = p.ps(pb + 1, 128, col0=256)
                o.mm(ps1, Mc, Nc)
                o.cp(Nn, ps1, eng="act")
                if lev < 6:
                    ps2 = p.ps(pb + 1, 128, col0=384)
                    o.mm(ps2, Nc, Mc)
                    o.cp(Mn, ps2, eng="dve")
                ps3 = p.ps(pb + 1, 128, col0=256)
                o.mm(ps3, Nn, Tc)
                o.tt(Tn, ps3, Tc, ALU.add)
                Nc, Nn = Nn, Nc
                Mc, Mn = Mn, Mc
                Tc, Tn = Tn, Tc
            psr2 = p.ps(pb + 1, 128, col0=256)
            o.mm(psr2, R["kgT"], S)
            o.stt(R["br"], psr2, nb, R["bv"], ALU.mult, ALU.add)
            psv = p.ps(pb + 1, 128, col0=256)
            o.mm(psv, Tc, R["br"])
            o.cp(R["vn"][:, 0:128], psv, eng="act")
        else:
            o.cp(R["vn"][:, 0:128], pst[:, 128:256], eng="act")
            if typ == 2:
                o.memset(R["vn"][:, 128:129], 1.0, eng="dve")
        vn = R["vn"][:, 0:dv]
        pso = p.ps(pb + 0, dv)
        o.mm(pso, R["qgT"], S, start=True, stop=False)
        o.mm(pso, R["AQ"], vn, start=False, stop=True)
        o.cp(R["o"][:, 0:dv], pso, eng="act")
        if typ == 2:
            p.dma(p.dview("OUTO%d_%d" % (h, d), k.OUTO.ap()[h, d, tok, 0:dv], c * 128 * 132, (c + 1) * 128 * 132), R["o"][:, 0:dv], q="pool")
        else:
            slot8 = typ * 4 + h
            p.dma(p.dview("OUTM%d_%d" % (slot8, d), k.OUTM.ap()[slot8, d, tok, :], c * 128 * 128, (c + 1) * 128 * 128), R["o"][:, 0:128], q="pool")
        pss = p.ps(pb + 1, dv)
        o.mm(pss, R["kend"], vn)
        o.stt(Sn, S, cc[:, 3:4], pss, ALU.mult, ALU.add)

    for si in range(NT):
        for c0 in range(0, len(chains), 4):
            caps = []
            for j, ch in enumerate(chains[c0:c0 + 4]):
                ch["pb"] = 2 * j
                p.begin_capture()
                step(ch, ch["order"][si], si)
                caps.append(p.end_capture())
            p.replay(caps)
    p.pop()


def merge_even(k, l):
    p, o, IN, C = k.p, k.o, k.IN, k.C
    e = l // 2
    p.push()
    wout = p.sb([8, D], BF16)
    wst = [p.sb([8, 256]) for _ in range(2)]
    for j in range(4):
        p.dma(wst[j % 2], p.dview("ev_w_out", IN["ev_w_out"].ap()[e, :, j * 256:(j + 1) * 256].rearrange("(kt q) f -> q kt f", q=128)))
        o.cp(wout[:, :, j * 256:(j + 1) * 256], wst[j % 2], eng="pool")
    gg = p.sb([128]); rg = p.sb([512])
    rowbc(k, gg, IN["ev_gdn_norm"].ap()[e:e + 1, :], "ev_gdn_norm")
    rowbc(k, rg, IN["ev_ret_norm"].ap()[e:e + 1, :], "ev_ret_norm")
    g1 = [p.sb([D]), p.sb([D])]
    for r in range(2):
        rowbc(k, g1[r], k.MOD.ap()[l, r:r + 1, 2048:3072], "MOD")
    lng = p.sb([D]); lnb = p.sb([D])
    rowbc(k, lng, IN["ln1_g"].ap()[l:l + 1, :], "ln1_g")
    rowbc(k, lnb, IN["ln1_b"].ap()[l:l + 1, :], "ln1_b")
    of = [p.sb([8, 128]) for _ in range(2)]
    ob = [p.sb([8, 128]) for _ in range(2)]
    zt = [p.sb([D]) for _ in range(2)]
    ht = [p.sb([D]) for _ in range(2)]
    Y = [p.sb([D]) for _ in range(2)]
    YT = [p.sb([8, 128], BF16) for _ in range(2)]
    st = [p.sb([32]) for _ in range(2)]
    junks = [p.sb([D]) for _ in range(2)]
    U = [p.sb([D]) for _ in range(2)]
    caps_ = []
    for t in range(NT):
        par = t % 2
        junk = junks[par]
        p.begin_capture()
        r = 1 if t < NCTX else 0
        a, b, z, h, y, yT, s, u = of[t % 2], ob[t % 2], zt[t % 2], ht[t % 2], Y[t % 2], YT[t % 2], st[t % 2], U[t % 2]
        tok = slice(t * 128, (t + 1) * 128)
        for hs in range(8):
            p.dma(a[:, hs, :], p.dview("OUTM%d_0" % hs, k.OUTM.ap()[hs, 0, tok, :], t * 128 * 128, (t + 1) * 128 * 128))
            p.dma(b[:, hs, :], p.dview("OUTM%d_1" % hs, k.OUTM.ap()[hs, 1, tok, :], t * 128 * 128, (t + 1) * 128 * 128))
        p.dma(z, p.dview("ZS", k.ZS.ap()[tok, :], t * 128 * D, (t + 1) * 128 * D))
        p.dma(h, p.dview("H", k.H.ap()[tok, :], t * 128 * D, (t + 1) * 128 * D))
        o.tt(a, a, b, ALU.add)
        for hs in range(8):
            o.act(junk[:, 0:128], a[:, hs, :], AF.Square, accum=s[:, hs:hs + 1])
        for hs in range(4, 8):
            o.act(junk[:, 0:128], a[:, hs, :], AF.Identity, accum=s[:, 8 + hs:9 + hs])
        o.act(s[:, 16:20], s[:, 0:4], AF.Sqrt, bias=k.eps[:, 0:1], scale=1.0 / 128)
        o.recip(s[:, 20:24], s[:, 16:20])
        o.ts(s[:, 24:28], s[:, 12:16], 1.0 / 128, None, ALU.mult)
        o.tt(s[:, 28:32], s[:, 24:28], s[:, 24:28], ALU.mult)
        o.stt(s[:, 16:20], s[:, 4:8], 1.0 / 128, s[:, 28:32], ALU.mult, ALU.subtract)
        o.act(s[:, 28:32], s[:, 16:20], AF.Sqrt, bias=k.eps[:, 0:1])
        o.recip(s[:, 16:20], s[:, 28:32])
        for hs in range(4):
            o.stt(y[:, hs * 128:(hs + 1) * 128], a[:, hs, :], s[:, 20 + hs:21 + hs], gg, ALU.mult, ALU.mult)
        for hs in range(4):
            o.ts(junk[:, 0:128], a[:, 4 + hs, :], s[:, 24 + hs:25 + hs], s[:, 16 + hs:17 + hs], ALU.subtract, ALU.mult)
            o.tt(y[:, 512 + hs * 128:512 + (hs + 1) * 128], junk[:, 0:128], rg[:, hs * 128:(hs + 1) * 128], ALU.mult)
        o.tt(y, y, z, ALU.mult)
        for half in range(2):
            ps = p.ps(half + 4 * par)
            for q in range(4):
                kt = half * 4 + q
                o.tr(ps[:, q * 128:(q + 1) * 128], y[:, kt * 128:(kt + 1) * 128], C["IDENT"])
            o.cp(yT.v(yT.ap[:, half * 4:(half + 1) * 4, :].rearrange("q a b -> q (a b)")), ps, eng="act")
        for half in range(2):
            ps = p.ps(2 + half + 4 * par)
            for kt in range(8):
                o.mm(ps, yT[:, kt, :], wout[:, kt, half * 512:(half + 1) * 512], start=(kt == 0), stop=(kt == 7))
            hs_ = slice(half * 512, (half + 1) * 512)
            o.tt(u[:, hs_], ps, g1[r][:, hs_], ALU.mult)
        if dbg_here(k, "y"):
            p.dma(p.dview("y_out", k.YOUT.ap()[tok, :], t * 128 * D, (t + 1) * 128 * D), u, q="pool")
        o.stt(u, h, ALPHA, u, ALU.mult, ALU.add)
        layer_norm(k, u, h, lng, lnb, s, junk)
        p.dma(p.dview("H", k.H.ap()[tok, :], t * 128 * D, (t + 1) * 128 * D), h, q="pool")
        if dbg_here(k, "h1"):
            p.dma(p.dview("y_out", k.YOUT.ap()[tok, :], t * 128 * D, (t + 1) * 128 * D), h, q="pool")
        caps_.append(p.end_capture())
        if par == 1 or t == NT - 1:
            p.replay(caps_)
            caps_ = []
    p.pop()


def layer_norm(k, u, out, g, b, s, junk):
    o = k.o
    o.act(junk, u, AF.Identity, accum=s[:, 0:1])
    o.act(junk, u, AF.Square, accum=s[:, 1:2])
    o.ts(s[:, 2:3], s[:, 0:1], 1.0 / D, None, ALU.mult)
    o.tt(s[:, 3:4], s[:, 2:3], s[:, 2:3], ALU.mult)
    o.stt(s[:, 4:5], s[:, 1:2], 1.0 / D, s[:, 3:4], ALU.mult, ALU.subtract)
    o.act(s[:, 5:6], s[:, 4:5], AF.Sqrt, bias=k.eps[:, 0:1])
    o.recip(s[:, 6:7], s[:, 5:6])
    o.ts(junk, u, s[:, 2:3], s[:, 6:7], ALU.subtract, ALU.mult)
    o.tt(junk, junk, g, ALU.mult)
    o.tt(out, junk, b, ALU.add)


def moe_layer(k, l, update_ctx=True):
    p, o, IN, C = k.p, k.o, k.IN, k.C
    p.push()
    g2 = [p.sb([D]), p.sb([D])]
    for r in range(2):
        rowbc(k, g2[r], k.MOD.ap()[l, r:r + 1, 5120:6144], "MOD")
    lng = p.sb([D]); lnb = p.sb([D])
    rowbc(k, lng, IN["ln2_g"].ap()[l:l + 1, :], "ln2_g")
    rowbc(k, lnb, IN["ln2_b"].ap()[l:l + 1, :], "ln2_b")
    wr = p.sb([8, 16])
    p.dma(wr, p.dview("w_router", IN["w_router"].ap()[l].rearrange("(kt q) e -> q kt e", q=128)))
    in2T = p.sb([8, TT], BF16)
    aff = p.sb([NT, 16]); gw = p.sb([NT, 16])
    p.push()
    affT = p.sb([TT])
    sc2 = [p.sb([D]), p.sb([D])]; sh2 = [p.sb([D]), p.sb([D])]
    for r in range(2):
        rowbc(k, sc2[r], k.MOD.ap()[l, r:r + 1, 4096:5120], "MOD")
        o.ts(sc2[r], sc2[r], 1.0, None, ALU.add)
        rowbc(k, sh2[r], k.MOD.ap()[l, r:r + 1, 3072:4096], "MOD")
    p.push()
    ht = [p.sb([D]) for _ in range(2)]
    x2 = [p.sb([D]) for _ in range(2)]
    xTf = [p.sb([8, 128]) for _ in range(2)]
    sm = [p.sb([40]) for _ in range(2)]
    for t in range(NT):
        r = 1 if t < NCTX else 0
        h, x, xf, s = ht[t % 2], x2[t % 2], xTf[t % 2], sm[t % 2]
        tok = slice(t * 128, (t + 1) * 128)
        p.dma(h, p.dview("H", k.H.ap()[tok, :], t * 128 * D, (t + 1) * 128 * D))
        o.tt(x, h, sc2[r], ALU.mult)
        o.tt(x, x, sh2[r], ALU.add)
        for half in range(2):
            ps = p.ps(half)
            for q in range(4):
                kt = half * 4 + q
                o.tr(ps[:, q * 128:(q + 1) * 128], x[:, kt * 128:(kt + 1) * 128], C["IDENT"])
            ps3 = ps.v(ps.ap.rearrange("q (a b) -> q a b", a=4))
            o.cp(xf[:, half * 4:(half + 1) * 4, :], ps3, eng="act")
            o.cp(in2T[:, half * 4:(half + 1) * 4, tok], ps3, eng="dve")
        psl = p.ps(2, 16)
        for kt in range(8):
            o.mm(psl, xf[:, kt, :], wr[:, kt, :], start=(kt == 0), stop=(kt == 7))
        p.op("dve", lambda e, s=s, psl=psl: e.reduce_max(s.ap[:, 0:1], psl.ap, AX.X), [psl], [s])
        o.ts(s[:, 1:2], s[:, 0:1], -1.0, None, ALU.mult)
        o.act(s[:, 8:24], psl, AF.Exp, bias=s[:, 1:2], accum=s[:, 2:3])
        o.recip(s[:, 3:4], s[:, 2:3])
        o.ts(aff[:, t, :], s[:, 8:24], s[:, 3:4], None, ALU.mult)
        pst = p.ps(3, 128)
        o.tr(pst[0:16, :], aff[:, t, :], C["IDENT"])
        o.cp(affT[0:16, tok], pst[0:16, :], eng="act")
    p.pop()
    p.push()
    st = p.sb([16]); junk = p.sb([4096]); thr = [p.sb([16]), p.sb([16])]; dt = p.sb([16])
    sets = [(0, 256, 32.0, 1), (256, TT, 512.0, 0)]
    for (c0, c1, cap, r) in sets:
        lo, hi, mid, cnt, ge, d1 = [st[0:16, i:i + 1] for i in range(6)]
        o.memset(lo, 0.0, eng="dve"); o.memset(hi, 1.0, eng="dve")
        for it in range(32):
            o.tt(mid, lo, hi, ALU.add)
            o.ts(mid, mid, 0.5, None, ALU.mult)
            o.ts(junk[0:16, 0:c1 - c0], affT[0:16, c0:c1], mid, None, ALU.is_ge, ALU.add, accum=cnt)
            o.ts(ge, cnt, cap - 0.5, None, ALU.is_ge)
            o.tt(d1, mid, lo, ALU.subtract)
            o.stt(lo, d1, ge, lo, ALU.mult, ALU.add)
            o.tt(d1, hi, mid, ALU.subtract)
            o.stt(hi, d1, ge, mid, ALU.mult, ALU.add)
        o.ts(dt[0:16, 0:16], C["IDENT"][0:16, 0:16], lo, None, ALU.mult)
        pth = p.ps(2, 16)
        o.mm(pth, C["ONES"][0:16, :], dt[0:16, 0:16])
        o.cp(thr[r], pth, eng="act")
    for t in range(NT):
        r = 1 if t < NCTX else 0
        o.tt(gw[:, t, :], aff[:, t, :], thr[r], ALU.is_ge)
        o.tt(gw[:, t, :], gw[:, t, :], aff[:, t, :], ALU.mult)
    p.pop()
    p.pop()
    p.push()
    wgf = [p.sb([8, 256]) for _ in range(2)]; wuf = [p.sb([8, 256]) for _ in range(2)]
    wgb = [p.sb([8, 256], BF16) for _ in range(2)]; wub = [p.sb([8, 256], BF16) for _ in range(2)]
    wdf = [p.sb([2, 512]) for _ in range(2)]; wdb = [p.sb([2, 512], BF16) for _ in range(2)]
    hT = p.sb([16, 512], BF16)
    sg = [p.sb([512]) for _ in range(2)]
    acc = [p.sb([D]) for _ in range(4)]
    hh = [p.sb([D])] * 2
    s8 = p.sb([8]); junk2 = p.sb([D])
    wi = 0
    di = 0
    import os
    nexp = int(os.environ.get("MOE_NEXP", 16))
    for (t0, n) in TOKCH:
        nst = n // 128
        for e in range(nexp):
            for fg in range(8):
                a32, b32, a16, b16 = wgf[wi % 2], wuf[wi % 2], wgb[wi % 2], wub[wi % 2]
                wi += 1
                p.dma(a32, p.dview("w_gate", IN["w_gate"].ap()[l, e, :, fg * 256:(fg + 1) * 256].rearrange("(kt q) f -> q kt f", q=128)))
                p.dma(b32, p.dview("w_up", IN["w_up"].ap()[l, e, :, fg * 256:(fg + 1) * 256].rearrange("(kt q) f -> q kt f", q=128)))
                o.cp(a16, a32, eng="pool")
                o.cp(b16, b32, eng="pool")
                for f2 in range(2):
                    ft = fg * 2 + f2
                    psg = p.ps(4 + (ft % 2) * 2, n)
                    psu = p.ps(5 + (ft % 2) * 2, n)
                    for kt in range(8):
                        o.mm(psg, a16[:, kt, f2 * 128:(f2 + 1) * 128], in2T[:, kt, t0:t0 + n], start=(kt == 0), stop=(kt == 7))
                    for kt in range(8):
                        o.mm(psu, b16[:, kt, f2 * 128:(f2 + 1) * 128], in2T[:, kt, t0:t0 + n], start=(kt == 0), stop=(kt == 7))
                    s_ = sg[ft % 2]
                    o.act(s_[:, 0:n], psg, AF.Silu)
                    o.tt(hT[:, ft, 0:n], s_[:, 0:n], psu, ALU.mult)
            for half in range(2):
                psd = [p.ps(st_, 512) for st_ in range(nst)]
                for fp_ in range(8):
                    d32, d16 = wdf[di % 2], wdb[di % 2]
                    di += 1
                    p.dma(d32, p.dview("w_down", IN["w_down"].ap()[l, e, fp_ * 256:(fp_ + 1) * 256, half * 512:(half + 1) * 512].rearrange("(a q) d -> q a d", q=128)))
                    o.cp(d16, d32, eng="pool")
                    for f2 in range(2):
                        ft = fp_ * 2 + f2
                        for st_ in range(nst):
                            o.mm(psd[st_], hT[:, ft, st_ * 128:(st_ + 1) * 128], d16[:, f2, :], start=(ft == 0), stop=(ft == 15))
                for st_ in range(nst):
                    t = t0 // 128 + st_
                    a_ = acc[st_][:, half * 512:(half + 1) * 512]
                    if e == 0:
                        o.ts(a_, psd[st_], gw[:, t, e:e + 1], None, ALU.mult)
                    else:
                        o.stt(a_, psd[st_], gw[:, t, e:e + 1], a_, ALU.mult, ALU.add)
        for st_ in range(nst):
            t = t0 // 128 + st_
            r = 1 if t < NCTX else 0
            tok = slice(t * 128, (t + 1) * 128)
            if dbg_here(k, "f"):
                p.dma(p.dview("y_out", k.YOUT.ap()[tok, :], t * 128 * D, (t + 1) * 128 * D), acc[st_], q="pool")
            h = hh[st_ % 2]
            p.dma(h, p.dview("H", k.H.ap()[tok, :], t * 128 * D, (t + 1) * 128 * D))
            u = acc[st_]
            o.tt(u, u, g2[r], ALU.mult)
            o.stt(u, h, ALPHA, u, ALU.mult, ALU.add)
            layer_norm(k, u, h, lng, lnb, s8, junk2)
            if update_ctx or r == 0:
                p.dma(p.dview("H", k.H.ap()[tok, :], t * 128 * D, (t + 1) * 128 * D), h, q="pool")
    p.pop()
    p.pop()


def rev(t, n):
    a = t.ap
    st = a.ap[-1][0]
    return t.v(bass.AP(a.tensor, a.offset + (n - 1) * st, [list(a.ap[0]), [-st, n]]))


def bcast_cols(t, n):
    a = t.ap
    return t.v(bass.AP(a.tensor, a.offset, [list(a.ap[0]), [0, n]]))


SIN_C = [-1.0 / 6, 1.0 / 120, -1.0 / 5040, 1.0 / 362880, -1.0 / 39916800]
COS_C = [-0.5, 1.0 / 24, -1.0 / 720, 1.0 / 40320, -1.0 / 3628800, 1.0 / 479001600]


def s5_scan(k, l):
    p, o, IN, C = k.p, k.o, k.IN, k.C
    e = l // 2
    MK = C["MK"]
    p.push()
    WCf = [[p.sb([128]) for _ in range(16)] for _ in range(2)]
    WB = [[[p.sb([128]) for _ in range(16)] for _ in range(2)] for _ in range(2)]
    mag = [p.sb([16]) for _ in range(2)]
    CLv = [p.sb([16, 10]) for _ in range(2)]; SLv = [p.sb([16, 10]) for _ in range(2)]; NSLv = [p.sb([16, 10]) for _ in range(2)]
    p.push()
    BR = p.sb([16, 16]); BI = p.sb([16, 16])
    CLr = p.sb([16, 64]); CLi = p.sb([16, 64])
    for gl in range(2):
        pr = slice(gl * 64, (gl + 1) * 64)
        p.dma(BR[pr], p.dview("od_b_re", IN["od_b_re"].ap()[e].rearrange("(st gl) q h -> gl q st h", gl=2)[gl]))
        p.dma(BI[pr], p.dview("od_b_im", IN["od_b_im"].ap()[e].rearrange("(st gl) q h -> gl q st h", gl=2)[gl]))
        pc = slice(gl * 16, (gl + 1) * 16)
        p.dma(CLr[pc], p.dview("od_c_re", IN["od_c_re"].ap()[e].rearrange("(st gl) h q -> gl h st q", gl=2)[gl]))
        p.dma(CLi[pc], p.dview("od_c_im", IN["od_c_im"].ap()[e].rearrange("(st gl) h q -> gl h st q", gl=2)[gl]))
    X = [p.sb([128]) for _ in range(2)]
    for ri, CLx in enumerate((CLr, CLi)):
        for st in range(16):
            s4 = st % 4
            x = X[st % 2]
            for gl2 in range(2):
                o.ts(x[0:32, gl2 * 64:(gl2 + 1) * 64], CLx[0:32, st, :], MK[0:32, 2 + gl2:3 + gl2], None, ALU.mult)
            ps = p.ps(st % 2, 32)
            o.tr(ps, x[0:32, :], C["IDENT"][0:32, 0:32])
            w = WCf[ri][st]
            o.memset(w, 0.0, eng="pool")
            if ri == 0:
                o.cp(w[:, s4 * 32:(s4 + 1) * 32], ps, eng="act")
            else:
                o.ts(w[:, s4 * 32:(s4 + 1) * 32], ps, -1.0, None, ALU.mult)
    LR = p.sb([16]); LI = p.sb([16]); DT = p.sb([16])
    tl = {n: p.sb([16]) for n in ["lr", "dt", "a", "th", "x", "z", "q", "s", "c", "cc", "ss", "cs", "abr", "abi", "xr", "den",
                                  "t1", "t2", "fre", "fim", "nfim"]}
    fm = {n: p.sb([16]) for n in ["fre0", "fre1", "fim0", "fim1", "nfim0", "nfim1"]}
    INre = [p.sb([128]) for _ in range(4)]; INim = [p.sb([128]) for _ in range(4)]
    tb = p.sb([16])
    for d in range(2):
        for gl in range(2):
            pr = slice(gl * 64, (gl + 1) * 64)
            p.dma(LR[pr], p.dview("od_lam_re", IN["od_lam_re"].ap()[e, d].rearrange("(st gl) q -> gl q st", gl=2)[gl]))
            p.dma(LI[pr], p.dview("od_lam_im", IN["od_lam_im"].ap()[e, d].rearrange("(st gl) q -> gl q st", gl=2)[gl]))
            p.dma(DT[pr], p.dview("od_log_dt", IN["od_log_dt"].ap()[e, d:d + 1, :].rearrange("a (st gl) -> a gl st", gl=2)[:, gl, :].partition_broadcast(64)))
        T_ = tl
        o.ts(T_["lr"], LR, -1e-4, None, ALU.min)
        o.act(T_["dt"], DT, AF.Exp)
        o.tt(T_["a"], T_["lr"], T_["dt"], ALU.mult)
        o.act(mag[d], T_["a"], AF.Exp)
        o.tt(T_["th"], LI, T_["dt"], ALU.mult)
        o.ts(T_["x"], T_["th"], 1.0 / 16, None, ALU.mult)
        o.tt(T_["z"], T_["x"], T_["x"], ALU.mult)
        o.ts(T_["q"], T_["z"], SIN_C[4], None, ALU.mult)
        for a_ in (SIN_C[3], SIN_C[2], SIN_C[1], SIN_C[0]):
            o.stt(T_["q"], T_["q"], a_, T_["z"], ALU.add, ALU.mult)
        o.stt(T_["s"], T_["q"], 1.0, T_["x"], ALU.add, ALU.mult)
        o.ts(T_["q"], T_["z"], COS_C[5], None, ALU.mult)
        for a_ in (COS_C[4], COS_C[3], COS_C[2], COS_C[1], COS_C[0]):
            o.stt(T_["q"], T_["q"], a_, T_["z"], ALU.add, ALU.mult)
        o.ts(T_["c"], T_["q"], 1.0, None, ALU.add)
        for _ in range(4):
            o.tt(T_["cc"], T_["c"], T_["c"], ALU.mult)
            o.tt(T_["ss"], T_["s"], T_["s"], ALU.mult)
            o.tt(T_["cs"], T_["c"], T_["s"], ALU.mult)
            o.tt(T_["c"], T_["cc"], T_["ss"], ALU.subtract)
            o.ts(T_["s"], T_["cs"], 2.0, None, ALU.mult)
        o.cp(CLv[d][:, :, 0], T_["c"]); o.cp(SLv[d][:, :, 0], T_["s"])
        for kk in range(1, 10):
            o.tt(T_["cc"], CLv[d][:, :, kk - 1], CLv[d][:, :, kk - 1], ALU.mult)
            o.tt(T_["ss"], SLv[d][:, :, kk - 1], SLv[d][:, :, kk - 1], ALU.mult)
            o.tt(T_["cs"], CLv[d][:, :, kk - 1], SLv[d][:, :, kk - 1], ALU.mult)
            o.tt(CLv[d][:, :, kk], T_["cc"], T_["ss"], ALU.subtract)
            o.ts(SLv[d][:, :, kk], T_["cs"], 2.0, None, ALU.mult)
        o.ts(NSLv[d], SLv[d], -1.0, None, ALU.mult)
        o.tt(T_["abr"], mag[d], T_["c"], ALU.mult)
        o.tt(T_["abi"], mag[d], T_["s"], ALU.mult)
        o.ts(T_["xr"], T_["abr"], -1.0, None, ALU.add)
        o.tt(T_["t1"], T_["lr"], T_["lr"], ALU.mult)
        o.tt(T_["t2"], LI, LI, ALU.mult)
        o.tt(T_["den"], T_["t1"], T_["t2"], ALU.add)
        o.recip(T_["den"], T_["den"])
        o.tt(T_["t1"], T_["xr"], T_["lr"], ALU.mult)
        o.tt(T_["t2"], T_["abi"], LI, ALU.mult)
        o.tt(T_["t1"], T_["t1"], T_["t2"], ALU.add)
        o.tt(T_["fre"], T_["t1"], T_["den"], ALU.mult)
        o.tt(T_["t1"], T_["abi"], T_["lr"], ALU.mult)
        o.tt(T_["t2"], T_["xr"], LI, ALU.mult)
        o.tt(T_["t1"], T_["t1"], T_["t2"], ALU.subtract)
        o.tt(T_["fim"], T_["t1"], T_["den"], ALU.mult)
        o.ts(T_["nfim"], T_["fim"], -1.0, None, ALU.mult)
        for gl in range(2):
            o.ts(fm["fre%d" % gl], T_["fre"], MK[:, gl:gl + 1], None, ALU.mult)
            o.ts(fm["fim%d" % gl], T_["fim"], MK[:, gl:gl + 1], None, ALU.mult)
            o.ts(fm["nfim%d" % gl], T_["nfim"], MK[:, gl:gl + 1], None, ALU.mult)
        for st in range(16):
            ft_, s4 = divmod(st, 4)
            for gl in range(2):
                cs_ = slice(s4 * 32 + gl * 16, s4 * 32 + gl * 16 + 16)
                fre, fim, nfim = fm["fre%d" % gl][:, st:st + 1], fm["fim%d" % gl][:, st:st + 1], fm["nfim%d" % gl][:, st:st + 1]
                o.ts(tb, BR[:, st, :], fre, None, ALU.mult)
                o.stt(INre[ft_][:, cs_], BI[:, st, :], nfim, tb, ALU.mult, ALU.add)
                o.ts(tb, BR[:, st, :], fim, None, ALU.mult)
                o.stt(INim[ft_][:, cs_], BI[:, st, :], fre, tb, ALU.mult, ALU.add)
        for ft_ in range(4):
            for ri, INx in enumerate((INre, INim)):
                ps = p.ps(2 + ri, 128)
                o.tr(ps, INx[ft_], C["IDENT"])
                for s4 in range(4):
                    o.ts(WB[d][ri][ft_ * 4 + s4], ps, MK[:, 4 + s4:5 + s4], None, ALU.mult)
    p.pop()
    NSEG = [(0, 256)] + [(256 + 512 * i, 512) for i in range(8)]
    cosT = [p.sb([516]) for _ in range(2)]; sinT = [p.sb([516]) for _ in range(2)]
    tAs = [p.sb([256]) for _ in range(2)]; tBs = [p.sb([256]) for _ in range(2)]
    uF = p.sb([TT]); yaccs = [p.sb([TT]) for _ in range(2)]
    wk = {n: [p.sb([512]) for _ in range(2)] for n in ["t1", "t2", "t3", "t4", "wre", "wim", "gre", "gim", "hre", "him"]}
    inis = [[p.sb([8]) for _ in range(2)] for _ in range(2)]
    crs = [p.sb([8]) for _ in range(2)]

    def stream(d, ft_, sidx, s4list, segs):
        cosv, sinv, tA, tB, yacc, cr = cosT[sidx], sinT[sidx], tAs[sidx], tBs[sidx], yaccs[sidx], crs[sidx]
        W_ = {nm: wk[nm][sidx] for nm in wk}
        ini = inis[sidx]
        bk_r, bk_i, bk_y = (4, 5, 0) if sidx == 0 else (6, 7, 1)
        for s4 in s4list:
            st = ft_ * 4 + s4
            o.memset(cosv[:, 0:1], 1.0, eng="dve"); o.memset(sinv[:, 0:1], 0.0, eng="dve")
            for kk in range(9):
                w = 1 << kk
                c_, s_, ns_ = CLv[d][:, st, kk:kk + 1], SLv[d][:, st, kk:kk + 1], NSLv[d][:, st, kk:kk + 1]
                o.ts(tA[:, 0:w], cosv[:, 0:w], c_, None, ALU.mult)
                o.ts(tB[:, 0:w], sinv[:, 0:w], c_, None, ALU.mult)
                o.stt(tA[:, 0:w], sinv[:, 0:w], ns_, tA[:, 0:w], ALU.mult, ALU.add)
                o.stt(tB[:, 0:w], cosv[:, 0:w], s_, tB[:, 0:w], ALU.mult, ALU.add)
                o.cp(cosv[:, w:2 * w], tA[:, 0:w]); o.cp(sinv[:, w:2 * w], tB[:, 0:w])
            o.cp(cosv[:, 512:513], CLv[d][:, st, 9:10]); o.cp(sinv[:, 512:513], SLv[d][:, st, 9:10])
            cur = ini[0]
            o.memset(cur[:, 0:2], 0.0, eng="dve")
            for si, (t0, n) in enumerate(segs):
                psr = p.ps(bk_r, n); psi = p.ps(bk_i, n)
                o.mm(psr, WB[d][0][st], uF[:, t0:t0 + n])
                o.mm(psi, WB[d][1][st], uF[:, t0:t0 + n])
                V = (lambda t_: rev(t_, n)) if d == 1 else (lambda t_: t_[:, 0:n])
                cs_n, sn_n = cosv[:, 0:n], sinv[:, 0:n]
                o.tt(W_["t1"][:, 0:n], V(psr), cs_n, ALU.mult)
                o.tt(W_["t2"][:, 0:n], V(psi), sn_n, ALU.mult)
                o.tt(W_["wre"][:, 0:n], W_["t1"][:, 0:n], W_["t2"][:, 0:n], ALU.add)
                o.tt(W_["t3"][:, 0:n], V(psi), cs_n, ALU.mult)
                o.tt(W_["t4"][:, 0:n], V(psr), sn_n, ALU.mult)
                o.tt(W_["wim"][:, 0:n], W_["t3"][:, 0:n], W_["t4"][:, 0:n], ALU.subtract)
                gre, gim = W_["gre"], W_["gim"]
                mb = bcast_cols(mag[d][:, st:st + 1], n)
                p.op("dve", lambda e_, gre=gre, mb=mb, w=W_["wre"], cur=cur, n=n: e_.tensor_tensor_scan(
                    gre.ap[:, 0:n], mb.ap, w.ap[:, 0:n], cur.ap[:, 0:1], ALU.mult, ALU.add), [mb, W_["wre"], cur], [gre])
                p.op("dve", lambda e_, gim=gim, mb=mb, w=W_["wim"], cur=cur, n=n: e_.tensor_tensor_scan(
                    gim.ap[:, 0:n], mb.ap, w.ap[:, 0:n], cur.ap[:, 1:2], ALU.mult, ALU.add), [mb, W_["wim"], cur], [gim])
                o.tt(W_["t1"][:, 0:n], gre[:, 0:n], cs_n, ALU.mult)
                o.tt(W_["t2"][:, 0:n], gim[:, 0:n], sn_n, ALU.mult)
                o.tt(V(W_["hre"]), W_["t1"][:, 0:n], W_["t2"][:, 0:n], ALU.subtract)
                o.tt(W_["t3"][:, 0:n], gim[:, 0:n], cs_n, ALU.mult)
                o.tt(W_["t4"][:, 0:n], gre[:, 0:n], sn_n, ALU.mult)
                o.tt(V(W_["him"]), W_["t3"][:, 0:n], W_["t4"][:, 0:n], ALU.add)
                nxt = ini[(si + 1) % 2]
                o.ts(cr[:, 0:1], gre[:, n - 1:n], cosv[:, n:n + 1], None, ALU.mult)
                o.ts(cr[:, 1:2], gim[:, n - 1:n], sinv[:, n:n + 1], None, ALU.mult)
                o.tt(nxt[:, 0:1], cr[:, 0:1], cr[:, 1:2], ALU.subtract)
                o.ts(cr[:, 2:3], gim[:, n - 1:n], cosv[:, n:n + 1], None, ALU.mult)
                o.stt(nxt[:, 1:2], gre[:, n - 1:n], sinv[:, n:n + 1], cr[:, 2:3], ALU.mult, ALU.add)
                cur = nxt
                psy = p.ps(bk_y, n)
                o.mm(psy, WCf[0][st], W_["hre"][:, 0:n], start=True, stop=False)
                o.mm(psy, WCf[1][st], W_["him"][:, 0:n], start=False, stop=True)
                if s4 == s4list[0]:
                    o.cp(yacc[:, t0:t0 + n], psy, eng="act")
                else:
                    o.tt(yacc[:, t0:t0 + n], psy, yacc[:, t0:t0 + n], ALU.add)

    for d in range(2):
        segs = NSEG if d == 0 else [NSEG[0]] + NSEG[:0:-1]
        for ft_ in range(4):
            p.dma(uF, p.dview("FM", k.FM.ap()[12 + ft_], (12 + ft_) * 128 * TT, (13 + ft_) * 128 * TT))
            caps = []
            for sidx, s4list in enumerate(((0, 2), (1, 3))):
                p.begin_capture()
                stream(d, ft_, sidx, s4list, segs)
                caps.append(p.end_capture())
            p.replay(caps)
            o.tt(yaccs[0], yaccs[0], yaccs[1], ALU.add)
            p.dma(p.dview("YS5", k.YS5.ap()[d, ft_], (d * 4 + ft_) * 128 * TT, (d * 4 + ft_ + 1) * 128 * TT), yaccs[0], q="pool")
    p.pop()


def merge_odd(k, l):
    p, o, IN, C = k.p, k.o, k.IN, k.C
    e = l // 2
    p.push()
    wout = p.sb([8, D], BF16)
    wst = [p.sb([8, 256]) for _ in range(2)]
    for j in range(4):
        p.dma(wst[j % 2], p.dview("od_w_out", IN["od_w_out"].ap()[e, :, j * 256:(j + 1) * 256].rearrange("(kt q) f -> q kt f", q=128)))
        o.cp(wout[:, :, j * 256:(j + 1) * 256], wst[j % 2], eng="pool")
    wglu = p.sb([4, 512], BF16)
    for j in range(2):
        w32 = wst[j % 2]
        w3 = w32.v(w32.ap.rearrange("q a b -> q (a b)")[:, 0:1024].rearrange("q (a b) -> q a b", a=4))
        p.dma(w3, p.dview("od_w_glu", IN["od_w_glu"].ap()[e, :, j * 256:(j + 1) * 256].rearrange("(kt q) f -> q kt f", q=128)))
        o.cp(wglu[:, :, j * 256:(j + 1) * 256], w3, eng="pool")
    mg = p.sb([512]); dsk = p.sb([512]); bgl = p.sb([512])
    rowbc(k, mg, IN["od_mlstm_norm"].ap()[e:e + 1, :], "od_mlstm_norm")
    rowbc(k, dsk, IN["od_d_skip"].ap()[e:e + 1, :], "od_d_skip")
    rowbc(k, bgl, IN["od_b_glu"].ap()[e:e + 1, :], "od_b_glu")
    g1 = [p.sb([D]), p.sb([D])]
    for r in range(2):
        rowbc(k, g1[r], k.MOD.ap()[l, r:r + 1, 2048:3072], "MOD")
    lng = p.sb([D]); lnb = p.sb([D])
    rowbc(k, lng, IN["ln1_g"].ap()[l:l + 1, :], "ln1_g")
    rowbc(k, lnb, IN["ln1_b"].ap()[l:l + 1, :], "ln1_b")
    A2 = [p.sb([2, 4, 132]) for _ in range(2)]
    YS = [p.sb([2, 4, 128]) for _ in range(2)]
    zt = [p.sb([D]) for _ in range(2)]
    ht = [p.sb([D]) for _ in range(2)]
    Y = [p.sb([D]) for _ in range(2)]
    YT = [p.sb([8, 128], BF16) for _ in range(2)]
    st_ = [p.sb([48]) for _ in range(2)]
    junks = [p.sb([D]) for _ in range(2)]
    U = [p.sb([D]) for _ in range(2)]
    HM = [p.sb([4, 128]) for _ in range(2)]
    YG = [p.sb([512]) for _ in range(2)]
    YGT = [p.sb([4, 128], BF16) for _ in range(2)]
    caps_ = []
    for t in range(NT):
        par = t % 2
        junk = junks[par]
        p.begin_capture()
        r = 1 if t < NCTX else 0
        a2, ys, z, h, y, yT, s, u, hm, yg, ygT = (A2[t % 2], YS[t % 2], zt[t % 2], ht[t % 2], Y[t % 2], YT[t % 2], st_[t % 2],
                                                  U[t % 2], HM[t % 2], YG[t % 2], YGT[t % 2])
        tok = slice(t * 128, (t + 1) * 128)
        for d in range(2):
            for hh in range(4):
                p.dma(a2[:, d, hh, 0:129], p.dview("OUTO%d_%d" % (hh, d), k.OUTO.ap()[hh, d, tok, 0:129], t * 128 * 132, (t + 1) * 128 * 132))
                fi = d * 4 + hh
                p.dma(ys[:, d, hh, :], p.dview("YS5", k.YS5.ap()[d, hh][:, tok], fi * 128 * TT, (fi + 1) * 128 * TT))
        p.dma(z, p.dview("ZS", k.ZS.ap()[tok, :], t * 128 * D, (t + 1) * 128 * D))
        p.dma(h, p.dview("H", k.H.ap()[tok, :], t * 128 * D, (t + 1) * 128 * D))
        den = a2[:, :, :, 128]
        s3 = lambda c0: s.v(s.ap[:, c0:c0 + 8].rearrange("q (a b) -> q a b", a=2))
        o.ts(s3(0), den, -1.0, None, ALU.mult)
        o.tt(s3(0), s3(0), den, ALU.max)
        o.ts(s3(0), s3(0), 1.0, None, ALU.max)
        o.recip(s[:, 8:16], s[:, 0:8])
        for hh in range(4):
            o.ts(junk[:, 0:128], a2[:, 0, hh, 0:128], s[:, 8 + hh:9 + hh], None, ALU.mult)
            o.stt(hm[:, hh, :], a2[:, 1, hh, 0:128], s[:, 12 + hh:13 + hh], junk[:, 0:128], ALU.mult, ALU.add)
        for hh in range(4):
            o.act(junk[:, 0:128], hm[:, hh, :], AF.Square, accum=s[:, 16 + hh:17 + hh])
            o.act(junk[:, 128:256], hm[:, hh, :], AF.Identity, accum=s[:, 20 + hh:21 + hh])
        o.ts(s[:, 24:28], s[:, 20:24], 1.0 / 128, None, ALU.mult)
        o.tt(s[:, 28:32], s[:, 24:28], s[:, 24:28], ALU.mult)
        o.stt(s[:, 32:36], s[:, 16:20], 1.0 / 128, s[:, 28:32], ALU.mult, ALU.subtract)
        o.act(s[:, 36:40], s[:, 32:36], AF.Sqrt, bias=k.eps[:, 0:1])
        o.recip(s[:, 40:44], s[:, 36:40])
        for hh in range(4):
            o.ts(junk[:, 0:128], hm[:, hh, :], s[:, 24 + hh:25 + hh], s[:, 40 + hh:41 + hh], ALU.subtract, ALU.mult)
            o.tt(y[:, hh * 128:(hh + 1) * 128], junk[:, 0:128], mg[:, hh * 128:(hh + 1) * 128], ALU.mult)
        o.tt(y[:, 0:512], y[:, 0:512], z[:, 0:512], ALU.mult)
        o.tt(ys[:, 0], ys[:, 0], ys[:, 1], ALU.add)
        ps = p.ps(0 + 4 * par)
        for ft_ in range(4):
            o.tr(ps[:, ft_ * 128:(ft_ + 1) * 128], ys[:, 0, ft_, :], C["IDENT"])
        o.tt(junk[:, 0:512], z[:, 512:1024], dsk, ALU.mult)
        o.tt(junk[:, 0:512], junk[:, 0:512], ps, ALU.add)
        xg = junk[:, 0:512]
        o.tt(junk[:, 512:1024], xg, xg, ALU.mult)
        o.ts(junk[:, 512:1024], junk[:, 512:1024], 0.044715, 1.0, ALU.mult, ALU.add)
        o.tt(junk[:, 512:1024], junk[:, 512:1024], xg, ALU.mult)
        o.act(yg, junk[:, 512:1024], AF.Tanh, scale=math.sqrt(2.0 / math.pi))
        o.stt(yg, yg, 1.0, xg, ALU.add, ALU.mult)
        o.ts(yg, yg, 0.5, None, ALU.mult)
        ps2 = p.ps(1 + 4 * par)
        for ft_ in range(4):
            o.tr(ps2[:, ft_ * 128:(ft_ + 1) * 128], yg[:, ft_ * 128:(ft_ + 1) * 128], C["IDENT"])
        o.cp(ygT.v(ygT.ap.rearrange("q a b -> q (a b)")), ps2, eng="act")
        ps3 = p.ps(2 + 4 * par)
        for kt in range(4):
            o.mm(ps3, ygT[:, kt, :], wglu[:, kt, :], start=(kt == 0), stop=(kt == 3))
        o.tt(junk[:, 0:512], ps3, bgl, ALU.add)
        o.act(junk[:, 512:1024], junk[:, 0:512], AF.Sigmoid)
        o.tt(y[:, 512:1024], yg, junk[:, 512:1024], ALU.mult)
        for half in range(2):
            ps = p.ps(half + 4 * par)
            for q in range(4):
                kt = half * 4 + q
                o.tr(ps[:, q * 128:(q + 1) * 128], y[:, kt * 128:(kt + 1) * 128], C["IDENT"])
            o.cp(yT.v(yT.ap[:, half * 4:(half + 1) * 4, :].rearrange("q a b -> q (a b)")), ps, eng="act")
        for half in range(2):
            ps = p.ps(2 + half + 4 * par)
            for kt in range(8):
                o.mm(ps, yT[:, kt, :], wout[:, kt, half * 512:(half + 1) * 512], start=(kt == 0), stop=(kt == 7))
            hs_ = slice(half * 512, (half + 1) * 512)
            o.tt(u[:, hs_], ps, g1[r][:, hs_], ALU.mult)
        if dbg_here(k, "y"):
            p.dma(p.dview("y_out", k.YOUT.ap()[tok, :], t * 128 * D, (t + 1) * 128 * D), u, q="pool")
        o.stt(u, h, ALPHA, u, ALU.mult, ALU.add)
        layer_norm(k, u, h, lng, lnb, s, junk)
        p.dma(p.dview("H", k.H.ap()[tok, :], t * 128 * D, (t + 1) * 128 * D), h, q="pool")
        if dbg_here(k, "h1"):
            p.dma(p.dview("y_out", k.YOUT.ap()[tok, :], t * 128 * D, (t + 1) * 128 * D), h, q="pool")
        caps_.append(p.end_capture())
        if par == 1 or t == NT - 1:
            p.replay(caps_)
            caps_ = []
    p.pop()


def moe_layer2(k, l):
    p, o, IN, C = k.p, k.o, k.IN, k.C
    IOTA = k.IOTA
    IC = C["IC"]
    SETS = [(list(range(NCTX, NT)), 512, [(0, 128), (128, 128), (256, 128), (384, 128)], 0),
            (list(range(0, NCTX)), 32, [(0, 32)], 1)]
    p.push()
    wr = p.sb([8, 16])
    p.dma(wr, p.dview("w_router", IN["w_router"].ap()[l].rearrange("(kt q) e -> q kt e", q=128)))
    aff = p.sb([NT, 16]); gw = p.sb([NT, 16]); msk = p.sb([NT, 16]); posm = p.sb([NT, 16])
    posmT = p.sb([TT])
    ghl = p.sb([NT, 16, 2], BF16)
    p.push()
    in2tm = p.sb([NT, D], BF16)
    p.push()
    affT = p.sb([TT])
    sc2 = [p.sb([D]), p.sb([D])]; sh2 = [p.sb([D]), p.sb([D])]
    for r in range(2):
        rowbc(k, sc2[r], k.MOD.ap()[l, r:r + 1, 4096:5120], "MOD")
        o.ts(sc2[r], sc2[r], 1.0, None, ALU.add)
        rowbc(k, sh2[r], k.MOD.ap()[l, r:r + 1, 3072:4096], "MOD")
    ht = [p.sb([D]) for _ in range(2)]
    x2 = [p.sb([D]) for _ in range(2)]
    xTf = [p.sb([8, 128]) for _ in range(2)]
    sm = [p.sb([40]) for _ in range(2)]
    caps_ = []
    for t in range(NT):
        par = t % 2
        p.begin_capture()
        r = 1 if t < NCTX else 0
        h, x, xf, s = ht[t % 2], x2[t % 2], xTf[t % 2], sm[t % 2]
        tok = slice(t * 128, (t + 1) * 128)
        p.dma(h, p.dview("H", k.H.ap()[tok, :], t * 128 * D, (t + 1) * 128 * D))
        o.tt(x, h, sc2[r], ALU.mult)
        o.tt(x, x, sh2[r], ALU.add)
        o.cp(in2tm[:, t, :], x, eng="pool")
        for half in range(2):
            ps = p.ps(half + 4 * par)
            for q in range(4):
                kt = half * 4 + q
                o.tr(ps[:, q * 128:(q + 1) * 128], x[:, kt * 128:(kt + 1) * 128], C["IDENT"])
            ps3 = ps.v(ps.ap.rearrange("q (a b) -> q a b", a=4))
            o.cp(xf[:, half * 4:(half + 1) * 4, :], ps3, eng="act")
        psl = p.ps(2 + 4 * par, 16)
        for kt in range(8):
            o.mm(psl, xf[:, kt, :], wr[:, kt, :], start=(kt == 0), stop=(kt == 7))
        p.op("dve", lambda e, s=s, psl=psl: e.reduce_max(s.ap[:, 0:1], psl.ap, AX.X), [psl], [s])
        o.ts(s[:, 1:2], s[:, 0:1], -1.0, None, ALU.mult)
        o.act(s[:, 8:24], psl, AF.Exp, bias=s[:, 1:2], accum=s[:, 2:3])
        o.recip(s[:, 3:4], s[:, 2:3])
        o.ts(aff[:, t, :], s[:, 8:24], s[:, 3:4], None, ALU.mult)
        pst = p.ps(3 + 4 * par, 128)
        o.tr(pst[0:16, :], aff[:, t, :], C["IDENT"])
        o.cp(affT[0:16, tok], pst[0:16, :], eng="act")
        caps_.append(p.end_capture())
        if par == 1 or t == NT - 1:
            p.replay(caps_)
            caps_ = []
    sts = [p.sb([16]), p.sb([16])]; junks = [p.sb([4096]), p.sb([256])]; thr = [p.sb([16]), p.sb([16])]; dts = [p.sb([16]), p.sb([16])]
    sets = [(256, TT, 512.0, 0), (0, 256, 32.0, 1)]
    caps_ = []
    for si_, (c0, c1, cap, r) in enumerate(sets):
        p.begin_capture()
        st, junk, dt = sts[si_], junks[si_], dts[si_]
        lo, hi, mid, cnt, ge, d1, d2 = [st[0:16, i:i + 1] for i in range(7)]
        o.memset(lo, 0.0, eng="dve"); o.memset(hi, 1.0, eng="dve")
        for it in range(32):
            o.tt(mid, lo, hi, ALU.add)
            o.ts(mid, mid, 0.5, None, ALU.mult)
            o.ts(junk[0:16, 0:c1 - c0], affT[0:16, c0:c1], mid, None, ALU.is_ge, ALU.add, accum=cnt)
            o.ts(ge, cnt, cap - 0.5, None, ALU.is_ge)
            o.tt(d1, mid, lo, ALU.subtract)
            o.tt(d2, hi, mid, ALU.subtract)
            o.stt(lo, d1, ge, lo, ALU.mult, ALU.add)
            o.stt(hi, d2, ge, mid, ALU.mult, ALU.add)
        o.ts(dt[0:16, 0:16], C["IDENT"][0:16, 0:16], lo, None, ALU.mult)
        pth = p.ps(2 + si_, 16)
        o.mm(pth, C["ONES"][0:16, :], dt[0:16, 0:16])
        o.cp(thr[r], pth, eng="act")
        caps_.append(p.end_capture())
    p.replay(caps_)
    tot = p.sb([16]); tmp16 = p.sb([16])
    for (tiles, cap, jts, yi) in SETS:
        r = 1 if tiles[0] < NCTX else 0
        o.memset(tot, 0.0, eng="dve")
        for t in tiles:
            tok = slice(t * 128, (t + 1) * 128)
            o.tt(msk[:, t, :], aff[:, t, :], thr[r], ALU.is_ge)
            o.tt(gw[:, t, :], msk[:, t, :], aff[:, t, :], ALU.mult)
            ps = p.ps(0, 16); ps2 = p.ps(1, 16)
            o.mm(ps, C["CSs"], msk[:, t, :])
            o.mm(ps2, C["ONES"], msk[:, t, :])
            o.tt(tmp16, ps, tot, ALU.add)
            o.tt(tot, tot, ps2, ALU.add)
            o.stt(tmp16, tmp16, 1.0, msk[:, t, :], ALU.add, ALU.mult)
            o.ts(posm[:, t, :], tmp16, -1.0, None, ALU.add)
            pst = p.ps(3, 128)
            o.tr(pst[0:16, :], posm[:, t, :], C["IDENT"])
            o.cp(posmT[0:16, tok], pst[0:16, :], eng="act")
    hi16 = p.sb([NT, 16], BF16); hi32 = p.sb([NT, 16]); lo32 = p.sb([NT, 16])
    o.cp(hi16, gw); o.cp(hi32, hi16); o.tt(lo32, gw, hi32, ALU.subtract)
    o.cp(ghl[:, :, :, 0], hi32); o.cp(ghl[:, :, :, 1], lo32)
    p.pop()
    p.push()
    wgf = [p.sb([8, 128]) for _ in range(2)]; wuf = [p.sb([8, 128]) for _ in range(2)]
    wgb = [p.sb([8, 128], BF16) for _ in range(2)]; wub = [p.sb([8, 128], BF16) for _ in range(2)]
    wdf = [p.sb([2, 512]) for _ in range(2)]; wdb = [p.sb([2, 512], BF16) for _ in range(2)]
    xsT = [p.sb([8, 512], BF16), p.sb([8, 32], BF16)]
    hT = [p.sb([16, 512], BF16), p.sb([16, 32], BF16)]
    sg = [p.sb([512]) for _ in range(2)]
    sel = [p.sb([512], BF16) for _ in range(3)]
    gsel = [p.sb([4]), p.sb([4])]
    ysb = [[p.sb([D], BF16) for _ in range(5)] for _ in range(2)]
    wi = 0; di = 0; si = 0
    for e in range(16):
        for (tiles, cap, jts, yi) in SETS:
            for ps_ in range(2):
                psx = [p.ps(q, cap) for q in range(4)]
                psg = [p.ps(4 + ji, 2) for ji in range(len(jts))]
                for t in tiles:
                    sl = sel[si % 3]; si += 1
                    o.ts(sl[:, 0:cap], IOTA[:, 0:cap], posm[:, t, e:e + 1], None, ALU.is_equal)
                    for q in range(4):
                        kt = ps_ * 4 + q
                        o.mm(psx[q], in2tm[:, t, kt * 128:(kt + 1) * 128], sl[:, 0:cap], start=(t == tiles[0]), stop=(t == tiles[-1]))
                    if ps_ == 0:
                        for ji, (j0, nj) in enumerate(jts):
                            o.mm(psg[ji][0:nj, :], sl[:, j0:j0 + nj], ghl[:, t, e, :], start=(t == tiles[0]), stop=(t == tiles[-1]))
                for q in range(4):
                    kt = ps_ * 4 + q
                    o.cp(xsT[yi][:, kt, 0:cap], psx[q], eng=("act" if q % 2 else "dve"))
                if ps_ == 0:
                    for ji, (j0, nj) in enumerate(jts):
                        gs_, pg_ = gsel[yi][0:nj, ji:ji + 1], psg[ji][0:nj, 0:2]
                        p.op("dve", lambda e_, gs_=gs_, pg_=pg_: e_.reduce_sum(gs_.ap, pg_.ap, AX.X), [pg_], [gs_])
        for ft_ in range(16):
            a32, b32, a16, b16 = wgf[wi % 2], wuf[wi % 2], wgb[wi % 2], wub[wi % 2]
            wi += 1
            p.dma(a32, p.dview("w_gate", IN["w_gate"].ap()[l, e, :, ft_ * 128:(ft_ + 1) * 128].rearrange("(kt q) f -> q kt f", q=128)))
            p.dma(b32, p.dview("w_up", IN["w_up"].ap()[l, e, :, ft_ * 128:(ft_ + 1) * 128].rearrange("(kt q) f -> q kt f", q=128)))
            o.cp(a16, a32, eng="pool")
            o.cp(b16, b32, eng="pool")
            for (tiles, cap, jts, yi) in SETS:
                pg = p.ps(4 + (ft_ % 2) * 2, cap); pu = p.ps(5 + (ft_ % 2) * 2, cap)
                for kt in range(8):
                    o.mm(pg, a16[:, kt, :], xsT[yi][:, kt, 0:cap], start=(kt == 0), stop=(kt == 7))
                for kt in range(8):
                    o.mm(pu, b16[:, kt, :], xsT[yi][:, kt, 0:cap], start=(kt == 0), stop=(kt == 7))
                s_ = sg[(ft_ + yi) % 2]
                o.act(s_[:, 0:cap], pg, AF.Silu)
                o.tt(hT[yi][:, ft_, 0:cap], s_[:, 0:cap], pu, ALU.mult)
        yt_ = ysb[e % 2]
        for half in range(2):
            hs_ = slice(half * 512, (half + 1) * 512)
            psd = {}
            for (tiles, cap, jts, yi) in SETS:
                for ji in range(len(jts)):
                    psd[(yi, ji)] = p.ps(ji if yi == 0 else 4, 512)
            for fp_ in range(8):
                d32, d16 = wdf[di % 2], wdb[di % 2]
                di += 1
                p.dma(d32, p.dview("w_down", IN["w_down"].ap()[l, e, fp_ * 256:(fp_ + 1) * 256, half * 512:(half + 1) * 512].rearrange("(a q) d -> q a d", q=128)))
                o.cp(d16, d32, eng="pool")
                for f2 in range(2):
                    ft_ = fp_ * 2 + f2
                    for (tiles, cap, jts, yi) in SETS:
                        for ji, (j0, nj) in enumerate(jts):
                            o.mm(psd[(yi, ji)][0:nj, :], hT[yi][:, ft_, j0:j0 + nj], d16[:, f2, :], start=(ft_ == 0), stop=(ft_ == 15))
            for (tiles, cap, jts, yi) in SETS:
                for ji, (j0, nj) in enumerate(jts):
                    yt = yt_[ji if yi == 0 else 4]
                    o.ts(yt[0:nj, hs_], psd[(yi, ji)][0:nj, :], gsel[yi][0:nj, ji:ji + 1], None, ALU.mult)
        for (tiles, cap, jts, yi) in SETS:
            for ji, (j0, nj) in enumerate(jts):
                yt = yt_[ji if yi == 0 else 4]
                p.dma(p.dview("YSD%d" % yi, k.YSD[yi].ap()[e, j0:j0 + nj, :]), yt[0:nj, :], q="pool")
    p.pop()
    p.pop()
    p.push()
    g2 = [p.sb([D]), p.sb([D])]
    for r in range(2):
        rowbc(k, g2[r], k.MOD.ap()[l, r:r + 1, 5120:6144], "MOD")
    lng = p.sb([D]); lnb = p.sb([D])
    rowbc(k, lng, IN["ln2_g"].ap()[l:l + 1, :], "ln2_g")
    rowbc(k, lnb, IN["ln2_b"].ap()[l:l + 1, :], "ln2_b")
    selT = [[p.sb([512], BF16) for _ in range(4)] for _ in range(16)]
    ysl = [p.sb([4, 128], BF16) for _ in range(4)]
    fT = p.sb([8, 512])
    hh = [p.sb([D]) for _ in range(2)]
    uu = [p.sb([D]) for _ in range(2)]
    s8 = p.sb([8]); junk2 = p.sb([D])
    groups = [(0, 256, SETS[1])] + [(256 + 512 * i, 512, SETS[0]) for i in range(8)]
    yl = 0
    for (t0, n, (tiles, cap, jts, yi)) in groups:
        for e in range(16):
            psr = p.ps(e % 2, n)
            o.mm(psr, C["OH%d" % e][0:16, :], posmT[0:16, t0:t0 + n])
            for ji, (j0, nj) in enumerate(jts):
                o.ts(selT[e][ji][0:nj, 0:n], psr[0:nj, :], IC[0:nj, ji:ji + 1], None, ALU.is_equal)
        for dt_ in range(8):
            psf = p.ps(2 + dt_ % 2, n)
            for e in range(16):
                y_ = ysl[yl % 4]; yl += 1
                nrow = jts[-1][0] + jts[-1][1]
                if yi == 0:
                    p.dma(y_, p.dview("YSD0", k.YSD[0].ap()[e, :, dt_ * 128:(dt_ + 1) * 128].rearrange("(jt q) d -> q jt d", q=128)))
                else:
                    p.dma(y_[0:32, 0, :], p.dview("YSD1", k.YSD[1].ap()[e, 0:32, dt_ * 128:(dt_ + 1) * 128]))
                for ji, (j0, nj) in enumerate(jts):
                    o.mm(psf, y_[0:nj, ji, :], selT[e][ji][0:nj, 0:n], start=(e == 0 and ji == 0), stop=(e == 15 and ji == len(jts) - 1))
            o.cp(fT[:, dt_, 0:n], psf, eng=("act" if dt_ % 2 else "dve"))
        for st_ in range(n // 128):
            t = t0 // 128 + st_
            r = 1 if t < NCTX else 0
            tok = slice(t * 128, (t + 1) * 128)
            h = hh[st_ % 2]; u = uu[st_ % 2]
            p.dma(h, p.dview("H", k.H.ap()[tok, :], t * 128 * D, (t + 1) * 128 * D))
            for half in range(2):
                ps = p.ps(4 + half)
                for q in range(4):
                    dt_ = half * 4 + q
                    o.tr(ps[:, q * 128:(q + 1) * 128], fT[:, dt_, st_ * 128:(st_ + 1) * 128], C["IDENT"])
                hs_ = slice(half * 512, (half + 1) * 512)
                if dbg_here(k, "f"):
                    o.cp(junk2[:, hs_], ps, eng="act")
                o.tt(u[:, hs_], ps, g2[r][:, hs_], ALU.mult)
            if dbg_here(k, "f"):
                p.dma(p.dview("y_out", k.YOUT.ap()[tok, :], t * 128 * D, (t + 1) * 128 * D), junk2, q="pool")
            o.stt(u, h, ALPHA, u, ALU.mult, ALU.add)
            layer_norm(k, u, h, lng, lnb, s8, junk2)
            p.dma(p.dview("H", k.H.ap()[tok, :], t * 128 * D, (t + 1) * 128 * D), h, q="pool")
    p.pop()
    p.pop()


_W_NAMES = ["w_mod", "b_mod", "ln1_g", "ln1_b", "ln2_g", "ln2_b", "w_router", "w_gate", "w_up", "w_down",
            "ev_w_in", "ev_w_out", "ev_conv", "ev_a_log", "ev_dt_bias", "ev_gdn_norm", "ev_ret_norm",
            "od_w_in", "od_w_out", "od_conv", "od_gate_bias", "od_mlstm_norm", "od_lam_re", "od_lam_im", "od_log_dt",
            "od_b_re", "od_b_im", "od_c_re", "od_c_im", "od_d_skip", "od_w_glu", "od_b_glu"]


def kernel(**inputs):
    nc = build(n_layers=4)
    cst = make_consts()
    retg = np.concatenate([ret_log_decay(0), ret_log_decay(1)])[None, :].astype(np.float32)
    in_maps = []
    B = inputs["x"].shape[0]
    for b in range(B):
        m = {}
        m["h0"] = np.ascontiguousarray(np.concatenate([inputs["ctx"][b], inputs["x"][b]], 0).astype(np.float32))
        m["cvec"] = np.ascontiguousarray(np.stack([inputs["c"][b], inputs["c_ctx"]], 0).astype(np.float32))
        m["cst"] = cst
        m["retg"] = retg
        for n in _W_NAMES:
            if n in nc.used_inputs:
                m[n] = np.ascontiguousarray(inputs[n], dtype=np.float32)
        in_maps.append({n: m[n] for n in nc.used_inputs})
    res = run_bass_kernel_spmd(nc, in_maps, core_ids=list(range(B)))
    out = np.stack([np.asarray(res.results[b]["y_out"])[256:] for b in range(B)], 0)
    return out.astype(np.float32)
```
